# Optimizing a Trainium2 kernel written in Bass

```python
import jax, jax.numpy as jnp
from jax import lax
import numpy as np

D_MODEL = 2048
BATCH = 1
SEQ = 8192
DEPTH = 2

CHUNK = 64
Q_BLOCK = 128
ROPE_THETA = 10000.0
EPS = 1e-6

SB_HEADS = 4
SB_HD = 128
RET_HEADS = 4
RET_DK = 128
RET_DV = 256
DIFF_HEADS = 4
DIFF_HD = 64
D_FF = 5632
CONV_W = 3

SB_W = SB_HEADS * SB_HD
RET_QK = RET_HEADS * RET_DK
RET_V = RET_HEADS * RET_DV
DIFF_QK = DIFF_HEADS * 2 * DIFF_HD
DIFF_V = DIFF_HEADS * 2 * DIFF_HD
IN_SPLITS = (SB_W, SB_W, SB_W, RET_QK, RET_QK, RET_V, RET_V, DIFF_QK, DIFF_QK, DIFF_V)
D_IN = sum(IN_SPLITS)
IN_IDX = [int(i) for i in np.cumsum(IN_SPLITS)[:-1]]
BRANCH_W = SB_W + RET_V + DIFF_V

kernel_name = "hybrid_sb_retention_diffattn_convffn_adaln"


def rms_norm(x, g):
    xf = x.astype(jnp.float32)
    y = xf * lax.rsqrt(jnp.mean(xf * xf, axis=-1, keepdims=True) + EPS)
    return (y * g.astype(jnp.float32)).astype(x.dtype)


def rope(x, pos):
    d = x.shape[-1]
    inv = ROPE_THETA ** (-jnp.arange(0, d, 2, dtype=jnp.float32) / d)
    ang = pos.astype(jnp.float32)[..., None] * inv
    ang = ang.reshape(ang.shape[:2] + (1,) * (x.ndim - 3) + (d // 2,))
    cos, sin = jnp.cos(ang), jnp.sin(ang)
    x1, x2 = jnp.split(x.astype(jnp.float32), 2, axis=-1)
    return jnp.concatenate([x1 * cos - x2 * sin, x1 * sin + x2 * cos], axis=-1).astype(x.dtype)


def stick_breaking_attention(q, k, v):
    B, S, H, d = q.shape
    nb = S // Q_BLOCK
    scale = d ** -0.5
    qb = q.reshape(B, nb, Q_BLOCK, H, d).transpose(1, 0, 3, 2, 4)
    kpos = jnp.arange(S)

    def block(args):
        qi, i = args
        qpos = i * Q_BLOCK + jnp.arange(Q_BLOCK)
        z = jnp.einsum('bhqd,bshd->bhqs', qi, k).astype(jnp.float32) * scale
        mask = kpos[None, :] < qpos[:, None]
        log_beta = jax.nn.log_sigmoid(z)
        log_fail = jnp.where(mask, jax.nn.log_sigmoid(-z), 0.0)
        between = lax.cumsum(log_fail, axis=3, reverse=True) - log_fail
        w = jnp.where(mask, jnp.exp(log_beta + between), 0.0)
        return jnp.einsum('bhqs,bshd->bqhd', w.astype(v.dtype), v)

    out = lax.map(block, (qb, jnp.arange(nb)))
    return out.transpose(1, 0, 2, 3, 4).reshape(B, S, H * d)


def retention_chunkwise(q, k, v, log_gamma):
    B, S, H, dk = q.shape
    dv = v.shape[-1]
    N = S // CHUNK
    k = k * (dk ** -0.5)
    to_chunks = lambda t: t.reshape(B, N, CHUNK, H, t.shape[-1]).transpose(1, 0, 3, 2, 4)
    qc, kc, vc = to_chunks(q), to_chunks(k), to_chunks(v)
    j = jnp.arange(CHUNK, dtype=jnp.float32)
    intra_decay = jnp.exp(jnp.abs(j[:, None] - j[None, :]) * log_gamma[:, None, None])
    xi = jnp.exp((j + 1.0) * log_gamma[:, None])[..., None]
    zeta = jnp.exp((CHUNK - 1.0 - j) * log_gamma[:, None])[..., None]
    chunk_decay = jnp.exp(CHUNK * log_gamma)[:, None, None]

    def step(state, xs):
        qi, ki, vi = xs
        scores = jnp.einsum('bhqd,bhkd->bhqk', qi, ki) * intra_decay
        intra = jnp.einsum('bhqk,bhke->bhqe', scores, vi)
        cross = jnp.einsum('bhqd,bhde->bhqe', qi, state) * xi
        state = state * chunk_decay + jnp.einsum('bhkd,bhke->bhde', ki * zeta, vi)
        return state, intra + cross

    state0 = jnp.zeros((B, H, dk, dv), jnp.float32)
    _, out = lax.scan(step, state0, (qc, kc, vc))
    return out.transpose(1, 0, 3, 2, 4).reshape(B, S, H, dv)


def head_group_norm(y, g, b):
    B, S, H, dv = y.shape
    yf = y.astype(jnp.float32)
    mu = jnp.mean(yf, axis=-1, keepdims=True)
    var = jnp.mean(jnp.square(yf - mu), axis=-1, keepdims=True)
    yn = ((yf - mu) * lax.rsqrt(var + EPS)).reshape(B, S, H * dv)
    return yn * g.astype(jnp.float32) + b.astype(jnp.float32)


def differential_attention(q, k, v, lam):
    B, S, H, _, dh = q.shape
    nb = S // Q_BLOCK
    scale = dh ** -0.5
    qb = q.reshape(B, nb, Q_BLOCK, H, 2, dh).transpose(1, 0, 3, 4, 2, 5)
    kchunk = jnp.arange(S) // CHUNK

    def block(args):
        qi, i = args
        qchunk = (i * Q_BLOCK + jnp.arange(Q_BLOCK)) // CHUNK
        mask = kchunk[None, :] <= qchunk[:, None]
        s = jnp.einsum('bhmqd,bshmd->bhmqs', qi, k).astype(jnp.float32) * scale
        p = jax.nn.softmax(jnp.where(mask, s, -jnp.inf), axis=-1)
        a = p[:, :, 0] - lam.astype(jnp.float32) * p[:, :, 1]
        return jnp.einsum('bhqs,bshe->bqhe', a.astype(v.dtype), v)

    out = lax.map(block, (qb, jnp.arange(nb)))
    return out.transpose(1, 0, 2, 3, 4).reshape(B, S, H, 2 * dh)


def causal_depthwise_conv(u, w, b):
    C = u.shape[-1]
    y = lax.conv_general_dilated(u, w[:, None, :].astype(u.dtype), window_strides=(1,),
                                 padding=[(CONV_W - 1, 0)],
                                 dimension_numbers=('NWC', 'WIO', 'NWC'),
                                 feature_group_count=C)
    return y + b


def setup_inputs(seed: int = 0) -> dict:
    key = jax.random.key(seed)
    ks = jax.random.split(key, 32)
    f32 = jnp.float32
    nrm = lambda k, shape, s: jax.random.normal(k, shape, f32) * s
    x = nrm(ks[0], (BATCH, SEQ, D_MODEL), 1.0)
    c = nrm(ks[1], (BATCH, D_MODEL), 1.0)
    positions = jnp.broadcast_to(jnp.arange(SEQ, dtype=jnp.int32), (BATCH, SEQ))
    w_ada = nrm(ks[2], (DEPTH, D_MODEL, 6 * D_MODEL), 0.3 * D_MODEL ** -0.5)
    b_ada = nrm(ks[3], (DEPTH, 6 * D_MODEL), 0.02)
    norm_mix = 1.0 + nrm(ks[4], (DEPTH, D_MODEL), 0.02)
    w_in = nrm(ks[5], (DEPTH, D_MODEL, D_IN), D_MODEL ** -0.5)
    w_gate = nrm(ks[6], (DEPTH, D_MODEL, 3 * D_MODEL), D_MODEL ** -0.5)
    ret_gn_g = 1.0 + nrm(ks[7], (DEPTH, RET_V), 0.02)
    ret_gn_b = nrm(ks[8], (DEPTH, RET_V), 0.02)
    diff_qn = 1.0 + nrm(ks[9], (DEPTH, DIFF_HD), 0.02)
    diff_kn = 1.0 + nrm(ks[10], (DEPTH, DIFF_HD), 0.02)
    diff_on = 1.0 + nrm(ks[11], (DEPTH, 2 * DIFF_HD), 0.02)
    lam_q1 = nrm(ks[12], (DEPTH, DIFF_HD), 0.1)
    lam_k1 = nrm(ks[13], (DEPTH, DIFF_HD), 0.1)
    lam_q2 = nrm(ks[14], (DEPTH, DIFF_HD), 0.1)
    lam_k2 = nrm(ks[15], (DEPTH, DIFF_HD), 0.1)
    w_branch = jnp.concatenate([
        nrm(ks[16], (DEPTH, SB_W, D_MODEL), SB_W ** -0.5),
        nrm(ks[17], (DEPTH, RET_V, D_MODEL), RET_V ** -0.5),
        nrm(ks[18], (DEPTH, DIFF_V, D_MODEL), DIFF_V ** -0.5)], axis=1)
    w_out = nrm(ks[19], (DEPTH, D_MODEL, D_MODEL), D_MODEL ** -0.5)
    norm_ffn = 1.0 + nrm(ks[20], (DEPTH, D_MODEL), 0.02)
    w_up = nrm(ks[21], (DEPTH, D_MODEL, 2 * D_FF), D_MODEL ** -0.5)
    conv_w = nrm(ks[22], (DEPTH, CONV_W, 2 * D_FF), CONV_W ** -0.5)
    conv_b = nrm(ks[23], (DEPTH, 2 * D_FF), 0.02)
    w_down = nrm(ks[24], (DEPTH, D_FF, D_MODEL), D_FF ** -0.5)
    return {"x": x, "c": c, "positions": positions, "w_ada": w_ada, "b_ada": b_ada,
            "norm_mix": norm_mix, "w_in": w_in, "w_gate": w_gate,
            "ret_gn_g": ret_gn_g, "ret_gn_b": ret_gn_b,
            "diff_qn": diff_qn, "diff_kn": diff_kn, "diff_on": diff_on,
            "lam_q1": lam_q1, "lam_k1": lam_k1, "lam_q2": lam_q2, "lam_k2": lam_k2,
            "w_branch": w_branch, "w_out": w_out, "norm_ffn": norm_ffn,
            "w_up": w_up, "conv_w": conv_w, "conv_b": conv_b, "w_down": w_down}


def reference(x, c, positions, w_ada, b_ada, norm_mix, w_in, w_gate, ret_gn_g, ret_gn_b,
              diff_qn, diff_kn, diff_on, lam_q1, lam_k1, lam_q2, lam_k2,
              w_branch, w_out, norm_ffn, w_up, conv_w, conv_b, w_down):
    B, S, _ = x.shape
    log_gamma = jnp.log(1.0 - 2.0 ** (-5.0 - jnp.arange(RET_HEADS, dtype=jnp.float32)))
    c_act = jax.nn.silu(c)
    for l in range(DEPTH):
        mod = c_act @ w_ada[l] + b_ada[l]
        shift1, scale1, gate1, shift2, scale2, gate2 = [m[:, None, :] for m in jnp.split(mod, 6, axis=-1)]

        h = rms_norm(x, norm_mix[l]) * (1.0 + scale1) + shift1
        (sb_q, sb_k, sb_v, r_q, r_k, r_v, r_g, d_q, d_k, d_v) = jnp.split(h @ w_in[l], IN_IDX, axis=-1)

        y_sb = stick_breaking_attention(sb_q.reshape(B, S, SB_HEADS, SB_HD),
                                        sb_k.reshape(B, S, SB_HEADS, SB_HD),
                                        sb_v.reshape(B, S, SB_HEADS, SB_HD))

        rq = rope(r_q.reshape(B, S, RET_HEADS, RET_DK), positions)
        rk = rope(r_k.reshape(B, S, RET_HEADS, RET_DK), positions)
        y_ret = retention_chunkwise(rq, rk, r_v.reshape(B, S, RET_HEADS, RET_DV), log_gamma)
        y_ret = (head_group_norm(y_ret, ret_gn_g[l], ret_gn_b[l]) * jax.nn.silu(r_g.astype(jnp.float32))).astype(x.dtype)

        lam_init = 0.8 - 0.6 * float(np.exp(-0.3 * l))
        lam = (jnp.exp(jnp.sum(lam_q1[l] * lam_k1[l])) - jnp.exp(jnp.sum(lam_q2[l] * lam_k2[l])) + lam_init)
        dq = rope(rms_norm(d_q.reshape(B, S, DIFF_HEADS, 2, DIFF_HD), diff_qn[l]), positions)
        dk = rope(rms_norm(d_k.reshape(B, S, DIFF_HEADS, 2, DIFF_HD), diff_kn[l]), positions)
        y_diff = differential_attention(dq, dk, d_v.reshape(B, S, DIFF_HEADS, 2 * DIFF_HD), lam)
        y_diff = (rms_norm(y_diff, diff_on[l]) * (1.0 - lam_init)).reshape(B, S, DIFF_V)

        g_sb, g_ret, g_diff = jnp.split(jax.nn.sigmoid(h @ w_gate[l]), 3, axis=-1)
        wb = w_branch[l]
        merged = (g_sb * (y_sb @ wb[:SB_W])
                  + g_ret * (y_ret @ wb[SB_W:SB_W + RET_V])
                  + g_diff * (y_diff @ wb[SB_W + RET_V:]))
        x = x + gate1 * (merged @ w_out[l])

        h2 = rms_norm(x, norm_ffn[l]) * (1.0 + scale2) + shift2
        u = causal_depthwise_conv(h2 @ w_up[l], conv_w[l], conv_b[l])
        u_gate, u_val = jnp.split(u, 2, axis=-1)
        x = x + gate2 * ((jax.nn.silu(u_gate) * u_val) @ w_down[l])
    return x
```

```python
from contextlib import ExitStack
import numpy as np
import ml_dtypes
import concourse.bass as bass
import concourse.mybir as mybir
from concourse.bass_utils import run_bass_kernel_spmd

F32, BF16, I32 = mybir.dt.float32, mybir.dt.bfloat16, mybir.dt.int32
AF = mybir.ActivationFunctionType
ALU = mybir.AluOpType
NPBF = ml_dtypes.bfloat16


class Buf:
    def __init__(self, t, name):
        self.t = t
        self.name = name
        self.w = {}
        self.r = {}

    def __getitem__(self, k):
        return self.t[k]


class Prog:
    def __init__(self, n_dma_sems=24):
        self.nc = bass.Bass("TRN2", target_bir_lowering=False)
        nc = self.nc
        self.es = ExitStack()
        self.eng = dict(pe=nc.tensor, act=nc.scalar, dve=nc.vector, pool=nc.gpsimd, sp=nc.sync)
        self.semh = {k: nc.alloc_semaphore("s_" + k) for k in self.eng}
        self.cnt = {k: 0 for k in self.eng}
        self.waited = {k: {} for k in self.eng}
        self.nd = n_dma_sems
        for i in range(self.nd):
            self.semh[("d", i)] = nc.alloc_semaphore(f"dsem{i}")
        self.dcnt = [0] * self.nd
        self.drr = 0
        self.out_tokens = []
        self.n_ins = 0

    def _u(self, name):
        self.uid = getattr(self, "uid", 0) + 1
        return f"{name}_{self.uid}"

    def sb(self, name, shape, dt):
        name = self._u(name)
        t = self.es.enter_context(self.nc.sbuf_tensor(name, list(shape), dt))
        return Buf(t, name)

    def ps(self, name, shape=(128, 512), dt=F32):
        name = self._u(name)
        t = self.es.enter_context(self.nc.psum_tensor(name, list(shape), dt))
        return Buf(t, name)

    def dram(self, name, shape, dt, kind="Internal", shared=False):
        if kind == "Internal":
            name = self._u(name)
        if shared:
            t = self.nc.dram_tensor(name, list(shape), dt, kind=kind, addr_space="Shared")
        else:
            t = self.nc.dram_tensor(name, list(shape), dt, kind=kind)
        return Buf(t.ap(), name)

    def _deps(self, reads, writes):
        deps = {}

        def add(k, v):
            if deps.get(k, 0) < v:
                deps[k] = v

        for b in reads:
            for k, v in b.w.items():
                add(k, v)
        for b in writes:
            for k, v in b.w.items():
                add(k, v)
            for k, v in b.r.items():
                add(k, v)
        return deps

    def _wait(self, e, deps):
        for k, v in deps.items():
            if k == "pe" and e == "pe":
                continue
            if self.waited[e].get(k, 0) < v:
                self.eng[e].wait_ge(self.semh[k], v)
                self.waited[e][k] = v

    def _record(self, tok, reads, writes):
        k, v = tok
        for b in reads:
            if b.r.get(k, 0) < v:
                b.r[k] = v
        for b in writes:
            if b.w.get(k, 0) < v:
                b.w[k] = v
            b.r = {}

    def op(self, e, fn, reads=(), writes=(), inc=True):
        self._wait(e, self._deps(reads, writes))
        ins = fn(self.eng[e])
        self.n_ins += 1
        if inc:
            self.cnt[e] += 1
            ins.then_inc(self.semh[e], 1)
            tok = (e, self.cnt[e])
        else:
            tok = (e, self.cnt[e] + 1)
        self._record(tok, reads, writes)
        return ins

    def dma(self, out_ap, in_ap, reads=(), writes=(), q="sp", is_output=False, **kw):
        k = self.drr
        self.drr = (self.drr + 1) % self.nd
        deps = self._deps(reads, writes)
        key = ("d", k)
        if self.dcnt[k] > 0 and deps.get(key, 0) < self.dcnt[k] * 16:
            deps[key] = self.dcnt[k] * 16
        self._wait(q, deps)
        ins = self.eng[q].dma_start(out=out_ap, in_=in_ap, **kw)
        self.n_ins += 1
        self.dcnt[k] += 1
        ins.then_inc(self.semh[key], 16)
        tok = (key, self.dcnt[k] * 16)
        self._record(tok, reads, writes)
        if is_output:
            self.out_tokens.append(tok)
        return ins

    def barrier(self):
        full = {k: v for k, v in self.cnt.items() if v > 0}
        for i in range(self.nd):
            if self.dcnt[i] > 0:
                full[("d", i)] = self.dcnt[i] * 16
        for e in self.eng:
            self._wait(e, dict(full))

    def phase_begin(self):
        self.barrier()
        self.es_outer = self.es
        self.es = ExitStack()

    def phase_end(self):
        self.barrier()
        self.es.close()
        self.es = self.es_outer

    def all_reduce(self, in_ap, out_ap, reads=(), writes=()):
        k = self.drr
        self.drr = (self.drr + 1) % self.nd
        deps = self._deps(reads, writes)
        key = ("d", k)
        if self.dcnt[k] > 0 and deps.get(key, 0) < self.dcnt[k] * 16:
            deps[key] = self.dcnt[k] * 16
        self._wait("pool", deps)
        ins = self.eng["pool"].collective_compute("AllReduce", ALU.add, replica_groups=[list(range(8))], ins=[in_ap], outs=[out_ap])
        self.dcnt[k] += 1
        ins.then_inc(self.semh[key], 16)
        tok = (key, self.dcnt[k] * 16)
        self._record(tok, reads, writes)

    def finish(self):
        final = {}
        for k, v in self.out_tokens:
            if final.get(k, 0) < v:
                final[k] = v
        self._wait("sp", final)
        self.es.close()
        return self.nc


class RR:
    def __init__(self, bufs):
        self.bufs = bufs
        self.i = 0

    def next(self):
        b = self.bufs[self.i]
        self.i = (self.i + 1) % len(self.bufs)
        return b


D = 2048
S = 8192
NCORE = 8
T = 1024
NB = 8
KC = D // 128
D_FF = 5632
EPS = 1e-6
THETA = 10000.0
LOG_GAMMA = np.log(1.0 - 2.0 ** (-5.0 - np.arange(4, dtype=np.float64)))
PI = float(np.pi)
TWO_PI = float(2 * np.pi)


def _run(nc, in_maps):
    res = run_bass_kernel_spmd(nc, in_maps, core_ids=list(range(NCORE)))
    return res.results


def build_L0():
    P = Prog()
    nc = P.nc
    c_in = P.dram("c_in", [128, KC], F32, kind="ExternalInput")
    wada = P.dram("wada", [2, 128, KC, 1536], F32, kind="ExternalInput")
    bada = P.dram("bada", [2, 1536], F32, kind="ExternalInput")
    modo = P.dram("modo", [2, 1536], F32, kind="ExternalOutput")
    ct = P.sb("ct", [128, KC], F32)
    ca = P.sb("ca", [128, KC], F32)
    wt = P.sb("wt", [128, KC, 1536], F32)
    bt = P.sb("bt", [1, 2, 1536], F32)
    ot = P.sb("ot", [1, 2, 1536], F32)
    pss = [P.ps(f"ps{i}") for i in range(3)]
    P.dma(ct[:], c_in[:, :], writes=[ct])
    P.dma(bt[0:1, :, :], bada[:, :].unsqueeze(0), writes=[bt])
    P.op("act", lambda e: e.activation(out=ca[:], in_=ct[:], func=AF.Silu), reads=[ct], writes=[ca])
    for l in range(2):
        for q in range(4):
            P.dma(wt[:, 4 * q:4 * q + 4, :], wada[l, :, 4 * q:4 * q + 4, :], writes=[wt])
        for n in range(3):
            for kc in range(KC):
                P.op("pe", lambda e, n=n, kc=kc: e.matmul(pss[n][0:1, :], lhsT=ca[:, kc:kc + 1],
                                                       rhs=wt[:, kc, n * 512:(n + 1) * 512],
                                                       start=(kc == 0), stop=(kc == KC - 1)),
                     reads=[ca, wt], writes=[pss[n]], inc=(kc == KC - 1))
            P.op("dve", lambda e, n=n, l=l: e.tensor_tensor(out=ot[0:1, l, n * 512:(n + 1) * 512], in0=pss[n][0:1, :],
                                                         in1=bt[0:1, l, n * 512:(n + 1) * 512], op=ALU.add),
                 reads=[pss[n], bt], writes=[ot])
    P.dma(modo[:, :].unsqueeze(0), ot[0:1, :, :], reads=[ot], writes=[modo], is_output=True)
    return P.finish()


def run_L0(inputs):
    nc = build_L0()
    c = np.ascontiguousarray(inputs["c"].reshape(KC, 128).T)
    in_maps = []
    for r in range(NCORE):
        w = inputs["w_ada"][:, :, r * 1536:(r + 1) * 1536]
        w = np.ascontiguousarray(w.reshape(2, KC, 128, 1536).transpose(0, 2, 1, 3))
        b = np.ascontiguousarray(inputs["b_ada"][:, r * 1536:(r + 1) * 1536])
        in_maps.append({"c_in": c, "wada": w, "bada": b})
    res = _run(nc, in_maps)
    mod = np.concatenate([res[r]["modo"] for r in range(NCORE)], axis=1)
    return mod


def load_slab(P, wst, wbf, wsrc_ap, nkc, cast_eng="pool"):
    st = wst.next()
    P.dma(st[:, 0:nkc, :], wsrc_ap, writes=[st])
    wb = wbf.next()
    if cast_eng == "act":
        P.op("act", lambda e: e.copy(out=wb[:, 0:nkc, :], in_=st[:, 0:nkc, :]), reads=[st], writes=[wb])
    else:
        P.op(cast_eng, lambda e: e.tensor_copy(out=wb[:, 0:nkc, :], in_=st[:, 0:nkc, :]), reads=[st], writes=[wb])
    return wb


def adaln_norm(P, xT, hT, ntok, gs, shift_ap_fn, ones32, psA, psB, rstd, tmpRR):
    nt = (ntok + 511) // 512
    for h in range(nt):
        t0, t1 = h * 512, min(ntok, (h + 1) * 512)
        w = t1 - t0
        pb = psA if h % 2 == 0 else psB
        for kc in range(KC):
            sq = tmpRR.next()
            P.op("act", lambda e, kc=kc, sq=sq: e.activation(out=sq[:, 0:w], in_=xT[:, kc, t0:t1], func=AF.Square),
                 reads=[xT], writes=[sq])
            P.op("pe", lambda e, kc=kc, sq=sq: e.matmul(pb[:, 0:w], lhsT=ones32[:, :], rhs=sq[:, 0:w],
                                                       start=(kc == 0), stop=(kc == KC - 1)),
                 reads=[ones32, sq], writes=[pb], inc=True)
        P.op("dve", lambda e: e.tensor_scalar(out=rstd[:, t0:t1], in0=pb[:, 0:w], scalar1=1.0 / D, scalar2=EPS,
                                              op0=ALU.mult, op1=ALU.add), reads=[pb], writes=[rstd])
        P.op("act", lambda e: e.activation(out=rstd[:, t0:t1], in_=rstd[:, t0:t1], func=AF.Sqrt), reads=[rstd], writes=[rstd])
        P.op("dve", lambda e: e.reciprocal(out=rstd[:, t0:t1], in_=rstd[:, t0:t1]), reads=[rstd], writes=[rstd])
        for kc in range(KC):
            tm = tmpRR.next()
            P.op("dve", lambda e, kc=kc, tm=tm: e.tensor_tensor(out=tm[:, 0:w], in0=xT[:, kc, t0:t1], in1=rstd[:, t0:t1],
                                                             op=ALU.mult), reads=[xT, rstd], writes=[tm])
            P.op("act", lambda e, kc=kc, tm=tm: e.activation(out=hT[:, kc, t0:t1], in_=tm[:, 0:w], func=AF.Identity,
                                                          scale=gs[:, kc:kc + 1], bias=shift_ap_fn(kc)),
                 reads=[tm, gs], writes=[hT])


LA_BF = dict(qsb=0, ksb=512, vsb=1024, rq=1536, rqx=2048, rk=2560, rkz=3072, rv=3584, dq=4608, dk=5120, dv=5632)
LA_BF_ROWS = 6144
LA_F32 = dict(rg=0, gates=1024)
LA_F32_ROWS = 1024 + 6144
N_SLAB_A = 48 + 16 + 48


def la_phase(P, d, w_bf16):
    xTd, modT, nmix, posd, cst, dec, bones, wsl, obf, of32 = (d[k] for k in
        ("xT", "modT", "nmix", "pos", "cst", "dec", "bones", "wsl", "obf", "of32"))
    out_flag = d.get("is_output", False)

    xT = P.sb("xTs", [128, KC, T], F32)
    hT = P.sb("hT", [128, KC, T], BF16)
    modS = P.sb("modS", [128, 96], F32)
    nmS = P.sb("nmS", [128, KC], F32)
    gs = P.sb("gs", [128, KC], F32)
    cS = P.sb("cS", [128, 8], F32)
    decS = P.sb("decS", [128, 8, 128], F32)
    posI = P.sb("posI", [128, T], I32)
    posF = P.sb("posF", [128, T], F32)
    cosR = P.sb("cosR", [128, T], F32)
    sinR = P.sb("sinR", [128, T], F32)
    cosD = P.sb("cosD", [128, T], F32)
    sinD = P.sb("sinD", [128, T], F32)
    cqT = P.sb("cqT", [128, T], F32)
    sqT = P.sb("sqT", [128, T], F32)
    ones32 = P.sb("ones32", [128, 128], F32)
    bonesB = P.sb("bonesB", [128, 128], BF16)
    bonesF = P.sb("bonesF", [128, 128], F32)
    rstd = P.sb("rstd", [128, T], F32)
    ang = rstd
    ckT, skT = cosD, sinD
    pib = P.sb("pib", [128, 1], F32)
    tmpRR = RR([P.sb(f"tmp{i}", [128, 512], F32) for i in range(4)])
    wst = None if w_bf16 else RR([P.sb(f"wst{i}", [128, KC, 128], F32) for i in range(2)])
    wbf = RR([P.sb(f"wbf{i}", [128, KC, 128], BF16) for i in range(4)])
    obR = RR([P.sb(f"ob{i}", [128, T], BF16) for i in range(4)])
    ofR = RR([P.sb(f"of{i}", [128, T], F32) for i in range(3)])
    sqR = RR([P.sb(f"sqb{i}", [128, 512], BF16) for i in range(2)])
    psR = RR([P.ps(f"ps{i}") for i in range(6 if w_bf16 else 8)])

    for q in range(4):
        P.dma(xT[:, 4 * q:4 * q + 4, :], xTd[:, 4 * q:4 * q + 4, :], writes=[xT])
    if w_bf16:
        P.dma(modS[:, :].rearrange("p (r j) -> p r j", r=8), modT[:, :, :], writes=[modS])
    else:
        P.dma(modS[:], modT[:, :], writes=[modS])
    P.dma(nmS[:], nmix[:, :], writes=[nmS])
    P.dma(cS[:], cst[:, :], writes=[cS])
    P.dma(decS[:], dec[:, :, :], writes=[decS])
    P.dma(posI[:], posd[:, :], writes=[posI])
    P.dma(bonesF[:], bones[:, :], writes=[bonesF])
    P.op("pool", lambda e: e.memset(ones32[:], 1.0), writes=[ones32])
    P.op("pool", lambda e: e.memset(pib[:], PI), writes=[pib])
    P.op("pool", lambda e: e.tensor_copy(out=bonesB[:], in_=bonesF[:]), reads=[bonesF], writes=[bonesB])
    P.op("dve", lambda e: e.scalar_tensor_tensor(out=gs[:], in0=modS[:, 16:32], scalar=1.0, in1=nmS[:],
                                                 op0=ALU.add, op1=ALU.mult), reads=[modS, nmS], writes=[gs])
    P.op("dve", lambda e: e.tensor_copy(out=posF[:], in_=posI[:]), reads=[posI], writes=[posF])

    def sincos(inv_col, sign_col, cosT, sinT):
        def one(outT, shift):
            P.op("dve", lambda e: e.tensor_scalar(out=ang[:], in0=posF[:], scalar1=cS[:, inv_col:inv_col + 1], scalar2=shift,
                                                  op0=ALU.mult, op1=ALU.add), reads=[posF, cS], writes=[ang])
            P.op("dve", lambda e: e.tensor_copy(out=posI[:], in_=ang[:]), reads=[ang], writes=[posI])
            P.op("dve", lambda e: e.tensor_copy(out=outT[:], in_=posI[:]), reads=[posI], writes=[outT])
            P.op("dve", lambda e: e.tensor_tensor(out=ang[:], in0=ang[:], in1=outT[:], op=ALU.subtract), reads=[ang, outT], writes=[ang])
            P.op("dve", lambda e: e.tensor_single_scalar(out=outT[:], in_=ang[:], scalar=0.5, op=ALU.is_gt), reads=[ang], writes=[outT])
            P.op("dve", lambda e: e.tensor_tensor(out=ang[:], in0=ang[:], in1=outT[:], op=ALU.subtract), reads=[ang, outT], writes=[ang])
            P.op("dve", lambda e: e.tensor_single_scalar(out=outT[:], in_=ang[:], scalar=-0.5, op=ALU.is_lt), reads=[ang], writes=[outT])
            P.op("dve", lambda e: e.tensor_tensor(out=ang[:], in0=ang[:], in1=outT[:], op=ALU.add), reads=[ang, outT], writes=[ang])
            P.op("act", lambda e: e.activation(out=outT[:], in_=ang[:], func=AF.Sin, scale=TWO_PI), reads=[ang], writes=[outT])
        one(sinT, 0.0)
        P.op("dve", lambda e: e.tensor_scalar(out=sinT[:], in0=sinT[:], scalar1=cS[:, sign_col:sign_col + 1], scalar2=None,
                                              op0=ALU.mult), reads=[sinT, cS], writes=[sinT])
        one(cosT, 0.25)

    sincos(0, 1, cosR, sinR)
    sincos(2, 3, cosD, sinD)
    P.op("dve", lambda e: e.tensor_scalar(out=cqT[:], in0=cosD[:], scalar1=cS[:, 4:5], scalar2=0.125, op0=ALU.mult, op1=ALU.mult),
         reads=[cosD, cS], writes=[cqT])
    P.op("dve", lambda e: e.tensor_scalar(out=sqT[:], in0=sinD[:], scalar1=cS[:, 5:6], scalar2=0.125, op0=ALU.mult, op1=ALU.mult),
         reads=[sinD, cS], writes=[sqT])
    P.op("dve", lambda e: e.tensor_scalar(out=ckT[:], in0=cosD[:], scalar1=cS[:, 6:7], scalar2=None, op0=ALU.mult),
         reads=[cosD, cS], writes=[ckT])
    P.op("dve", lambda e: e.tensor_scalar(out=skT[:], in0=sinD[:], scalar1=cS[:, 7:8], scalar2=None, op0=ALU.mult),
         reads=[sinD, cS], writes=[skT])

    psA, psB = psR.next(), psR.next()
    adaln_norm(P, xT, hT, T, gs, lambda kc: modS[:, kc:kc + 1], ones32, psA, psB, rstd, tmpRR)

    def proj(slab_idx):
        if w_bf16:
            wb = wbf.next()
            P.dma(wb[:, :, :], wsl[slab_idx], reads=[wsl], writes=[wb])
        else:
            wb = load_slab(P, wst, wbf, wsl[slab_idx], KC)
        outs = []
        for h in range(2):
            pb = psR.next()
            for kc in range(KC):
                P.op("pe", lambda e, kc=kc, pb=pb, h=h: e.matmul(pb[:, :], lhsT=wb[:, kc, :], rhs=hT[:, kc, h * 512:(h + 1) * 512],
                                                              start=(kc == 0), stop=(kc == KC - 1)),
                     reads=[wb, hT], writes=[pb], inc=(kc == KC - 1))
            outs.append(pb)
        return outs

    def store_bf(ob, row):
        P.dma(obf[row:row + 128, :], ob[:, :], reads=[ob], writes=[obf], is_output=out_flag)

    def store_f32(of, row):
        P.dma(of32[row:row + 128, :], of[:, :], reads=[of], writes=[of32], is_output=out_flag)

    def simple(slab_idx, row, scale=None):
        pbs = proj(slab_idx)
        ob = obR.next()
        for h in range(2):
            if scale is None:
                P.op("act", lambda e, h=h: e.copy(out=ob[:, h * 512:(h + 1) * 512], in_=pbs[h][:, :]), reads=[pbs[h]], writes=[ob])
            else:
                P.op("act", lambda e, h=h: e.mul(out=ob[:, h * 512:(h + 1) * 512], in_=pbs[h][:, :], mul=scale), reads=[pbs[h]], writes=[ob])
        store_bf(ob, row)

    def actf32(slab_idx, row, func):
        pbs = proj(slab_idx)
        of = ofR.next()
        for h in range(2):
            P.op("act", lambda e, h=h: e.activation(out=of[:, h * 512:(h + 1) * 512], in_=pbs[h][:, :], func=func),
                 reads=[pbs[h]], writes=[of])
        store_f32(of, row)

    def rope_pair(slab_idx, rot_idx, cT, sT, h):
        raise NotImplementedError

    def rope_ret(slab_idx, rot_idx, head, is_q):
        pa = proj(slab_idx)
        pbr = proj(rot_idx)
        o1, o2 = obR.next(), obR.next()
        for h in range(2):
            sl = slice(h * 512, (h + 1) * 512)
            t1, t2 = tmpRR.next(), tmpRR.next()
            P.op("dve", lambda e: e.tensor_tensor(out=t1[:, :], in0=pa[h][:, :], in1=cosR[:, sl], op=ALU.mult),
                 reads=[pa[h], cosR], writes=[t1])
            P.op("dve", lambda e: e.tensor_tensor(out=t2[:, :], in0=pbr[h][:, :], in1=sinR[:, sl], op=ALU.mult),
                 reads=[pbr[h], sinR], writes=[t2])
            P.op("pool", lambda e: e.tensor_tensor(out=t1[:, :], in0=t1[:, :], in1=t2[:, :], op=ALU.add),
                 reads=[t1, t2], writes=[t1])
            tab = decS[:, (head if is_q else 4 + head), :].unsqueeze(1).broadcast_to([128, 4, 128])
            t1v = t1[:, :].rearrange("p (b t) -> p b t", t=128)
            if is_q:
                P.op("act", lambda e: e.copy(out=o1[:, sl], in_=t1[:, :]), reads=[t1], writes=[o1])
                P.op("pool", lambda e: e.tensor_tensor(out=o2[:, sl].rearrange("p (b t) -> p b t", t=128), in0=t1v, in1=tab, op=ALU.mult),
                     reads=[t1, decS], writes=[o2])
            else:
                sc = 128.0 ** -0.5
                P.op("act", lambda e: e.mul(out=o1[:, sl], in_=t1[:, :], mul=sc), reads=[t1], writes=[o1])
                P.op("dve", lambda e: e.scalar_tensor_tensor(out=o2[:, sl].rearrange("p (b t) -> p b t", t=128), in0=t1v, scalar=sc,
                                                              in1=tab, op0=ALU.mult, op1=ALU.mult),
                     reads=[t1, decS], writes=[o2])
        if is_q:
            store_bf(o1, LA_BF["rq"] + head * 128)
            store_bf(o2, LA_BF["rqx"] + head * 128)
        else:
            store_bf(o1, LA_BF["rk"] + head * 128)
            store_bf(o2, LA_BF["rkz"] + head * 128)

    def rope_diff(slab_idx, rot_idx, head, is_q):
        pa = proj(slab_idx)
        pbr = proj(rot_idx)
        cT, sT = (cqT, sqT) if is_q else (ckT, skT)
        o1 = obR.next()
        for h in range(2):
            sl = slice(h * 512, (h + 1) * 512)
            sq = sqR.next()
            P.op("act", lambda e: e.activation(out=sq[:, :], in_=pa[h][:, :], func=AF.Square), reads=[pa[h]], writes=[sq])
            pc = psR.next()
            P.op("pe", lambda e: e.matmul(pc[:, :], lhsT=bonesB[:, :], rhs=sq[:, :], start=True, stop=True),
                 reads=[bonesB, sq], writes=[pc])
            t1, t2, t3 = tmpRR.next(), tmpRR.next(), tmpRR.next()
            P.op("dve", lambda e: e.tensor_scalar(out=t3[:, :], in0=pc[:, :], scalar1=1.0 / 64, scalar2=EPS, op0=ALU.mult, op1=ALU.add),
                 reads=[pc], writes=[t3])
            P.op("act", lambda e: e.activation(out=t3[:, :], in_=t3[:, :], func=AF.Sqrt), reads=[t3], writes=[t3])
            P.op("dve", lambda e: e.reciprocal(out=t3[:, :], in_=t3[:, :]), reads=[t3], writes=[t3])
            P.op("dve", lambda e: e.tensor_tensor(out=t1[:, :], in0=pa[h][:, :], in1=cT[:, sl], op=ALU.mult),
                 reads=[pa[h], cT], writes=[t1])
            P.op("dve", lambda e: e.tensor_tensor(out=t2[:, :], in0=pbr[h][:, :], in1=sT[:, sl], op=ALU.mult),
                 reads=[pbr[h], sT], writes=[t2])
            P.op("pool", lambda e: e.tensor_tensor(out=t1[:, :], in0=t1[:, :], in1=t2[:, :], op=ALU.add), reads=[t1, t2], writes=[t1])
            P.op("pool", lambda e: e.tensor_tensor(out=o1[:, sl], in0=t1[:, :], in1=t3[:, :], op=ALU.mult), reads=[t1, t3], writes=[o1])
        store_bf(o1, (LA_BF["dq"] if is_q else LA_BF["dk"]) + head * 128)

    for j in range(4):
        simple(j, LA_BF["qsb"] + j * 128, scale=128.0 ** -0.5)
    for j in range(4):
        simple(4 + j, LA_BF["ksb"] + j * 128)
    for j in range(4):
        simple(8 + j, LA_BF["vsb"] + j * 128)
    for j in range(4):
        rope_ret(12 + j, 48 + j, j, True)
    for j in range(4):
        rope_ret(16 + j, 52 + j, j, False)
    for j in range(8):
        simple(20 + j, LA_BF["rv"] + j * 128)
    for j in range(8):
        actf32(28 + j, LA_F32["rg"] + j * 128, AF.Silu)
    for j in range(4):
        rope_diff(36 + j, 56 + j, j, True)
    for j in range(4):
        rope_diff(40 + j, 60 + j, j, False)
    for j in range(4):
        simple(44 + j, LA_BF["dv"] + j * 128)
    for j in range(48):
        actf32(64 + j, LA_F32["gates"] + j * 128, AF.Sigmoid)


def build_LA():
    P = Prog()
    d = dict(xT=P.dram("xT", [128, KC, T], F32, kind="ExternalInput"), modT=P.dram("modT", [128, 96], F32, kind="ExternalInput"),
             nmix=P.dram("nmix", [128, KC], F32, kind="ExternalInput"), pos=P.dram("pos", [128, T], I32, kind="ExternalInput"),
             cst=P.dram("cst", [128, 8], F32, kind="ExternalInput"), dec=P.dram("dec", [128, 8, 128], F32, kind="ExternalInput"),
             bones=P.dram("bones", [128, 128], F32, kind="ExternalInput"), wsl=P.dram("wsl", [N_SLAB_A, 128, KC, 128], F32, kind="ExternalInput"),
             obf=P.dram("obf", [LA_BF_ROWS, T], BF16, kind="ExternalOutput"), of32=P.dram("of32", [LA_F32_ROWS, T], F32, kind="ExternalOutput"),
             is_output=True)
    la_phase(P, d, False)
    return P.finish()


def own_tokens(core):
    return np.concatenate([np.arange((8 * i + core) * 128, (8 * i + core + 1) * 128) for i in range(NB)])


def to_fm(x_tok):
    Tn, Fn = x_tok.shape
    return np.ascontiguousarray(x_tok.T.reshape(Fn // 128, 128, Tn).transpose(1, 0, 2))


def vec_pm(v):
    return np.ascontiguousarray(v.reshape(-1, 128).T)


def slabs(W, nkc):
    K_, N_ = W.shape
    return np.ascontiguousarray(W.reshape(nkc, 128, N_ // 128, 128).transpose(2, 1, 0, 3))


def la_consts(diff_qn, diff_kn):
    p = np.arange(128)
    inv_ret = THETA ** (-(2.0 * (p % 64)) / 128.0)
    sign_ret = np.where(p < 64, -1.0, 1.0)
    q = p % 64
    inv_diff = THETA ** (-(2.0 * (q % 32)) / 64.0)
    sign_diff = np.where(q < 32, -1.0, 1.0)
    partner = np.where(q < 32, q + 32, q - 32)
    cst = np.stack([inv_ret / (2 * np.pi), sign_ret, inv_diff / (2 * np.pi), sign_diff, diff_qn[q], diff_qn[partner], diff_kn[q], diff_kn[partner]], axis=1)
    tl = np.arange(128)
    dec = np.zeros((8, 128), np.float64)
    for h in range(4):
        dec[h] = np.exp((tl + 1.0) * LOG_GAMMA[h])
        dec[4 + h] = np.exp((127.0 - tl) * LOG_GAMMA[h])
    dec = np.broadcast_to(dec[None], (128, 8, 128))
    bones = np.zeros((128, 128), np.float32)
    bones[:64, :64] = 1.0
    bones[64:, 64:] = 1.0
    return cst.astype(np.float32), np.ascontiguousarray(dec).astype(np.float32), bones


def la_weights(w_in, w_gate):
    cols = np.arange(6144)
    perm = cols.copy()
    for base in (1536, 2048):
        for hd in range(4):
            o = base + hd * 128
            perm[o:o + 64] = np.arange(o + 64, o + 128)
            perm[o + 64:o + 128] = np.arange(o, o + 64)
    for base in (4608, 5120):
        for m in range(8):
            o = base + m * 64
            perm[o:o + 32] = np.arange(o + 32, o + 64)
            perm[o + 32:o + 64] = np.arange(o, o + 32)
    rot_cols = np.concatenate([np.arange(1536, 2048), np.arange(2048, 2560), np.arange(4608, 5120), np.arange(5120, 5632)])
    Wcat = np.concatenate([w_in, w_in[:, perm[rot_cols]], w_gate], axis=1)
    return slabs(Wcat, KC)


_CACHE = {}


def get_nc(name, fn):
    if name not in _CACHE:
        _CACHE[name] = fn()
    return _CACHE[name]


def run_LA(inputs, l, x_full, mod):
    nc = get_nc("LA", build_LA)
    cst, dec, bones = la_consts(inputs["diff_qn"][l], inputs["diff_kn"][l])
    wsl = la_weights(inputs["w_in"][l], inputs["w_gate"][l])
    modT = vec_pm(mod[l])
    nmix = vec_pm(inputs["norm_mix"][l])
    in_maps = []
    for r in range(NCORE):
        tok = own_tokens(r)
        pos = np.ascontiguousarray(np.broadcast_to(inputs["positions"][0, tok][None, :], (128, T))).astype(np.int32)
        in_maps.append(dict(xT=to_fm(x_full[tok]), modT=modT, nmix=nmix, pos=pos, cst=cst, dec=dec, bones=bones, wsl=wsl))
    res = _run(nc, in_maps)
    return [(res[r]["obf"], res[r]["of32"]) for r in range(NCORE)]


GT_V_SB, GT_V_D, GT_KZ, GT_RV = 0, 512, 1024, 1536


def lb_phase(P, c, obf, of32, GF, GT, xTd, xoutd, modS, wbr, wout, rank, lam_init, dbg=None):
    psR = c["psR"]
    ident = c["identF"]
    ysbT = c["ybr"]
    qsb = c["qown"]
    P.dma(qsb[:, :, :], obf[LA_BF["qsb"]:LA_BF["qsb"] + 512, :].rearrange("(h p) t -> p h t", p=128), writes=[qsb])
    kT, nkT, vS = c["kT"], c["nkT"], c["vS"]
    R = c["R"]
    for hd in range(4):
        P.dma(kT[:, :], GF[hd * 128:(hd + 1) * 128, :], reads=[GF], writes=[kT])
        P.dma(vS[:, :, 0:128], GT[1024:, GT_V_SB + hd * 128:GT_V_SB + (hd + 1) * 128].rearrange("(b p) d -> p b d", p=128),
              reads=[GT], writes=[vS])
        P.op("pool", lambda e: e.tensor_scalar(out=nkT[:, :], in0=kT[:, :], scalar1=-1.0, scalar2=None, op0=ALU.mult),
             reads=[kT], writes=[nkT])
        for i in range(NB):
            ng = 2 * i + 2
            Y = c["psA"][i % 2]
            P.op("pool", lambda e: e.memset(R[:, :], 0.0), writes=[R])
            first = True
            for g in range(ng - 1, -1, -1):
                diag = g >= 2 * i
                Z = psR.next()
                for j in range(4):
                    blk = 4 * g + j
                    P.op("pe", lambda e, j=j, blk=blk: e.matmul(Z[:, j * 128:(j + 1) * 128], lhsT=kT[:, blk * 128:(blk + 1) * 128],
                                                             rhs=qsb[:, hd, i * 128:(i + 1) * 128], start=True, stop=True),
                         reads=[kT, qsb], writes=[Z], inc=(j == 3))
                ef = c["efR"].next()
                P.op("act", lambda e: e.activation(out=ef[:, :], in_=Z[:, :], func=AF.Exp), reads=[Z], writes=[ef])
                sp = c["spR"].next()
                P.op("act", lambda e: e.activation(out=sp[:, :], in_=ef[:, :], func=AF.Ln, bias=c["one"][:, 0:1]),
                     reads=[ef, c["one"]], writes=[sp])
                if diag:
                    P.op("dve", lambda e: e.tensor_tensor(out=sp[:, :], in0=sp[:, :], in1=c["msb"][:, g - 2 * i, :], op=ALU.mult),
                         reads=[sp, c["msb"]], writes=[sp])
                Pb = psR.next()
                Tb = psR.next()
                for j in range(4):
                    blk = 4 * g + j
                    P.op("pe", lambda e, j=j: e.matmul(Pb[:, j * 128:(j + 1) * 128], lhsT=c["uincl"][:, :], rhs=sp[:, j * 128:(j + 1) * 128],
                                                     start=True, stop=False), reads=[c["uincl"], sp], writes=[Pb], inc=False)
                    for j2 in range(j + 1, 4):
                        P.op("pe", lambda e, j=j, j2=j2: e.matmul(Pb[:, j * 128:(j + 1) * 128], lhsT=c["onesB"][:, :], rhs=sp[:, j2 * 128:(j2 + 1) * 128],
                                                               start=False, stop=False), reads=[c["onesB"], sp], writes=[Pb], inc=False)
                    P.op("pe", lambda e, j=j, blk=blk: e.matmul(Pb[:, j * 128:(j + 1) * 128], lhsT=nkT[:, blk * 128:(blk + 1) * 128],
                                                             rhs=qsb[:, hd, i * 128:(i + 1) * 128], start=False, stop=True),
                         reads=[nkT, qsb], writes=[Pb], inc=False)
                for j in range(4):
                    P.op("pe", lambda e, j=j: e.matmul(Tb[:, 0:128], lhsT=c["onesB"][:, :], rhs=sp[:, j * 128:(j + 1) * 128],
                                                     start=(j == 0), stop=(j == 3)), reads=[c["onesB"], sp], writes=[Tb, Pb], inc=(j == 3))
                tf = c["efR"].next()
                P.op("dve", lambda e: e.tensor_tensor(out=tf[:, :].rearrange("p (b t) -> p b t", t=128),
                                                      in0=Pb[:, :].rearrange("p (b t) -> p b t", t=128),
                                                      in1=R[:, :].unsqueeze(1).broadcast_to([128, 4, 128]), op=ALU.add),
                     reads=[Pb, R], writes=[tf])
                wt = c["wR"].next()
                P.op("act", lambda e: e.activation(out=wt[:, :], in_=tf[:, :], func=AF.Exp, scale=-1.0), reads=[tf], writes=[wt])
                if diag:
                    P.op("dve", lambda e: e.tensor_tensor(out=wt[:, :], in0=wt[:, :], in1=c["msb"][:, g - 2 * i, :], op=ALU.mult),
                         reads=[wt, c["msb"]], writes=[wt])
                P.op("dve", lambda e: e.tensor_tensor(out=R[:, :], in0=R[:, :], in1=Tb[:, 0:128], op=ALU.add), reads=[R, Tb], writes=[R])
                for j in range(4):
                    blk = 4 * g + j
                    last = (g == 0 and j == 3)
                    P.op("pe", lambda e, j=j, blk=blk, first=first, last=last: e.matmul(
                        Y[:, 0:128], lhsT=vS[:, blk, 0:128], rhs=wt[:, j * 128:(j + 1) * 128], start=first, stop=last),
                        reads=[vS, wt], writes=[Y], inc=(j == 3))
                    first = False
            P.op("act", lambda e: e.copy(out=ysbT[:, hd, i * 128:(i + 1) * 128], in_=Y[:, 0:128]), reads=[Y], writes=[ysbT])

    qd = c["qown"]
    P.dma(qd[:, :, :], obf[LA_BF["dq"]:LA_BF["dq"] + 512, :].rearrange("(h p) t -> p h t", p=128), writes=[qd])
    for hd in range(4):
        P.dma(kT[:, :], GF[512 + hd * 128:512 + (hd + 1) * 128, :], reads=[GF], writes=[kT])
        P.dma(vS[:, :, 0:128], GT[1024:, GT_V_D + hd * 128:GT_V_D + (hd + 1) * 128].rearrange("(b p) d -> p b d", p=128),
              reads=[GT], writes=[vS])
        P.op("pool", lambda e: e.memset(vS[:, :, 128:129], 1.0), writes=[vS])
        for i in range(NB):
            ng = 2 * i + 2
            Ym = c["psA"]
            for gi, g in enumerate(range(ng)):
                diag = g >= 2 * i
                for m in range(2):
                    Sb = psR.next()
                    for j in range(4):
                        blk = 4 * g + j
                        P.op("pe", lambda e, j=j, blk=blk, m=m: e.matmul(
                            Sb[:, j * 128:(j + 1) * 128], lhsT=kT[m * 64:(m + 1) * 64, blk * 128:(blk + 1) * 128],
                            rhs=qd[m * 64:(m + 1) * 64, hd, i * 128:(i + 1) * 128], start=True, stop=True),
                            reads=[kT, qd], writes=[Sb], inc=(j == 3))
                    eb = c["wR"].next()
                    P.op("act", lambda e: e.activation(out=eb[:, :], in_=Sb[:, :], func=AF.Exp), reads=[Sb], writes=[eb])
                    if diag:
                        P.op("dve", lambda e: e.tensor_tensor(out=eb[:, :], in0=eb[:, :], in1=c["mdf"][:, g - 2 * i, :], op=ALU.mult),
                             reads=[eb, c["mdf"]], writes=[eb])
                    for j in range(4):
                        blk = 4 * g + j
                        P.op("pe", lambda e, j=j, blk=blk, m=m: e.matmul(
                            Ym[m][:, 0:129], lhsT=eb[:, j * 128:(j + 1) * 128], rhs=vS[:, blk, 0:129],
                            start=(g == 0 and j == 0), stop=(g == ng - 1 and j == 3)),
                            reads=[eb, vS], writes=[Ym[m]], inc=(j == 3))
            rc = c["small"].next()
            P.op("dve", lambda e: e.reciprocal(out=rc[:, 0:1], in_=Ym[0][:, 128:129]), reads=[Ym[0]], writes=[rc])
            P.op("dve", lambda e: e.reciprocal(out=rc[:, 1:2], in_=Ym[1][:, 128:129]), reads=[Ym[1]], writes=[rc])
            P.op("dve", lambda e: e.tensor_tensor(out=rc[:, 1:2], in0=rc[:, 1:2], in1=c["nlam"][:, 0:1], op=ALU.mult),
                 reads=[rc, c["nlam"]], writes=[rc])
            ya = c["tokR"].next()
            yb = c["tokR"].next()
            P.op("dve", lambda e: e.tensor_scalar(out=ya[:, 0:128], in0=Ym[0][:, 0:128], scalar1=rc[:, 0:1], scalar2=None, op0=ALU.mult),
                 reads=[Ym[0], rc], writes=[ya])
            P.op("dve", lambda e: e.scalar_tensor_tensor(out=yb[:, 0:128], in0=Ym[1][:, 0:128], scalar=rc[:, 1:2], in1=ya[:, 0:128],
                                                         op0=ALU.mult, op1=ALU.add), reads=[Ym[1], rc, ya], writes=[yb])
            P.op("act", lambda e: e.activation(out=ya[:, 0:128], in_=yb[:, 0:128], func=AF.Square, accum_out=rc[:, 2:3]),
                 reads=[yb], writes=[ya, rc])
            P.op("dve", lambda e: e.tensor_scalar(out=rc[:, 2:3], in0=rc[:, 2:3], scalar1=1.0 / 128, scalar2=EPS, op0=ALU.mult, op1=ALU.add),
                 reads=[rc], writes=[rc])
            P.op("act", lambda e: e.activation(out=rc[:, 2:3], in_=rc[:, 2:3], func=AF.Sqrt), reads=[rc], writes=[rc])
            P.op("dve", lambda e: e.reciprocal(out=rc[:, 2:3], in_=rc[:, 2:3]), reads=[rc], writes=[rc])
            P.op("dve", lambda e: e.scalar_tensor_tensor(out=ya[:, 0:128], in0=yb[:, 0:128], scalar=rc[:, 2:3], in1=c["don"][:, :],
                                                         op0=ALU.mult, op1=ALU.mult), reads=[yb, rc, c["don"]], writes=[ya])
            P.op("dve", lambda e: e.tensor_scalar(out=ya[:, 0:128], in0=ya[:, 0:128], scalar1=float(1.0 - lam_init), scalar2=None, op0=ALU.mult),
                 reads=[ya], writes=[ya])
            Tp = psR.next()
            P.op("pe", lambda e: e.transpose(Tp[:, 0:128], ya[:, 0:128], ident[:, :]), reads=[ya, ident], writes=[Tp])
            P.op("act", lambda e: e.copy(out=ysbT[:, 12 + hd, i * 128:(i + 1) * 128], in_=Tp[:, 0:128]), reads=[Tp], writes=[ysbT])

    rq, rqx, rk, rvO, st32, stB = c["rq"], c["rqx"], c["rk"], c["rvO"], c["st32"], c["stB"]
    GTb = GT[:, :].rearrange("(b p) d -> b p d", p=128)
    if isinstance(rank, int):
        P.dma(c["RS"][:, :].rearrange("(b p) d -> b p d", p=128), GTb[rank:rank + 65, :, GT_KZ:GT_KZ + 1536], reads=[GT], writes=[c["RS"]])
    else:
        P.dma(c["RS"][:, :].rearrange("(b p) d -> b p d", p=128), GTb[bass.ds(rank, 65), :, GT_KZ:GT_KZ + 1536],
              reads=[GT], writes=[c["RS"]], q="pool")
    rgT = c["gch"]
    for hd in range(4):
        P.dma(rq[:, :], obf[LA_BF["rq"] + hd * 128:LA_BF["rq"] + (hd + 1) * 128, :], writes=[rq])
        P.dma(rqx[:, :], obf[LA_BF["rqx"] + hd * 128:LA_BF["rqx"] + (hd + 1) * 128, :], writes=[rqx])
        P.dma(rk[:, :], obf[LA_BF["rk"] + hd * 128:LA_BF["rk"] + (hd + 1) * 128, :], writes=[rk])
        P.op("pool", lambda e: e.memset(st32[:, :], 0.0), writes=[st32])
        g128 = float(np.exp(128.0 * LOG_GAMMA[hd]))
        for i in range(NB):
            kzs, vss = c["kzs"].next(), c["vss"].next()
            RSb = c["RS"][:, :].rearrange("(b p) d -> b p d", p=128)
            P.dma(kzs[:, :, :], RSb[8 * i:8 * i + 8, :, hd * 128:(hd + 1) * 128].rearrange("b p d -> p b d"),
                  reads=[c["RS"]], writes=[kzs])
            P.dma(vss[:, :, :], RSb[8 * i:8 * i + 9, :, 512 + hd * 256:512 + (hd + 1) * 256].rearrange("b p d -> p b d"),
                  reads=[c["RS"]], writes=[vss])
            for n in range(8):
                Sp = psR.next()
                P.op("pe", lambda e, n=n: e.matmul(Sp[:, 0:256], lhsT=kzs[:, n, :], rhs=vss[:, n, :], start=True, stop=True),
                     reads=[kzs, vss], writes=[Sp])
                P.op("dve", lambda e: e.scalar_tensor_tensor(out=st32[:, :], in0=st32[:, :], scalar=g128, in1=Sp[:, 0:256],
                                                             op0=ALU.mult, op1=ALU.add), reads=[st32, Sp], writes=[st32])
            P.op("act", lambda e: e.copy(out=stB[:, :], in_=st32[:, :]), reads=[st32], writes=[stB])
            Sc = psR.next()
            P.op("pe", lambda e: e.matmul(Sc[:, 0:128], lhsT=rk[:, i * 128:(i + 1) * 128], rhs=rq[:, i * 128:(i + 1) * 128], start=True, stop=True),
                 reads=[rk, rq], writes=[Sc])
            scb = c["scb"].next()
            P.op("dve", lambda e: e.tensor_tensor(out=scb[:, :], in0=Sc[:, 0:128], in1=c["dmask"][:, hd, :], op=ALU.mult),
                 reads=[Sc, c["dmask"]], writes=[scb])
            for ec in range(2):
                Yr = psR.next()
                P.op("pe", lambda e, ec=ec: e.matmul(Yr[:, 0:128], lhsT=vss[:, 8, ec * 128:(ec + 1) * 128], rhs=scb[:, :], start=True, stop=False),
                     reads=[vss, scb], writes=[Yr], inc=False)
                P.op("pe", lambda e, ec=ec: e.matmul(Yr[:, 0:128], lhsT=stB[:, ec * 128:(ec + 1) * 128], rhs=rqx[:, i * 128:(i + 1) * 128],
                                                     start=False, stop=True), reads=[stB, rqx], writes=[Yr])
                yr = c["yr"][ec]
                P.op("act", lambda e: e.copy(out=yr[:, :], in_=Yr[:, 0:128]), reads=[Yr], writes=[yr])
            St = psR.next()
            for ec in range(2):
                P.op("pe", lambda e, ec=ec: e.matmul(St[:, 0:128], lhsT=c["ones32"][:, :], rhs=c["yr"][ec][:, :], start=(ec == 0), stop=(ec == 1)),
                     reads=[c["ones32"], c["yr"][ec]], writes=[St], inc=(ec == 1))
            for ec in range(2):
                P.op("act", lambda e, ec=ec: e.activation(out=c["ysq"][ec][:, :], in_=c["yr"][ec][:, :], func=AF.Square),
                     reads=[c["yr"][ec]], writes=[c["ysq"][ec]])
            for ec in range(2):
                P.op("pe", lambda e, ec=ec: e.matmul(St[:, 128:256], lhsT=c["ones32"][:, :], rhs=c["ysq"][ec][:, :], start=(ec == 0), stop=(ec == 1)),
                     reads=[c["ones32"], c["ysq"][ec]], writes=[St], inc=(ec == 1))
            mu = c["tokR"].next()
            P.op("dve", lambda e: e.tensor_scalar(out=mu[:, 0:256], in0=St[:, 0:256], scalar1=1.0 / 256, scalar2=None, op0=ALU.mult),
                 reads=[St], writes=[mu])
            var = c["tokR"].next()
            P.op("dve", lambda e: e.tensor_tensor(out=var[:, 0:128], in0=mu[:, 0:128], in1=mu[:, 0:128], op=ALU.mult), reads=[mu], writes=[var])
            P.op("dve", lambda e: e.tensor_tensor(out=var[:, 0:128], in0=mu[:, 128:256], in1=var[:, 0:128], op=ALU.subtract),
                 reads=[mu, var], writes=[var])
            P.op("dve", lambda e: e.tensor_scalar(out=var[:, 0:128], in0=var[:, 0:128], scalar1=EPS, scalar2=None, op0=ALU.add),
                 reads=[var], writes=[var])
            P.op("act", lambda e: e.activation(out=var[:, 0:128], in_=var[:, 0:128], func=AF.Sqrt), reads=[var], writes=[var])
            P.op("dve", lambda e: e.reciprocal(out=var[:, 0:128], in_=var[:, 0:128]), reads=[var], writes=[var])
            for ec in range(2):
                yr = c["yr"][ec]
                fch = hd * 2 + ec
                P.dma(rgT[:, 0:128], of32[LA_F32["rg"] + fch * 128:LA_F32["rg"] + (fch + 1) * 128, i * 128:(i + 1) * 128], writes=[rgT])
                P.op("dve", lambda e: e.tensor_tensor(out=yr[:, :], in0=yr[:, :], in1=mu[:, 0:128], op=ALU.subtract), reads=[yr, mu], writes=[yr])
                P.op("dve", lambda e: e.tensor_tensor(out=yr[:, :], in0=yr[:, :], in1=var[:, 0:128], op=ALU.mult), reads=[yr, var], writes=[yr])
                P.op("dve", lambda e, fch=fch: e.tensor_scalar(out=yr[:, :], in0=yr[:, :], scalar1=c["gng"][:, fch:fch + 1], scalar2=c["gnb"][:, fch:fch + 1],
                                                            op0=ALU.mult, op1=ALU.add), reads=[yr, c["gng"], c["gnb"]], writes=[yr])
                P.op("dve", lambda e, fch=fch: e.tensor_tensor(out=ysbT[:, 4 + fch, i * 128:(i + 1) * 128], in0=yr[:, :], in1=rgT[:, 0:128], op=ALU.mult),
                     reads=[yr, rgT], writes=[ysbT])

    if dbg is not None:
        P.dma(dbg[:, :, :], ysbT[:, :, :], reads=[ysbT], writes=[dbg], is_output=True)
    mT = c["mT"]
    wbfR = c["wbf"]

    def slab(srcbuf, idx):
        wb = wbfR.next()
        P.dma(wb[:, :, :], srcbuf[idx], reads=[srcbuf], writes=[wb])
        return wb

    br_k = [(0, 4), (4, 12), (12, 16)]
    for fc in range(16):
        wb = slab(wbr, fc)
        gch = c["gch"]
        acc = c["acc"]
        for b, (k0, k1) in enumerate(br_k):
            P.dma(gch[:, :], of32[LA_F32["gates"] + b * 2048 + fc * 128:LA_F32["gates"] + b * 2048 + (fc + 1) * 128, :], writes=[gch])
            for h in range(2):
                Bp = psR.next()
                for kc in range(k0, k1):
                    P.op("pe", lambda e, kc=kc, h=h: e.matmul(Bp[:, :], lhsT=wb[:, kc, :], rhs=ysbT[:, kc, h * 512:(h + 1) * 512],
                                                           start=(kc == k0), stop=(kc == k1 - 1)), reads=[wb, ysbT], writes=[Bp], inc=(kc == k1 - 1))
                sl = slice(h * 512, (h + 1) * 512)
                if b == 0:
                    P.op("dve", lambda e: e.tensor_tensor(out=acc[:, sl], in0=Bp[:, :], in1=gch[:, sl], op=ALU.mult), reads=[Bp, gch], writes=[acc])
                else:
                    tmp = c["efR"].next()
                    P.op("dve", lambda e: e.tensor_tensor(out=tmp[:, :], in0=Bp[:, :], in1=gch[:, sl], op=ALU.mult), reads=[Bp, gch], writes=[tmp])
                    if b == 1:
                        P.op("pool", lambda e: e.tensor_tensor(out=acc[:, sl], in0=acc[:, sl], in1=tmp[:, :], op=ALU.add), reads=[acc, tmp], writes=[acc])
                    else:
                        P.op("pool", lambda e: e.tensor_tensor(out=mT[:, fc, sl], in0=acc[:, sl], in1=tmp[:, :], op=ALU.add), reads=[acc, tmp], writes=[mT])
    for fc in range(16):
        wb = slab(wout, fc)
        xch = c["xch"].next()
        P.dma(xch[:, :], xTd[:, fc, :], reads=[xTd], writes=[xch])
        for h in range(2):
            Op = psR.next()
            for kc in range(KC):
                P.op("pe", lambda e, kc=kc, h=h: e.matmul(Op[:, :], lhsT=wb[:, kc, :], rhs=mT[:, kc, h * 512:(h + 1) * 512],
                                                       start=(kc == 0), stop=(kc == KC - 1)), reads=[wb, mT], writes=[Op], inc=(kc == KC - 1))
            sl = slice(h * 512, (h + 1) * 512)
            P.op("dve", lambda e: e.scalar_tensor_tensor(out=xch[:, sl], in0=Op[:, :], scalar=modS[:, 32 + fc:33 + fc], in1=xch[:, sl],
                                                         op0=ALU.mult, op1=ALU.add), reads=[Op, modS, xch], writes=[xch])
        if c.get("hs") is not None:
            P.op("pool", lambda e, fc=fc: e.tensor_copy(out=c["hs"][:, :, fc, :], in_=xch[:, :].rearrange("p (i t) -> p i t", t=128)[:, :, 126:128]),
                 reads=[xch], writes=[c["hs"]])
        P.dma(xoutd[:, fc, :], xch[:, :], reads=[xch], writes=[xoutd], is_output=(dbg is not None))


def lb_consts(P, srcs):
    c = {}
    c["RS"] = P.dram("RSscr", [65 * 128, 1536], BF16)
    c["psR"] = RR([P.ps(f"lps{i}") for i in range(6)])
    c["psA"] = [P.ps("lpsA0"), P.ps("lpsA1")]
    c["ybr"] = P.sb("ybr", [128, 16, T], BF16)
    c["mT"] = P.sb("mT", [128, 16, T], BF16)
    c["qown"] = P.sb("qown", [128, 4, T], BF16)
    c["kT"] = P.sb("kTs", [128, S], BF16)
    c["nkT"] = P.sb("nkTs", [128, S], BF16)
    c["vS"] = P.sb("vS", [128, 64, 130], BF16)
    c["R"] = P.sb("Rsum", [128, 128], F32)
    c["efR"] = RR([P.sb(f"ef{i}", [128, 512], F32) for i in range(3)])
    c["spR"] = RR([P.sb(f"sp{i}", [128, 512], BF16) for i in range(2)])
    c["wR"] = RR([P.sb(f"wt{i}", [128, 512], BF16) for i in range(3)])
    c["small"] = RR([P.sb(f"sm{i}", [128, 4], F32) for i in range(2)])
    c["tokR"] = RR([P.sb(f"tok{i}", [128, 256], F32) for i in range(4)])
    c["rq"] = P.sb("rq", [128, T], BF16)
    c["rqx"] = P.sb("rqx", [128, T], BF16)
    c["rk"] = P.sb("rk", [128, T], BF16)
    c["rvO"] = P.sb("rvO", [128, 256], BF16)
    c["st32"] = P.sb("st32", [128, 256], F32)
    c["stB"] = P.sb("stB", [128, 256], BF16)
    c["kzs"] = RR([P.sb(f"kzs{i}", [128, 8, 128], BF16) for i in range(2)])
    c["vss"] = RR([P.sb(f"vss{i}", [128, 9, 256], BF16) for i in range(2)])
    c["scb"] = RR([P.sb(f"scb{i}", [128, 128], BF16) for i in range(2)])
    c["yr"] = [P.sb(f"yr{i}", [128, 128], F32) for i in range(2)]
    c["ysq"] = [P.sb(f"ysq{i}", [128, 128], F32) for i in range(2)]
    c["gch"] = P.sb("gch", [128, T], F32)
    c["acc"] = P.sb("acc", [128, T], F32)
    c["xch"] = RR([P.sb(f"xch{i}", [128, T], F32) for i in range(2)])
    c["wbf"] = RR([P.sb(f"lwbf{i}", [128, KC, 128], BF16) for i in range(3)])
    c["one"] = P.sb("one", [128, 1], F32)
    c["ones32"] = P.sb("lones32", [128, 128], F32)
    c["onesB"] = P.sb("onesB", [128, 128], BF16)
    c["uincl"] = P.sb("uincl", [128, 128], BF16)
    c["identF"] = P.sb("identF", [128, 128], F32)
    c["msb"] = P.sb("msb", [128, 2, 512], BF16)
    c["mdf"] = P.sb("mdf", [128, 2, 512], BF16)
    c["dmask"] = P.sb("dmask", [128, 4, 128], F32)
    c["don"] = P.sb("don", [128, 128], F32)
    c["gng"] = P.sb("gng", [128, 8], F32)
    c["gnb"] = P.sb("gnb", [128, 8], F32)
    c["nlam"] = P.sb("nlam", [128, 1], F32)
    c["lamv"] = P.sb("lamv", [64, 4], F32)
    P.op("pool", lambda e: e.memset(c["one"][:], 1.0), writes=[c["one"]])
    P.op("pool", lambda e: e.memset(c["ones32"][:], 1.0), writes=[c["ones32"]])
    P.op("pool", lambda e: e.memset(c["onesB"][:], 1.0), writes=[c["onesB"]])
    return c


def lb_load_consts(P, c, s):
    P.dma(c["uincl"][:], s["uincl"][:, :], writes=[c["uincl"]])
    P.dma(c["identF"][:], s["ident"][:, :], writes=[c["identF"]])
    P.dma(c["msb"][:], s["masks"][:, 0, :, :], writes=[c["msb"]])
    P.dma(c["mdf"][:], s["masks"][:, 1, :, :], writes=[c["mdf"]])
    P.dma(c["dmask"][:], s["dmask"][:, :, :], writes=[c["dmask"]])


def lb_load_layer(P, c, s, lam_init):
    P.dma(c["don"][:], s["don"][:, :], writes=[c["don"]])
    P.dma(c["gng"][:], s["gn"][:, 0:8], writes=[c["gng"]])
    P.dma(c["gnb"][:], s["gn"][:, 8:16], writes=[c["gnb"]])
    P.dma(c["lamv"][:], s["lamv"][:, :], writes=[c["lamv"]])
    pr = c["small"].next()
    P.op("dve", lambda e: e.tensor_tensor(out=pr[0:64, 0:1], in0=c["lamv"][:, 0:1], in1=c["lamv"][:, 1:2], op=ALU.mult), reads=[c["lamv"]], writes=[pr])
    P.op("dve", lambda e: e.tensor_tensor(out=pr[0:64, 1:2], in0=c["lamv"][:, 2:3], in1=c["lamv"][:, 3:4], op=ALU.mult), reads=[c["lamv"]], writes=[pr])
    Lp = c["psR"].next()
    P.op("pe", lambda e: e.matmul(Lp[:, 0:2], lhsT=c["ones32"][0:64, :], rhs=pr[0:64, 0:2], start=True, stop=True),
         reads=[c["ones32"], pr], writes=[Lp])
    ex = c["small"].next()
    P.op("act", lambda e: e.activation(out=ex[:, 0:2], in_=Lp[:, 0:2], func=AF.Exp), reads=[Lp], writes=[ex])
    P.op("dve", lambda e: e.tensor_tensor(out=c["nlam"][:, 0:1], in0=ex[:, 1:2], in1=ex[:, 0:1], op=ALU.subtract), reads=[ex], writes=[c["nlam"]])
    P.op("dve", lambda e: e.tensor_scalar(out=c["nlam"][:, 0:1], in0=c["nlam"][:, 0:1], scalar1=float(-lam_init), scalar2=None, op0=ALU.add),
         reads=[c["nlam"]], writes=[c["nlam"]])


def lb_host_consts(core):
    p = np.arange(128)
    uincl = (p[:, None] >= p[None, :]).astype(np.float32)
    ident = np.eye(128, dtype=np.float32)
    masks = np.zeros((128, 2, 2, 512), np.float32)
    for g2 in range(2):
        for j in range(4):
            kb = 4 * g2 + j
            sl = slice(j * 128, (j + 1) * 128)
            if kb < core:
                masks[:, 0, g2, sl] = 1.0
                masks[:, 1, g2, sl] = 1.0
            elif kb == core:
                masks[:, 0, g2, sl] = (p[:, None] < p[None, :])
                masks[:, 1, g2, sl] = ((p[:, None] // 64) <= (p[None, :] // 64))
    dm = np.zeros((128, 4, 128), np.float64)
    n = p[None, :]
    m = p[:, None]
    for h in range(4):
        same = (m // 64) == (n // 64)
        earlier = (m // 64) < (n // 64)
        dm[:, h, :] = np.where(same, np.exp(np.abs(n - m) * LOG_GAMMA[h]), np.where(earlier, np.exp((n - m) * LOG_GAMMA[h]), 0.0))
    return uincl.astype(NPBF), ident, masks.astype(NPBF), dm.astype(np.float32)


def build_LB_test(lam_init):
    P = Prog()
    obf = P.dram("obf", [LA_BF_ROWS, T], BF16, kind="ExternalInput")
    of32 = P.dram("of32", [LA_F32_ROWS, T], F32, kind="ExternalInput")
    GF = P.dram("GF", [1024, S], BF16, kind="ExternalInput")
    GT = P.dram("GT", [1024 + S, 2560], BF16, kind="ExternalInput")
    xTd = P.dram("xT", [128, KC, T], F32, kind="ExternalInput")
    modT = P.dram("modT", [128, 96], F32, kind="ExternalInput")
    wbr = P.dram("wbr", [16, 128, KC, 128], BF16, kind="ExternalInput")
    wout = P.dram("wout", [16, 128, KC, 128], BF16, kind="ExternalInput")
    s = {k: P.dram("i_" + k, shp, dt, kind="ExternalInput") for k, shp, dt in [
        ("uincl", [128, 128], BF16), ("ident", [128, 128], F32), ("masks", [128, 2, 2, 512], BF16), ("dmask", [128, 4, 128], F32),
        ("don", [128, 128], F32), ("gn", [128, 16], F32), ("lamv", [64, 4], F32)]}
    xoutd = P.dram("xout", [128, KC, T], F32, kind="ExternalOutput")
    dbg = P.dram("dbg", [128, 16, T], BF16, kind="ExternalOutput")
    modS = P.sb("modS", [128, 96], F32)
    P.dma(modS[:], modT[:, :], writes=[modS])
    c = lb_consts(P, s)
    lb_load_consts(P, c, s)
    lb_load_layer(P, c, s, lam_init)
    rank = P.nc.gpsimd.partition_id()
    lb_phase(P, c, obf, of32, GF, GT, xTd, xoutd, modS, wbr, wout, rank, lam_init, dbg=dbg)
    return P.finish()


def lc_alloc(P):
    c = {}
    c["xT"] = P.sb("c_xT", [128, KC, T], F32)
    c["hT"] = P.sb("c_hT", [128, KC, T], BF16)
    c["xh"] = P.sb("c_xh", [128, KC, 16], F32)
    c["hh"] = P.sb("c_hh", [128, KC, 16], BF16)
    c["actT"] = P.sb("c_actT", [128, 44, 512], BF16)
    c["U"] = RR([P.sb(f"c_U{i}", [128, 4, 130], F32) for i in range(3)])
    c["Y"] = RR([P.sb(f"c_Y{i}", [128, 4, 128], F32) for i in range(4)])
    c["wup"] = RR([P.sb(f"c_wup{i}", [128, KC, 128], BF16) for i in range(3)])
    c["wdn"] = RR([P.sb(f"c_wdn{i}", [128, 48, 128], BF16) for i in range(1)])
    c["xo"] = RR([P.sb(f"c_xo{i}", [128, 512], F32) for i in range(2)])
    c["gs"] = P.sb("c_gs", [128, KC], F32)
    c["nf"] = P.sb("c_nf", [128, KC], F32)
    c["cw"] = P.sb("c_cw", [128, 3, 88], F32)
    c["cb"] = P.sb("c_cb", [128, 88], F32)
    c["flag"] = P.sb("c_flag", [128, 8, 2], F32)
    c["rstd"] = P.sb("c_rstd", [128, T], F32)
    c["tmp"] = RR([P.sb(f"c_tmp{i}", [128, 512], F32) for i in range(3)])
    c["ones32"] = P.sb("c_ones32", [128, 128], F32)
    P.op("pool", lambda e: e.memset(c["ones32"][:], 1.0), writes=[c["ones32"]])
    return c


def lc_phase(P, c, psR, xmid, halo_ap, halo_buf, xout, modS, nffn_d, cw_d, cb_d, flag_d, wup, wdn, out_is_final):
    xT, hT, xh, hh, actT = c["xT"], c["hT"], c["xh"], c["hh"], c["actT"]
    for q in range(4):
        P.dma(xT[:, 4 * q:4 * q + 4, :], xmid[:, 4 * q:4 * q + 4, :], reads=[xmid], writes=[xT])
    if len(halo_ap.shape) == 3:
        P.dma(c["xhi"][:, :, :], halo_ap, reads=[halo_buf], writes=[c["xhi"]])
    else:
        P.dma(c["xhi"][:, :, :].unsqueeze(2), halo_ap, reads=[halo_buf], writes=[c["xhi"]], q="pool")
    P.op("pool", lambda e: e.tensor_copy(out=xh[:, :, :].rearrange("p k (i t) -> p k i t", t=2),
                                         in_=c["xhi"][:, :, :].rearrange("p i (k t) -> p k i t", t=2)), reads=[c["xhi"]], writes=[xh])
    P.dma(c["nf"][:], nffn_d[:, :], writes=[c["nf"]])
    P.dma(c["cw"][:], cw_d[:, :, :], writes=[c["cw"]])
    P.dma(c["cb"][:], cb_d[:, :], writes=[c["cb"]])
    P.dma(c["flag"][:], flag_d[:, :, :], writes=[c["flag"]])
    gs = c["gs"]
    P.op("dve", lambda e: e.scalar_tensor_tensor(out=gs[:], in0=modS[:, 64:80], scalar=1.0, in1=c["nf"][:], op0=ALU.add, op1=ALU.mult),
         reads=[modS, c["nf"]], writes=[gs])
    psA, psB = psR.next(), psR.next()
    adaln_norm(P, xT, hT, T, gs, lambda kc: modS[:, 48 + kc:49 + kc], c["ones32"], psA, psB, c["rstd"], c["tmp"])
    adaln_norm(P, xh, hh, 16, gs, lambda kc: modS[:, 48 + kc:49 + kc], c["ones32"], psA, psB, c["rstd"], c["tmp"])
    for hf in range(2):
        tsl = slice(hf * 512, (hf + 1) * 512)
        for j in range(44):
            Ys = []
            for which in range(2):
                sidx = j + 44 * which
                wb = c["wup"].next()
                P.dma(wb[:, :, :], wup[sidx], reads=[wup], writes=[wb])
                pm, ph = psR.next(), psR.next()
                for kc in range(KC):
                    P.op("pe", lambda e, kc=kc: e.matmul(pm[:, :], lhsT=wb[:, kc, :], rhs=hT[:, kc, tsl], start=(kc == 0), stop=(kc == KC - 1)),
                         reads=[wb, hT], writes=[pm], inc=(kc == KC - 1))
                for kc in range(KC):
                    P.op("pe", lambda e, kc=kc: e.matmul(ph[:, 0:8], lhsT=wb[:, kc, :], rhs=hh[:, kc, hf * 8:(hf + 1) * 8], start=(kc == 0), stop=(kc == KC - 1)),
                         reads=[wb, hh], writes=[ph], inc=(kc == KC - 1))
                U = c["U"].next()
                P.op("act", lambda e: e.copy(out=U[:, :, 2:130], in_=pm[:, :].rearrange("p (b t) -> p b t", t=128)), reads=[pm], writes=[U])
                P.op("dve", lambda e: e.tensor_tensor(out=U[:, :, 0:2], in0=ph[:, 0:8].rearrange("p (b t) -> p b t", t=2),
                                                      in1=c["flag"][:, hf * 4:(hf + 1) * 4, :], op=ALU.mult), reads=[ph, c["flag"]], writes=[U])
                Y = c["Y"].next()
                P.op("act", lambda e, sidx=sidx: e.activation(out=Y[:, :, :], in_=U[:, :, 2:130], func=AF.Identity,
                                                          scale=c["cw"][:, 2, sidx:sidx + 1], bias=c["cb"][:, sidx:sidx + 1]),
                     reads=[U, c["cw"], c["cb"]], writes=[Y])
                P.op("dve", lambda e, sidx=sidx: e.scalar_tensor_tensor(out=Y[:, :, :], in0=U[:, :, 1:129], scalar=c["cw"][:, 1, sidx:sidx + 1], in1=Y[:, :, :],
                                                                     op0=ALU.mult, op1=ALU.add), reads=[U, c["cw"], Y], writes=[Y])
                P.op("dve", lambda e, sidx=sidx: e.scalar_tensor_tensor(out=Y[:, :, :], in0=U[:, :, 0:128], scalar=c["cw"][:, 0, sidx:sidx + 1], in1=Y[:, :, :],
                                                                     op0=ALU.mult, op1=ALU.add), reads=[U, c["cw"], Y], writes=[Y])
                Ys.append(Y)
            sg = c["Y"].next()
            P.op("act", lambda e: e.activation(out=sg[:, :, :], in_=Ys[0][:, :, :], func=AF.Silu), reads=[Ys[0]], writes=[sg])
            P.op("pool", lambda e, j=j: e.tensor_tensor(out=actT[:, j, :].rearrange("p (b t) -> p b t", t=128), in0=sg[:, :, :], in1=Ys[1][:, :, :], op=ALU.mult),
                 reads=[sg, Ys[1]], writes=[actT])
        for fc in range(KC):
            wb = c["wdn"].next()
            P.dma(wb[:, :, :].rearrange("p (g k) n -> p g k n", g=3), wdn[fc], reads=[wdn], writes=[wb])
            po = psR.next()
            for kc in range(44):
                P.op("pe", lambda e, kc=kc: e.matmul(po[:, :], lhsT=wb[:, kc, :], rhs=actT[:, kc, :], start=(kc == 0), stop=(kc == 43)),
                     reads=[wb, actT], writes=[po], inc=(kc == 43))
            xo = c["xo"].next()
            P.op("dve", lambda e, fc=fc: e.scalar_tensor_tensor(out=xo[:, :], in0=po[:, :], scalar=modS[:, 80 + fc:81 + fc], in1=xT[:, fc, tsl],
                                                             op0=ALU.mult, op1=ALU.add), reads=[po, modS, xT], writes=[xo])
            P.dma(xout[:, fc, tsl], xo[:, :], reads=[xo], writes=[xout], is_output=out_is_final)


NU_LAYER = 280
U_OFF = dict(la=0, br=112, out=128, up=144, dn=232)
NU = 2 * NU_LAYER
NU_CORE = NU // NCORE
AR_CHUNK = 56
LAM_INIT = [0.8 - 0.6 * float(np.exp(-0.3 * l)) for l in range(2)]


def build_full():
    P = Prog()
    X = lambda n, shp, dt: P.dram(n, shp, dt, kind="ExternalInput")
    xT_in = X("xT", [128, KC, T], F32)
    pos = X("pos", [128, T], I32)
    c_in = X("c_in", [128, KC], F32)
    wada = X("wada", [2, 128, KC, 1536], F32)
    bada = X("bada", [128, 2, 12], F32)
    wsh = X("wsh", [NU_CORE, 128, KC, 128], F32)
    nmix = X("nmix", [2, 128, KC], F32)
    nffn = X("nffn", [2, 128, KC], F32)
    cst = X("cst", [2, 128, 8], F32)
    cwd = X("cw", [2, 128, 3, 88], F32)
    cbd = X("cb", [2, 128, 88], F32)
    dond = X("don", [2, 128, 128], F32)
    gnd = X("gn", [2, 128, 16], F32)
    lamd = X("lamv", [2, 64, 4], F32)
    dec = X("dec", [128, 8, 128], F32)
    bones = X("bones", [128, 128], F32)
    uincl = X("uincl", [128, 128], BF16)
    ident = X("ident", [128, 128], F32)
    masks = X("masks", [128, 2, 2, 512], BF16)
    dmask = X("dmask", [128, 4, 128], F32)
    flagd = X("flag", [128, 8, 2], F32)
    xfin = P.dram("xoutT", [128, KC, T], F32, kind="ExternalOutput")

    Wown = P.dram("Wown", [NU_CORE, 128, KC, 128], BF16)
    WinL = [P.dram(f"Win{l}", [NU_LAYER, 128, KC, 128], BF16) for l in range(2)]
    WoutL = [P.dram(f"Wout{l}", [NU_LAYER, 128, KC, 128], BF16, shared=True) for l in range(2)]
    MODin = P.dram("MODin", [8, 128, 2, 12], F32)
    MOD = P.dram("MOD", [8, 128, 2, 12], F32, shared=True)
    GFin = P.dram("GFin", [1024, S], BF16)
    GF = P.dram("GF", [1024, S], BF16, shared=True)
    GTin = P.dram("GTin", [1024 + S, 2560], BF16)
    GT = P.dram("GT", [1024 + S, 2560], BF16, shared=True)
    HLin = P.dram("HLin", [65, 128, KC, 2], F32)
    HL = P.dram("HL", [65, 128, KC, 2], F32, shared=True)
    OT = P.dram("OT", [T, 2560], BF16)
    obf = P.dram("obf", [LA_BF_ROWS, T], BF16)
    of32 = P.dram("of32", [LA_F32_ROWS, T], F32)
    xmid = P.dram("xmid", [128, KC, T], F32)
    xnext = P.dram("xnext", [128, KC, T], F32)
    rank = P.nc.gpsimd.partition_id()

    P.phase_begin()
    Zt = P.sb("Zt", [128, 8192], BF16)
    Zf = P.sb("Zf", [128, 65 * 32], F32)
    P.op("pool", lambda e: e.memset(Zt[:], 0.0), writes=[Zt])
    P.op("pool", lambda e: e.memset(Zf[:], 0.0), writes=[Zf])
    for Win in WinL:
        for u in range(0, NU_LAYER, 4):
            P.dma(Win[u:u + 4].rearrange("u p k n -> p u (k n)"), Zt[:, :].rearrange("p (u x) -> p u x", u=4), reads=[Zt], writes=[Win])
    for r0 in range(0, 1024, 128):
        P.dma(GFin[r0:r0 + 128, :], Zt[:, :], reads=[Zt], writes=[GFin])
    for b0 in range(0, 72, 3):
        P.dma(GTin[b0 * 128:(b0 + 3) * 128, :].rearrange("(b p) d -> p b d", p=128), Zt[:, 0:7680].rearrange("p (b d) -> p b d", d=2560),
              reads=[Zt], writes=[GTin])
    P.dma(HLin[:, :, :, :].rearrange("b p k t -> p b (k t)"), Zf[:, :].rearrange("p (b x) -> p b x", x=32), reads=[Zf], writes=[HLin])
    P.dma(MODin[:, :, :, :].rearrange("r p l j -> p r (l j)"), Zf[:, 0:192].rearrange("p (r x) -> p r x", x=24), reads=[Zf], writes=[MODin])
    wst = RR([P.sb(f"Wst{i}", [128, KC, 128], F32) for i in range(3)])
    wcb = RR([P.sb(f"Wcb{i}", [128, KC, 128], BF16) for i in range(3)])
    cast_engs = ["pool", "dve", "act"]
    for j in range(NU_CORE):
        st, wb = wst.next(), wcb.next()
        P.dma(st[:, :, :], wsh[j], writes=[st])
        ce = cast_engs[j % 3]
        if ce == "act":
            P.op("act", lambda e: e.copy(out=wb[:, :, :], in_=st[:, :, :]), reads=[st], writes=[wb])
        else:
            P.op(ce, lambda e: e.tensor_copy(out=wb[:, :, :], in_=st[:, :, :]), reads=[st], writes=[wb])
        P.dma(Wown[j], wb[:, :, :], reads=[wb], writes=[Wown])
    for l in range(2):
        Win, Wout = WinL[l], WoutL[l]
        P.dma(Win[:, :, :, :].rearrange("(r j) p k n -> r (j p) (k n)", r=8)[bass.ds(rank, 1), :, :],
              Wown[l * 35:(l + 1) * 35, :, :, :].rearrange("j p k n -> (j p) (k n)").unsqueeze(0), reads=[Wown], writes=[Win], q="pool")
        Win2 = Win[:, :, :, :].rearrange("u p k n -> (u p) (k n)")
        Wout2 = Wout[:, :, :, :].rearrange("u p k n -> (u p) (k n)")
        for u in range(0, NU_LAYER, AR_CHUNK):
            P.all_reduce(Win2[u * 128:(u + AR_CHUNK) * 128, :], Wout2[u * 128:(u + AR_CHUNK) * 128, :], reads=[Win], writes=[Wout])
    ct = P.sb("ct", [128, KC], F32)
    ca = P.sb("ca", [128, KC], F32)
    wt = P.sb("wadat", [128, KC, 1536], F32)
    bt = P.sb("badat", [128, 2, 12], F32)
    mo = P.sb("modown", [128, 2, 12], F32)
    pmod = P.ps("pmod")
    P.dma(ct[:], c_in[:, :], writes=[ct])
    P.dma(bt[:], bada[:, :, :], writes=[bt])
    P.op("act", lambda e: e.activation(out=ca[:], in_=ct[:], func=AF.Silu), reads=[ct], writes=[ca])
    for l in range(2):
        for q in range(4):
            P.dma(wt[:, 4 * q:4 * q + 4, :], wada[l, :, 4 * q:4 * q + 4, :], writes=[wt])
        for fch in range(12):
            for kc in range(KC):
                P.op("pe", lambda e, fch=fch, kc=kc, l=l: e.matmul(pmod[:, l * 12 + fch:l * 12 + fch + 1], lhsT=wt[:, kc, fch * 128:(fch + 1) * 128],
                                                                rhs=ca[:, kc:kc + 1], start=(kc == 0), stop=(kc == KC - 1)),
                     reads=[wt, ca], writes=[pmod], inc=(kc == KC - 1))
    P.op("dve", lambda e: e.tensor_tensor(out=mo[:, :, :], in0=pmod[:, 0:24].rearrange("p (l j) -> p l j", l=2), in1=bt[:, :, :], op=ALU.add),
         reads=[pmod, bt], writes=[mo])
    P.dma(MODin[bass.ds(rank, 1), :, :, :].rearrange("o p l j -> p o l j"), mo[:, :, :].unsqueeze(1), reads=[mo], writes=[MODin], q="pool")
    P.all_reduce(MODin[:, :, :, :].rearrange("r p l j -> (r p) (l j)"), MOD[:, :, :, :].rearrange("r p l j -> (r p) (l j)"),
                 reads=[MODin], writes=[MOD])
    P.phase_end()

    xcur = xT_in
    for l in range(2):
        u0 = 0
        Wout = WoutL[l]
        P.phase_begin()
        modTd = Buf(MOD[:, :, l, :].rearrange("r p j -> p r j"), "modTd")
        d = dict(xT=xcur, modT=modTd, nmix=Buf(nmix[l], "nm"), pos=pos, cst=Buf(cst[l], "cs"), dec=dec, bones=bones,
                 wsl=Buf(Wout[u0 + U_OFF["la"]:u0 + U_OFF["la"] + 112], "wsl"), obf=obf, of32=of32)
        la_phase(P, d, True)
        identS = P.sb("identS", [128, 128], F32)
        P.dma(identS[:], ident[:, :], writes=[identS])
        tin = RR([P.sb(f"tin{i}", [128, T], BF16) for i in range(2)])
        tf = RR([P.sb(f"tf{i}", [128, T], F32) for i in range(2)])
        tout = RR([P.sb(f"tout{i}", [128, 8, 128], BF16) for i in range(2)])
        tps = RR([P.ps(f"tps{i}") for i in range(2)])
        jobs = [(LA_BF["vsb"] + k * 128, GT_V_SB + k * 128) for k in range(4)] + [(LA_BF["dv"] + k * 128, GT_V_D + k * 128) for k in range(4)] + \
               [(LA_BF["rkz"] + k * 128, GT_KZ + k * 128) for k in range(4)] + [(LA_BF["rv"] + k * 128, GT_RV + k * 128) for k in range(8)]
        for row, col in jobs:
            ti, tff, to = tin.next(), tf.next(), tout.next()
            P.dma(ti[:, :], obf[row:row + 128, :], reads=[obf], writes=[ti])
            P.op("dve", lambda e: e.tensor_copy(out=tff[:, :], in_=ti[:, :]), reads=[ti], writes=[tff])
            for hb in range(2):
                tp = tps.next()
                for b4 in range(4):
                    b = hb * 4 + b4
                    P.op("pe", lambda e, b=b, b4=b4: e.transpose(tp[:, b4 * 128:(b4 + 1) * 128], tff[:, b * 128:(b + 1) * 128], identS[:, :]),
                         reads=[tff, identS], writes=[tp])
                P.op("act", lambda e, hb=hb: e.copy(out=to[:, hb * 4:(hb + 1) * 4, :], in_=tp[:, :].rearrange("p (b f) -> p b f", f=128)),
                     reads=[tp], writes=[to])
            P.dma(OT[:, col:col + 128].rearrange("(b p) d -> p b d", p=128), to[:, :, :], reads=[to], writes=[OT])
        GFv = GFin[:, :].rearrange("f (i r t) -> f i r t", i=8, r=8)
        P.dma(GFv[0:512, :, bass.ds(rank, 1), :], obf[LA_BF["ksb"]:LA_BF["ksb"] + 512, :].rearrange("f (i t) -> f i t", i=8).unsqueeze(2),
              reads=[obf], writes=[GFin], q="pool")
        P.dma(GFv[512:1024, :, bass.ds(rank, 1), :], obf[LA_BF["dk"]:LA_BF["dk"] + 512, :].rearrange("f (i t) -> f i t", i=8).unsqueeze(2),
              reads=[obf], writes=[GFin], q="pool")
        GTv = GTin[1024:, :].rearrange("(i r p) d -> i r p d", i=8, r=8)
        P.dma(GTv[:, bass.ds(rank, 1), :, :], OT[:, :].rearrange("(i p) d -> i p d", i=8).unsqueeze(1), reads=[OT], writes=[GTin], q="pool")
        P.all_reduce(GFin[:, :], GF[:, :], reads=[GFin], writes=[GF])
        hr = (1024 + S) // 2
        P.all_reduce(GTin[0:hr, :], GT[0:hr, :], reads=[GTin], writes=[GT])
        P.all_reduce(GTin[hr:, :], GT[hr:, :], reads=[GTin], writes=[GT])
        P.phase_end()
        P.phase_begin()
        modS = P.sb("modS", [128, 96], F32)
        P.dma(modS[:, :].rearrange("p (r j) -> p r j", r=8), MOD[:, :, l, :].rearrange("r p j -> p r j"), reads=[MOD], writes=[modS])
        srcs = dict(uincl=uincl, ident=ident, masks=masks, dmask=dmask, don=Buf(dond[l], "don"), gn=Buf(gnd[l], "gn"), lamv=Buf(lamd[l], "lamv"))
        c = lb_consts(P, srcs)
        c["hs"] = P.sb("hs", [128, 8, KC, 2], F32)
        lb_load_consts(P, c, srcs)
        lb_load_layer(P, c, srcs, LAM_INIT[l])
        lb_phase(P, c, obf, of32, GF, GT, xcur, xmid, modS, Buf(Wout[u0 + U_OFF["br"]:u0 + U_OFF["br"] + 16], "wbr"),
                 Buf(Wout[u0 + U_OFF["out"]:u0 + U_OFF["out"] + 16], "wout"), rank, LAM_INIT[l])
        HLv = HLin[1:65, :, :, :].rearrange("(i r) p k t -> p i r (k t)", i=8, r=8)
        P.dma(HLv[:, :, bass.ds(rank, 1), :], c["hs"][:, :, :, :].rearrange("p i k t -> p i (k t)").unsqueeze(2), reads=[c["hs"]], writes=[HLin], q="pool")
        P.all_reduce(HLin[:, :, :, :].rearrange("b p k t -> (b p) (k t)"), HL[:, :, :, :].rearrange("b p k t -> (b p) (k t)"), reads=[HLin], writes=[HL])
        P.phase_end()
        P.phase_begin()
        modS = P.sb("modS", [128, 96], F32)
        P.dma(modS[:, :].rearrange("p (r j) -> p r j", r=8), MOD[:, :, l, :].rearrange("r p j -> p r j"), reads=[MOD], writes=[modS])
        cc = lc_alloc(P)
        cc["xhi"] = P.sb("xhi", [128, 8, KC * 2], F32)
        psR = RR([P.ps(f"cps{i}") for i in range(8)])
        xdst = xfin if l == 1 else xnext
        HLr = HL[0:64, :, :, :].rearrange("(i r) p k t -> p i r (k t)", i=8, r=8)
        lc_phase(P, cc, psR, xmid, HLr[:, :, bass.ds(rank, 1), :], HL, xdst, modS, Buf(nffn[l], "nf"), Buf(cwd[l], "cw"), Buf(cbd[l], "cb"), flagd,
                 Buf(Wout[u0 + U_OFF["up"]:u0 + U_OFF["up"] + 88], "wup"),
                 Buf(Wout[u0 + U_OFF["dn"]:u0 + U_OFF["dn"] + 48].rearrange("(f g) p k n -> f p g k n", g=3), "wdn"), l == 1)
        P.phase_end()
        xcur = xnext
    return P.finish()


def build_single():
    P = Prog()
    X = lambda n, shp, dt: P.dram(n, shp, dt, kind="ExternalInput")
    xT_in = X("xT", [8, 128, KC, T], F32)
    posA = X("pos", [8, 128, T], I32)
    c_in = X("c_in", [128, KC], F32)
    wada = X("wada", [2, 8, 128, KC, 1536], F32)
    bada = X("bada", [8, 128, 2, 12], F32)
    wsh = X("wsh", [NU, 128, KC, 128], F32)
    nmix = X("nmix", [2, 128, KC], F32)
    nffn = X("nffn", [2, 128, KC], F32)
    cst = X("cst", [2, 128, 8], F32)
    cwd = X("cw", [2, 128, 3, 88], F32)
    cbd = X("cb", [2, 128, 88], F32)
    dond = X("don", [2, 128, 128], F32)
    gnd = X("gn", [2, 128, 16], F32)
    lamd = X("lamv", [2, 64, 4], F32)
    dec = X("dec", [128, 8, 128], F32)
    bones = X("bones", [128, 128], F32)
    uincl = X("uincl", [128, 128], BF16)
    ident = X("ident", [128, 128], F32)
    masksA = X("masks", [8, 128, 2, 2, 512], BF16)
    dmask = X("dmask", [128, 4, 128], F32)
    flagA = X("flag", [8, 128, 8, 2], F32)
    xfin = P.dram("xoutT", [8, 128, KC, T], F32, kind="ExternalOutput")

    WoutL = [P.dram(f"Wout{l}", [NU_LAYER, 128, KC, 128], BF16) for l in range(2)]
    MOD = P.dram("MOD", [8, 128, 2, 12], F32)
    GF = P.dram("GF", [1024, S], BF16)
    GT = P.dram("GT", [1024 + S, 2560], BF16)
    HL = P.dram("HL", [65, 128, KC, 2], F32)
    OT = P.dram("OT", [T, 2560], BF16)
    obfA = [P.dram(f"obf{r}", [LA_BF_ROWS, T], BF16) for r in range(8)]
    of32A = [P.dram(f"of32{r}", [LA_F32_ROWS, T], F32) for r in range(8)]
    xmidA = [P.dram(f"xmid{r}", [128, KC, T], F32) for r in range(8)]
    xnextA = [P.dram(f"xnext{r}", [128, KC, T], F32) for r in range(8)]

    P.phase_begin()
    Zt = P.sb("Zt", [128, 8192], BF16)
    Zf = P.sb("Zf", [128, 32], F32)
    P.op("pool", lambda e: e.memset(Zt[:], 0.0), writes=[Zt])
    P.op("pool", lambda e: e.memset(Zf[:], 0.0), writes=[Zf])
    for b0 in range(0, 8, 2):
        P.dma(GT[b0 * 128:(b0 + 2) * 128, :].rearrange("(b p) d -> p b d", p=128), Zt[:, 0:5120].rearrange("p (b d) -> p b d", d=2560),
              reads=[Zt], writes=[GT])
    P.dma(HL[0, :, :, :].rearrange("p k t -> p (k t)"), Zf[:, :], reads=[Zf], writes=[HL])
    wst = RR([P.sb(f"Wst{i}", [128, KC, 128], F32) for i in range(3)])
    wcb = RR([P.sb(f"Wcb{i}", [128, KC, 128], BF16) for i in range(3)])
    cast_engs = ["pool", "dve", "act"]
    for j in range(NU):
        st, wb = wst.next(), wcb.next()
        P.dma(st[:, :, :], wsh[j], writes=[st])
        ce = cast_engs[j % 3]
        if ce == "act":
            P.op("act", lambda e: e.copy(out=wb[:, :, :], in_=st[:, :, :]), reads=[st], writes=[wb])
        else:
            P.op(ce, lambda e: e.tensor_copy(out=wb[:, :, :], in_=st[:, :, :]), reads=[st], writes=[wb])
        P.dma(WoutL[j // NU_LAYER][j % NU_LAYER], wb[:, :, :], reads=[wb], writes=[WoutL[j // NU_LAYER]])
    ct = P.sb("ct", [128, KC], F32)
    ca = P.sb("ca", [128, KC], F32)
    wt = P.sb("wadat", [128, KC, 1536], F32)
    bt = P.sb("badat", [128, 2, 12], F32)
    moR = RR([P.sb(f"modown{i}", [128, 2, 12], F32) for i in range(2)])
    pmR = RR([P.ps(f"pmod{i}") for i in range(2)])
    P.dma(ct[:], c_in[:, :], writes=[ct])
    P.op("act", lambda e: e.activation(out=ca[:], in_=ct[:], func=AF.Silu), reads=[ct], writes=[ca])
    for r in range(8):
        pmod, mo = pmR.next(), moR.next()
        P.dma(bt[:], bada[r], writes=[bt])
        for l in range(2):
            for q in range(4):
                P.dma(wt[:, 4 * q:4 * q + 4, :], wada[l, r, :, 4 * q:4 * q + 4, :], writes=[wt])
            for fch in range(12):
                for kc in range(KC):
                    P.op("pe", lambda e, fch=fch, kc=kc, l=l: e.matmul(pmod[:, l * 12 + fch:l * 12 + fch + 1], lhsT=wt[:, kc, fch * 128:(fch + 1) * 128],
                                                                    rhs=ca[:, kc:kc + 1], start=(kc == 0), stop=(kc == KC - 1)),
                         reads=[wt, ca], writes=[pmod], inc=(kc == KC - 1))
        P.op("dve", lambda e: e.tensor_tensor(out=mo[:, :, :], in0=pmod[:, 0:24].rearrange("p (l j) -> p l j", l=2), in1=bt[:, :, :], op=ALU.add),
             reads=[pmod, bt], writes=[mo])
        P.dma(MOD[r], mo[:, :, :], reads=[mo], writes=[MOD])
    P.phase_end()

    xcurA = [Buf(xT_in[r], f"xin{r}") for r in range(8)]
    for l in range(2):
        u0 = 0
        Wout = WoutL[l]
        for rank in range(8):
            xcur, obf, of32 = xcurA[rank], obfA[rank], of32A[rank]
            P.phase_begin()
            modTd = Buf(MOD[:, :, l, :].rearrange("r p j -> p r j"), "modTd")
            d = dict(xT=xcur, modT=modTd, nmix=Buf(nmix[l], "nm"), pos=Buf(posA[rank], "pos"), cst=Buf(cst[l], "cs"), dec=dec, bones=bones,
                     wsl=Buf(Wout[u0 + U_OFF["la"]:u0 + U_OFF["la"] + 112], "wsl"), obf=obf, of32=of32)
            la_phase(P, d, True)
            identS = P.sb("identS", [128, 128], F32)
            P.dma(identS[:], ident[:, :], writes=[identS])
            tin = RR([P.sb(f"tin{i}", [128, T], BF16) for i in range(2)])
            tf = RR([P.sb(f"tf{i}", [128, T], F32) for i in range(2)])
            tout = RR([P.sb(f"tout{i}", [128, 8, 128], BF16) for i in range(2)])
            tps = RR([P.ps(f"tps{i}") for i in range(2)])
            jobs = [(LA_BF["vsb"] + k * 128, GT_V_SB + k * 128) for k in range(4)] + [(LA_BF["dv"] + k * 128, GT_V_D + k * 128) for k in range(4)] + \
                   [(LA_BF["rkz"] + k * 128, GT_KZ + k * 128) for k in range(4)] + [(LA_BF["rv"] + k * 128, GT_RV + k * 128) for k in range(8)]
            GTv = GT[1024:, :].rearrange("(i r p) d -> p i r d", i=8, r=8)
            for row, col in jobs:
                ti, tff, to = tin.next(), tf.next(), tout.next()
                P.dma(ti[:, :], obf[row:row + 128, :], reads=[obf], writes=[ti])
                P.op("dve", lambda e: e.tensor_copy(out=tff[:, :], in_=ti[:, :]), reads=[ti], writes=[tff])
                for hb in range(2):
                    tp = tps.next()
                    for b4 in range(4):
                        b = hb * 4 + b4
                        P.op("pe", lambda e, b=b, b4=b4: e.transpose(tp[:, b4 * 128:(b4 + 1) * 128], tff[:, b * 128:(b + 1) * 128], identS[:, :]),
                             reads=[tff, identS], writes=[tp])
                    P.op("act", lambda e, hb=hb: e.copy(out=to[:, hb * 4:(hb + 1) * 4, :], in_=tp[:, :].rearrange("p (b f) -> p b f", f=128)),
                         reads=[tp], writes=[to])
                P.dma(GTv[:, :, rank, col:col + 128], to[:, :, :], reads=[to], writes=[GT])
            GFv = GF[:, :].rearrange("f (i r t) -> f i r t", i=8, r=8)
            P.dma(GFv[0:512, :, rank, :], obf[LA_BF["ksb"]:LA_BF["ksb"] + 512, :].rearrange("f (i t) -> f i t", i=8), reads=[obf], writes=[GF])
            P.dma(GFv[512:1024, :, rank, :], obf[LA_BF["dk"]:LA_BF["dk"] + 512, :].rearrange("f (i t) -> f i t", i=8), reads=[obf], writes=[GF])
            P.phase_end()
        for rank in range(8):
            xcur, obf, of32, xmid = xcurA[rank], obfA[rank], of32A[rank], xmidA[rank]
            P.phase_begin()
            modS = P.sb("modS", [128, 96], F32)
            P.dma(modS[:, :].rearrange("p (r j) -> p r j", r=8), MOD[:, :, l, :].rearrange("r p j -> p r j"), reads=[MOD], writes=[modS])
            srcs = dict(uincl=uincl, ident=ident, masks=Buf(masksA[rank], "masks"), dmask=dmask, don=Buf(dond[l], "don"), gn=Buf(gnd[l], "gn"),
                        lamv=Buf(lamd[l], "lamv"))
            c = lb_consts(P, srcs)
            c["hs"] = P.sb("hs", [128, 8, KC, 2], F32)
            lb_load_consts(P, c, srcs)
            lb_load_layer(P, c, srcs, LAM_INIT[l])
            lb_phase(P, c, obf, of32, GF, GT, xcur, xmid, modS, Buf(Wout[u0 + U_OFF["br"]:u0 + U_OFF["br"] + 16], "wbr"),
                     Buf(Wout[u0 + U_OFF["out"]:u0 + U_OFF["out"] + 16], "wout"), rank, LAM_INIT[l])
            HLv = HL[1:65, :, :, :].rearrange("(i r) p k t -> p i r (k t)", i=8, r=8)
            P.dma(HLv[:, :, rank, :], c["hs"][:, :, :, :].rearrange("p i k t -> p i (k t)"), reads=[c["hs"]], writes=[HL])
            P.phase_end()
        for rank in range(8):
            xmid = xmidA[rank]
            P.phase_begin()
            modS = P.sb("modS", [128, 96], F32)
            P.dma(modS[:, :].rearrange("p (r j) -> p r j", r=8), MOD[:, :, l, :].rearrange("r p j -> p r j"), reads=[MOD], writes=[modS])
            cc = lc_alloc(P)
            cc["xhi"] = P.sb("xhi", [128, 8, KC * 2], F32)
            psR = RR([P.ps(f"cps{i}") for i in range(8)])
            xdst = Buf(xfin[rank], "xfin") if l == 1 else xnextA[rank]
            HLr = HL[0:64, :, :, :].rearrange("(i r) p k t -> p i r (k t)", i=8, r=8)
            lc_phase(P, cc, psR, xmid, HLr[:, :, rank, :], HL, xdst, modS, Buf(nffn[l], "nf"), Buf(cwd[l], "cw"), Buf(cbd[l], "cb"), Buf(flagA[rank], "flag"),
                     Buf(Wout[u0 + U_OFF["up"]:u0 + U_OFF["up"] + 88], "wup"),
                     Buf(Wout[u0 + U_OFF["dn"]:u0 + U_OFF["dn"] + 48].rearrange("(f g) p k n -> f p g k n", g=3), "wdn"), l == 1)
            P.phase_end()
        xcurA = xnextA
    print("n_ins", P.n_ins)
    return P.finish()


def _weight_units(inputs):
    us = []
    for l in range(2):
        us.append(la_weights(inputs["w_in"][l], inputs["w_gate"][l]))
        us.append(slabs(inputs["w_branch"][l], KC))
        us.append(slabs(inputs["w_out"][l], KC))
        us.append(slabs(inputs["w_up"][l], KC))
        wd = slabs(inputs["w_down"][l], 44)
        wdp = np.zeros((16, 128, 48, 128), np.float32)
        wdp[:, :, :44, :] = wd
        us.append(np.ascontiguousarray(wdp.reshape(16, 128, 3, 16, 128).transpose(0, 2, 1, 3, 4)).reshape(48, 128, 16, 128))
    return np.concatenate(us, axis=0)


def kernel(**inputs):
    inputs = {k: np.asarray(v) for k, v in inputs.items()}
    nc = get_nc("single", build_single)
    x = inputs["x"][0]
    units = _weight_units(inputs)
    c_in = np.ascontiguousarray(inputs["c"].reshape(KC, 128).T)
    nmix = np.stack([vec_pm(inputs["norm_mix"][l]) for l in range(2)])
    nffn = np.stack([vec_pm(inputs["norm_ffn"][l]) for l in range(2)])
    lc = [la_consts(inputs["diff_qn"][l], inputs["diff_kn"][l]) for l in range(2)]
    cst = np.stack([lc[l][0] for l in range(2)])
    dec, bones = lc[0][1], lc[0][2]
    cw = np.stack([np.ascontiguousarray(inputs["conv_w"][l].reshape(3, 88, 128).transpose(2, 0, 1)) for l in range(2)])
    cb = np.stack([vec_pm(inputs["conv_b"][l]) for l in range(2)])
    don = np.stack([np.ascontiguousarray(np.broadcast_to(inputs["diff_on"][l][None, :], (128, 128))) for l in range(2)]).astype(np.float32)
    gn = np.stack([np.concatenate([vec_pm(inputs["ret_gn_g"][l]), vec_pm(inputs["ret_gn_b"][l])], 1) for l in range(2)])
    lamv = np.stack([np.stack([inputs["lam_q1"][l], inputs["lam_k1"][l], inputs["lam_q2"][l], inputs["lam_k2"][l]], 1) for l in range(2)]).astype(np.float32)
    xT, pos, masks, flags, bada = [], [], [], [], []
    for r in range(NCORE):
        tok = own_tokens(r)
        uincl, ident, mk, dmask = lb_host_consts(r)
        flag = np.ones((128, 8, 2), np.float32)
        if r == 0:
            flag[:, 0, :] = 0.0
        xT.append(to_fm(x[tok]))
        pos.append(np.ascontiguousarray(np.broadcast_to(inputs["positions"][0, tok][None, :], (128, T))).astype(np.int32))
        masks.append(mk)
        flags.append(flag)
        bada.append(np.ascontiguousarray(inputs["b_ada"][:, r * 1536:(r + 1) * 1536].reshape(2, 12, 128).transpose(2, 0, 1)))
    wa = np.ascontiguousarray(inputs["w_ada"].reshape(2, KC, 128, 8, 1536).transpose(0, 3, 2, 1, 4))
    im = dict(xT=np.stack(xT), pos=np.stack(pos), c_in=c_in, wada=wa, bada=np.stack(bada), wsh=units,
              nmix=nmix, nffn=nffn, cst=cst, cw=cw, cb=cb, don=don, gn=gn, lamv=lamv, dec=dec, bones=bones,
              uincl=uincl, ident=ident, masks=np.stack(masks), dmask=dmask, flag=np.stack(flags))
    res = run_bass_kernel_spmd(nc, [im], core_ids=[0]).results
    xo = res[0]["xoutT"]
    out = np.zeros((S, D), np.float32)
    for r in range(NCORE):
        out[own_tokens(r)] = xo[r].transpose(2, 1, 0).reshape(T, D)
    return out[None]
```

```python
from contextlib import ExitStack
import numpy as np
import ml_dtypes
import concourse.bass as bass
import concourse.mybir as mybir
from concourse.bass_utils import run_bass_kernel_spmd

F32, BF16, I32 = mybir.dt.float32, mybir.dt.bfloat16, mybir.dt.int32
AF = mybir.ActivationFunctionType
ALU = mybir.AluOpType
NPBF = ml_dtypes.bfloat16


class Buf:
    def __init__(self, t, name):
        self.t = t
        self.name = name
        self.w = {}
        self.r = {}

    def __getitem__(self, k):
        return self.t[k]


class Prog:
    def __init__(self, n_dma_sems=24):
        self.nc = bass.Bass("TRN2", target_bir_lowering=False)
        nc = self.nc
        self.es = ExitStack()
        self.eng = dict(pe=nc.tensor, act=nc.scalar, dve=nc.vector, pool=nc.gpsimd, sp=nc.sync)
        self.semh = {k: nc.alloc_semaphore("s_" + k) for k in self.eng}
        self.cnt = {k: 0 for k in self.eng}
        self.waited = {k: {} for k in self.eng}
        self.nd = n_dma_sems
        for i in range(self.nd):
            self.semh[("d", i)] = nc.alloc_semaphore(f"dsem{i}")
        self.dcnt = [0] * self.nd
        self.drr = 0
        self.out_tokens = []
        self.n_ins = 0

    def _u(self, name):
        self.uid = getattr(self, "uid", 0) + 1
        return f"{name}_{self.uid}"

    def sb(self, name, shape, dt):
        name = self._u(name)
        t = self.es.enter_context(self.nc.sbuf_tensor(name, list(shape), dt))
        return Buf(t, name)

    def ps(self, name, shape=(128, 512), dt=F32):
        name = self._u(name)
        t = self.es.enter_context(self.nc.psum_tensor(name, list(shape), dt))
        return Buf(t, name)

    def dram(self, name, shape, dt, kind="Internal", shared=False):
        if kind == "Internal":
            name = self._u(name)
        if shared:
            t = self.nc.dram_tensor(name, list(shape), dt, kind=kind, addr_space="Shared")
        else:
            t = self.nc.dram_tensor(name, list(shape), dt, kind=kind)
        return Buf(t.ap(), name)

    def _deps(self, reads, writes):
        deps = {}

        def add(k, v):
            if deps.get(k, 0) < v:
                deps[k] = v

        for b in reads:
            for k, v in b.w.items():
                add(k, v)
        for b in writes:
            for k, v in b.w.items():
                add(k, v)
            for k, v in b.r.items():
                add(k, v)
        return deps

    def _wait(self, e, deps):
        for k, v in deps.items():
            if k == "pe" and e == "pe":
                continue
            if self.waited[e].get(k, 0) < v:
                self.eng[e].wait_ge(self.semh[k], v)
                self.waited[e][k] = v

    def _record(self, tok, reads, writes):
        k, v = tok
        for b in reads:
            if b.r.get(k, 0) < v:
                b.r[k] = v
        for b in writes:
            if b.w.get(k, 0) < v:
                b.w[k] = v
            b.r = {}

    def op(self, e, fn, reads=(), writes=(), inc=True):
        self._wait(e, self._deps(reads, writes))
        ins = fn(self.eng[e])
        self.n_ins += 1
        if inc:
            self.cnt[e] += 1
            ins.then_inc(self.semh[e], 1)
            tok = (e, self.cnt[e])
        else:
            tok = (e, self.cnt[e] + 1)
        self._record(tok, reads, writes)
        return ins

    def dma(self, out_ap, in_ap, reads=(), writes=(), q="sp", is_output=False, **kw):
        k = self.drr
        self.drr = (self.drr + 1) % self.nd
        deps = self._deps(reads, writes)
        key = ("d", k)
        if self.dcnt[k] > 0 and deps.get(key, 0) < self.dcnt[k] * 16:
            deps[key] = self.dcnt[k] * 16
        self._wait(q, deps)
        ins = self.eng[q].dma_start(out=out_ap, in_=in_ap, **kw)
        self.n_ins += 1
        self.dcnt[k] += 1
        ins.then_inc(self.semh[key], 16)
        tok = (key, self.dcnt[k] * 16)
        self._record(tok, reads, writes)
        if is_output:
            self.out_tokens.append(tok)
        return ins

    def barrier(self):
        full = {k: v for k, v in self.cnt.items() if v > 0}
        for i in range(self.nd):
            if self.dcnt[i] > 0:
                full[("d", i)] = self.dcnt[i] * 16
        for e in self.eng:
            self._wait(e, dict(full))

    def phase_begin(self):
        self.barrier()
        self.es_outer = self.es
        self.es = ExitStack()

    def phase_end(self):
        self.barrier()
        self.es.close()
        self.es = self.es_outer

    def all_reduce(self, in_ap, out_ap, reads=(), writes=()):
        k = self.drr
        self.drr = (self.drr + 1) % self.nd
        deps = self._deps(reads, writes)
        key = ("d", k)
        if self.dcnt[k] > 0 and deps.get(key, 0) < self.dcnt[k] * 16:
            deps[key] = self.dcnt[k] * 16
        self._wait("pool", deps)
        ins = self.eng["pool"].collective_compute("AllReduce", ALU.add, replica_groups=[list(range(8))], ins=[in_ap], outs=[out_ap])
        self.dcnt[k] += 1
        ins.then_inc(self.semh[key], 16)
        tok = (key, self.dcnt[k] * 16)
        self._record(tok, reads, writes)

    def finish(self):
        final = {}
        for k, v in self.out_tokens:
            if final.get(k, 0) < v:
                final[k] = v
        self._wait("sp", final)
        self.es.close()
        return self.nc


class RR:
    def __init__(self, bufs):
        self.bufs = bufs
        self.i = 0

    def next(self):
        b = self.bufs[self.i]
        self.i = (self.i + 1) % len(self.bufs)
        return b


D = 2048
S = 8192
NCORE = 8
T = 1024
NB = 8
KC = D // 128
D_FF = 5632
EPS = 1e-6
THETA = 10000.0
LOG_GAMMA = np.log(1.0 - 2.0 ** (-5.0 - np.arange(4, dtype=np.float64)))
PI = float(np.pi)
TWO_PI = float(2 * np.pi)


def _run(nc, in_maps):
    res = run_bass_kernel_spmd(nc, in_maps, core_ids=list(range(NCORE)))
    return res.results


def build_L0():
    P = Prog()
    nc = P.nc
    c_in = P.dram("c_in", [128, KC], F32, kind="ExternalInput")
    wada = P.dram("wada", [2, 128, KC, 1536], F32, kind="ExternalInput")
    bada = P.dram("bada", [2, 1536], F32, kind="ExternalInput")
    modo = P.dram("modo", [2, 1536], F32, kind="ExternalOutput")
    ct = P.sb("ct", [128, KC], F32)
    ca = P.sb("ca", [128, KC], F32)
    wt = P.sb("wt", [128, KC, 1536], F32)
    bt = P.sb("bt", [1, 2, 1536], F32)
    ot = P.sb("ot", [1, 2, 1536], F32)
    pss = [P.ps(f"ps{i}") for i in range(3)]
    P.dma(ct[:], c_in[:, :], writes=[ct])
    P.dma(bt[0:1, :, :], bada[:, :].unsqueeze(0), writes=[bt])
    P.op("act", lambda e: e.activation(out=ca[:], in_=ct[:], func=AF.Silu), reads=[ct], writes=[ca])
    for l in range(2):
        for q in range(4):
            P.dma(wt[:, 4 * q:4 * q + 4, :], wada[l, :, 4 * q:4 * q + 4, :], writes=[wt])
        for n in range(3):
            for kc in range(KC):
                P.op("pe", lambda e, n=n, kc=kc: e.matmul(pss[n][0:1, :], lhsT=ca[:, kc:kc + 1],
                                                       rhs=wt[:, kc, n * 512:(n + 1) * 512],
                                                       start=(kc == 0), stop=(kc == KC - 1)),
                     reads=[ca, wt], writes=[pss[n]], inc=(kc == KC - 1))
            P.op("dve", lambda e, n=n, l=l: e.tensor_tensor(out=ot[0:1, l, n * 512:(n + 1) * 512], in0=pss[n][0:1, :],
                                                         in1=bt[0:1, l, n * 512:(n + 1) * 512], op=ALU.add),
                 reads=[pss[n], bt], writes=[ot])
    P.dma(modo[:, :].unsqueeze(0), ot[0:1, :, :], reads=[ot], writes=[modo], is_output=True)
    return P.finish()


def run_L0(inputs):
    nc = build_L0()
    c = np.ascontiguousarray(inputs["c"].reshape(KC, 128).T)
    in_maps = []
    for r in range(NCORE):
        w = inputs["w_ada"][:, :, r * 1536:(r + 1) * 1536]
        w = np.ascontiguousarray(w.reshape(2, KC, 128, 1536).transpose(0, 2, 1, 3))
        b = np.ascontiguousarray(inputs["b_ada"][:, r * 1536:(r + 1) * 1536])
        in_maps.append({"c_in": c, "wada": w, "bada": b})
    res = _run(nc, in_maps)
    mod = np.concatenate([res[r]["modo"] for r in range(NCORE)], axis=1)
    return mod


def load_slab(P, wst, wbf, wsrc_ap, nkc, cast_eng="pool"):
    st = wst.next()
    P.dma(st[:, 0:nkc, :], wsrc_ap, writes=[st])
    wb = wbf.next()
    if cast_eng == "act":
        P.op("act", lambda e: e.copy(out=wb[:, 0:nkc, :], in_=st[:, 0:nkc, :]), reads=[st], writes=[wb])
    else:
        P.op(cast_eng, lambda e: e.tensor_copy(out=wb[:, 0:nkc, :], in_=st[:, 0:nkc, :]), reads=[st], writes=[wb])
    return wb


def adaln_norm(P, xT, hT, ntok, gs, shift_ap_fn, ones32, psA, psB, rstd, tmpRR):
    nt = (ntok + 511) // 512
    for h in range(nt):
        t0, t1 = h * 512, min(ntok, (h + 1) * 512)
        w = t1 - t0
        pb = psA if h % 2 == 0 else psB
        for kc in range(KC):
            sq = tmpRR.next()
            P.op("act", lambda e, kc=kc, sq=sq: e.activation(out=sq[:, 0:w], in_=xT[:, kc, t0:t1], func=AF.Square),
                 reads=[xT], writes=[sq])
            P.op("pe", lambda e, kc=kc, sq=sq: e.matmul(pb[:, 0:w], lhsT=ones32[:, :], rhs=sq[:, 0:w],
                                                       start=(kc == 0), stop=(kc == KC - 1)),
                 reads=[ones32, sq], writes=[pb], inc=True)
        P.op("dve", lambda e: e.tensor_scalar(out=rstd[:, t0:t1], in0=pb[:, 0:w], scalar1=1.0 / D, scalar2=EPS,
                                              op0=ALU.mult, op1=ALU.add), reads=[pb], writes=[rstd])
        P.op("act", lambda e: e.activation(out=rstd[:, t0:t1], in_=rstd[:, t0:t1], func=AF.Sqrt), reads=[rstd], writes=[rstd])
        P.op("dve", lambda e: e.reciprocal(out=rstd[:, t0:t1], in_=rstd[:, t0:t1]), reads=[rstd], writes=[rstd])
        for kc in range(KC):
            tm = tmpRR.next()
            P.op("dve", lambda e, kc=kc, tm=tm: e.tensor_tensor(out=tm[:, 0:w], in0=xT[:, kc, t0:t1], in1=rstd[:, t0:t1],
                                                             op=ALU.mult), reads=[xT, rstd], writes=[tm])
            P.op("act", lambda e, kc=kc, tm=tm: e.activation(out=hT[:, kc, t0:t1], in_=tm[:, 0:w], func=AF.Identity,
                                                          scale=gs[:, kc:kc + 1], bias=shift_ap_fn(kc)),
                 reads=[tm, gs], writes=[hT])


LA_BF = dict(qsb=0, ksb=512, vsb=1024, rq=1536, rqx=2048, rk=2560, rkz=3072, rv=3584, dq=4608, dk=5120, dv=5632)
LA_BF_ROWS = 6144
LA_F32 = dict(rg=0, gates=1024)
LA_F32_ROWS = 1024 + 6144
N_SLAB_A = 48 + 16 + 48


def la_phase(P, d, w_bf16):
    xTd, modT, nmix, posd, cst, dec, bones, wsl, obf, of32 = (d[k] for k in
        ("xT", "modT", "nmix", "pos", "cst", "dec", "bones", "wsl", "obf", "of32"))
    out_flag = d.get("is_output", False)

    xT = P.sb("xTs", [128, KC, T], F32)
    hT = P.sb("hT", [128, KC, T], BF16)
    modS = P.sb("modS", [128, 96], F32)
    nmS = P.sb("nmS", [128, KC], F32)
    gs = P.sb("gs", [128, KC], F32)
    cS = P.sb("cS", [128, 8], F32)
    decS = P.sb("decS", [128, 8, 128], F32)
    posI = P.sb("posI", [128, T], I32)
    posF = P.sb("posF", [128, T], F32)
    cosR = P.sb("cosR", [128, T], F32)
    sinR = P.sb("sinR", [128, T], F32)
    cosD = P.sb("cosD", [128, T], F32)
    sinD = P.sb("sinD", [128, T], F32)
    cqT = P.sb("cqT", [128, T], F32)
    sqT = P.sb("sqT", [128, T], F32)
    ones32 = P.sb("ones32", [128, 128], F32)
    bonesB = P.sb("bonesB", [128, 128], BF16)
    bonesF = P.sb("bonesF", [128, 128], F32)
    rstd = P.sb("rstd", [128, T], F32)
    ang = rstd
    ckT, skT = cosD, sinD
    pib = P.sb("pib", [128, 1], F32)
    tmpRR = RR([P.sb(f"tmp{i}", [128, 512], F32) for i in range(4)])
    wst = None if w_bf16 else RR([P.sb(f"wst{i}", [128, KC, 128], F32) for i in range(2)])
    wbf = RR([P.sb(f"wbf{i}", [128, KC, 128], BF16) for i in range(4)])
    obR = RR([P.sb(f"ob{i}", [128, T], BF16) for i in range(4)])
    ofR = RR([P.sb(f"of{i}", [128, T], F32) for i in range(3)])
    sqR = RR([P.sb(f"sqb{i}", [128, 512], BF16) for i in range(2)])
    psR = RR([P.ps(f"ps{i}") for i in range(6 if w_bf16 else 8)])

    for q in range(4):
        P.dma(xT[:, 4 * q:4 * q + 4, :], xTd[:, 4 * q:4 * q + 4, :], writes=[xT])
    if w_bf16:
        P.dma(modS[:, :].rearrange("p (r j) -> p r j", r=8), modT[:, :, :], writes=[modS])
    else:
        P.dma(modS[:], modT[:, :], writes=[modS])
    P.dma(nmS[:], nmix[:, :], writes=[nmS])
    P.dma(cS[:], cst[:, :], writes=[cS])
    P.dma(decS[:], dec[:, :, :], writes=[decS])
    P.dma(posI[:], posd[:, :], writes=[posI])
    P.dma(bonesF[:], bones[:, :], writes=[bonesF])
    P.op("pool", lambda e: e.memset(ones32[:], 1.0), writes=[ones32])
    P.op("pool", lambda e: e.memset(pib[:], PI), writes=[pib])
    P.op("pool", lambda e: e.tensor_copy(out=bonesB[:], in_=bonesF[:]), reads=[bonesF], writes=[bonesB])
    P.op("dve", lambda e: e.scalar_tensor_tensor(out=gs[:], in0=modS[:, 16:32], scalar=1.0, in1=nmS[:],
                                                 op0=ALU.add, op1=ALU.mult), reads=[modS, nmS], writes=[gs])
    P.op("dve", lambda e: e.tensor_copy(out=posF[:], in_=posI[:]), reads=[posI], writes=[posF])

    def sincos(inv_col, sign_col, cosT, sinT):
        def one(outT, shift):
            P.op("dve", lambda e: e.tensor_scalar(out=ang[:], in0=posF[:], scalar1=cS[:, inv_col:inv_col + 1], scalar2=shift,
                                                  op0=ALU.mult, op1=ALU.add), reads=[posF, cS], writes=[ang])
            P.op("dve", lambda e: e.tensor_copy(out=posI[:], in_=ang[:]), reads=[ang], writes=[posI])
            P.op("dve", lambda e: e.tensor_copy(out=outT[:], in_=posI[:]), reads=[posI], writes=[outT])
            P.op("dve", lambda e: e.tensor_tensor(out=ang[:], in0=ang[:], in1=outT[:], op=ALU.subtract), reads=[ang, outT], writes=[ang])
            P.op("dve", lambda e: e.tensor_single_scalar(out=outT[:], in_=ang[:], scalar=0.5, op=ALU.is_gt), reads=[ang], writes=[outT])
            P.op("dve", lambda e: e.tensor_tensor(out=ang[:], in0=ang[:], in1=outT[:], op=ALU.subtract), reads=[ang, outT], writes=[ang])
            P.op("dve", lambda e: e.tensor_single_scalar(out=outT[:], in_=ang[:], scalar=-0.5, op=ALU.is_lt), reads=[ang], writes=[outT])
            P.op("dve", lambda e: e.tensor_tensor(out=ang[:], in0=ang[:], in1=outT[:], op=ALU.add), reads=[ang, outT], writes=[ang])
            P.op("act", lambda e: e.activation(out=outT[:], in_=ang[:], func=AF.Sin, scale=TWO_PI), reads=[ang], writes=[outT])
        one(sinT, 0.0)
        P.op("dve", lambda e: e.tensor_scalar(out=sinT[:], in0=sinT[:], scalar1=cS[:, sign_col:sign_col + 1], scalar2=None,
                                              op0=ALU.mult), reads=[sinT, cS], writes=[sinT])
        one(cosT, 0.25)

    sincos(0, 1, cosR, sinR)
    sincos(2, 3, cosD, sinD)
    P.op("dve", lambda e: e.tensor_scalar(out=cqT[:], in0=cosD[:], scalar1=cS[:, 4:5], scalar2=0.125, op0=ALU.mult, op1=ALU.mult),
         reads=[cosD, cS], writes=[cqT])
    P.op("dve", lambda e: e.tensor_scalar(out=sqT[:], in0=sinD[:], scalar1=cS[:, 5:6], scalar2=0.125, op0=ALU.mult, op1=ALU.mult),
         reads=[sinD, cS], writes=[sqT])
    P.op("dve", lambda e: e.tensor_scalar(out=ckT[:], in0=cosD[:], scalar1=cS[:, 6:7], scalar2=None, op0=ALU.mult),
         reads=[cosD, cS], writes=[ckT])
    P.op("dve", lambda e: e.tensor_scalar(out=skT[:], in0=sinD[:], scalar1=cS[:, 7:8], scalar2=None, op0=ALU.mult),
         reads=[sinD, cS], writes=[skT])

    psA, psB = psR.next(), psR.next()
    adaln_norm(P, xT, hT, T, gs, lambda kc: modS[:, kc:kc + 1], ones32, psA, psB, rstd, tmpRR)

    def proj(slab_idx):
        if w_bf16:
            wb = wbf.next()
            P.dma(wb[:, :, :], wsl[slab_idx], reads=[wsl], writes=[wb])
        else:
            wb = load_slab(P, wst, wbf, wsl[slab_idx], KC)
        outs = []
        for h in range(2):
            pb = psR.next()
            for kc in range(KC):
                P.op("pe", lambda e, kc=kc, pb=pb, h=h: e.matmul(pb[:, :], lhsT=wb[:, kc, :], rhs=hT[:, kc, h * 512:(h + 1) * 512],
                                                              start=(kc == 0), stop=(kc == KC - 1)),
                     reads=[wb, hT], writes=[pb], inc=(kc == KC - 1))
            outs.append(pb)
        return outs

    def store_bf(ob, row):
        P.dma(obf[row:row + 128, :], ob[:, :], reads=[ob], writes=[obf], is_output=out_flag, q="pool")

    def store_f32(of, row):
        P.dma(of32[row:row + 128, :], of[:, :], reads=[of], writes=[of32], is_output=out_flag, q="pool")

    def simple(slab_idx, row, scale=None):
        pbs = proj(slab_idx)
        ob = obR.next()
        for h in range(2):
            if scale is None:
                P.op("act", lambda e, h=h: e.copy(out=ob[:, h * 512:(h + 1) * 512], in_=pbs[h][:, :]), reads=[pbs[h]], writes=[ob])
            else:
                P.op("act", lambda e, h=h: e.mul(out=ob[:, h * 512:(h + 1) * 512], in_=pbs[h][:, :], mul=scale), reads=[pbs[h]], writes=[ob])
        store_bf(ob, row)

    def actf32(slab_idx, row, func):
        pbs = proj(slab_idx)
        of = ofR.next()
        for h in range(2):
            P.op("act", lambda e, h=h: e.activation(out=of[:, h * 512:(h + 1) * 512], in_=pbs[h][:, :], func=func),
                 reads=[pbs[h]], writes=[of])
        store_f32(of, row)

    def rope_pair(slab_idx, rot_idx, cT, sT, h):
        raise NotImplementedError

    def rope_ret(slab_idx, rot_idx, head, is_q):
        pa = proj(slab_idx)
        pbr = proj(rot_idx)
        o1, o2 = obR.next(), obR.next()
        for h in range(2):
            sl = slice(h * 512, (h + 1) * 512)
            t1, t2 = tmpRR.next(), tmpRR.next()
            P.op("dve", lambda e: e.tensor_tensor(out=t1[:, :], in0=pa[h][:, :], in1=cosR[:, sl], op=ALU.mult),
                 reads=[pa[h], cosR], writes=[t1])
            P.op("dve", lambda e: e.tensor_tensor(out=t2[:, :], in0=pbr[h][:, :], in1=sinR[:, sl], op=ALU.mult),
                 reads=[pbr[h], sinR], writes=[t2])
            P.op("pool", lambda e: e.tensor_tensor(out=t1[:, :], in0=t1[:, :], in1=t2[:, :], op=ALU.add),
                 reads=[t1, t2], writes=[t1])
            tab = decS[:, (head if is_q else 4 + head), :].unsqueeze(1).broadcast_to([128, 4, 128])
            t1v = t1[:, :].rearrange("p (b t) -> p b t", t=128)
            if is_q:
                P.op("act", lambda e: e.copy(out=o1[:, sl], in_=t1[:, :]), reads=[t1], writes=[o1])
                P.op("pool", lambda e: e.tensor_tensor(out=o2[:, sl].rearrange("p (b t) -> p b t", t=128), in0=t1v, in1=tab, op=ALU.mult),
                     reads=[t1, decS], writes=[o2])
            else:
                sc = 128.0 ** -0.5
                P.op("act", lambda e: e.mul(out=o1[:, sl], in_=t1[:, :], mul=sc), reads=[t1], writes=[o1])
                P.op("dve", lambda e: e.scalar_tensor_tensor(out=o2[:, sl].rearrange("p (b t) -> p b t", t=128), in0=t1v, scalar=sc,
                                                              in1=tab, op0=ALU.mult, op1=ALU.mult),
                     reads=[t1, decS], writes=[o2])
        if is_q:
            store_bf(o1, LA_BF["rq"] + head * 128)
            store_bf(o2, LA_BF["rqx"] + head * 128)
        else:
            store_bf(o1, LA_BF["rk"] + head * 128)
            store_bf(o2, LA_BF["rkz"] + head * 128)

    def rope_diff(slab_idx, rot_idx, head, is_q):
        pa = proj(slab_idx)
        pbr = proj(rot_idx)
        cT, sT = (cqT, sqT) if is_q else (ckT, skT)
        o1 = obR.next()
        for h in range(2):
            sl = slice(h * 512, (h + 1) * 512)
            sq = sqR.next()
            P.op("act", lambda e: e.activation(out=sq[:, :], in_=pa[h][:, :], func=AF.Square), reads=[pa[h]], writes=[sq])
            pc = psR.next()
            P.op("pe", lambda e: e.matmul(pc[:, :], lhsT=bonesB[:, :], rhs=sq[:, :], start=True, stop=True),
                 reads=[bonesB, sq], writes=[pc])
            t1, t2, t3 = tmpRR.next(), tmpRR.next(), tmpRR.next()
            P.op("dve", lambda e: e.tensor_scalar(out=t3[:, :], in0=pc[:, :], scalar1=1.0 / 64, scalar2=EPS, op0=ALU.mult, op1=ALU.add),
                 reads=[pc], writes=[t3])
            P.op("act", lambda e: e.activation(out=t3[:, :], in_=t3[:, :], func=AF.Sqrt), reads=[t3], writes=[t3])
            P.op("dve", lambda e: e.reciprocal(out=t3[:, :], in_=t3[:, :]), reads=[t3], writes=[t3])
            P.op("dve", lambda e: e.tensor_tensor(out=t1[:, :], in0=pa[h][:, :], in1=cT[:, sl], op=ALU.mult),
                 reads=[pa[h], cT], writes=[t1])
            P.op("dve", lambda e: e.tensor_tensor(out=t2[:, :], in0=pbr[h][:, :], in1=sT[:, sl], op=ALU.mult),
                 reads=[pbr[h], sT], writes=[t2])
            P.op("pool", lambda e: e.tensor_tensor(out=t1[:, :], in0=t1[:, :], in1=t2[:, :], op=ALU.add), reads=[t1, t2], writes=[t1])
            P.op("pool", lambda e: e.tensor_tensor(out=o1[:, sl], in0=t1[:, :], in1=t3[:, :], op=ALU.mult), reads=[t1, t3], writes=[o1])
        store_bf(o1, (LA_BF["dq"] if is_q else LA_BF["dk"]) + head * 128)

    for j in range(4):
        simple(j, LA_BF["qsb"] + j * 128, scale=128.0 ** -0.5)
    for j in range(4):
        simple(4 + j, LA_BF["ksb"] + j * 128)
    for j in range(4):
        simple(8 + j, LA_BF["vsb"] + j * 128)
    for j in range(4):
        rope_ret(12 + j, 48 + j, j, True)
    for j in range(4):
        rope_ret(16 + j, 52 + j, j, False)
    for j in range(8):
        simple(20 + j, LA_BF["rv"] + j * 128)
    for j in range(8):
        actf32(28 + j, LA_F32["rg"] + j * 128, AF.Silu)
    for j in range(4):
        rope_diff(36 + j, 56 + j, j, True)
    for j in range(4):
        rope_diff(40 + j, 60 + j, j, False)
    for j in range(4):
        simple(44 + j, LA_BF["dv"] + j * 128)
    for j in range(48):
        actf32(64 + j, LA_F32["gates"] + j * 128, AF.Sigmoid)


def build_LA():
    P = Prog()
    d = dict(xT=P.dram("xT", [128, KC, T], F32, kind="ExternalInput"), modT=P.dram("modT", [128, 96], F32, kind="ExternalInput"),
             nmix=P.dram("nmix", [128, KC], F32, kind="ExternalInput"), pos=P.dram("pos", [128, T], I32, kind="ExternalInput"),
             cst=P.dram("cst", [128, 8], F32, kind="ExternalInput"), dec=P.dram("dec", [128, 8, 128], F32, kind="ExternalInput"),
             bones=P.dram("bones", [128, 128], F32, kind="ExternalInput"), wsl=P.dram("wsl", [N_SLAB_A, 128, KC, 128], F32, kind="ExternalInput"),
             obf=P.dram("obf", [LA_BF_ROWS, T], BF16, kind="ExternalOutput"), of32=P.dram("of32", [LA_F32_ROWS, T], F32, kind="ExternalOutput"),
             is_output=True)
    la_phase(P, d, False)
    return P.finish()


def own_tokens(core):
    return np.concatenate([np.arange((8 * i + core) * 128, (8 * i + core + 1) * 128) for i in range(NB)])


def to_fm(x_tok):
    Tn, Fn = x_tok.shape
    return np.ascontiguousarray(x_tok.T.reshape(Fn // 128, 128, Tn).transpose(1, 0, 2))


def vec_pm(v):
    return np.ascontiguousarray(v.reshape(-1, 128).T)


def slabs(W, nkc):
    K_, N_ = W.shape
    return np.ascontiguousarray(W.reshape(nkc, 128, N_ // 128, 128).transpose(2, 1, 0, 3))


def la_consts(diff_qn, diff_kn):
    p = np.arange(128)
    inv_ret = THETA ** (-(2.0 * (p % 64)) / 128.0)
    sign_ret = np.where(p < 64, -1.0, 1.0)
    q = p % 64
    inv_diff = THETA ** (-(2.0 * (q % 32)) / 64.0)
    sign_diff = np.where(q < 32, -1.0, 1.0)
    partner = np.where(q < 32, q + 32, q - 32)
    cst = np.stack([inv_ret / (2 * np.pi), sign_ret, inv_diff / (2 * np.pi), sign_diff, diff_qn[q], diff_qn[partner], diff_kn[q], diff_kn[partner]], axis=1)
    tl = np.arange(128)
    dec = np.zeros((8, 128), np.float64)
    for h in range(4):
        dec[h] = np.exp((tl + 1.0) * LOG_GAMMA[h])
        dec[4 + h] = np.exp((127.0 - tl) * LOG_GAMMA[h])
    dec = np.broadcast_to(dec[None], (128, 8, 128))
    bones = np.zeros((128, 128), np.float32)
    bones[:64, :64] = 1.0
    bones[64:, 64:] = 1.0
    return cst.astype(np.float32), np.ascontiguousarray(dec).astype(np.float32), bones


def la_weights(w_in, w_gate):
    cols = np.arange(6144)
    perm = cols.copy()
    for base in (1536, 2048):
        for hd in range(4):
            o = base + hd * 128
            perm[o:o + 64] = np.arange(o + 64, o + 128)
            perm[o + 64:o + 128] = np.arange(o, o + 64)
    for base in (4608, 5120):
        for m in range(8):
            o = base + m * 64
            perm[o:o + 32] = np.arange(o + 32, o + 64)
            perm[o + 32:o + 64] = np.arange(o, o + 32)
    rot_cols = np.concatenate([np.arange(1536, 2048), np.arange(2048, 2560), np.arange(4608, 5120), np.arange(5120, 5632)])
    Wcat = np.concatenate([w_in, w_in[:, perm[rot_cols]], w_gate], axis=1)
    return slabs(Wcat, KC)


_CACHE = {}


def get_nc(name, fn):
    if name not in _CACHE:
        _CACHE[name] = fn()
    return _CACHE[name]


def run_LA(inputs, l, x_full, mod):
    nc = get_nc("LA", build_LA)
    cst, dec, bones = la_consts(inputs["diff_qn"][l], inputs["diff_kn"][l])
    wsl = la_weights(inputs["w_in"][l], inputs["w_gate"][l])
    modT = vec_pm(mod[l])
    nmix = vec_pm(inputs["norm_mix"][l])
    in_maps = []
    for r in range(NCORE):
        tok = own_tokens(r)
        pos = np.ascontiguousarray(np.broadcast_to(inputs["positions"][0, tok][None, :], (128, T))).astype(np.int32)
        in_maps.append(dict(xT=to_fm(x_full[tok]), modT=modT, nmix=nmix, pos=pos, cst=cst, dec=dec, bones=bones, wsl=wsl))
    res = _run(nc, in_maps)
    return [(res[r]["obf"], res[r]["of32"]) for r in range(NCORE)]


GT_V_SB, GT_V_D, GT_KZ, GT_RV = 0, 512, 1024, 1536


def lb_phase(P, c, obf, of32, GF, GT, xTd, xoutd, modS, wbr, wout, rank, lam_init, dbg=None):
    psR = c["psR"]
    ident = c["identF"]
    ysbT = c["ybr"]
    qsb = c["qown"]
    P.dma(qsb[:, :, :], obf[LA_BF["qsb"]:LA_BF["qsb"] + 512, :].rearrange("(h p) t -> p h t", p=128), writes=[qsb])
    kT, nkT, vS = c["kT"], c["nkT"], c["vS"]
    R = c["R"]
    for hd in range(4):
        P.dma(kT[:, :], GF[hd * 128:(hd + 1) * 128, :], reads=[GF], writes=[kT])
        P.dma(vS[:, :, 0:128], GT[1024:, GT_V_SB + hd * 128:GT_V_SB + (hd + 1) * 128].rearrange("(b p) d -> p b d", p=128),
              reads=[GT], writes=[vS])
        P.op("pool", lambda e: e.tensor_scalar(out=nkT[:, :], in0=kT[:, :], scalar1=-1.0, scalar2=None, op0=ALU.mult),
             reads=[kT], writes=[nkT])
        for i in range(NB):
            ng = 2 * i + 2
            Y = c["psA"][i % 2]
            P.op("pool", lambda e: e.memset(R[:, :], 0.0), writes=[R])
            first = True
            for g in range(ng - 1, -1, -1):
                diag = g >= 2 * i
                Z = psR.next()
                for j in range(4):
                    blk = 4 * g + j
                    P.op("pe", lambda e, j=j, blk=blk: e.matmul(Z[:, j * 128:(j + 1) * 128], lhsT=kT[:, blk * 128:(blk + 1) * 128],
                                                             rhs=qsb[:, hd, i * 128:(i + 1) * 128], start=True, stop=True),
                         reads=[kT, qsb], writes=[Z], inc=(j == 3))
                ef = c["efR"].next()
                P.op("act", lambda e: e.activation(out=ef[:, :], in_=Z[:, :], func=AF.Exp), reads=[Z], writes=[ef])
                sp = c["spR"].next()
                P.op("act", lambda e: e.activation(out=sp[:, :], in_=ef[:, :], func=AF.Ln, bias=c["one"][:, 0:1]),
                     reads=[ef, c["one"]], writes=[sp])
                if diag:
                    P.op("dve", lambda e: e.tensor_tensor(out=sp[:, :], in0=sp[:, :], in1=c["msb"][:, g - 2 * i, :], op=ALU.mult),
                         reads=[sp, c["msb"]], writes=[sp])
                Pb = psR.next()
                Tb = psR.next()
                for j in range(4):
                    blk = 4 * g + j
                    P.op("pe", lambda e, j=j: e.matmul(Pb[:, j * 128:(j + 1) * 128], lhsT=c["uincl"][:, :], rhs=sp[:, j * 128:(j + 1) * 128],
                                                     start=True, stop=False), reads=[c["uincl"], sp], writes=[Pb], inc=False)
                    for j2 in range(j + 1, 4):
                        P.op("pe", lambda e, j=j, j2=j2: e.matmul(Pb[:, j * 128:(j + 1) * 128], lhsT=c["onesB"][:, :], rhs=sp[:, j2 * 128:(j2 + 1) * 128],
                                                               start=False, stop=False), reads=[c["onesB"], sp], writes=[Pb], inc=False)
                    P.op("pe", lambda e, j=j, blk=blk: e.matmul(Pb[:, j * 128:(j + 1) * 128], lhsT=nkT[:, blk * 128:(blk + 1) * 128],
                                                             rhs=qsb[:, hd, i * 128:(i + 1) * 128], start=False, stop=True),
                         reads=[nkT, qsb], writes=[Pb], inc=False)
                for j in range(4):
                    P.op("pe", lambda e, j=j: e.matmul(Tb[:, 0:128], lhsT=c["onesB"][:, :], rhs=sp[:, j * 128:(j + 1) * 128],
                                                     start=(j == 0), stop=(j == 3)), reads=[c["onesB"], sp], writes=[Tb, Pb], inc=(j == 3))
                tf = c["efR"].next()
                P.op("dve", lambda e: e.tensor_tensor(out=tf[:, :].rearrange("p (b t) -> p b t", t=128),
                                                      in0=Pb[:, :].rearrange("p (b t) -> p b t", t=128),
                                                      in1=R[:, :].unsqueeze(1).broadcast_to([128, 4, 128]), op=ALU.add),
                     reads=[Pb, R], writes=[tf])
                wt = c["wR"].next()
                P.op("act", lambda e: e.activation(out=wt[:, :], in_=tf[:, :], func=AF.Exp, scale=-1.0), reads=[tf], writes=[wt])
                if diag:
                    P.op("dve", lambda e: e.tensor_tensor(out=wt[:, :], in0=wt[:, :], in1=c["msb"][:, g - 2 * i, :], op=ALU.mult),
                         reads=[wt, c["msb"]], writes=[wt])
                P.op("dve", lambda e: e.tensor_tensor(out=R[:, :], in0=R[:, :], in1=Tb[:, 0:128], op=ALU.add), reads=[R, Tb], writes=[R])
                for j in range(4):
                    blk = 4 * g + j
                    last = (g == 0 and j == 3)
                    P.op("pe", lambda e, j=j, blk=blk, first=first, last=last: e.matmul(
                        Y[:, 0:128], lhsT=vS[:, blk, 0:128], rhs=wt[:, j * 128:(j + 1) * 128], start=first, stop=last),
                        reads=[vS, wt], writes=[Y], inc=(j == 3))
                    first = False
            P.op("act", lambda e: e.copy(out=ysbT[:, hd, i * 128:(i + 1) * 128], in_=Y[:, 0:128]), reads=[Y], writes=[ysbT])

    qd = c["qown"]
    P.dma(qd[:, :, :], obf[LA_BF["dq"]:LA_BF["dq"] + 512, :].rearrange("(h p) t -> p h t", p=128), writes=[qd])
    for hd in range(4):
        P.dma(kT[:, :], GF[512 + hd * 128:512 + (hd + 1) * 128, :], reads=[GF], writes=[kT])
        P.dma(vS[:, :, 0:128], GT[1024:, GT_V_D + hd * 128:GT_V_D + (hd + 1) * 128].rearrange("(b p) d -> p b d", p=128),
              reads=[GT], writes=[vS])
        P.op("pool", lambda e: e.memset(vS[:, :, 128:129], 1.0), writes=[vS])
        for i in range(NB):
            ng = 2 * i + 2
            Ym = c["psA"]
            for gi, g in enumerate(range(ng)):
                diag = g >= 2 * i
                for m in range(2):
                    Sb = psR.next()
                    for j in range(4):
                        blk = 4 * g + j
                        P.op("pe", lambda e, j=j, blk=blk, m=m: e.matmul(
                            Sb[:, j * 128:(j + 1) * 128], lhsT=kT[m * 64:(m + 1) * 64, blk * 128:(blk + 1) * 128],
                            rhs=qd[m * 64:(m + 1) * 64, hd, i * 128:(i + 1) * 128], start=True, stop=True),
                            reads=[kT, qd], writes=[Sb], inc=(j == 3))
                    eb = c["wR"].next()
                    P.op("act", lambda e: e.activation(out=eb[:, :], in_=Sb[:, :], func=AF.Exp), reads=[Sb], writes=[eb])
                    if diag:
                        P.op("dve", lambda e: e.tensor_tensor(out=eb[:, :], in0=eb[:, :], in1=c["mdf"][:, g - 2 * i, :], op=ALU.mult),
                             reads=[eb, c["mdf"]], writes=[eb])
                    for j in range(4):
                        blk = 4 * g + j
                        P.op("pe", lambda e, j=j, blk=blk, m=m: e.matmul(
                            Ym[m][:, 0:129], lhsT=eb[:, j * 128:(j + 1) * 128], rhs=vS[:, blk, 0:129],
                            start=(g == 0 and j == 0), stop=(g == ng - 1 and j == 3)),
                            reads=[eb, vS], writes=[Ym[m]], inc=(j == 3))
            rc = c["small"].next()
            P.op("dve", lambda e: e.reciprocal(out=rc[:, 0:1], in_=Ym[0][:, 128:129]), reads=[Ym[0]], writes=[rc])
            P.op("dve", lambda e: e.reciprocal(out=rc[:, 1:2], in_=Ym[1][:, 128:129]), reads=[Ym[1]], writes=[rc])
            P.op("dve", lambda e: e.tensor_tensor(out=rc[:, 1:2], in0=rc[:, 1:2], in1=c["nlam"][:, 0:1], op=ALU.mult),
                 reads=[rc, c["nlam"]], writes=[rc])
            ya = c["tokR"].next()
            yb = c["tokR"].next()
            P.op("dve", lambda e: e.tensor_scalar(out=ya[:, 0:128], in0=Ym[0][:, 0:128], scalar1=rc[:, 0:1], scalar2=None, op0=ALU.mult),
                 reads=[Ym[0], rc], writes=[ya])
            P.op("dve", lambda e: e.scalar_tensor_tensor(out=yb[:, 0:128], in0=Ym[1][:, 0:128], scalar=rc[:, 1:2], in1=ya[:, 0:128],
                                                         op0=ALU.mult, op1=ALU.add), reads=[Ym[1], rc, ya], writes=[yb])
            P.op("act", lambda e: e.activation(out=ya[:, 0:128], in_=yb[:, 0:128], func=AF.Square, accum_out=rc[:, 2:3]),
                 reads=[yb], writes=[ya, rc])
            P.op("dve", lambda e: e.tensor_scalar(out=rc[:, 2:3], in0=rc[:, 2:3], scalar1=1.0 / 128, scalar2=EPS, op0=ALU.mult, op1=ALU.add),
                 reads=[rc], writes=[rc])
            P.op("act", lambda e: e.activation(out=rc[:, 2:3], in_=rc[:, 2:3], func=AF.Sqrt), reads=[rc], writes=[rc])
            P.op("dve", lambda e: e.reciprocal(out=rc[:, 2:3], in_=rc[:, 2:3]), reads=[rc], writes=[rc])
            P.op("dve", lambda e: e.scalar_tensor_tensor(out=ya[:, 0:128], in0=yb[:, 0:128], scalar=rc[:, 2:3], in1=c["don"][:, :],
                                                         op0=ALU.mult, op1=ALU.mult), reads=[yb, rc, c["don"]], writes=[ya])
            P.op("dve", lambda e: e.tensor_scalar(out=ya[:, 0:128], in0=ya[:, 0:128], scalar1=float(1.0 - lam_init), scalar2=None, op0=ALU.mult),
                 reads=[ya], writes=[ya])
            Tp = psR.next()
            P.op("pe", lambda e: e.transpose(Tp[:, 0:128], ya[:, 0:128], ident[:, :]), reads=[ya, ident], writes=[Tp])
            P.op("act", lambda e: e.copy(out=ysbT[:, 12 + hd, i * 128:(i + 1) * 128], in_=Tp[:, 0:128]), reads=[Tp], writes=[ysbT])

    rq, rqx, rk, rvO, st32, stB = c["rq"], c["rqx"], c["rk"], c["rvO"], c["st32"], c["stB"]
    GTb = GT[:, :].rearrange("(b p) d -> b p d", p=128)
    if isinstance(rank, int):
        P.dma(c["RS"][:, :].rearrange("(b p) d -> b p d", p=128), GTb[rank:rank + 65, :, GT_KZ:GT_KZ + 1536], reads=[GT], writes=[c["RS"]])
    else:
        P.dma(c["RS"][:, :].rearrange("(b p) d -> b p d", p=128), GTb[bass.ds(rank, 65), :, GT_KZ:GT_KZ + 1536],
              reads=[GT], writes=[c["RS"]], q="pool")
    rgT = c["gch"]
    for hd in range(4):
        P.dma(rq[:, :], obf[LA_BF["rq"] + hd * 128:LA_BF["rq"] + (hd + 1) * 128, :], writes=[rq])
        P.dma(rqx[:, :], obf[LA_BF["rqx"] + hd * 128:LA_BF["rqx"] + (hd + 1) * 128, :], writes=[rqx])
        P.dma(rk[:, :], obf[LA_BF["rk"] + hd * 128:LA_BF["rk"] + (hd + 1) * 128, :], writes=[rk])
        P.op("pool", lambda e: e.memset(st32[:, :], 0.0), writes=[st32])
        g128 = float(np.exp(128.0 * LOG_GAMMA[hd]))
        for i in range(NB):
            kzs, vss = c["kzs"].next(), c["vss"].next()
            RSb = c["RS"][:, :].rearrange("(b p) d -> b p d", p=128)
            P.dma(kzs[:, :, :], RSb[8 * i:8 * i + 8, :, hd * 128:(hd + 1) * 128].rearrange("b p d -> p b d"),
                  reads=[c["RS"]], writes=[kzs])
            P.dma(vss[:, :, :], RSb[8 * i:8 * i + 9, :, 512 + hd * 256:512 + (hd + 1) * 256].rearrange("b p d -> p b d"),
                  reads=[c["RS"]], writes=[vss])
            for n in range(8):
                Sp = psR.next()
                P.op("pe", lambda e, n=n: e.matmul(Sp[:, 0:256], lhsT=kzs[:, n, :], rhs=vss[:, n, :], start=True, stop=True),
                     reads=[kzs, vss], writes=[Sp])
                P.op("dve", lambda e: e.scalar_tensor_tensor(out=st32[:, :], in0=st32[:, :], scalar=g128, in1=Sp[:, 0:256],
                                                             op0=ALU.mult, op1=ALU.add), reads=[st32, Sp], writes=[st32])
            P.op("act", lambda e: e.copy(out=stB[:, :], in_=st32[:, :]), reads=[st32], writes=[stB])
            Sc = psR.next()
            P.op("pe", lambda e: e.matmul(Sc[:, 0:128], lhsT=rk[:, i * 128:(i + 1) * 128], rhs=rq[:, i * 128:(i + 1) * 128], start=True, stop=True),
                 reads=[rk, rq], writes=[Sc])
            scb = c["scb"].next()
            P.op("dve", lambda e: e.tensor_tensor(out=scb[:, :], in0=Sc[:, 0:128], in1=c["dmask"][:, hd, :], op=ALU.mult),
                 reads=[Sc, c["dmask"]], writes=[scb])
            for ec in range(2):
                Yr = psR.next()
                P.op("pe", lambda e, ec=ec: e.matmul(Yr[:, 0:128], lhsT=vss[:, 8, ec * 128:(ec + 1) * 128], rhs=scb[:, :], start=True, stop=False),
                     reads=[vss, scb], writes=[Yr], inc=False)
                P.op("pe", lambda e, ec=ec: e.matmul(Yr[:, 0:128], lhsT=stB[:, ec * 128:(ec + 1) * 128], rhs=rqx[:, i * 128:(i + 1) * 128],
                                                     start=False, stop=True), reads=[stB, rqx], writes=[Yr])
                yr = c["yr"][ec]
                P.op("act", lambda e: e.copy(out=yr[:, :], in_=Yr[:, 0:128]), reads=[Yr], writes=[yr])
            St = psR.next()
            for ec in range(2):
                P.op("pe", lambda e, ec=ec: e.matmul(St[:, 0:128], lhsT=c["ones32"][:, :], rhs=c["yr"][ec][:, :], start=(ec == 0), stop=(ec == 1)),
                     reads=[c["ones32"], c["yr"][ec]], writes=[St], inc=(ec == 1))
            for ec in range(2):
                P.op("act", lambda e, ec=ec: e.activation(out=c["ysq"][ec][:, :], in_=c["yr"][ec][:, :], func=AF.Square),
                     reads=[c["yr"][ec]], writes=[c["ysq"][ec]])
            for ec in range(2):
                P.op("pe", lambda e, ec=ec: e.matmul(St[:, 128:256], lhsT=c["ones32"][:, :], rhs=c["ysq"][ec][:, :], start=(ec == 0), stop=(ec == 1)),
                     reads=[c["ones32"], c["ysq"][ec]], writes=[St], inc=(ec == 1))
            mu = c["tokR"].next()
            P.op("dve", lambda e: e.tensor_scalar(out=mu[:, 0:256], in0=St[:, 0:256], scalar1=1.0 / 256, scalar2=None, op0=ALU.mult),
                 reads=[St], writes=[mu])
            var = c["tokR"].next()
            P.op("dve", lambda e: e.tensor_tensor(out=var[:, 0:128], in0=mu[:, 0:128], in1=mu[:, 0:128], op=ALU.mult), reads=[mu], writes=[var])
            P.op("dve", lambda e: e.tensor_tensor(out=var[:, 0:128], in0=mu[:, 128:256], in1=var[:, 0:128], op=ALU.subtract),
                 reads=[mu, var], writes=[var])
            P.op("dve", lambda e: e.tensor_scalar(out=var[:, 0:128], in0=var[:, 0:128], scalar1=EPS, scalar2=None, op0=ALU.add),
                 reads=[var], writes=[var])
            P.op("act", lambda e: e.activation(out=var[:, 0:128], in_=var[:, 0:128], func=AF.Sqrt), reads=[var], writes=[var])
            P.op("dve", lambda e: e.reciprocal(out=var[:, 0:128], in_=var[:, 0:128]), reads=[var], writes=[var])
            for ec in range(2):
                yr = c["yr"][ec]
                fch = hd * 2 + ec
                P.dma(rgT[:, 0:128], of32[LA_F32["rg"] + fch * 128:LA_F32["rg"] + (fch + 1) * 128, i * 128:(i + 1) * 128], writes=[rgT])
                P.op("dve", lambda e: e.tensor_tensor(out=yr[:, :], in0=yr[:, :], in1=mu[:, 0:128], op=ALU.subtract), reads=[yr, mu], writes=[yr])
                P.op("dve", lambda e: e.tensor_tensor(out=yr[:, :], in0=yr[:, :], in1=var[:, 0:128], op=ALU.mult), reads=[yr, var], writes=[yr])
                P.op("dve", lambda e, fch=fch: e.tensor_scalar(out=yr[:, :], in0=yr[:, :], scalar1=c["gng"][:, fch:fch + 1], scalar2=c["gnb"][:, fch:fch + 1],
                                                            op0=ALU.mult, op1=ALU.add), reads=[yr, c["gng"], c["gnb"]], writes=[yr])
                P.op("dve", lambda e, fch=fch: e.tensor_tensor(out=ysbT[:, 4 + fch, i * 128:(i + 1) * 128], in0=yr[:, :], in1=rgT[:, 0:128], op=ALU.mult),
                     reads=[yr, rgT], writes=[ysbT])

    if dbg is not None:
        P.dma(dbg[:, :, :], ysbT[:, :, :], reads=[ysbT], writes=[dbg], is_output=True)
    mT = c["mT"]
    wbfR = c["wbf"]

    def slab(srcbuf, idx):
        wb = wbfR.next()
        P.dma(wb[:, :, :], srcbuf[idx], reads=[srcbuf], writes=[wb])
        return wb

    br_k = [(0, 4), (4, 12), (12, 16)]
    for fc in range(16):
        wb = slab(wbr, fc)
        gch = c["gch"]
        acc = c["acc"]
        for b, (k0, k1) in enumerate(br_k):
            P.dma(gch[:, :], of32[LA_F32["gates"] + b * 2048 + fc * 128:LA_F32["gates"] + b * 2048 + (fc + 1) * 128, :], writes=[gch])
            for h in range(2):
                Bp = psR.next()
                for kc in range(k0, k1):
                    P.op("pe", lambda e, kc=kc, h=h: e.matmul(Bp[:, :], lhsT=wb[:, kc, :], rhs=ysbT[:, kc, h * 512:(h + 1) * 512],
                                                           start=(kc == k0), stop=(kc == k1 - 1)), reads=[wb, ysbT], writes=[Bp], inc=(kc == k1 - 1))
                sl = slice(h * 512, (h + 1) * 512)
                if b == 0:
                    P.op("dve", lambda e: e.tensor_tensor(out=acc[:, sl], in0=Bp[:, :], in1=gch[:, sl], op=ALU.mult), reads=[Bp, gch], writes=[acc])
                else:
                    tmp = c["efR"].next()
                    P.op("dve", lambda e: e.tensor_tensor(out=tmp[:, :], in0=Bp[:, :], in1=gch[:, sl], op=ALU.mult), reads=[Bp, gch], writes=[tmp])
                    if b == 1:
                        P.op("pool", lambda e: e.tensor_tensor(out=acc[:, sl], in0=acc[:, sl], in1=tmp[:, :], op=ALU.add), reads=[acc, tmp], writes=[acc])
                    else:
                        P.op("pool", lambda e: e.tensor_tensor(out=mT[:, fc, sl], in0=acc[:, sl], in1=tmp[:, :], op=ALU.add), reads=[acc, tmp], writes=[mT])
    for fc in range(16):
        wb = slab(wout, fc)
        xch = c["xch"].next()
        P.dma(xch[:, :], xTd[:, fc, :], reads=[xTd], writes=[xch])
        for h in range(2):
            Op = psR.next()
            for kc in range(KC):
                P.op("pe", lambda e, kc=kc, h=h: e.matmul(Op[:, :], lhsT=wb[:, kc, :], rhs=mT[:, kc, h * 512:(h + 1) * 512],
                                                       start=(kc == 0), stop=(kc == KC - 1)), reads=[wb, mT], writes=[Op], inc=(kc == KC - 1))
            sl = slice(h * 512, (h + 1) * 512)
            P.op("dve", lambda e: e.scalar_tensor_tensor(out=xch[:, sl], in0=Op[:, :], scalar=modS[:, 32 + fc:33 + fc], in1=xch[:, sl],
                                                         op0=ALU.mult, op1=ALU.add), reads=[Op, modS, xch], writes=[xch])
        if c.get("hs") is not None:
            P.op("pool", lambda e, fc=fc: e.tensor_copy(out=c["hs"][:, :, fc, :], in_=xch[:, :].rearrange("p (i t) -> p i t", t=128)[:, :, 126:128]),
                 reads=[xch], writes=[c["hs"]])
        P.dma(xoutd[:, fc, :], xch[:, :], reads=[xch], writes=[xoutd], is_output=(dbg is not None), q="pool")


def lb_consts(P, srcs):
    c = {}
    c["RS"] = P.dram("RSscr", [65 * 128, 1536], BF16)
    c["psR"] = RR([P.ps(f"lps{i}") for i in range(6)])
    c["psA"] = [P.ps("lpsA0"), P.ps("lpsA1")]
    c["ybr"] = P.sb("ybr", [128, 16, T], BF16)
    c["mT"] = P.sb("mT", [128, 16, T], BF16)
    c["qown"] = P.sb("qown", [128, 4, T], BF16)
    c["kT"] = P.sb("kTs", [128, S], BF16)
    c["nkT"] = P.sb("nkTs", [128, S], BF16)
    c["vS"] = P.sb("vS", [128, 64, 130], BF16)
    c["R"] = P.sb("Rsum", [128, 128], F32)
    c["efR"] = RR([P.sb(f"ef{i}", [128, 512], F32) for i in range(3)])
    c["spR"] = RR([P.sb(f"sp{i}", [128, 512], BF16) for i in range(2)])
    c["wR"] = RR([P.sb(f"wt{i}", [128, 512], BF16) for i in range(3)])
    c["small"] = RR([P.sb(f"sm{i}", [128, 4], F32) for i in range(2)])
    c["tokR"] = RR([P.sb(f"tok{i}", [128, 256], F32) for i in range(4)])
    c["rq"] = P.sb("rq", [128, T], BF16)
    c["rqx"] = P.sb("rqx", [128, T], BF16)
    c["rk"] = P.sb("rk", [128, T], BF16)
    c["rvO"] = P.sb("rvO", [128, 256], BF16)
    c["st32"] = P.sb("st32", [128, 256], F32)
    c["stB"] = P.sb("stB", [128, 256], BF16)
    c["kzs"] = RR([P.sb(f"kzs{i}", [128, 8, 128], BF16) for i in range(2)])
    c["vss"] = RR([P.sb(f"vss{i}", [128, 9, 256], BF16) for i in range(2)])
    c["scb"] = RR([P.sb(f"scb{i}", [128, 128], BF16) for i in range(2)])
    c["yr"] = [P.sb(f"yr{i}", [128, 128], F32) for i in range(2)]
    c["ysq"] = [P.sb(f"ysq{i}", [128, 128], F32) for i in range(2)]
    c["gch"] = P.sb("gch", [128, T], F32)
    c["acc"] = P.sb("acc", [128, T], F32)
    c["xch"] = RR([P.sb(f"xch{i}", [128, T], F32) for i in range(2)])
    c["wbf"] = RR([P.sb(f"lwbf{i}", [128, KC, 128], BF16) for i in range(3)])
    c["one"] = P.sb("one", [128, 1], F32)
    c["ones32"] = P.sb("lones32", [128, 128], F32)
    c["onesB"] = P.sb("onesB", [128, 128], BF16)
    c["uincl"] = P.sb("uincl", [128, 128], BF16)
    c["identF"] = P.sb("identF", [128, 128], F32)
    c["msb"] = P.sb("msb", [128, 2, 512], BF16)
    c["mdf"] = P.sb("mdf", [128, 2, 512], BF16)
    c["dmask"] = P.sb("dmask", [128, 4, 128], F32)
    c["don"] = P.sb("don", [128, 128], F32)
    c["gng"] = P.sb("gng", [128, 8], F32)
    c["gnb"] = P.sb("gnb", [128, 8], F32)
    c["nlam"] = P.sb("nlam", [128, 1], F32)
    c["lamv"] = P.sb("lamv", [64, 4], F32)
    P.op("pool", lambda e: e.memset(c["one"][:], 1.0), writes=[c["one"]])
    P.op("pool", lambda e: e.memset(c["ones32"][:], 1.0), writes=[c["ones32"]])
    P.op("pool", lambda e: e.memset(c["onesB"][:], 1.0), writes=[c["onesB"]])
    return c


def lb_load_consts(P, c, s):
    P.dma(c["uincl"][:], s["uincl"][:, :], writes=[c["uincl"]])
    P.dma(c["identF"][:], s["ident"][:, :], writes=[c["identF"]])
    P.dma(c["msb"][:], s["masks"][:, 0, :, :], writes=[c["msb"]])
    P.dma(c["mdf"][:], s["masks"][:, 1, :, :], writes=[c["mdf"]])
    P.dma(c["dmask"][:], s["dmask"][:, :, :], writes=[c["dmask"]])


def lb_load_layer(P, c, s, lam_init):
    P.dma(c["don"][:], s["don"][:, :], writes=[c["don"]])
    P.dma(c["gng"][:], s["gn"][:, 0:8], writes=[c["gng"]])
    P.dma(c["gnb"][:], s["gn"][:, 8:16], writes=[c["gnb"]])
    P.dma(c["lamv"][:], s["lamv"][:, :], writes=[c["lamv"]])
    pr = c["small"].next()
    P.op("dve", lambda e: e.tensor_tensor(out=pr[0:64, 0:1], in0=c["lamv"][:, 0:1], in1=c["lamv"][:, 1:2], op=ALU.mult), reads=[c["lamv"]], writes=[pr])
    P.op("dve", lambda e: e.tensor_tensor(out=pr[0:64, 1:2], in0=c["lamv"][:, 2:3], in1=c["lamv"][:, 3:4], op=ALU.mult), reads=[c["lamv"]], writes=[pr])
    Lp = c["psR"].next()
    P.op("pe", lambda e: e.matmul(Lp[:, 0:2], lhsT=c["ones32"][0:64, :], rhs=pr[0:64, 0:2], start=True, stop=True),
         reads=[c["ones32"], pr], writes=[Lp])
    ex = c["small"].next()
    P.op("act", lambda e: e.activation(out=ex[:, 0:2], in_=Lp[:, 0:2], func=AF.Exp), reads=[Lp], writes=[ex])
    P.op("dve", lambda e: e.tensor_tensor(out=c["nlam"][:, 0:1], in0=ex[:, 1:2], in1=ex[:, 0:1], op=ALU.subtract), reads=[ex], writes=[c["nlam"]])
    P.op("dve", lambda e: e.tensor_scalar(out=c["nlam"][:, 0:1], in0=c["nlam"][:, 0:1], scalar1=float(-lam_init), scalar2=None, op0=ALU.add),
         reads=[c["nlam"]], writes=[c["nlam"]])


def lb_host_consts(core):
    p = np.arange(128)
    uincl = (p[:, None] >= p[None, :]).astype(np.float32)
    ident = np.eye(128, dtype=np.float32)
    masks = np.zeros((128, 2, 2, 512), np.float32)
    for g2 in range(2):
        for j in range(4):
            kb = 4 * g2 + j
            sl = slice(j * 128, (j + 1) * 128)
            if kb < core:
                masks[:, 0, g2, sl] = 1.0
                masks[:, 1, g2, sl] = 1.0
            elif kb == core:
                masks[:, 0, g2, sl] = (p[:, None] < p[None, :])
                masks[:, 1, g2, sl] = ((p[:, None] // 64) <= (p[None, :] // 64))
    dm = np.zeros((128, 4, 128), np.float64)
    n = p[None, :]
    m = p[:, None]
    for h in range(4):
        same = (m // 64) == (n // 64)
        earlier = (m // 64) < (n // 64)
        dm[:, h, :] = np.where(same, np.exp(np.abs(n - m) * LOG_GAMMA[h]), np.where(earlier, np.exp((n - m) * LOG_GAMMA[h]), 0.0))
    return uincl.astype(NPBF), ident, masks.astype(NPBF), dm.astype(np.float32)


def build_LB_test(lam_init):
    P = Prog()
    obf = P.dram("obf", [LA_BF_ROWS, T], BF16, kind="ExternalInput")
    of32 = P.dram("of32", [LA_F32_ROWS, T], F32, kind="ExternalInput")
    GF = P.dram("GF", [1024, S], BF16, kind="ExternalInput")
    GT = P.dram("GT", [1024 + S, 2560], BF16, kind="ExternalInput")
    xTd = P.dram("xT", [128, KC, T], F32, kind="ExternalInput")
    modT = P.dram("modT", [128, 96], F32, kind="ExternalInput")
    wbr = P.dram("wbr", [16, 128, KC, 128], BF16, kind="ExternalInput")
    wout = P.dram("wout", [16, 128, KC, 128], BF16, kind="ExternalInput")
    s = {k: P.dram("i_" + k, shp, dt, kind="ExternalInput") for k, shp, dt in [
        ("uincl", [128, 128], BF16), ("ident", [128, 128], F32), ("masks", [128, 2, 2, 512], BF16), ("dmask", [128, 4, 128], F32),
        ("don", [128, 128], F32), ("gn", [128, 16], F32), ("lamv", [64, 4], F32)]}
    xoutd = P.dram("xout", [128, KC, T], F32, kind="ExternalOutput")
    dbg = P.dram("dbg", [128, 16, T], BF16, kind="ExternalOutput")
    modS = P.sb("modS", [128, 96], F32)
    P.dma(modS[:], modT[:, :], writes=[modS])
    c = lb_consts(P, s)
    lb_load_consts(P, c, s)
    lb_load_layer(P, c, s, lam_init)
    rank = P.nc.gpsimd.partition_id()
    lb_phase(P, c, obf, of32, GF, GT, xTd, xoutd, modS, wbr, wout, rank, lam_init, dbg=dbg)
    return P.finish()


def lc_alloc(P):
    c = {}
    c["xT"] = P.sb("c_xT", [128, KC, T], F32)
    c["hT"] = P.sb("c_hT", [128, KC, T], BF16)
    c["xh"] = P.sb("c_xh", [128, KC, 16], F32)
    c["hh"] = P.sb("c_hh", [128, KC, 16], BF16)
    c["actT"] = P.sb("c_actT", [128, 44, 512], BF16)
    c["U"] = RR([P.sb(f"c_U{i}", [128, 4, 130], F32) for i in range(3)])
    c["Y"] = RR([P.sb(f"c_Y{i}", [128, 4, 128], F32) for i in range(4)])
    c["wup"] = RR([P.sb(f"c_wup{i}", [128, KC, 128], BF16) for i in range(3)])
    c["wdn"] = RR([P.sb(f"c_wdn{i}", [128, 48, 128], BF16) for i in range(1)])
    c["xo"] = RR([P.sb(f"c_xo{i}", [128, 512], F32) for i in range(2)])
    c["gs"] = P.sb("c_gs", [128, KC], F32)
    c["nf"] = P.sb("c_nf", [128, KC], F32)
    c["cw"] = P.sb("c_cw", [128, 3, 88], F32)
    c["cb"] = P.sb("c_cb", [128, 88], F32)
    c["flag"] = P.sb("c_flag", [128, 8, 2], F32)
    c["rstd"] = P.sb("c_rstd", [128, T], F32)
    c["tmp"] = RR([P.sb(f"c_tmp{i}", [128, 512], F32) for i in range(3)])
    c["ones32"] = P.sb("c_ones32", [128, 128], F32)
    P.op("pool", lambda e: e.memset(c["ones32"][:], 1.0), writes=[c["ones32"]])
    return c


def lc_phase(P, c, psR, xmid, halo_ap, halo_buf, xout, modS, nffn_d, cw_d, cb_d, flag_d, wup, wdn, out_is_final):
    xT, hT, xh, hh, actT = c["xT"], c["hT"], c["xh"], c["hh"], c["actT"]
    for q in range(4):
        P.dma(xT[:, 4 * q:4 * q + 4, :], xmid[:, 4 * q:4 * q + 4, :], reads=[xmid], writes=[xT])
    if len(halo_ap.shape) == 3:
        P.dma(c["xhi"][:, :, :], halo_ap, reads=[halo_buf], writes=[c["xhi"]])
    else:
        P.dma(c["xhi"][:, :, :].unsqueeze(2), halo_ap, reads=[halo_buf], writes=[c["xhi"]], q="pool")
    P.op("pool", lambda e: e.tensor_copy(out=xh[:, :, :].rearrange("p k (i t) -> p k i t", t=2),
                                         in_=c["xhi"][:, :, :].rearrange("p i (k t) -> p k i t", t=2)), reads=[c["xhi"]], writes=[xh])
    P.dma(c["nf"][:], nffn_d[:, :], writes=[c["nf"]])
    P.dma(c["cw"][:], cw_d[:, :, :], writes=[c["cw"]])
    P.dma(c["cb"][:], cb_d[:, :], writes=[c["cb"]])
    P.dma(c["flag"][:], flag_d[:, :, :], writes=[c["flag"]])
    gs = c["gs"]
    P.op("dve", lambda e: e.scalar_tensor_tensor(out=gs[:], in0=modS[:, 64:80], scalar=1.0, in1=c["nf"][:], op0=ALU.add, op1=ALU.mult),
         reads=[modS, c["nf"]], writes=[gs])
    psA, psB = psR.next(), psR.next()
    adaln_norm(P, xT, hT, T, gs, lambda kc: modS[:, 48 + kc:49 + kc], c["ones32"], psA, psB, c["rstd"], c["tmp"])
    adaln_norm(P, xh, hh, 16, gs, lambda kc: modS[:, 48 + kc:49 + kc], c["ones32"], psA, psB, c["rstd"], c["tmp"])
    for hf in range(2):
        tsl = slice(hf * 512, (hf + 1) * 512)
        for j in range(44):
            Ys = []
            for which in range(2):
                sidx = j + 44 * which
                wb = c["wup"].next()
                P.dma(wb[:, :, :], wup[sidx], reads=[wup], writes=[wb])
                pm, ph = psR.next(), psR.next()
                for kc in range(KC):
                    P.op("pe", lambda e, kc=kc: e.matmul(pm[:, :], lhsT=wb[:, kc, :], rhs=hT[:, kc, tsl], start=(kc == 0), stop=(kc == KC - 1)),
                         reads=[wb, hT], writes=[pm], inc=(kc == KC - 1))
                for kc in range(KC):
                    P.op("pe", lambda e, kc=kc: e.matmul(ph[:, 0:8], lhsT=wb[:, kc, :], rhs=hh[:, kc, hf * 8:(hf + 1) * 8], start=(kc == 0), stop=(kc == KC - 1)),
                         reads=[wb, hh], writes=[ph], inc=(kc == KC - 1))
                U = c["U"].next()
                P.op("act", lambda e: e.copy(out=U[:, :, 2:130], in_=pm[:, :].rearrange("p (b t) -> p b t", t=128)), reads=[pm], writes=[U])
                P.op("dve", lambda e: e.tensor_tensor(out=U[:, :, 0:2], in0=ph[:, 0:8].rearrange("p (b t) -> p b t", t=2),
                                                      in1=c["flag"][:, hf * 4:(hf + 1) * 4, :], op=ALU.mult), reads=[ph, c["flag"]], writes=[U])
                Y = c["Y"].next()
                P.op("act", lambda e, sidx=sidx: e.activation(out=Y[:, :, :], in_=U[:, :, 2:130], func=AF.Identity,
                                                          scale=c["cw"][:, 2, sidx:sidx + 1], bias=c["cb"][:, sidx:sidx + 1]),
                     reads=[U, c["cw"], c["cb"]], writes=[Y])
                P.op("dve", lambda e, sidx=sidx: e.scalar_tensor_tensor(out=Y[:, :, :], in0=U[:, :, 1:129], scalar=c["cw"][:, 1, sidx:sidx + 1], in1=Y[:, :, :],
                                                                     op0=ALU.mult, op1=ALU.add), reads=[U, c["cw"], Y], writes=[Y])
                P.op("dve", lambda e, sidx=sidx: e.scalar_tensor_tensor(out=Y[:, :, :], in0=U[:, :, 0:128], scalar=c["cw"][:, 0, sidx:sidx + 1], in1=Y[:, :, :],
                                                                     op0=ALU.mult, op1=ALU.add), reads=[U, c["cw"], Y], writes=[Y])
                Ys.append(Y)
            sg = c["Y"].next()
            P.op("act", lambda e: e.activation(out=sg[:, :, :], in_=Ys[0][:, :, :], func=AF.Silu), reads=[Ys[0]], writes=[sg])
            P.op("pool", lambda e, j=j: e.tensor_tensor(out=actT[:, j, :].rearrange("p (b t) -> p b t", t=128), in0=sg[:, :, :], in1=Ys[1][:, :, :], op=ALU.mult),
                 reads=[sg, Ys[1]], writes=[actT])
        for fc in range(KC):
            wb = c["wdn"].next()
            P.dma(wb[:, :, :].rearrange("p (g k) n -> p g k n", g=3), wdn[fc], reads=[wdn], writes=[wb])
            po = psR.next()
            for kc in range(44):
                P.op("pe", lambda e, kc=kc: e.matmul(po[:, :], lhsT=wb[:, kc, :], rhs=actT[:, kc, :], start=(kc == 0), stop=(kc == 43)),
                     reads=[wb, actT], writes=[po], inc=(kc == 43))
            xo = c["xo"].next()
            P.op("dve", lambda e, fc=fc: e.scalar_tensor_tensor(out=xo[:, :], in0=po[:, :], scalar=modS[:, 80 + fc:81 + fc], in1=xT[:, fc, tsl],
                                                             op0=ALU.mult, op1=ALU.add), reads=[po, modS, xT], writes=[xo])
            P.dma(xout[:, fc, tsl], xo[:, :], reads=[xo], writes=[xout], is_output=out_is_final, q="pool")


NU_LAYER = 280
U_OFF = dict(la=0, br=112, out=128, up=144, dn=232)
NU = 2 * NU_LAYER
NU_CORE = NU // NCORE
AR_CHUNK = 56
LAM_INIT = [0.8 - 0.6 * float(np.exp(-0.3 * l)) for l in range(2)]


def build_full():
    P = Prog()
    X = lambda n, shp, dt: P.dram(n, shp, dt, kind="ExternalInput")
    xT_in = X("xT", [128, KC, T], F32)
    pos = X("pos", [128, T], I32)
    c_in = X("c_in", [128, KC], F32)
    wada = X("wada", [2, 128, KC, 1536], F32)
    bada = X("bada", [128, 2, 12], F32)
    wsh = X("wsh", [NU_CORE, 128, KC, 128], F32)
    nmix = X("nmix", [2, 128, KC], F32)
    nffn = X("nffn", [2, 128, KC], F32)
    cst = X("cst", [2, 128, 8], F32)
    cwd = X("cw", [2, 128, 3, 88], F32)
    cbd = X("cb", [2, 128, 88], F32)
    dond = X("don", [2, 128, 128], F32)
    gnd = X("gn", [2, 128, 16], F32)
    lamd = X("lamv", [2, 64, 4], F32)
    dec = X("dec", [128, 8, 128], F32)
    bones = X("bones", [128, 128], F32)
    uincl = X("uincl", [128, 128], BF16)
    ident = X("ident", [128, 128], F32)
    masks = X("masks", [128, 2, 2, 512], BF16)
    dmask = X("dmask", [128, 4, 128], F32)
    flagd = X("flag", [128, 8, 2], F32)
    xfin = P.dram("xoutT", [128, KC, T], F32, kind="ExternalOutput")

    Wown = P.dram("Wown", [NU_CORE, 128, KC, 128], BF16)
    WinL = [P.dram(f"Win{l}", [NU_LAYER, 128, KC, 128], BF16) for l in range(2)]
    WoutL = [P.dram(f"Wout{l}", [NU_LAYER, 128, KC, 128], BF16, shared=True) for l in range(2)]
    MODin = P.dram("MODin", [8, 128, 2, 12], F32)
    MOD = P.dram("MOD", [8, 128, 2, 12], F32, shared=True)
    GFin = P.dram("GFin", [1024, S], BF16)
    GF = P.dram("GF", [1024, S], BF16, shared=True)
    GTin = P.dram("GTin", [1024 + S, 2560], BF16)
    GT = P.dram("GT", [1024 + S, 2560], BF16, shared=True)
    HLin = P.dram("HLin", [65, 128, KC, 2], F32)
    HL = P.dram("HL", [65, 128, KC, 2], F32, shared=True)
    OT = P.dram("OT", [T, 2560], BF16)
    obf = P.dram("obf", [LA_BF_ROWS, T], BF16)
    of32 = P.dram("of32", [LA_F32_ROWS, T], F32)
    xmid = P.dram("xmid", [128, KC, T], F32)
    xnext = P.dram("xnext", [128, KC, T], F32)
    rank = P.nc.gpsimd.partition_id()

    P.phase_begin()
    Zt = P.sb("Zt", [128, 8192], BF16)
    Zf = P.sb("Zf", [128, 65 * 32], F32)
    P.op("pool", lambda e: e.memset(Zt[:], 0.0), writes=[Zt])
    P.op("pool", lambda e: e.memset(Zf[:], 0.0), writes=[Zf])
    for Win in WinL:
        for u in range(0, NU_LAYER, 4):
            P.dma(Win[u:u + 4].rearrange("u p k n -> p u (k n)"), Zt[:, :].rearrange("p (u x) -> p u x", u=4), reads=[Zt], writes=[Win])
    for r0 in range(0, 1024, 128):
        P.dma(GFin[r0:r0 + 128, :], Zt[:, :], reads=[Zt], writes=[GFin])
    for b0 in range(0, 72, 3):
        P.dma(GTin[b0 * 128:(b0 + 3) * 128, :].rearrange("(b p) d -> p b d", p=128), Zt[:, 0:7680].rearrange("p (b d) -> p b d", d=2560),
              reads=[Zt], writes=[GTin])
    P.dma(HLin[:, :, :, :].rearrange("b p k t -> p b (k t)"), Zf[:, :].rearrange("p (b x) -> p b x", x=32), reads=[Zf], writes=[HLin])
    P.dma(MODin[:, :, :, :].rearrange("r p l j -> p r (l j)"), Zf[:, 0:192].rearrange("p (r x) -> p r x", x=24), reads=[Zf], writes=[MODin])
    wst = RR([P.sb(f"Wst{i}", [128, KC, 128], F32) for i in range(3)])
    wcb = RR([P.sb(f"Wcb{i}", [128, KC, 128], BF16) for i in range(3)])
    cast_engs = ["pool", "dve", "act"]
    for j in range(NU_CORE):
        st, wb = wst.next(), wcb.next()
        P.dma(st[:, :, :], wsh[j], writes=[st])
        ce = cast_engs[j % 3]
        if ce == "act":
            P.op("act", lambda e: e.copy(out=wb[:, :, :], in_=st[:, :, :]), reads=[st], writes=[wb])
        else:
            P.op(ce, lambda e: e.tensor_copy(out=wb[:, :, :], in_=st[:, :, :]), reads=[st], writes=[wb])
        P.dma(Wown[j], wb[:, :, :], reads=[wb], writes=[Wown])
    for l in range(2):
        Win, Wout = WinL[l], WoutL[l]
        P.dma(Win[:, :, :, :].rearrange("(r j) p k n -> r (j p) (k n)", r=8)[bass.ds(rank, 1), :, :],
              Wown[l * 35:(l + 1) * 35, :, :, :].rearrange("j p k n -> (j p) (k n)").unsqueeze(0), reads=[Wown], writes=[Win], q="pool")
        Win2 = Win[:, :, :, :].rearrange("u p k n -> (u p) (k n)")
        Wout2 = Wout[:, :, :, :].rearrange("u p k n -> (u p) (k n)")
        for u in range(0, NU_LAYER, AR_CHUNK):
            P.all_reduce(Win2[u * 128:(u + AR_CHUNK) * 128, :], Wout2[u * 128:(u + AR_CHUNK) * 128, :], reads=[Win], writes=[Wout])
    ct = P.sb("ct", [128, KC], F32)
    ca = P.sb("ca", [128, KC], F32)
    wt = P.sb("wadat", [128, KC, 1536], F32)
    bt = P.sb("badat", [128, 2, 12], F32)
    mo = P.sb("modown", [128, 2, 12], F32)
    pmod = P.ps("pmod")
    P.dma(ct[:], c_in[:, :], writes=[ct])
    P.dma(bt[:], bada[:, :, :], writes=[bt])
    P.op("act", lambda e: e.activation(out=ca[:], in_=ct[:], func=AF.Silu), reads=[ct], writes=[ca])
    for l in range(2):
        for q in range(4):
            P.dma(wt[:, 4 * q:4 * q + 4, :], wada[l, :, 4 * q:4 * q + 4, :], writes=[wt])
        for fch in range(12):
            for kc in range(KC):
                P.op("pe", lambda e, fch=fch, kc=kc, l=l: e.matmul(pmod[:, l * 12 + fch:l * 12 + fch + 1], lhsT=wt[:, kc, fch * 128:(fch + 1) * 128],
                                                                rhs=ca[:, kc:kc + 1], start=(kc == 0), stop=(kc == KC - 1)),
                     reads=[wt, ca], writes=[pmod], inc=(kc == KC - 1))
    P.op("dve", lambda e: e.tensor_tensor(out=mo[:, :, :], in0=pmod[:, 0:24].rearrange("p (l j) -> p l j", l=2), in1=bt[:, :, :], op=ALU.add),
         reads=[pmod, bt], writes=[mo])
    P.dma(MODin[bass.ds(rank, 1), :, :, :].rearrange("o p l j -> p o l j"), mo[:, :, :].unsqueeze(1), reads=[mo], writes=[MODin], q="pool")
    P.all_reduce(MODin[:, :, :, :].rearrange("r p l j -> (r p) (l j)"), MOD[:, :, :, :].rearrange("r p l j -> (r p) (l j)"),
                 reads=[MODin], writes=[MOD])
    P.phase_end()

    xcur = xT_in
    for l in range(2):
        u0 = 0
        Wout = WoutL[l]
        P.phase_begin()
        modTd = Buf(MOD[:, :, l, :].rearrange("r p j -> p r j"), "modTd")
        d = dict(xT=xcur, modT=modTd, nmix=Buf(nmix[l], "nm"), pos=pos, cst=Buf(cst[l], "cs"), dec=dec, bones=bones,
                 wsl=Buf(Wout[u0 + U_OFF["la"]:u0 + U_OFF["la"] + 112], "wsl"), obf=obf, of32=of32)
        la_phase(P, d, True)
        identS = P.sb("identS", [128, 128], F32)
        P.dma(identS[:], ident[:, :], writes=[identS])
        tin = RR([P.sb(f"tin{i}", [128, T], BF16) for i in range(2)])
        tf = RR([P.sb(f"tf{i}", [128, T], F32) for i in range(2)])
        tout = RR([P.sb(f"tout{i}", [128, 8, 128], BF16) for i in range(2)])
        tps = RR([P.ps(f"tps{i}") for i in range(2)])
        jobs = [(LA_BF["vsb"] + k * 128, GT_V_SB + k * 128) for k in range(4)] + [(LA_BF["dv"] + k * 128, GT_V_D + k * 128) for k in range(4)] + \
               [(LA_BF["rkz"] + k * 128, GT_KZ + k * 128) for k in range(4)] + [(LA_BF["rv"] + k * 128, GT_RV + k * 128) for k in range(8)]
        for row, col in jobs:
            ti, tff, to = tin.next(), tf.next(), tout.next()
            P.dma(ti[:, :], obf[row:row + 128, :], reads=[obf], writes=[ti])
            P.op("dve", lambda e: e.tensor_copy(out=tff[:, :], in_=ti[:, :]), reads=[ti], writes=[tff])
            for hb in range(2):
                tp = tps.next()
                for b4 in range(4):
                    b = hb * 4 + b4
                    P.op("pe", lambda e, b=b, b4=b4: e.transpose(tp[:, b4 * 128:(b4 + 1) * 128], tff[:, b * 128:(b + 1) * 128], identS[:, :]),
                         reads=[tff, identS], writes=[tp])
                P.op("act", lambda e, hb=hb: e.copy(out=to[:, hb * 4:(hb + 1) * 4, :], in_=tp[:, :].rearrange("p (b f) -> p b f", f=128)),
                     reads=[tp], writes=[to])
            P.dma(OT[:, col:col + 128].rearrange("(b p) d -> p b d", p=128), to[:, :, :], reads=[to], writes=[OT])
        GFv = GFin[:, :].rearrange("f (i r t) -> f i r t", i=8, r=8)
        P.dma(GFv[0:512, :, bass.ds(rank, 1), :], obf[LA_BF["ksb"]:LA_BF["ksb"] + 512, :].rearrange("f (i t) -> f i t", i=8).unsqueeze(2),
              reads=[obf], writes=[GFin], q="pool")
        P.dma(GFv[512:1024, :, bass.ds(rank, 1), :], obf[LA_BF["dk"]:LA_BF["dk"] + 512, :].rearrange("f (i t) -> f i t", i=8).unsqueeze(2),
              reads=[obf], writes=[GFin], q="pool")
        GTv = GTin[1024:, :].rearrange("(i r p) d -> i r p d", i=8, r=8)
        P.dma(GTv[:, bass.ds(rank, 1), :, :], OT[:, :].rearrange("(i p) d -> i p d", i=8).unsqueeze(1), reads=[OT], writes=[GTin], q="pool")
        P.all_reduce(GFin[:, :], GF[:, :], reads=[GFin], writes=[GF])
        hr = (1024 + S) // 2
        P.all_reduce(GTin[0:hr, :], GT[0:hr, :], reads=[GTin], writes=[GT])
        P.all_reduce(GTin[hr:, :], GT[hr:, :], reads=[GTin], writes=[GT])
        P.phase_end()
        P.phase_begin()
        modS = P.sb("modS", [128, 96], F32)
        P.dma(modS[:, :].rearrange("p (r j) -> p r j", r=8), MOD[:, :, l, :].rearrange("r p j -> p r j"), reads=[MOD], writes=[modS])
        srcs = dict(uincl=uincl, ident=ident, masks=masks, dmask=dmask, don=Buf(dond[l], "don"), gn=Buf(gnd[l], "gn"), lamv=Buf(lamd[l], "lamv"))
        c = lb_consts(P, srcs)
        c["hs"] = P.sb("hs", [128, 8, KC, 2], F32)
        lb_load_consts(P, c, srcs)
        lb_load_layer(P, c, srcs, LAM_INIT[l])
        lb_phase(P, c, obf, of32, GF, GT, xcur, xmid, modS, Buf(Wout[u0 + U_OFF["br"]:u0 + U_OFF["br"] + 16], "wbr"),
                 Buf(Wout[u0 + U_OFF["out"]:u0 + U_OFF["out"] + 16], "wout"), rank, LAM_INIT[l])
        HLv = HLin[1:65, :, :, :].rearrange("(i r) p k t -> p i r (k t)", i=8, r=8)
        P.dma(HLv[:, :, bass.ds(rank, 1), :], c["hs"][:, :, :, :].rearrange("p i k t -> p i (k t)").unsqueeze(2), reads=[c["hs"]], writes=[HLin], q="pool")
        P.all_reduce(HLin[:, :, :, :].rearrange("b p k t -> (b p) (k t)"), HL[:, :, :, :].rearrange("b p k t -> (b p) (k t)"), reads=[HLin], writes=[HL])
        P.phase_end()
        P.phase_begin()
        modS = P.sb("modS", [128, 96], F32)
        P.dma(modS[:, :].rearrange("p (r j) -> p r j", r=8), MOD[:, :, l, :].rearrange("r p j -> p r j"), reads=[MOD], writes=[modS])
        cc = lc_alloc(P)
        cc["xhi"] = P.sb("xhi", [128, 8, KC * 2], F32)
        psR = RR([P.ps(f"cps{i}") for i in range(8)])
        xdst = xfin if l == 1 else xnext
        HLr = HL[0:64, :, :, :].rearrange("(i r) p k t -> p i r (k t)", i=8, r=8)
        lc_phase(P, cc, psR, xmid, HLr[:, :, bass.ds(rank, 1), :], HL, xdst, modS, Buf(nffn[l], "nf"), Buf(cwd[l], "cw"), Buf(cbd[l], "cb"), flagd,
                 Buf(Wout[u0 + U_OFF["up"]:u0 + U_OFF["up"] + 88], "wup"),
                 Buf(Wout[u0 + U_OFF["dn"]:u0 + U_OFF["dn"] + 48].rearrange("(f g) p k n -> f p g k n", g=3), "wdn"), l == 1)
        P.phase_end()
        xcur = xnext
    return P.finish()


def build_single():
    P = Prog()
    X = lambda n, shp, dt: P.dram(n, shp, dt, kind="ExternalInput")
    xT_in = X("xT", [8, 128, KC, T], F32)
    posA = X("pos", [8, 128, T], I32)
    c_in = X("c_in", [128, KC], F32)
    wada = X("wada", [2, 8, 128, KC, 1536], F32)
    bada = X("bada", [8, 128, 2, 12], F32)
    wsh = X("wsh", [NU, 128, KC, 128], F32)
    nmix = X("nmix", [2, 128, KC], F32)
    nffn = X("nffn", [2, 128, KC], F32)
    cst = X("cst", [2, 128, 8], F32)
    cwd = X("cw", [2, 128, 3, 88], F32)
    cbd = X("cb", [2, 128, 88], F32)
    dond = X("don", [2, 128, 128], F32)
    gnd = X("gn", [2, 128, 16], F32)
    lamd = X("lamv", [2, 64, 4], F32)
    dec = X("dec", [128, 8, 128], F32)
    bones = X("bones", [128, 128], F32)
    uincl = X("uincl", [128, 128], BF16)
    ident = X("ident", [128, 128], F32)
    masksA = X("masks", [8, 128, 2, 2, 512], BF16)
    dmask = X("dmask", [128, 4, 128], F32)
    flagA = X("flag", [8, 128, 8, 2], F32)
    xfin = P.dram("xoutT", [8, 128, KC, T], F32, kind="ExternalOutput")

    WoutL = [P.dram(f"Wout{l}", [NU_LAYER, 128, KC, 128], BF16) for l in range(2)]
    MOD = P.dram("MOD", [8, 128, 2, 12], F32)
    GF = P.dram("GF", [1024, S], BF16)
    GT = P.dram("GT", [1024 + S, 2560], BF16)
    HL = P.dram("HL", [65, 128, KC, 2], F32)
    OT = P.dram("OT", [T, 2560], BF16)
    obfA = [P.dram(f"obf{r}", [LA_BF_ROWS, T], BF16) for r in range(8)]
    of32A = [P.dram(f"of32{r}", [LA_F32_ROWS, T], F32) for r in range(8)]
    xmidA = [P.dram(f"xmid{r}", [128, KC, T], F32) for r in range(8)]
    xnextA = [P.dram(f"xnext{r}", [128, KC, T], F32) for r in range(8)]

    P.phase_begin()
    Zt = P.sb("Zt", [128, 8192], BF16)
    Zf = P.sb("Zf", [128, 32], F32)
    P.op("pool", lambda e: e.memset(Zt[:], 0.0), writes=[Zt])
    P.op("pool", lambda e: e.memset(Zf[:], 0.0), writes=[Zf])
    for b0 in range(0, 8, 2):
        P.dma(GT[b0 * 128:(b0 + 2) * 128, :].rearrange("(b p) d -> p b d", p=128), Zt[:, 0:5120].rearrange("p (b d) -> p b d", d=2560),
              reads=[Zt], writes=[GT])
    P.dma(HL[0, :, :, :].rearrange("p k t -> p (k t)"), Zf[:, :], reads=[Zf], writes=[HL])
    wst = RR([P.sb(f"Wst{i}", [128, KC, 128], F32) for i in range(3)])
    wcb = RR([P.sb(f"Wcb{i}", [128, KC, 128], BF16) for i in range(3)])
    cast_engs = ["pool", "dve", "act"]
    for j in range(NU):
        st, wb = wst.next(), wcb.next()
        P.dma(st[:, :, :], wsh[j], writes=[st])
        ce = cast_engs[j % 3]
        if ce == "act":
            P.op("act", lambda e: e.copy(out=wb[:, :, :], in_=st[:, :, :]), reads=[st], writes=[wb])
        else:
            P.op(ce, lambda e: e.tensor_copy(out=wb[:, :, :], in_=st[:, :, :]), reads=[st], writes=[wb])
        P.dma(WoutL[j // NU_LAYER][j % NU_LAYER], wb[:, :, :], reads=[wb], writes=[WoutL[j // NU_LAYER]])
    ct = P.sb("ct", [128, KC], F32)
    ca = P.sb("ca", [128, KC], F32)
    wt = P.sb("wadat", [128, KC, 1536], F32)
    bt = P.sb("badat", [128, 2, 12], F32)
    moR = RR([P.sb(f"modown{i}", [128, 2, 12], F32) for i in range(2)])
    pmR = RR([P.ps(f"pmod{i}") for i in range(2)])
    P.dma(ct[:], c_in[:, :], writes=[ct])
    P.op("act", lambda e: e.activation(out=ca[:], in_=ct[:], func=AF.Silu), reads=[ct], writes=[ca])
    for r in range(8):
        pmod, mo = pmR.next(), moR.next()
        P.dma(bt[:], bada[r], writes=[bt])
        for l in range(2):
            for q in range(4):
                P.dma(wt[:, 4 * q:4 * q + 4, :], wada[l, r, :, 4 * q:4 * q + 4, :], writes=[wt])
            for fch in range(12):
                for kc in range(KC):
                    P.op("pe", lambda e, fch=fch, kc=kc, l=l: e.matmul(pmod[:, l * 12 + fch:l * 12 + fch + 1], lhsT=wt[:, kc, fch * 128:(fch + 1) * 128],
                                                                    rhs=ca[:, kc:kc + 1], start=(kc == 0), stop=(kc == KC - 1)),
                         reads=[wt, ca], writes=[pmod], inc=(kc == KC - 1))
        P.op("dve", lambda e: e.tensor_tensor(out=mo[:, :, :], in0=pmod[:, 0:24].rearrange("p (l j) -> p l j", l=2), in1=bt[:, :, :], op=ALU.add),
             reads=[pmod, bt], writes=[mo])
        P.dma(MOD[r], mo[:, :, :], reads=[mo], writes=[MOD])
    P.phase_end()

    xcurA = [Buf(xT_in[r], f"xin{r}") for r in range(8)]
    for l in range(2):
        u0 = 0
        Wout = WoutL[l]
        for rank in range(8):
            xcur, obf, of32 = xcurA[rank], obfA[rank], of32A[rank]
            P.phase_begin()
            modTd = Buf(MOD[:, :, l, :].rearrange("r p j -> p r j"), "modTd")
            d = dict(xT=xcur, modT=modTd, nmix=Buf(nmix[l], "nm"), pos=Buf(posA[rank], "pos"), cst=Buf(cst[l], "cs"), dec=dec, bones=bones,
                     wsl=Buf(Wout[u0 + U_OFF["la"]:u0 + U_OFF["la"] + 112], "wsl"), obf=obf, of32=of32)
            la_phase(P, d, True)
            identS = P.sb("identS", [128, 128], F32)
            P.dma(identS[:], ident[:, :], writes=[identS])
            tin = RR([P.sb(f"tin{i}", [128, T], BF16) for i in range(2)])
            tf = RR([P.sb(f"tf{i}", [128, T], F32) for i in range(2)])
            tout = RR([P.sb(f"tout{i}", [128, 8, 128], BF16) for i in range(2)])
            tps = RR([P.ps(f"tps{i}") for i in range(2)])
            jobs = [(LA_BF["vsb"] + k * 128, GT_V_SB + k * 128) for k in range(4)] + [(LA_BF["dv"] + k * 128, GT_V_D + k * 128) for k in range(4)] + \
                   [(LA_BF["rkz"] + k * 128, GT_KZ + k * 128) for k in range(4)] + [(LA_BF["rv"] + k * 128, GT_RV + k * 128) for k in range(8)]
            GTv = GT[1024:, :].rearrange("(i r p) d -> p i r d", i=8, r=8)
            for row, col in jobs:
                ti, tff, to = tin.next(), tf.next(), tout.next()
                P.dma(ti[:, :], obf[row:row + 128, :], reads=[obf], writes=[ti])
                P.op("dve", lambda e: e.tensor_copy(out=tff[:, :], in_=ti[:, :]), reads=[ti], writes=[tff])
                for hb in range(2):
                    tp = tps.next()
                    for b4 in range(4):
                        b = hb * 4 + b4
                        P.op("pe", lambda e, b=b, b4=b4: e.transpose(tp[:, b4 * 128:(b4 + 1) * 128], tff[:, b * 128:(b + 1) * 128], identS[:, :]),
                             reads=[tff, identS], writes=[tp])
                    P.op("act", lambda e, hb=hb: e.copy(out=to[:, hb * 4:(hb + 1) * 4, :], in_=tp[:, :].rearrange("p (b f) -> p b f", f=128)),
                         reads=[tp], writes=[to])
                P.dma(GTv[:, :, rank, col:col + 128], to[:, :, :], reads=[to], writes=[GT], q="pool")
            GFv = GF[:, :].rearrange("f (i r t) -> f i r t", i=8, r=8)
            P.dma(GFv[0:512, :, rank, :], obf[LA_BF["ksb"]:LA_BF["ksb"] + 512, :].rearrange("f (i t) -> f i t", i=8), reads=[obf], writes=[GF])
            P.dma(GFv[512:1024, :, rank, :], obf[LA_BF["dk"]:LA_BF["dk"] + 512, :].rearrange("f (i t) -> f i t", i=8), reads=[obf], writes=[GF])
            P.phase_end()
        for rank in range(8):
            xcur, obf, of32, xmid = xcurA[rank], obfA[rank], of32A[rank], xmidA[rank]
            P.phase_begin()
            modS = P.sb("modS", [128, 96], F32)
            P.dma(modS[:, :].rearrange("p (r j) -> p r j", r=8), MOD[:, :, l, :].rearrange("r p j -> p r j"), reads=[MOD], writes=[modS])
            srcs = dict(uincl=uincl, ident=ident, masks=Buf(masksA[rank], "masks"), dmask=dmask, don=Buf(dond[l], "don"), gn=Buf(gnd[l], "gn"),
                        lamv=Buf(lamd[l], "lamv"))
            c = lb_consts(P, srcs)
            c["hs"] = P.sb("hs", [128, 8, KC, 2], F32)
            lb_load_consts(P, c, srcs)
            lb_load_layer(P, c, srcs, LAM_INIT[l])
            lb_phase(P, c, obf, of32, GF, GT, xcur, xmid, modS, Buf(Wout[u0 + U_OFF["br"]:u0 + U_OFF["br"] + 16], "wbr"),
                     Buf(Wout[u0 + U_OFF["out"]:u0 + U_OFF["out"] + 16], "wout"), rank, LAM_INIT[l])
            HLv = HL[1:65, :, :, :].rearrange("(i r) p k t -> p i r (k t)", i=8, r=8)
            P.dma(HLv[:, :, rank, :], c["hs"][:, :, :, :].rearrange("p i k t -> p i (k t)"), reads=[c["hs"]], writes=[HL])
            P.phase_end()
        for rank in range(8):
            xmid = xmidA[rank]
            P.phase_begin()
            modS = P.sb("modS", [128, 96], F32)
            P.dma(modS[:, :].rearrange("p (r j) -> p r j", r=8), MOD[:, :, l, :].rearrange("r p j -> p r j"), reads=[MOD], writes=[modS])
            cc = lc_alloc(P)
            cc["xhi"] = P.sb("xhi", [128, 8, KC * 2], F32)
            psR = RR([P.ps(f"cps{i}") for i in range(8)])
            xdst = Buf(xfin[rank], "xfin") if l == 1 else xnextA[rank]
            HLr = HL[0:64, :, :, :].rearrange("(i r) p k t -> p i r (k t)", i=8, r=8)
            lc_phase(P, cc, psR, xmid, HLr[:, :, rank, :], HL, xdst, modS, Buf(nffn[l], "nf"), Buf(cwd[l], "cw"), Buf(cbd[l], "cb"), Buf(flagA[rank], "flag"),
                     Buf(Wout[u0 + U_OFF["up"]:u0 + U_OFF["up"] + 88], "wup"),
                     Buf(Wout[u0 + U_OFF["dn"]:u0 + U_OFF["dn"] + 48].rearrange("(f g) p k n -> f p g k n", g=3), "wdn"), l == 1)
            P.phase_end()
        xcurA = xnextA
    print("n_ins", P.n_ins)
    return P.finish()


def _weight_units(inputs):
    us = []
    for l in range(2):
        us.append(la_weights(inputs["w_in"][l], inputs["w_gate"][l]))
        us.append(slabs(inputs["w_branch"][l], KC))
        us.append(slabs(inputs["w_out"][l], KC))
        us.append(slabs(inputs["w_up"][l], KC))
        wd = slabs(inputs["w_down"][l], 44)
        wdp = np.zeros((16, 128, 48, 128), np.float32)
        wdp[:, :, :44, :] = wd
        us.append(np.ascontiguousarray(wdp.reshape(16, 128, 3, 16, 128).transpose(0, 2, 1, 3, 4)).reshape(48, 128, 16, 128))
    return np.concatenate(us, axis=0)


def kernel(**inputs):
    inputs = {k: np.asarray(v) for k, v in inputs.items()}
    nc = get_nc("single", build_single)
    x = inputs["x"][0]
    units = _weight_units(inputs)
    c_in = np.ascontiguousarray(inputs["c"].reshape(KC, 128).T)
    nmix = np.stack([vec_pm(inputs["norm_mix"][l]) for l in range(2)])
    nffn = np.stack([vec_pm(inputs["norm_ffn"][l]) for l in range(2)])
    lc = [la_consts(inputs["diff_qn"][l], inputs["diff_kn"][l]) for l in range(2)]
    cst = np.stack([lc[l][0] for l in range(2)])
    dec, bones = lc[0][1], lc[0][2]
    cw = np.stack([np.ascontiguousarray(inputs["conv_w"][l].reshape(3, 88, 128).transpose(2, 0, 1)) for l in range(2)])
    cb = np.stack([vec_pm(inputs["conv_b"][l]) for l in range(2)])
    don = np.stack([np.ascontiguousarray(np.broadcast_to(inputs["diff_on"][l][None, :], (128, 128))) for l in range(2)]).astype(np.float32)
    gn = np.stack([np.concatenate([vec_pm(inputs["ret_gn_g"][l]), vec_pm(inputs["ret_gn_b"][l])], 1) for l in range(2)])
    lamv = np.stack([np.stack([inputs["lam_q1"][l], inputs["lam_k1"][l], inputs["lam_q2"][l], inputs["lam_k2"][l]], 1) for l in range(2)]).astype(np.float32)
    xT, pos, masks, flags, bada = [], [], [], [], []
    for r in range(NCORE):
        tok = own_tokens(r)
        uincl, ident, mk, dmask = lb_host_consts(r)
        flag = np.ones((128, 8, 2), np.float32)
        if r == 0:
            flag[:, 0, :] = 0.0
        xT.append(to_fm(x[tok]))
        pos.append(np.ascontiguousarray(np.broadcast_to(inputs["positions"][0, tok][None, :], (128, T))).astype(np.int32))
        masks.append(mk)
        flags.append(flag)
        bada.append(np.ascontiguousarray(inputs["b_ada"][:, r * 1536:(r + 1) * 1536].reshape(2, 12, 128).transpose(2, 0, 1)))
    wa = np.ascontiguousarray(inputs["w_ada"].reshape(2, KC, 128, 8, 1536).transpose(0, 3, 2, 1, 4))
    im = dict(xT=np.stack(xT), pos=np.stack(pos), c_in=c_in, wada=wa, bada=np.stack(bada), wsh=units,
              nmix=nmix, nffn=nffn, cst=cst, cw=cw, cb=cb, don=don, gn=gn, lamv=lamv, dec=dec, bones=bones,
              uincl=uincl, ident=ident, masks=np.stack(masks), dmask=dmask, flag=np.stack(flags))
    res = run_bass_kernel_spmd(nc, [im], core_ids=[0]).results
    xo = res[0]["xoutT"]
    out = np.zeros((S, D), np.float32)
    for r in range(NCORE):
        out[own_tokens(r)] = xo[r].transpose(2, 1, 0).reshape(T, D)
    return out[None]
```

```python
from contextlib import ExitStack
import numpy as np
import ml_dtypes
import concourse.bass as bass
import concourse.mybir as mybir
from concourse.bass_utils import run_bass_kernel_spmd

F32, BF16, I32 = mybir.dt.float32, mybir.dt.bfloat16, mybir.dt.int32
AF = mybir.ActivationFunctionType
ALU = mybir.AluOpType
NPBF = ml_dtypes.bfloat16


class Buf:
    def __init__(self, t, name):
        self.t = t
        self.name = name
        self.w = {}
        self.r = {}

    def __getitem__(self, k):
        return self.t[k]


class Prog:
    def __init__(self, n_dma_sems=24):
        self.nc = bass.Bass("TRN2", target_bir_lowering=False)
        nc = self.nc
        self.es = ExitStack()
        self.eng = dict(pe=nc.tensor, act=nc.scalar, dve=nc.vector, pool=nc.gpsimd, sp=nc.sync)
        self.semh = {k: nc.alloc_semaphore("s_" + k) for k in self.eng}
        self.cnt = {k: 0 for k in self.eng}
        self.waited = {k: {} for k in self.eng}
        self.nd = n_dma_sems
        for i in range(self.nd):
            self.semh[("d", i)] = nc.alloc_semaphore(f"dsem{i}")
        self.dcnt = [0] * self.nd
        self.drr = 0
        self.out_tokens = []
        self.n_ins = 0

    def _u(self, name):
        self.uid = getattr(self, "uid", 0) + 1
        return f"{name}_{self.uid}"

    def sb(self, name, shape, dt):
        name = self._u(name)
        t = self.es.enter_context(self.nc.sbuf_tensor(name, list(shape), dt))
        return Buf(t, name)

    def ps(self, name, shape=(128, 512), dt=F32):
        name = self._u(name)
        t = self.es.enter_context(self.nc.psum_tensor(name, list(shape), dt))
        return Buf(t, name)

    def dram(self, name, shape, dt, kind="Internal", shared=False):
        if kind == "Internal":
            name = self._u(name)
        if shared:
            t = self.nc.dram_tensor(name, list(shape), dt, kind=kind, addr_space="Shared")
        else:
            t = self.nc.dram_tensor(name, list(shape), dt, kind=kind)
        return Buf(t.ap(), name)

    def _deps(self, reads, writes):
        deps = {}

        def add(k, v):
            if deps.get(k, 0) < v:
                deps[k] = v

        for b in reads:
            for k, v in b.w.items():
                add(k, v)
        for b in writes:
            for k, v in b.w.items():
                add(k, v)
            for k, v in b.r.items():
                add(k, v)
        return deps

    def _wait(self, e, deps):
        for k, v in deps.items():
            if k == "pe" and e == "pe":
                continue
            if self.waited[e].get(k, 0) < v:
                self.eng[e].wait_ge(self.semh[k], v)
                self.waited[e][k] = v

    def _record(self, tok, reads, writes):
        k, v = tok
        for b in reads:
            if b.r.get(k, 0) < v:
                b.r[k] = v
        for b in writes:
            if b.w.get(k, 0) < v:
                b.w[k] = v
            b.r = {}

    def op(self, e, fn, reads=(), writes=(), inc=True):
        self._wait(e, self._deps(reads, writes))
        ins = fn(self.eng[e])
        self.n_ins += 1
        if inc:
            self.cnt[e] += 1
            ins.then_inc(self.semh[e], 1)
            tok = (e, self.cnt[e])
        else:
            tok = (e, self.cnt[e] + 1)
        self._record(tok, reads, writes)
        return ins

    def dma(self, out_ap, in_ap, reads=(), writes=(), q="sp", is_output=False, **kw):
        k = self.drr
        self.drr = (self.drr + 1) % self.nd
        deps = self._deps(reads, writes)
        key = ("d", k)
        if self.dcnt[k] > 0 and deps.get(key, 0) < self.dcnt[k] * 16:
            deps[key] = self.dcnt[k] * 16
        self._wait(q, deps)
        ins = self.eng[q].dma_start(out=out_ap, in_=in_ap, **kw)
        self.n_ins += 1
        self.dcnt[k] += 1
        ins.then_inc(self.semh[key], 16)
        tok = (key, self.dcnt[k] * 16)
        self._record(tok, reads, writes)
        if is_output:
            self.out_tokens.append(tok)
        return ins

    def barrier(self):
        full = {k: v for k, v in self.cnt.items() if v > 0}
        for i in range(self.nd):
            if self.dcnt[i] > 0:
                full[("d", i)] = self.dcnt[i] * 16
        for e in self.eng:
            self._wait(e, dict(full))

    def phase_begin(self):
        self.barrier()
        self.es_outer = self.es
        self.es = ExitStack()

    def phase_end(self):
        self.barrier()
        self.es.close()
        self.es = self.es_outer

    def all_reduce(self, in_ap, out_ap, reads=(), writes=()):
        k = self.drr
        self.drr = (self.drr + 1) % self.nd
        deps = self._deps(reads, writes)
        key = ("d", k)
        if self.dcnt[k] > 0 and deps.get(key, 0) < self.dcnt[k] * 16:
            deps[key] = self.dcnt[k] * 16
        self._wait("pool", deps)
        ins = self.eng["pool"].collective_compute("AllReduce", ALU.add, replica_groups=[list(range(8))], ins=[in_ap], outs=[out_ap])
        self.dcnt[k] += 1
        ins.then_inc(self.semh[key], 16)
        tok = (key, self.dcnt[k] * 16)
        self._record(tok, reads, writes)

    def finish(self):
        final = {}
        for k, v in self.out_tokens:
            if final.get(k, 0) < v:
                final[k] = v
        self._wait("sp", final)
        self.es.close()
        return self.nc


class RR:
    def __init__(self, bufs):
        self.bufs = bufs
        self.i = 0

    def next(self):
        b = self.bufs[self.i]
        self.i = (self.i + 1) % len(self.bufs)
        return b


D = 2048
S = 8192
NCORE = 8
T = 1024
NB = 8
KC = D // 128
D_FF = 5632
EPS = 1e-6
THETA = 10000.0
LOG_GAMMA = np.log(1.0 - 2.0 ** (-5.0 - np.arange(4, dtype=np.float64)))
PI = float(np.pi)
TWO_PI = float(2 * np.pi)


def _run(nc, in_maps):
    res = run_bass_kernel_spmd(nc, in_maps, core_ids=list(range(NCORE)))
    return res.results


def build_L0():
    P = Prog()
    nc = P.nc
    c_in = P.dram("c_in", [128, KC], F32, kind="ExternalInput")
    wada = P.dram("wada", [2, 128, KC, 1536], F32, kind="ExternalInput")
    bada = P.dram("bada", [2, 1536], F32, kind="ExternalInput")
    modo = P.dram("modo", [2, 1536], F32, kind="ExternalOutput")
    ct = P.sb("ct", [128, KC], F32)
    ca = P.sb("ca", [128, KC], F32)
    wt = P.sb("wt", [128, KC, 1536], F32)
    bt = P.sb("bt", [1, 2, 1536], F32)
    ot = P.sb("ot", [1, 2, 1536], F32)
    pss = [P.ps(f"ps{i}") for i in range(3)]
    P.dma(ct[:], c_in[:, :], writes=[ct])
    P.dma(bt[0:1, :, :], bada[:, :].unsqueeze(0), writes=[bt])
    P.op("act", lambda e: e.activation(out=ca[:], in_=ct[:], func=AF.Silu), reads=[ct], writes=[ca])
    for l in range(2):
        for q in range(4):
            P.dma(wt[:, 4 * q:4 * q + 4, :], wada[l, :, 4 * q:4 * q + 4, :], writes=[wt])
        for n in range(3):
            for kc in range(KC):
                P.op("pe", lambda e, n=n, kc=kc: e.matmul(pss[n][0:1, :], lhsT=ca[:, kc:kc + 1],
                                                       rhs=wt[:, kc, n * 512:(n + 1) * 512],
                                                       start=(kc == 0), stop=(kc == KC - 1)),
                     reads=[ca, wt], writes=[pss[n]], inc=(kc == KC - 1))
            P.op("dve", lambda e, n=n, l=l: e.tensor_tensor(out=ot[0:1, l, n * 512:(n + 1) * 512], in0=pss[n][0:1, :],
                                                         in1=bt[0:1, l, n * 512:(n + 1) * 512], op=ALU.add),
                 reads=[pss[n], bt], writes=[ot])
    P.dma(modo[:, :].unsqueeze(0), ot[0:1, :, :], reads=[ot], writes=[modo], is_output=True)
    return P.finish()


def run_L0(inputs):
    nc = build_L0()
    c = np.ascontiguousarray(inputs["c"].reshape(KC, 128).T)
    in_maps = []
    for r in range(NCORE):
        w = inputs["w_ada"][:, :, r * 1536:(r + 1) * 1536]
        w = np.ascontiguousarray(w.reshape(2, KC, 128, 1536).transpose(0, 2, 1, 3))
        b = np.ascontiguousarray(inputs["b_ada"][:, r * 1536:(r + 1) * 1536])
        in_maps.append({"c_in": c, "wada": w, "bada": b})
    res = _run(nc, in_maps)
    mod = np.concatenate([res[r]["modo"] for r in range(NCORE)], axis=1)
    return mod


def load_slab(P, wst, wbf, wsrc_ap, nkc, cast_eng="pool"):
    st = wst.next()
    P.dma(st[:, 0:nkc, :], wsrc_ap, writes=[st])
    wb = wbf.next()
    if cast_eng == "act":
        P.op("act", lambda e: e.copy(out=wb[:, 0:nkc, :], in_=st[:, 0:nkc, :]), reads=[st], writes=[wb])
    else:
        P.op(cast_eng, lambda e: e.tensor_copy(out=wb[:, 0:nkc, :], in_=st[:, 0:nkc, :]), reads=[st], writes=[wb])
    return wb


def adaln_norm(P, xT, hT, ntok, gs, shift_ap_fn, ones32, psA, psB, rstd, tmpRR):
    nt = (ntok + 511) // 512
    for h in range(nt):
        t0, t1 = h * 512, min(ntok, (h + 1) * 512)
        w = t1 - t0
        pb = psA if h % 2 == 0 else psB
        for kc in range(KC):
            sq = tmpRR.next()
            P.op("act", lambda e, kc=kc, sq=sq: e.activation(out=sq[:, 0:w], in_=xT[:, kc, t0:t1], func=AF.Square),
                 reads=[xT], writes=[sq])
            P.op("pe", lambda e, kc=kc, sq=sq: e.matmul(pb[:, 0:w], lhsT=ones32[:, :], rhs=sq[:, 0:w],
                                                       start=(kc == 0), stop=(kc == KC - 1)),
                 reads=[ones32, sq], writes=[pb], inc=True)
        P.op("dve", lambda e: e.tensor_scalar(out=rstd[:, t0:t1], in0=pb[:, 0:w], scalar1=1.0 / D, scalar2=EPS,
                                              op0=ALU.mult, op1=ALU.add), reads=[pb], writes=[rstd])
        P.op("act", lambda e: e.activation(out=rstd[:, t0:t1], in_=rstd[:, t0:t1], func=AF.Sqrt), reads=[rstd], writes=[rstd])
        P.op("dve", lambda e: e.reciprocal(out=rstd[:, t0:t1], in_=rstd[:, t0:t1]), reads=[rstd], writes=[rstd])
        for kc in range(KC):
            tm = tmpRR.next()
            P.op("dve", lambda e, kc=kc, tm=tm: e.tensor_tensor(out=tm[:, 0:w], in0=xT[:, kc, t0:t1], in1=rstd[:, t0:t1],
                                                             op=ALU.mult), reads=[xT, rstd], writes=[tm])
            P.op("act", lambda e, kc=kc, tm=tm: e.activation(out=hT[:, kc, t0:t1], in_=tm[:, 0:w], func=AF.Identity,
                                                          scale=gs[:, kc:kc + 1], bias=shift_ap_fn(kc)),
                 reads=[tm, gs], writes=[hT])


LA_BF = dict(qsb=0, ksb=512, vsb=1024, rq=1536, rqx=2048, rk=2560, rkz=3072, rv=3584, dq=4608, dk=5120, dv=5632)
LA_BF_ROWS = 6144
LA_F32 = dict(rg=0, gates=1024)
LA_F32_ROWS = 1024 + 6144
N_SLAB_A = 48 + 16 + 48


def la_phase(P, d, w_bf16):
    xTd, modT, nmix, posd, cst, dec, bones, wsl, obf, of32 = (d[k] for k in
        ("xT", "modT", "nmix", "pos", "cst", "dec", "bones", "wsl", "obf", "of32"))
    out_flag = d.get("is_output", False)

    xT = P.sb("xTs", [128, KC, T], F32)
    hT = P.sb("hT", [128, KC, T], BF16)
    modS = P.sb("modS", [128, 96], F32)
    nmS = P.sb("nmS", [128, KC], F32)
    gs = P.sb("gs", [128, KC], F32)
    cS = P.sb("cS", [128, 8], F32)
    decS = P.sb("decS", [128, 8, 128], F32)
    posI = P.sb("posI", [128, T], I32)
    posF = P.sb("posF", [128, T], F32)
    cosR = P.sb("cosR", [128, T], F32)
    sinR = P.sb("sinR", [128, T], F32)
    cosD = P.sb("cosD", [128, T], F32)
    sinD = P.sb("sinD", [128, T], F32)
    cqT = P.sb("cqT", [128, T], F32)
    sqT = P.sb("sqT", [128, T], F32)
    ones32 = P.sb("ones32", [128, 128], F32)
    bonesB = P.sb("bonesB", [128, 128], BF16)
    bonesF = P.sb("bonesF", [128, 128], F32)
    rstd = P.sb("rstd", [128, T], F32)
    ang = rstd
    ckT, skT = cosD, sinD
    pib = P.sb("pib", [128, 1], F32)
    tmpRR = RR([P.sb(f"tmp{i}", [128, 512], F32) for i in range(4)])
    wst = None if w_bf16 else RR([P.sb(f"wst{i}", [128, KC, 128], F32) for i in range(2)])
    wbf = RR([P.sb(f"wbf{i}", [128, KC, 128], BF16) for i in range(4)])
    obR = RR([P.sb(f"ob{i}", [128, T], BF16) for i in range(4)])
    ofR = RR([P.sb(f"of{i}", [128, T], F32) for i in range(3)])
    sqR = RR([P.sb(f"sqb{i}", [128, 512], BF16) for i in range(2)])
    psR = RR([P.ps(f"ps{i}") for i in range(6 if w_bf16 else 8)])

    for q in range(4):
        P.dma(xT[:, 4 * q:4 * q + 4, :], xTd[:, 4 * q:4 * q + 4, :], writes=[xT])
    if w_bf16:
        P.dma(modS[:, :].rearrange("p (r j) -> p r j", r=8), modT[:, :, :], writes=[modS])
    else:
        P.dma(modS[:], modT[:, :], writes=[modS])
    P.dma(nmS[:], nmix[:, :], writes=[nmS])
    P.dma(cS[:], cst[:, :], writes=[cS])
    P.dma(decS[:], dec[:, :, :], writes=[decS])
    P.dma(posI[:], posd[:, :], writes=[posI])
    P.dma(bonesF[:], bones[:, :], writes=[bonesF])
    P.op("pool", lambda e: e.memset(ones32[:], 1.0), writes=[ones32])
    P.op("pool", lambda e: e.memset(pib[:], PI), writes=[pib])
    P.op("pool", lambda e: e.tensor_copy(out=bonesB[:], in_=bonesF[:]), reads=[bonesF], writes=[bonesB])
    P.op("dve", lambda e: e.scalar_tensor_tensor(out=gs[:], in0=modS[:, 16:32], scalar=1.0, in1=nmS[:],
                                                 op0=ALU.add, op1=ALU.mult), reads=[modS, nmS], writes=[gs])
    P.op("dve", lambda e: e.tensor_copy(out=posF[:], in_=posI[:]), reads=[posI], writes=[posF])

    def sincos(inv_col, sign_col, cosT, sinT):
        def one(outT, shift):
            P.op("dve", lambda e: e.tensor_scalar(out=ang[:], in0=posF[:], scalar1=cS[:, inv_col:inv_col + 1], scalar2=shift,
                                                  op0=ALU.mult, op1=ALU.add), reads=[posF, cS], writes=[ang])
            P.op("dve", lambda e: e.tensor_copy(out=posI[:], in_=ang[:]), reads=[ang], writes=[posI])
            P.op("dve", lambda e: e.tensor_copy(out=outT[:], in_=posI[:]), reads=[posI], writes=[outT])
            P.op("dve", lambda e: e.tensor_tensor(out=ang[:], in0=ang[:], in1=outT[:], op=ALU.subtract), reads=[ang, outT], writes=[ang])
            P.op("dve", lambda e: e.tensor_single_scalar(out=outT[:], in_=ang[:], scalar=0.5, op=ALU.is_gt), reads=[ang], writes=[outT])
            P.op("dve", lambda e: e.tensor_tensor(out=ang[:], in0=ang[:], in1=outT[:], op=ALU.subtract), reads=[ang, outT], writes=[ang])
            P.op("dve", lambda e: e.tensor_single_scalar(out=outT[:], in_=ang[:], scalar=-0.5, op=ALU.is_lt), reads=[ang], writes=[outT])
            P.op("dve", lambda e: e.tensor_tensor(out=ang[:], in0=ang[:], in1=outT[:], op=ALU.add), reads=[ang, outT], writes=[ang])
            P.op("act", lambda e: e.activation(out=outT[:], in_=ang[:], func=AF.Sin, scale=TWO_PI), reads=[ang], writes=[outT])
        one(sinT, 0.0)
        P.op("dve", lambda e: e.tensor_scalar(out=sinT[:], in0=sinT[:], scalar1=cS[:, sign_col:sign_col + 1], scalar2=None,
                                              op0=ALU.mult), reads=[sinT, cS], writes=[sinT])
        one(cosT, 0.25)

    sincos(0, 1, cosR, sinR)
    sincos(2, 3, cosD, sinD)
    P.op("dve", lambda e: e.tensor_scalar(out=cqT[:], in0=cosD[:], scalar1=cS[:, 4:5], scalar2=0.125, op0=ALU.mult, op1=ALU.mult),
         reads=[cosD, cS], writes=[cqT])
    P.op("dve", lambda e: e.tensor_scalar(out=sqT[:], in0=sinD[:], scalar1=cS[:, 5:6], scalar2=0.125, op0=ALU.mult, op1=ALU.mult),
         reads=[sinD, cS], writes=[sqT])
    P.op("dve", lambda e: e.tensor_scalar(out=ckT[:], in0=cosD[:], scalar1=cS[:, 6:7], scalar2=None, op0=ALU.mult),
         reads=[cosD, cS], writes=[ckT])
    P.op("dve", lambda e: e.tensor_scalar(out=skT[:], in0=sinD[:], scalar1=cS[:, 7:8], scalar2=None, op0=ALU.mult),
         reads=[sinD, cS], writes=[skT])

    psA, psB = psR.next(), psR.next()
    adaln_norm(P, xT, hT, T, gs, lambda kc: modS[:, kc:kc + 1], ones32, psA, psB, rstd, tmpRR)

    def proj(slab_idx):
        if w_bf16:
            wb = wbf.next()
            P.dma(wb[:, :, :], wsl[slab_idx], reads=[wsl], writes=[wb])
        else:
            wb = load_slab(P, wst, wbf, wsl[slab_idx], KC)
        outs = []
        for h in range(2):
            pb = psR.next()
            for kc in range(KC):
                P.op("pe", lambda e, kc=kc, pb=pb, h=h: e.matmul(pb[:, :], lhsT=wb[:, kc, :], rhs=hT[:, kc, h * 512:(h + 1) * 512],
                                                              start=(kc == 0), stop=(kc == KC - 1)),
                     reads=[wb, hT], writes=[pb], inc=(kc == KC - 1))
            outs.append(pb)
        return outs

    def store_bf(ob, row):
        P.dma(obf[row:row + 128, :], ob[:, :], reads=[ob], writes=[obf], is_output=out_flag, q="pool")

    def store_f32(of, row):
        P.dma(of32[row:row + 128, :], of[:, :], reads=[of], writes=[of32], is_output=out_flag, q="pool")

    def simple(slab_idx, row, scale=None):
        pbs = proj(slab_idx)
        ob = obR.next()
        for h in range(2):
            if scale is None:
                P.op("act", lambda e, h=h: e.copy(out=ob[:, h * 512:(h + 1) * 512], in_=pbs[h][:, :]), reads=[pbs[h]], writes=[ob])
            else:
                P.op("act", lambda e, h=h: e.mul(out=ob[:, h * 512:(h + 1) * 512], in_=pbs[h][:, :], mul=scale), reads=[pbs[h]], writes=[ob])
        store_bf(ob, row)

    def actf32(slab_idx, row, func):
        pbs = proj(slab_idx)
        of = ofR.next()
        for h in range(2):
            P.op("act", lambda e, h=h: e.activation(out=of[:, h * 512:(h + 1) * 512], in_=pbs[h][:, :], func=func),
                 reads=[pbs[h]], writes=[of])
        store_f32(of, row)

    def rope_pair(slab_idx, rot_idx, cT, sT, h):
        raise NotImplementedError

    def rope_ret(slab_idx, rot_idx, head, is_q):
        pa = proj(slab_idx)
        pbr = proj(rot_idx)
        o1, o2 = obR.next(), obR.next()
        for h in range(2):
            sl = slice(h * 512, (h + 1) * 512)
            t1, t2 = tmpRR.next(), tmpRR.next()
            P.op("dve", lambda e: e.tensor_tensor(out=t1[:, :], in0=pa[h][:, :], in1=cosR[:, sl], op=ALU.mult),
                 reads=[pa[h], cosR], writes=[t1])
            P.op("dve", lambda e: e.tensor_tensor(out=t2[:, :], in0=pbr[h][:, :], in1=sinR[:, sl], op=ALU.mult),
                 reads=[pbr[h], sinR], writes=[t2])
            P.op("pool", lambda e: e.tensor_tensor(out=t1[:, :], in0=t1[:, :], in1=t2[:, :], op=ALU.add),
                 reads=[t1, t2], writes=[t1])
            tab = decS[:, (head if is_q else 4 + head), :].unsqueeze(1).broadcast_to([128, 4, 128])
            t1v = t1[:, :].rearrange("p (b t) -> p b t", t=128)
            if is_q:
                P.op("act", lambda e: e.copy(out=o1[:, sl], in_=t1[:, :]), reads=[t1], writes=[o1])
                P.op("pool", lambda e: e.tensor_tensor(out=o2[:, sl].rearrange("p (b t) -> p b t", t=128), in0=t1v, in1=tab, op=ALU.mult),
                     reads=[t1, decS], writes=[o2])
            else:
                sc = 128.0 ** -0.5
                P.op("act", lambda e: e.mul(out=o1[:, sl], in_=t1[:, :], mul=sc), reads=[t1], writes=[o1])
                P.op("dve", lambda e: e.scalar_tensor_tensor(out=o2[:, sl].rearrange("p (b t) -> p b t", t=128), in0=t1v, scalar=sc,
                                                              in1=tab, op0=ALU.mult, op1=ALU.mult),
                     reads=[t1, decS], writes=[o2])
        if is_q:
            store_bf(o1, LA_BF["rq"] + head * 128)
            store_bf(o2, LA_BF["rqx"] + head * 128)
        else:
            store_bf(o1, LA_BF["rk"] + head * 128)
            store_bf(o2, LA_BF["rkz"] + head * 128)

    def rope_diff(slab_idx, rot_idx, head, is_q):
        pa = proj(slab_idx)
        pbr = proj(rot_idx)
        cT, sT = (cqT, sqT) if is_q else (ckT, skT)
        o1 = obR.next()
        for h in range(2):
            sl = slice(h * 512, (h + 1) * 512)
            sq = sqR.next()
            P.op("act", lambda e: e.activation(out=sq[:, :], in_=pa[h][:, :], func=AF.Square), reads=[pa[h]], writes=[sq])
            pc = psR.next()
            P.op("pe", lambda e: e.matmul(pc[:, :], lhsT=bonesB[:, :], rhs=sq[:, :], start=True, stop=True),
                 reads=[bonesB, sq], writes=[pc])
            t1, t2, t3 = tmpRR.next(), tmpRR.next(), tmpRR.next()
            P.op("dve", lambda e: e.tensor_scalar(out=t3[:, :], in0=pc[:, :], scalar1=1.0 / 64, scalar2=EPS, op0=ALU.mult, op1=ALU.add),
                 reads=[pc], writes=[t3])
            P.op("act", lambda e: e.activation(out=t3[:, :], in_=t3[:, :], func=AF.Sqrt), reads=[t3], writes=[t3])
            P.op("dve", lambda e: e.reciprocal(out=t3[:, :], in_=t3[:, :]), reads=[t3], writes=[t3])
            P.op("dve", lambda e: e.tensor_tensor(out=t1[:, :], in0=pa[h][:, :], in1=cT[:, sl], op=ALU.mult),
                 reads=[pa[h], cT], writes=[t1])
            P.op("dve", lambda e: e.tensor_tensor(out=t2[:, :], in0=pbr[h][:, :], in1=sT[:, sl], op=ALU.mult),
                 reads=[pbr[h], sT], writes=[t2])
            P.op("pool", lambda e: e.tensor_tensor(out=t1[:, :], in0=t1[:, :], in1=t2[:, :], op=ALU.add), reads=[t1, t2], writes=[t1])
            P.op("pool", lambda e: e.tensor_tensor(out=o1[:, sl], in0=t1[:, :], in1=t3[:, :], op=ALU.mult), reads=[t1, t3], writes=[o1])
        store_bf(o1, (LA_BF["dq"] if is_q else LA_BF["dk"]) + head * 128)

    for j in range(4):
        simple(j, LA_BF["qsb"] + j * 128, scale=128.0 ** -0.5)
    for j in range(4):
        simple(4 + j, LA_BF["ksb"] + j * 128)
    for j in range(4):
        simple(8 + j, LA_BF["vsb"] + j * 128)
    for j in range(4):
        rope_ret(12 + j, 48 + j, j, True)
    for j in range(4):
        rope_ret(16 + j, 52 + j, j, False)
    for j in range(8):
        simple(20 + j, LA_BF["rv"] + j * 128)
    for j in range(8):
        actf32(28 + j, LA_F32["rg"] + j * 128, AF.Silu)
    for j in range(4):
        rope_diff(36 + j, 56 + j, j, True)
    for j in range(4):
        rope_diff(40 + j, 60 + j, j, False)
    for j in range(4):
        simple(44 + j, LA_BF["dv"] + j * 128)
    for j in range(48):
        actf32(64 + j, LA_F32["gates"] + j * 128, AF.Sigmoid)


def build_LA():
    P = Prog()
    d = dict(xT=P.dram("xT", [128, KC, T], F32, kind="ExternalInput"), modT=P.dram("modT", [128, 96], F32, kind="ExternalInput"),
             nmix=P.dram("nmix", [128, KC], F32, kind="ExternalInput"), pos=P.dram("pos", [128, T], I32, kind="ExternalInput"),
             cst=P.dram("cst", [128, 8], F32, kind="ExternalInput"), dec=P.dram("dec", [128, 8, 128], F32, kind="ExternalInput"),
             bones=P.dram("bones", [128, 128], F32, kind="ExternalInput"), wsl=P.dram("wsl", [N_SLAB_A, 128, KC, 128], F32, kind="ExternalInput"),
             obf=P.dram("obf", [LA_BF_ROWS, T], BF16, kind="ExternalOutput"), of32=P.dram("of32", [LA_F32_ROWS, T], F32, kind="ExternalOutput"),
             is_output=True)
    la_phase(P, d, False)
    return P.finish()


def own_tokens(core):
    return np.concatenate([np.arange((8 * i + core) * 128, (8 * i + core + 1) * 128) for i in range(NB)])


def to_fm(x_tok):
    Tn, Fn = x_tok.shape
    return np.ascontiguousarray(x_tok.T.reshape(Fn // 128, 128, Tn).transpose(1, 0, 2))


def vec_pm(v):
    return np.ascontiguousarray(v.reshape(-1, 128).T)


def slabs(W, nkc):
    K_, N_ = W.shape
    return np.ascontiguousarray(W.reshape(nkc, 128, N_ // 128, 128).transpose(2, 1, 0, 3))


def la_consts(diff_qn, diff_kn):
    p = np.arange(128)
    inv_ret = THETA ** (-(2.0 * (p % 64)) / 128.0)
    sign_ret = np.where(p < 64, -1.0, 1.0)
    q = p % 64
    inv_diff = THETA ** (-(2.0 * (q % 32)) / 64.0)
    sign_diff = np.where(q < 32, -1.0, 1.0)
    partner = np.where(q < 32, q + 32, q - 32)
    cst = np.stack([inv_ret / (2 * np.pi), sign_ret, inv_diff / (2 * np.pi), sign_diff, diff_qn[q], diff_qn[partner], diff_kn[q], diff_kn[partner]], axis=1)
    tl = np.arange(128)
    dec = np.zeros((8, 128), np.float64)
    for h in range(4):
        dec[h] = np.exp((tl + 1.0) * LOG_GAMMA[h])
        dec[4 + h] = np.exp((127.0 - tl) * LOG_GAMMA[h])
    dec = np.broadcast_to(dec[None], (128, 8, 128))
    bones = np.zeros((128, 128), np.float32)
    bones[:64, :64] = 1.0
    bones[64:, 64:] = 1.0
    return cst.astype(np.float32), np.ascontiguousarray(dec).astype(np.float32), bones


def la_weights(w_in, w_gate):
    cols = np.arange(6144)
    perm = cols.copy()
    for base in (1536, 2048):
        for hd in range(4):
            o = base + hd * 128
            perm[o:o + 64] = np.arange(o + 64, o + 128)
            perm[o + 64:o + 128] = np.arange(o, o + 64)
    for base in (4608, 5120):
        for m in range(8):
            o = base + m * 64
            perm[o:o + 32] = np.arange(o + 32, o + 64)
            perm[o + 32:o + 64] = np.arange(o, o + 32)
    rot_cols = np.concatenate([np.arange(1536, 2048), np.arange(2048, 2560), np.arange(4608, 5120), np.arange(5120, 5632)])
    Wcat = np.concatenate([w_in, w_in[:, perm[rot_cols]], w_gate], axis=1)
    return slabs(Wcat, KC)


_CACHE = {}


def get_nc(name, fn):
    if name not in _CACHE:
        _CACHE[name] = fn()
    return _CACHE[name]


def run_LA(inputs, l, x_full, mod):
    nc = get_nc("LA", build_LA)
    cst, dec, bones = la_consts(inputs["diff_qn"][l], inputs["diff_kn"][l])
    wsl = la_weights(inputs["w_in"][l], inputs["w_gate"][l])
    modT = vec_pm(mod[l])
    nmix = vec_pm(inputs["norm_mix"][l])
    in_maps = []
    for r in range(NCORE):
        tok = own_tokens(r)
        pos = np.ascontiguousarray(np.broadcast_to(inputs["positions"][0, tok][None, :], (128, T))).astype(np.int32)
        in_maps.append(dict(xT=to_fm(x_full[tok]), modT=modT, nmix=nmix, pos=pos, cst=cst, dec=dec, bones=bones, wsl=wsl))
    res = _run(nc, in_maps)
    return [(res[r]["obf"], res[r]["of32"]) for r in range(NCORE)]


GT_V_SB, GT_V_D, GT_KZ, GT_RV = 0, 512, 1024, 1536


def lb_phase(P, c, obf, of32, GF, GT, xTd, xoutd, modS, wbr, wout, rank, lam_init, dbg=None):
    psR = c["psR"]
    ident = c["identF"]
    ysbT = c["ybr"]
    qsb = c["qown"]
    P.dma(qsb[:, :, :], obf[LA_BF["qsb"]:LA_BF["qsb"] + 512, :].rearrange("(h p) t -> p h t", p=128), writes=[qsb])
    kT, nkT, vS = c["kT"], c["nkT"], c["vS"]
    R = c["R"]
    psA = c["psA"]
    for hd in range(4):
        P.dma(kT[:, :], GF[hd * 128:(hd + 1) * 128, :], reads=[GF], writes=[kT])
        P.dma(vS[:, :, 0:128], GT[1024:, GT_V_SB + hd * 128:GT_V_SB + (hd + 1) * 128].rearrange("(b p) d -> p b d", p=128),
              reads=[GT], writes=[vS])
        P.op("pool", lambda e: e.tensor_scalar(out=nkT[:, :], in0=kT[:, :], scalar1=-1.0, scalar2=None, op0=ALU.mult),
             reads=[kT], writes=[nkT])
        items = [(i, g) for i in range(NB) for g in range(2 * i + 1, -1, -1)]
        st = {}

        def stA(k, hd=hd):
            i, g = items[k]
            diag = g >= 2 * i
            Z = psR.next()
            for j in range(4):
                blk = 4 * g + j
                P.op("pe", lambda e, j=j, blk=blk: e.matmul(Z[:, j * 128:(j + 1) * 128], lhsT=kT[:, blk * 128:(blk + 1) * 128],
                                                         rhs=qsb[:, hd, i * 128:(i + 1) * 128], start=True, stop=True),
                     reads=[kT, qsb], writes=[Z], inc=(j == 3))
            ef = c["efR"].next()
            P.op("act", lambda e: e.activation(out=ef[:, :], in_=Z[:, :], func=AF.Exp), reads=[Z], writes=[ef])
            sp = c["spR"].next()
            P.op("act", lambda e: e.activation(out=sp[:, :], in_=ef[:, :], func=AF.Ln, bias=c["one"][:, 0:1]),
                 reads=[ef, c["one"]], writes=[sp])
            if diag:
                P.op("dve", lambda e: e.tensor_tensor(out=sp[:, :], in0=sp[:, :], in1=c["msb"][:, g - 2 * i, :], op=ALU.mult),
                     reads=[sp, c["msb"]], writes=[sp])
            st[k] = dict(sp=sp)

        def stB(k, hd=hd):
            i, g = items[k]
            diag = g >= 2 * i
            sp = st[k]["sp"]
            if g == 2 * i + 1:
                P.op("pool", lambda e: e.memset(R[:, :], 0.0), writes=[R])
            Pb = psR.next()
            Tb = psR.next()
            for j in range(4):
                blk = 4 * g + j
                P.op("pe", lambda e, j=j: e.matmul(Pb[:, j * 128:(j + 1) * 128], lhsT=c["uincl"][:, :], rhs=sp[:, j * 128:(j + 1) * 128],
                                                 start=True, stop=False), reads=[c["uincl"], sp], writes=[Pb], inc=False)
                for j2 in range(j + 1, 4):
                    P.op("pe", lambda e, j=j, j2=j2: e.matmul(Pb[:, j * 128:(j + 1) * 128], lhsT=c["onesB"][:, :], rhs=sp[:, j2 * 128:(j2 + 1) * 128],
                                                           start=False, stop=False), reads=[c["onesB"], sp], writes=[Pb], inc=False)
                P.op("pe", lambda e, j=j, blk=blk: e.matmul(Pb[:, j * 128:(j + 1) * 128], lhsT=nkT[:, blk * 128:(blk + 1) * 128],
                                                         rhs=qsb[:, hd, i * 128:(i + 1) * 128], start=False, stop=True),
                     reads=[nkT, qsb], writes=[Pb], inc=False)
            for j in range(4):
                P.op("pe", lambda e, j=j: e.matmul(Tb[:, 0:128], lhsT=c["onesB"][:, :], rhs=sp[:, j * 128:(j + 1) * 128],
                                                 start=(j == 0), stop=(j == 3)), reads=[c["onesB"], sp], writes=[Tb, Pb], inc=(j == 3))
            tf = c["efR"].next()
            P.op("dve", lambda e: e.tensor_tensor(out=tf[:, :].rearrange("p (b t) -> p b t", t=128),
                                                  in0=Pb[:, :].rearrange("p (b t) -> p b t", t=128),
                                                  in1=R[:, :].unsqueeze(1).broadcast_to([128, 4, 128]), op=ALU.add),
                 reads=[Pb, R], writes=[tf])
            wt = c["wR"].next()
            P.op("act", lambda e: e.activation(out=wt[:, :], in_=tf[:, :], func=AF.Exp, scale=-1.0), reads=[tf], writes=[wt])
            if diag:
                P.op("dve", lambda e: e.tensor_tensor(out=wt[:, :], in0=wt[:, :], in1=c["msb"][:, g - 2 * i, :], op=ALU.mult),
                     reads=[wt, c["msb"]], writes=[wt])
            P.op("dve", lambda e: e.tensor_tensor(out=R[:, :], in0=R[:, :], in1=Tb[:, 0:128], op=ALU.add), reads=[R, Tb], writes=[R])
            st[k]["wt"] = wt

        def stC(k, hd=hd):
            i, g = items[k]
            wt = st[k]["wt"]
            Y = psA[i % 2]
            for j in range(4):
                blk = 4 * g + j
                P.op("pe", lambda e, j=j, blk=blk: e.matmul(Y[:, 0:128], lhsT=vS[:, blk, 0:128], rhs=wt[:, j * 128:(j + 1) * 128],
                                                         start=(g == 2 * i + 1 and j == 0), stop=(g == 0 and j == 3)),
                     reads=[vS, wt], writes=[Y], inc=(j == 3))
            if g == 0:
                P.op("act", lambda e: e.copy(out=ysbT[:, hd, i * 128:(i + 1) * 128], in_=Y[:, 0:128]), reads=[Y], writes=[ysbT])
            del st[k]

        n = len(items)
        for k in range(n + 2):
            if k < n:
                stA(k)
            if 0 <= k - 1 < n:
                stB(k - 1)
            if 0 <= k - 2 < n:
                stC(k - 2)

    qd = c["qown"]
    P.dma(qd[:, :, :], obf[LA_BF["dq"]:LA_BF["dq"] + 512, :].rearrange("(h p) t -> p h t", p=128), writes=[qd])
    for hd in range(4):
        P.dma(kT[:, :], GF[512 + hd * 128:512 + (hd + 1) * 128, :], reads=[GF], writes=[kT])
        P.dma(vS[:, :, 0:128], GT[1024:, GT_V_D + hd * 128:GT_V_D + (hd + 1) * 128].rearrange("(b p) d -> p b d", p=128),
              reads=[GT], writes=[vS])
        P.op("pool", lambda e: e.memset(vS[:, :, 128:129], 1.0), writes=[vS])
        items = [(i, g, m) for i in range(NB) for g in range(2 * i + 2) for m in range(2)]
        st = {}

        def dA(k, hd=hd):
            i, g, m = items[k]
            diag = g >= 2 * i
            Sb = psR.next()
            for j in range(4):
                blk = 4 * g + j
                P.op("pe", lambda e, j=j, blk=blk: e.matmul(
                    Sb[:, j * 128:(j + 1) * 128], lhsT=kT[m * 64:(m + 1) * 64, blk * 128:(blk + 1) * 128],
                    rhs=qd[m * 64:(m + 1) * 64, hd, i * 128:(i + 1) * 128], start=True, stop=True),
                    reads=[kT, qd], writes=[Sb], inc=(j == 3))
            eb = c["wR"].next()
            P.op("act", lambda e: e.activation(out=eb[:, :], in_=Sb[:, :], func=AF.Exp), reads=[Sb], writes=[eb])
            if diag:
                P.op("dve", lambda e: e.tensor_tensor(out=eb[:, :], in0=eb[:, :], in1=c["mdf"][:, g - 2 * i, :], op=ALU.mult),
                     reads=[eb, c["mdf"]], writes=[eb])
            st[k] = eb

        def dC(k, hd=hd):
            i, g, m = items[k]
            ng = 2 * i + 2
            eb = st.pop(k)
            Ym = [psA[2 * (i % 2)], psA[2 * (i % 2) + 1]]
            for j in range(4):
                blk = 4 * g + j
                P.op("pe", lambda e, j=j, blk=blk: e.matmul(
                    Ym[m][:, 0:129], lhsT=eb[:, j * 128:(j + 1) * 128], rhs=vS[:, blk, 0:129],
                    start=(g == 0 and j == 0), stop=(g == ng - 1 and j == 3)),
                    reads=[eb, vS], writes=[Ym[m]], inc=(j == 3))
            if not (g == ng - 1 and m == 1):
                return
            rc = c["small"].next()
            P.op("dve", lambda e: e.reciprocal(out=rc[:, 0:1], in_=Ym[0][:, 128:129]), reads=[Ym[0]], writes=[rc])
            P.op("dve", lambda e: e.reciprocal(out=rc[:, 1:2], in_=Ym[1][:, 128:129]), reads=[Ym[1]], writes=[rc])
            P.op("dve", lambda e: e.tensor_tensor(out=rc[:, 1:2], in0=rc[:, 1:2], in1=c["nlam"][:, 0:1], op=ALU.mult),
                 reads=[rc, c["nlam"]], writes=[rc])
            ya = c["tokR"].next()
            yb = c["tokR"].next()
            P.op("dve", lambda e: e.tensor_scalar(out=ya[:, 0:128], in0=Ym[0][:, 0:128], scalar1=rc[:, 0:1], scalar2=None, op0=ALU.mult),
                 reads=[Ym[0], rc], writes=[ya])
            P.op("dve", lambda e: e.scalar_tensor_tensor(out=yb[:, 0:128], in0=Ym[1][:, 0:128], scalar=rc[:, 1:2], in1=ya[:, 0:128],
                                                         op0=ALU.mult, op1=ALU.add), reads=[Ym[1], rc, ya], writes=[yb])
            P.op("act", lambda e: e.activation(out=ya[:, 0:128], in_=yb[:, 0:128], func=AF.Square, accum_out=rc[:, 2:3]),
                 reads=[yb], writes=[ya, rc])
            P.op("dve", lambda e: e.tensor_scalar(out=rc[:, 2:3], in0=rc[:, 2:3], scalar1=1.0 / 128, scalar2=EPS, op0=ALU.mult, op1=ALU.add),
                 reads=[rc], writes=[rc])
            P.op("act", lambda e: e.activation(out=rc[:, 2:3], in_=rc[:, 2:3], func=AF.Sqrt), reads=[rc], writes=[rc])
            P.op("dve", lambda e: e.reciprocal(out=rc[:, 2:3], in_=rc[:, 2:3]), reads=[rc], writes=[rc])
            P.op("dve", lambda e: e.scalar_tensor_tensor(out=ya[:, 0:128], in0=yb[:, 0:128], scalar=rc[:, 2:3], in1=c["don"][:, :],
                                                         op0=ALU.mult, op1=ALU.mult), reads=[yb, rc, c["don"]], writes=[ya])
            P.op("dve", lambda e: e.tensor_scalar(out=ya[:, 0:128], in0=ya[:, 0:128], scalar1=float(1.0 - lam_init), scalar2=None, op0=ALU.mult),
                 reads=[ya], writes=[ya])
            Tp = psR.next()
            P.op("pe", lambda e: e.transpose(Tp[:, 0:128], ya[:, 0:128], ident[:, :]), reads=[ya, ident], writes=[Tp])
            P.op("act", lambda e: e.copy(out=ysbT[:, 12 + hd, i * 128:(i + 1) * 128], in_=Tp[:, 0:128]), reads=[Tp], writes=[ysbT])

        n = len(items)
        for k in range(n + 2):
            if k < n:
                dA(k)
            if 0 <= k - 2 < n:
                dC(k - 2)

    rq, rqx, rk, rvO, st32, stB = c["rq"], c["rqx"], c["rk"], c["rvO"], c["st32"], c["stB"]
    GTb = GT[:, :].rearrange("(b p) d -> b p d", p=128)
    if isinstance(rank, int):
        P.dma(c["RS"][:, :].rearrange("(b p) d -> b p d", p=128), GTb[rank:rank + 65, :, GT_KZ:GT_KZ + 1536], reads=[GT], writes=[c["RS"]])
    else:
        P.dma(c["RS"][:, :].rearrange("(b p) d -> b p d", p=128), GTb[bass.ds(rank, 65), :, GT_KZ:GT_KZ + 1536],
              reads=[GT], writes=[c["RS"]], q="pool")
    rgT = c["gch"]
    for hd in range(4):
        P.dma(rq[:, :], obf[LA_BF["rq"] + hd * 128:LA_BF["rq"] + (hd + 1) * 128, :], writes=[rq])
        P.dma(rqx[:, :], obf[LA_BF["rqx"] + hd * 128:LA_BF["rqx"] + (hd + 1) * 128, :], writes=[rqx])
        P.dma(rk[:, :], obf[LA_BF["rk"] + hd * 128:LA_BF["rk"] + (hd + 1) * 128, :], writes=[rk])
        P.op("pool", lambda e: e.memset(st32[:, :], 0.0), writes=[st32])
        g128 = float(np.exp(128.0 * LOG_GAMMA[hd]))
        for i in range(NB):
            kzs, vss = c["kzs"].next(), c["vss"].next()
            RSb = c["RS"][:, :].rearrange("(b p) d -> b p d", p=128)
            P.dma(kzs[:, :, :], RSb[8 * i:8 * i + 8, :, hd * 128:(hd + 1) * 128].rearrange("b p d -> p b d"),
                  reads=[c["RS"]], writes=[kzs])
            P.dma(vss[:, :, :], RSb[8 * i:8 * i + 9, :, 512 + hd * 256:512 + (hd + 1) * 256].rearrange("b p d -> p b d"),
                  reads=[c["RS"]], writes=[vss])
            for n in range(8):
                Sp = psR.next()
                P.op("pe", lambda e, n=n: e.matmul(Sp[:, 0:256], lhsT=kzs[:, n, :], rhs=vss[:, n, :], start=True, stop=True),
                     reads=[kzs, vss], writes=[Sp])
                P.op("dve", lambda e: e.scalar_tensor_tensor(out=st32[:, :], in0=st32[:, :], scalar=g128, in1=Sp[:, 0:256],
                                                             op0=ALU.mult, op1=ALU.add), reads=[st32, Sp], writes=[st32])
            P.op("act", lambda e: e.copy(out=stB[:, :], in_=st32[:, :]), reads=[st32], writes=[stB])
            Sc = psR.next()
            P.op("pe", lambda e: e.matmul(Sc[:, 0:128], lhsT=rk[:, i * 128:(i + 1) * 128], rhs=rq[:, i * 128:(i + 1) * 128], start=True, stop=True),
                 reads=[rk, rq], writes=[Sc])
            scb = c["scb"].next()
            P.op("dve", lambda e: e.tensor_tensor(out=scb[:, :], in0=Sc[:, 0:128], in1=c["dmask"][:, hd, :], op=ALU.mult),
                 reads=[Sc, c["dmask"]], writes=[scb])
            for ec in range(2):
                Yr = psR.next()
                P.op("pe", lambda e, ec=ec: e.matmul(Yr[:, 0:128], lhsT=vss[:, 8, ec * 128:(ec + 1) * 128], rhs=scb[:, :], start=True, stop=False),
                     reads=[vss, scb], writes=[Yr], inc=False)
                P.op("pe", lambda e, ec=ec: e.matmul(Yr[:, 0:128], lhsT=stB[:, ec * 128:(ec + 1) * 128], rhs=rqx[:, i * 128:(i + 1) * 128],
                                                     start=False, stop=True), reads=[stB, rqx], writes=[Yr])
                yr = c["yr"][ec]
                P.op("act", lambda e: e.copy(out=yr[:, :], in_=Yr[:, 0:128]), reads=[Yr], writes=[yr])
            St = psR.next()
            for ec in range(2):
                P.op("pe", lambda e, ec=ec: e.matmul(St[:, 0:128], lhsT=c["ones32"][:, :], rhs=c["yr"][ec][:, :], start=(ec == 0), stop=(ec == 1)),
                     reads=[c["ones32"], c["yr"][ec]], writes=[St], inc=(ec == 1))
            for ec in range(2):
                P.op("act", lambda e, ec=ec: e.activation(out=c["ysq"][ec][:, :], in_=c["yr"][ec][:, :], func=AF.Square),
                     reads=[c["yr"][ec]], writes=[c["ysq"][ec]])
            for ec in range(2):
                P.op("pe", lambda e, ec=ec: e.matmul(St[:, 128:256], lhsT=c["ones32"][:, :], rhs=c["ysq"][ec][:, :], start=(ec == 0), stop=(ec == 1)),
                     reads=[c["ones32"], c["ysq"][ec]], writes=[St], inc=(ec == 1))
            mu = c["tokR"].next()
            P.op("dve", lambda e: e.tensor_scalar(out=mu[:, 0:256], in0=St[:, 0:256], scalar1=1.0 / 256, scalar2=None, op0=ALU.mult),
                 reads=[St], writes=[mu])
            var = c["tokR"].next()
            P.op("dve", lambda e: e.tensor_tensor(out=var[:, 0:128], in0=mu[:, 0:128], in1=mu[:, 0:128], op=ALU.mult), reads=[mu], writes=[var])
            P.op("dve", lambda e: e.tensor_tensor(out=var[:, 0:128], in0=mu[:, 128:256], in1=var[:, 0:128], op=ALU.subtract),
                 reads=[mu, var], writes=[var])
            P.op("dve", lambda e: e.tensor_scalar(out=var[:, 0:128], in0=var[:, 0:128], scalar1=EPS, scalar2=None, op0=ALU.add),
                 reads=[var], writes=[var])
            P.op("act", lambda e: e.activation(out=var[:, 0:128], in_=var[:, 0:128], func=AF.Sqrt), reads=[var], writes=[var])
            P.op("dve", lambda e: e.reciprocal(out=var[:, 0:128], in_=var[:, 0:128]), reads=[var], writes=[var])
            for ec in range(2):
                yr = c["yr"][ec]
                fch = hd * 2 + ec
                P.dma(rgT[:, 0:128], of32[LA_F32["rg"] + fch * 128:LA_F32["rg"] + (fch + 1) * 128, i * 128:(i + 1) * 128], writes=[rgT])
                P.op("dve", lambda e: e.tensor_tensor(out=yr[:, :], in0=yr[:, :], in1=mu[:, 0:128], op=ALU.subtract), reads=[yr, mu], writes=[yr])
                P.op("dve", lambda e: e.tensor_tensor(out=yr[:, :], in0=yr[:, :], in1=var[:, 0:128], op=ALU.mult), reads=[yr, var], writes=[yr])
                P.op("dve", lambda e, fch=fch: e.tensor_scalar(out=yr[:, :], in0=yr[:, :], scalar1=c["gng"][:, fch:fch + 1], scalar2=c["gnb"][:, fch:fch + 1],
                                                            op0=ALU.mult, op1=ALU.add), reads=[yr, c["gng"], c["gnb"]], writes=[yr])
                P.op("dve", lambda e, fch=fch: e.tensor_tensor(out=ysbT[:, 4 + fch, i * 128:(i + 1) * 128], in0=yr[:, :], in1=rgT[:, 0:128], op=ALU.mult),
                     reads=[yr, rgT], writes=[ysbT])

    if dbg is not None:
        P.dma(dbg[:, :, :], ysbT[:, :, :], reads=[ysbT], writes=[dbg], is_output=True)
    mT = c["mT"]
    wbfR = c["wbf"]

    def slab(srcbuf, idx):
        wb = wbfR.next()
        P.dma(wb[:, :, :], srcbuf[idx], reads=[srcbuf], writes=[wb])
        return wb

    br_k = [(0, 4), (4, 12), (12, 16)]
    for fc in range(16):
        wb = slab(wbr, fc)
        gch = c["gch"]
        acc = c["acc"]
        for b, (k0, k1) in enumerate(br_k):
            P.dma(gch[:, :], of32[LA_F32["gates"] + b * 2048 + fc * 128:LA_F32["gates"] + b * 2048 + (fc + 1) * 128, :], writes=[gch])
            for h in range(2):
                Bp = psR.next()
                for kc in range(k0, k1):
                    P.op("pe", lambda e, kc=kc, h=h: e.matmul(Bp[:, :], lhsT=wb[:, kc, :], rhs=ysbT[:, kc, h * 512:(h + 1) * 512],
                                                           start=(kc == k0), stop=(kc == k1 - 1)), reads=[wb, ysbT], writes=[Bp], inc=(kc == k1 - 1))
                sl = slice(h * 512, (h + 1) * 512)
                if b == 0:
                    P.op("dve", lambda e: e.tensor_tensor(out=acc[:, sl], in0=Bp[:, :], in1=gch[:, sl], op=ALU.mult), reads=[Bp, gch], writes=[acc])
                else:
                    tmp = c["efR"].next()
                    P.op("dve", lambda e: e.tensor_tensor(out=tmp[:, :], in0=Bp[:, :], in1=gch[:, sl], op=ALU.mult), reads=[Bp, gch], writes=[tmp])
                    if b == 1:
                        P.op("pool", lambda e: e.tensor_tensor(out=acc[:, sl], in0=acc[:, sl], in1=tmp[:, :], op=ALU.add), reads=[acc, tmp], writes=[acc])
                    else:
                        P.op("pool", lambda e: e.tensor_tensor(out=mT[:, fc, sl], in0=acc[:, sl], in1=tmp[:, :], op=ALU.add), reads=[acc, tmp], writes=[mT])
    for fc in range(16):
        wb = slab(wout, fc)
        xch = c["xch"].next()
        P.dma(xch[:, :], xTd[:, fc, :], reads=[xTd], writes=[xch])
        for h in range(2):
            Op = psR.next()
            for kc in range(KC):
                P.op("pe", lambda e, kc=kc, h=h: e.matmul(Op[:, :], lhsT=wb[:, kc, :], rhs=mT[:, kc, h * 512:(h + 1) * 512],
                                                       start=(kc == 0), stop=(kc == KC - 1)), reads=[wb, mT], writes=[Op], inc=(kc == KC - 1))
            sl = slice(h * 512, (h + 1) * 512)
            P.op("dve", lambda e: e.scalar_tensor_tensor(out=xch[:, sl], in0=Op[:, :], scalar=modS[:, 32 + fc:33 + fc], in1=xch[:, sl],
                                                         op0=ALU.mult, op1=ALU.add), reads=[Op, modS, xch], writes=[xch])
        if c.get("hs") is not None:
            P.op("pool", lambda e, fc=fc: e.tensor_copy(out=c["hs"][:, :, fc, :], in_=xch[:, :].rearrange("p (i t) -> p i t", t=128)[:, :, 126:128]),
                 reads=[xch], writes=[c["hs"]])
        P.dma(xoutd[:, fc, :], xch[:, :], reads=[xch], writes=[xoutd], is_output=(dbg is not None), q="pool")


def lb_consts(P, srcs):
    c = {}
    c["RS"] = P.dram("RSscr", [65 * 128, 1536], BF16)
    c["psR"] = RR([P.ps(f"lps{i}") for i in range(4)])
    c["psA"] = [P.ps(f"lpsA{i}") for i in range(4)]
    c["ybr"] = P.sb("ybr", [128, 16, T], BF16)
    c["mT"] = P.sb("mT", [128, 16, T], BF16)
    c["qown"] = P.sb("qown", [128, 4, T], BF16)
    c["kT"] = P.sb("kTs", [128, S], BF16)
    c["nkT"] = P.sb("nkTs", [128, S], BF16)
    c["vS"] = P.sb("vS", [128, 64, 130], BF16)
    c["R"] = P.sb("Rsum", [128, 128], F32)
    c["efR"] = RR([P.sb(f"ef{i}", [128, 512], F32) for i in range(4)])
    c["spR"] = RR([P.sb(f"sp{i}", [128, 512], BF16) for i in range(3)])
    c["wR"] = RR([P.sb(f"wt{i}", [128, 512], BF16) for i in range(4)])
    c["small"] = RR([P.sb(f"sm{i}", [128, 4], F32) for i in range(2)])
    c["tokR"] = RR([P.sb(f"tok{i}", [128, 256], F32) for i in range(4)])
    c["rq"] = P.sb("rq", [128, T], BF16)
    c["rqx"] = P.sb("rqx", [128, T], BF16)
    c["rk"] = P.sb("rk", [128, T], BF16)
    c["rvO"] = P.sb("rvO", [128, 256], BF16)
    c["st32"] = P.sb("st32", [128, 256], F32)
    c["stB"] = P.sb("stB", [128, 256], BF16)
    c["kzs"] = RR([P.sb(f"kzs{i}", [128, 8, 128], BF16) for i in range(2)])
    c["vss"] = RR([P.sb(f"vss{i}", [128, 9, 256], BF16) for i in range(2)])
    c["scb"] = RR([P.sb(f"scb{i}", [128, 128], BF16) for i in range(2)])
    c["yr"] = [P.sb(f"yr{i}", [128, 128], F32) for i in range(2)]
    c["ysq"] = [P.sb(f"ysq{i}", [128, 128], F32) for i in range(2)]
    c["gch"] = P.sb("gch", [128, T], F32)
    c["acc"] = P.sb("acc", [128, T], F32)
    c["xch"] = RR([P.sb(f"xch{i}", [128, T], F32) for i in range(2)])
    c["wbf"] = RR([P.sb(f"lwbf{i}", [128, KC, 128], BF16) for i in range(3)])
    c["one"] = P.sb("one", [128, 1], F32)
    c["ones32"] = P.sb("lones32", [128, 128], F32)
    c["onesB"] = P.sb("onesB", [128, 128], BF16)
    c["uincl"] = P.sb("uincl", [128, 128], BF16)
    c["identF"] = P.sb("identF", [128, 128], F32)
    c["msb"] = P.sb("msb", [128, 2, 512], BF16)
    c["mdf"] = P.sb("mdf", [128, 2, 512], BF16)
    c["dmask"] = P.sb("dmask", [128, 4, 128], F32)
    c["don"] = P.sb("don", [128, 128], F32)
    c["gng"] = P.sb("gng", [128, 8], F32)
    c["gnb"] = P.sb("gnb", [128, 8], F32)
    c["nlam"] = P.sb("nlam", [128, 1], F32)
    c["lamv"] = P.sb("lamv", [64, 4], F32)
    P.op("pool", lambda e: e.memset(c["one"][:], 1.0), writes=[c["one"]])
    P.op("pool", lambda e: e.memset(c["ones32"][:], 1.0), writes=[c["ones32"]])
    P.op("pool", lambda e: e.memset(c["onesB"][:], 1.0), writes=[c["onesB"]])
    return c


def lb_load_consts(P, c, s):
    P.dma(c["uincl"][:], s["uincl"][:, :], writes=[c["uincl"]])
    P.dma(c["identF"][:], s["ident"][:, :], writes=[c["identF"]])
    P.dma(c["msb"][:], s["masks"][:, 0, :, :], writes=[c["msb"]])
    P.dma(c["mdf"][:], s["masks"][:, 1, :, :], writes=[c["mdf"]])
    P.dma(c["dmask"][:], s["dmask"][:, :, :], writes=[c["dmask"]])


def lb_load_layer(P, c, s, lam_init):
    P.dma(c["don"][:], s["don"][:, :], writes=[c["don"]])
    P.dma(c["gng"][:], s["gn"][:, 0:8], writes=[c["gng"]])
    P.dma(c["gnb"][:], s["gn"][:, 8:16], writes=[c["gnb"]])
    P.dma(c["lamv"][:], s["lamv"][:, :], writes=[c["lamv"]])
    pr = c["small"].next()
    P.op("dve", lambda e: e.tensor_tensor(out=pr[0:64, 0:1], in0=c["lamv"][:, 0:1], in1=c["lamv"][:, 1:2], op=ALU.mult), reads=[c["lamv"]], writes=[pr])
    P.op("dve", lambda e: e.tensor_tensor(out=pr[0:64, 1:2], in0=c["lamv"][:, 2:3], in1=c["lamv"][:, 3:4], op=ALU.mult), reads=[c["lamv"]], writes=[pr])
    Lp = c["psR"].next()
    P.op("pe", lambda e: e.matmul(Lp[:, 0:2], lhsT=c["ones32"][0:64, :], rhs=pr[0:64, 0:2], start=True, stop=True),
         reads=[c["ones32"], pr], writes=[Lp])
    ex = c["small"].next()
    P.op("act", lambda e: e.activation(out=ex[:, 0:2], in_=Lp[:, 0:2], func=AF.Exp), reads=[Lp], writes=[ex])
    P.op("dve", lambda e: e.tensor_tensor(out=c["nlam"][:, 0:1], in0=ex[:, 1:2], in1=ex[:, 0:1], op=ALU.subtract), reads=[ex], writes=[c["nlam"]])
    P.op("dve", lambda e: e.tensor_scalar(out=c["nlam"][:, 0:1], in0=c["nlam"][:, 0:1], scalar1=float(-lam_init), scalar2=None, op0=ALU.add),
         reads=[c["nlam"]], writes=[c["nlam"]])


def lb_host_consts(core):
    p = np.arange(128)
    uincl = (p[:, None] >= p[None, :]).astype(np.float32)
    ident = np.eye(128, dtype=np.float32)
    masks = np.zeros((128, 2, 2, 512), np.float32)
    for g2 in range(2):
        for j in range(4):
            kb = 4 * g2 + j
            sl = slice(j * 128, (j + 1) * 128)
            if kb < core:
                masks[:, 0, g2, sl] = 1.0
                masks[:, 1, g2, sl] = 1.0
            elif kb == core:
                masks[:, 0, g2, sl] = (p[:, None] < p[None, :])
                masks[:, 1, g2, sl] = ((p[:, None] // 64) <= (p[None, :] // 64))
    dm = np.zeros((128, 4, 128), np.float64)
    n = p[None, :]
    m = p[:, None]
    for h in range(4):
        same = (m // 64) == (n // 64)
        earlier = (m // 64) < (n // 64)
        dm[:, h, :] = np.where(same, np.exp(np.abs(n - m) * LOG_GAMMA[h]), np.where(earlier, np.exp((n - m) * LOG_GAMMA[h]), 0.0))
    return uincl.astype(NPBF), ident, masks.astype(NPBF), dm.astype(np.float32)


def build_LB_test(lam_init):
    P = Prog()
    obf = P.dram("obf", [LA_BF_ROWS, T], BF16, kind="ExternalInput")
    of32 = P.dram("of32", [LA_F32_ROWS, T], F32, kind="ExternalInput")
    GF = P.dram("GF", [1024, S], BF16, kind="ExternalInput")
    GT = P.dram("GT", [1024 + S, 2560], BF16, kind="ExternalInput")
    xTd = P.dram("xT", [128, KC, T], F32, kind="ExternalInput")
    modT = P.dram("modT", [128, 96], F32, kind="ExternalInput")
    wbr = P.dram("wbr", [16, 128, KC, 128], BF16, kind="ExternalInput")
    wout = P.dram("wout", [16, 128, KC, 128], BF16, kind="ExternalInput")
    s = {k: P.dram("i_" + k, shp, dt, kind="ExternalInput") for k, shp, dt in [
        ("uincl", [128, 128], BF16), ("ident", [128, 128], F32), ("masks", [128, 2, 2, 512], BF16), ("dmask", [128, 4, 128], F32),
        ("don", [128, 128], F32), ("gn", [128, 16], F32), ("lamv", [64, 4], F32)]}
    xoutd = P.dram("xout", [128, KC, T], F32, kind="ExternalOutput")
    dbg = P.dram("dbg", [128, 16, T], BF16, kind="ExternalOutput")
    modS = P.sb("modS", [128, 96], F32)
    P.dma(modS[:], modT[:, :], writes=[modS])
    c = lb_consts(P, s)
    lb_load_consts(P, c, s)
    lb_load_layer(P, c, s, lam_init)
    rank = P.nc.gpsimd.partition_id()
    lb_phase(P, c, obf, of32, GF, GT, xTd, xoutd, modS, wbr, wout, rank, lam_init, dbg=dbg)
    return P.finish()


def lc_alloc(P):
    c = {}
    c["xT"] = P.sb("c_xT", [128, KC, T], F32)
    c["hT"] = P.sb("c_hT", [128, KC, T], BF16)
    c["xh"] = P.sb("c_xh", [128, KC, 16], F32)
    c["hh"] = P.sb("c_hh", [128, KC, 16], BF16)
    c["actT"] = P.sb("c_actT", [128, 44, 512], BF16)
    c["U"] = RR([P.sb(f"c_U{i}", [128, 4, 130], F32) for i in range(3)])
    c["Y"] = RR([P.sb(f"c_Y{i}", [128, 4, 128], F32) for i in range(4)])
    c["wup"] = RR([P.sb(f"c_wup{i}", [128, KC, 128], BF16) for i in range(3)])
    c["wdn"] = RR([P.sb(f"c_wdn{i}", [128, 48, 128], BF16) for i in range(1)])
    c["xo"] = RR([P.sb(f"c_xo{i}", [128, 512], F32) for i in range(2)])
    c["gs"] = P.sb("c_gs", [128, KC], F32)
    c["nf"] = P.sb("c_nf", [128, KC], F32)
    c["cw"] = P.sb("c_cw", [128, 3, 88], F32)
    c["cb"] = P.sb("c_cb", [128, 88], F32)
    c["flag"] = P.sb("c_flag", [128, 8, 2], F32)
    c["rstd"] = P.sb("c_rstd", [128, T], F32)
    c["tmp"] = RR([P.sb(f"c_tmp{i}", [128, 512], F32) for i in range(3)])
    c["ones32"] = P.sb("c_ones32", [128, 128], F32)
    P.op("pool", lambda e: e.memset(c["ones32"][:], 1.0), writes=[c["ones32"]])
    return c


def lc_phase(P, c, psR, xmid, halo_ap, halo_buf, xout, modS, nffn_d, cw_d, cb_d, flag_d, wup, wdn, out_is_final):
    xT, hT, xh, hh, actT = c["xT"], c["hT"], c["xh"], c["hh"], c["actT"]
    for q in range(4):
        P.dma(xT[:, 4 * q:4 * q + 4, :], xmid[:, 4 * q:4 * q + 4, :], reads=[xmid], writes=[xT])
    if len(halo_ap.shape) == 3:
        P.dma(c["xhi"][:, :, :], halo_ap, reads=[halo_buf], writes=[c["xhi"]])
    else:
        P.dma(c["xhi"][:, :, :].unsqueeze(2), halo_ap, reads=[halo_buf], writes=[c["xhi"]], q="pool")
    P.op("pool", lambda e: e.tensor_copy(out=xh[:, :, :].rearrange("p k (i t) -> p k i t", t=2),
                                         in_=c["xhi"][:, :, :].rearrange("p i (k t) -> p k i t", t=2)), reads=[c["xhi"]], writes=[xh])
    P.dma(c["nf"][:], nffn_d[:, :], writes=[c["nf"]])
    P.dma(c["cw"][:], cw_d[:, :, :], writes=[c["cw"]])
    P.dma(c["cb"][:], cb_d[:, :], writes=[c["cb"]])
    P.dma(c["flag"][:], flag_d[:, :, :], writes=[c["flag"]])
    gs = c["gs"]
    P.op("dve", lambda e: e.scalar_tensor_tensor(out=gs[:], in0=modS[:, 64:80], scalar=1.0, in1=c["nf"][:], op0=ALU.add, op1=ALU.mult),
         reads=[modS, c["nf"]], writes=[gs])
    psA, psB = psR.next(), psR.next()
    adaln_norm(P, xT, hT, T, gs, lambda kc: modS[:, 48 + kc:49 + kc], c["ones32"], psA, psB, c["rstd"], c["tmp"])
    adaln_norm(P, xh, hh, 16, gs, lambda kc: modS[:, 48 + kc:49 + kc], c["ones32"], psA, psB, c["rstd"], c["tmp"])
    for hf in range(2):
        tsl = slice(hf * 512, (hf + 1) * 512)
        for j in range(44):
            Ys = []
            for which in range(2):
                sidx = j + 44 * which
                wb = c["wup"].next()
                P.dma(wb[:, :, :], wup[sidx], reads=[wup], writes=[wb])
                pm, ph = psR.next(), psR.next()
                for kc in range(KC):
                    P.op("pe", lambda e, kc=kc: e.matmul(pm[:, :], lhsT=wb[:, kc, :], rhs=hT[:, kc, tsl], start=(kc == 0), stop=(kc == KC - 1)),
                         reads=[wb, hT], writes=[pm], inc=(kc == KC - 1))
                for kc in range(KC):
                    P.op("pe", lambda e, kc=kc: e.matmul(ph[:, 0:8], lhsT=wb[:, kc, :], rhs=hh[:, kc, hf * 8:(hf + 1) * 8], start=(kc == 0), stop=(kc == KC - 1)),
                         reads=[wb, hh], writes=[ph], inc=(kc == KC - 1))
                U = c["U"].next()
                P.op("act", lambda e: e.copy(out=U[:, :, 2:130], in_=pm[:, :].rearrange("p (b t) -> p b t", t=128)), reads=[pm], writes=[U])
                P.op("dve", lambda e: e.tensor_tensor(out=U[:, :, 0:2], in0=ph[:, 0:8].rearrange("p (b t) -> p b t", t=2),
                                                      in1=c["flag"][:, hf * 4:(hf + 1) * 4, :], op=ALU.mult), reads=[ph, c["flag"]], writes=[U])
                Y = c["Y"].next()
                P.op("act", lambda e, sidx=sidx: e.activation(out=Y[:, :, :], in_=U[:, :, 2:130], func=AF.Identity,
                                                          scale=c["cw"][:, 2, sidx:sidx + 1], bias=c["cb"][:, sidx:sidx + 1]),
                     reads=[U, c["cw"], c["cb"]], writes=[Y])
                P.op("dve", lambda e, sidx=sidx: e.scalar_tensor_tensor(out=Y[:, :, :], in0=U[:, :, 1:129], scalar=c["cw"][:, 1, sidx:sidx + 1], in1=Y[:, :, :],
                                                                     op0=ALU.mult, op1=ALU.add), reads=[U, c["cw"], Y], writes=[Y])
                P.op("dve", lambda e, sidx=sidx: e.scalar_tensor_tensor(out=Y[:, :, :], in0=U[:, :, 0:128], scalar=c["cw"][:, 0, sidx:sidx + 1], in1=Y[:, :, :],
                                                                     op0=ALU.mult, op1=ALU.add), reads=[U, c["cw"], Y], writes=[Y])
                Ys.append(Y)
            sg = c["Y"].next()
            P.op("act", lambda e: e.activation(out=sg[:, :, :], in_=Ys[0][:, :, :], func=AF.Silu), reads=[Ys[0]], writes=[sg])
            P.op("pool", lambda e, j=j: e.tensor_tensor(out=actT[:, j, :].rearrange("p (b t) -> p b t", t=128), in0=sg[:, :, :], in1=Ys[1][:, :, :], op=ALU.mult),
                 reads=[sg, Ys[1]], writes=[actT])
        for fc in range(KC):
            wb = c["wdn"].next()
            P.dma(wb[:, :, :].rearrange("p (g k) n -> p g k n", g=3), wdn[fc], reads=[wdn], writes=[wb])
            po = psR.next()
            for kc in range(44):
                P.op("pe", lambda e, kc=kc: e.matmul(po[:, :], lhsT=wb[:, kc, :], rhs=actT[:, kc, :], start=(kc == 0), stop=(kc == 43)),
                     reads=[wb, actT], writes=[po], inc=(kc == 43))
            xo = c["xo"].next()
            P.op("dve", lambda e, fc=fc: e.scalar_tensor_tensor(out=xo[:, :], in0=po[:, :], scalar=modS[:, 80 + fc:81 + fc], in1=xT[:, fc, tsl],
                                                             op0=ALU.mult, op1=ALU.add), reads=[po, modS, xT], writes=[xo])
            P.dma(xout[:, fc, tsl], xo[:, :], reads=[xo], writes=[xout], is_output=out_is_final, q="pool")


NU_LAYER = 280
U_OFF = dict(la=0, br=112, out=128, up=144, dn=232)
NU = 2 * NU_LAYER
NU_CORE = NU // NCORE
AR_CHUNK = 56
LAM_INIT = [0.8 - 0.6 * float(np.exp(-0.3 * l)) for l in range(2)]


def build_full():
    P = Prog()
    X = lambda n, shp, dt: P.dram(n, shp, dt, kind="ExternalInput")
    xT_in = X("xT", [128, KC, T], F32)
    pos = X("pos", [128, T], I32)
    c_in = X("c_in", [128, KC], F32)
    wada = X("wada", [2, 128, KC, 1536], F32)
    bada = X("bada", [128, 2, 12], F32)
    wsh = X("wsh", [NU_CORE, 128, KC, 128], F32)
    nmix = X("nmix", [2, 128, KC], F32)
    nffn = X("nffn", [2, 128, KC], F32)
    cst = X("cst", [2, 128, 8], F32)
    cwd = X("cw", [2, 128, 3, 88], F32)
    cbd = X("cb", [2, 128, 88], F32)
    dond = X("don", [2, 128, 128], F32)
    gnd = X("gn", [2, 128, 16], F32)
    lamd = X("lamv", [2, 64, 4], F32)
    dec = X("dec", [128, 8, 128], F32)
    bones = X("bones", [128, 128], F32)
    uincl = X("uincl", [128, 128], BF16)
    ident = X("ident", [128, 128], F32)
    masks = X("masks", [128, 2, 2, 512], BF16)
    dmask = X("dmask", [128, 4, 128], F32)
    flagd = X("flag", [128, 8, 2], F32)
    xfin = P.dram("xoutT", [128, KC, T], F32, kind="ExternalOutput")

    Wown = P.dram("Wown", [NU_CORE, 128, KC, 128], BF16)
    WinL = [P.dram(f"Win{l}", [NU_LAYER, 128, KC, 128], BF16) for l in range(2)]
    WoutL = [P.dram(f"Wout{l}", [NU_LAYER, 128, KC, 128], BF16, shared=True) for l in range(2)]
    MODin = P.dram("MODin", [8, 128, 2, 12], F32)
    MOD = P.dram("MOD", [8, 128, 2, 12], F32, shared=True)
    GFin = P.dram("GFin", [1024, S], BF16)
    GF = P.dram("GF", [1024, S], BF16, shared=True)
    GTin = P.dram("GTin", [1024 + S, 2560], BF16)
    GT = P.dram("GT", [1024 + S, 2560], BF16, shared=True)
    HLin = P.dram("HLin", [65, 128, KC, 2], F32)
    HL = P.dram("HL", [65, 128, KC, 2], F32, shared=True)
    OT = P.dram("OT", [T, 2560], BF16)
    obf = P.dram("obf", [LA_BF_ROWS, T], BF16)
    of32 = P.dram("of32", [LA_F32_ROWS, T], F32)
    xmid = P.dram("xmid", [128, KC, T], F32)
    xnext = P.dram("xnext", [128, KC, T], F32)
    rank = P.nc.gpsimd.partition_id()

    P.phase_begin()
    Zt = P.sb("Zt", [128, 8192], BF16)
    Zf = P.sb("Zf", [128, 65 * 32], F32)
    P.op("pool", lambda e: e.memset(Zt[:], 0.0), writes=[Zt])
    P.op("pool", lambda e: e.memset(Zf[:], 0.0), writes=[Zf])
    for Win in WinL:
        for u in range(0, NU_LAYER, 4):
            P.dma(Win[u:u + 4].rearrange("u p k n -> p u (k n)"), Zt[:, :].rearrange("p (u x) -> p u x", u=4), reads=[Zt], writes=[Win])
    for r0 in range(0, 1024, 128):
        P.dma(GFin[r0:r0 + 128, :], Zt[:, :], reads=[Zt], writes=[GFin])
    for b0 in range(0, 72, 3):
        P.dma(GTin[b0 * 128:(b0 + 3) * 128, :].rearrange("(b p) d -> p b d", p=128), Zt[:, 0:7680].rearrange("p (b d) -> p b d", d=2560),
              reads=[Zt], writes=[GTin])
    P.dma(HLin[:, :, :, :].rearrange("b p k t -> p b (k t)"), Zf[:, :].rearrange("p (b x) -> p b x", x=32), reads=[Zf], writes=[HLin])
    P.dma(MODin[:, :, :, :].rearrange("r p l j -> p r (l j)"), Zf[:, 0:192].rearrange("p (r x) -> p r x", x=24), reads=[Zf], writes=[MODin])
    wst = RR([P.sb(f"Wst{i}", [128, KC, 128], F32) for i in range(3)])
    wcb = RR([P.sb(f"Wcb{i}", [128, KC, 128], BF16) for i in range(3)])
    cast_engs = ["pool", "dve", "act"]
    for j in range(NU_CORE):
        st, wb = wst.next(), wcb.next()
        P.dma(st[:, :, :], wsh[j], writes=[st])
        ce = cast_engs[j % 3]
        if ce == "act":
            P.op("act", lambda e: e.copy(out=wb[:, :, :], in_=st[:, :, :]), reads=[st], writes=[wb])
        else:
            P.op(ce, lambda e: e.tensor_copy(out=wb[:, :, :], in_=st[:, :, :]), reads=[st], writes=[wb])
        P.dma(Wown[j], wb[:, :, :], reads=[wb], writes=[Wown])
    for l in range(2):
        Win, Wout = WinL[l], WoutL[l]
        P.dma(Win[:, :, :, :].rearrange("(r j) p k n -> r (j p) (k n)", r=8)[bass.ds(rank, 1), :, :],
              Wown[l * 35:(l + 1) * 35, :, :, :].rearrange("j p k n -> (j p) (k n)").unsqueeze(0), reads=[Wown], writes=[Win], q="pool")
        Win2 = Win[:, :, :, :].rearrange("u p k n -> (u p) (k n)")
        Wout2 = Wout[:, :, :, :].rearrange("u p k n -> (u p) (k n)")
        for u in range(0, NU_LAYER, AR_CHUNK):
            P.all_reduce(Win2[u * 128:(u + AR_CHUNK) * 128, :], Wout2[u * 128:(u + AR_CHUNK) * 128, :], reads=[Win], writes=[Wout])
    ct = P.sb("ct", [128, KC], F32)
    ca = P.sb("ca", [128, KC], F32)
    wt = P.sb("wadat", [128, KC, 1536], F32)
    bt = P.sb("badat", [128, 2, 12], F32)
    mo = P.sb("modown", [128, 2, 12], F32)
    pmod = P.ps("pmod")
    P.dma(ct[:], c_in[:, :], writes=[ct])
    P.dma(bt[:], bada[:, :, :], writes=[bt])
    P.op("act", lambda e: e.activation(out=ca[:], in_=ct[:], func=AF.Silu), reads=[ct], writes=[ca])
    for l in range(2):
        for q in range(4):
            P.dma(wt[:, 4 * q:4 * q + 4, :], wada[l, :, 4 * q:4 * q + 4, :], writes=[wt])
        for fch in range(12):
            for kc in range(KC):
                P.op("pe", lambda e, fch=fch, kc=kc, l=l: e.matmul(pmod[:, l * 12 + fch:l * 12 + fch + 1], lhsT=wt[:, kc, fch * 128:(fch + 1) * 128],
                                                                rhs=ca[:, kc:kc + 1], start=(kc == 0), stop=(kc == KC - 1)),
                     reads=[wt, ca], writes=[pmod], inc=(kc == KC - 1))
    P.op("dve", lambda e: e.tensor_tensor(out=mo[:, :, :], in0=pmod[:, 0:24].rearrange("p (l j) -> p l j", l=2), in1=bt[:, :, :], op=ALU.add),
         reads=[pmod, bt], writes=[mo])
    P.dma(MODin[bass.ds(rank, 1), :, :, :].rearrange("o p l j -> p o l j"), mo[:, :, :].unsqueeze(1), reads=[mo], writes=[MODin], q="pool")
    P.all_reduce(MODin[:, :, :, :].rearrange("r p l j -> (r p) (l j)"), MOD[:, :, :, :].rearrange("r p l j -> (r p) (l j)"),
                 reads=[MODin], writes=[MOD])
    P.phase_end()

    xcur = xT_in
    for l in range(2):
        u0 = 0
        Wout = WoutL[l]
        P.phase_begin()
        modTd = Buf(MOD[:, :, l, :].rearrange("r p j -> p r j"), "modTd")
        d = dict(xT=xcur, modT=modTd, nmix=Buf(nmix[l], "nm"), pos=pos, cst=Buf(cst[l], "cs"), dec=dec, bones=bones,
                 wsl=Buf(Wout[u0 + U_OFF["la"]:u0 + U_OFF["la"] + 112], "wsl"), obf=obf, of32=of32)
        la_phase(P, d, True)
        identS = P.sb("identS", [128, 128], F32)
        P.dma(identS[:], ident[:, :], writes=[identS])
        tin = RR([P.sb(f"tin{i}", [128, T], BF16) for i in range(2)])
        tf = RR([P.sb(f"tf{i}", [128, T], F32) for i in range(2)])
        tout = RR([P.sb(f"tout{i}", [128, 8, 128], BF16) for i in range(2)])
        tps = RR([P.ps(f"tps{i}") for i in range(2)])
        jobs = [(LA_BF["vsb"] + k * 128, GT_V_SB + k * 128) for k in range(4)] + [(LA_BF["dv"] + k * 128, GT_V_D + k * 128) for k in range(4)] + \
               [(LA_BF["rkz"] + k * 128, GT_KZ + k * 128) for k in range(4)] + [(LA_BF["rv"] + k * 128, GT_RV + k * 128) for k in range(8)]
        for row, col in jobs:
            ti, tff, to = tin.next(), tf.next(), tout.next()
            P.dma(ti[:, :], obf[row:row + 128, :], reads=[obf], writes=[ti])
            P.op("dve", lambda e: e.tensor_copy(out=tff[:, :], in_=ti[:, :]), reads=[ti], writes=[tff])
            for hb in range(2):
                tp = tps.next()
                for b4 in range(4):
                    b = hb * 4 + b4
                    P.op("pe", lambda e, b=b, b4=b4: e.transpose(tp[:, b4 * 128:(b4 + 1) * 128], tff[:, b * 128:(b + 1) * 128], identS[:, :]),
                         reads=[tff, identS], writes=[tp])
                P.op("act", lambda e, hb=hb: e.copy(out=to[:, hb * 4:(hb + 1) * 4, :], in_=tp[:, :].rearrange("p (b f) -> p b f", f=128)),
                     reads=[tp], writes=[to])
            P.dma(OT[:, col:col + 128].rearrange("(b p) d -> p b d", p=128), to[:, :, :], reads=[to], writes=[OT])
        GFv = GFin[:, :].rearrange("f (i r t) -> f i r t", i=8, r=8)
        P.dma(GFv[0:512, :, bass.ds(rank, 1), :], obf[LA_BF["ksb"]:LA_BF["ksb"] + 512, :].rearrange("f (i t) -> f i t", i=8).unsqueeze(2),
              reads=[obf], writes=[GFin], q="pool")
        P.dma(GFv[512:1024, :, bass.ds(rank, 1), :], obf[LA_BF["dk"]:LA_BF["dk"] + 512, :].rearrange("f (i t) -> f i t", i=8).unsqueeze(2),
              reads=[obf], writes=[GFin], q="pool")
        GTv = GTin[1024:, :].rearrange("(i r p) d -> i r p d", i=8, r=8)
        P.dma(GTv[:, bass.ds(rank, 1), :, :], OT[:, :].rearrange("(i p) d -> i p d", i=8).unsqueeze(1), reads=[OT], writes=[GTin], q="pool")
        P.all_reduce(GFin[:, :], GF[:, :], reads=[GFin], writes=[GF])
        hr = (1024 + S) // 2
        P.all_reduce(GTin[0:hr, :], GT[0:hr, :], reads=[GTin], writes=[GT])
        P.all_reduce(GTin[hr:, :], GT[hr:, :], reads=[GTin], writes=[GT])
        P.phase_end()
        P.phase_begin()
        modS = P.sb("modS", [128, 96], F32)
        P.dma(modS[:, :].rearrange("p (r j) -> p r j", r=8), MOD[:, :, l, :].rearrange("r p j -> p r j"), reads=[MOD], writes=[modS])
        srcs = dict(uincl=uincl, ident=ident, masks=masks, dmask=dmask, don=Buf(dond[l], "don"), gn=Buf(gnd[l], "gn"), lamv=Buf(lamd[l], "lamv"))
        c = lb_consts(P, srcs)
        c["hs"] = P.sb("hs", [128, 8, KC, 2], F32)
        lb_load_consts(P, c, srcs)
        lb_load_layer(P, c, srcs, LAM_INIT[l])
        lb_phase(P, c, obf, of32, GF, GT, xcur, xmid, modS, Buf(Wout[u0 + U_OFF["br"]:u0 + U_OFF["br"] + 16], "wbr"),
                 Buf(Wout[u0 + U_OFF["out"]:u0 + U_OFF["out"] + 16], "wout"), rank, LAM_INIT[l])
        HLv = HLin[1:65, :, :, :].rearrange("(i r) p k t -> p i r (k t)", i=8, r=8)
        P.dma(HLv[:, :, bass.ds(rank, 1), :], c["hs"][:, :, :, :].rearrange("p i k t -> p i (k t)").unsqueeze(2), reads=[c["hs"]], writes=[HLin], q="pool")
        P.all_reduce(HLin[:, :, :, :].rearrange("b p k t -> (b p) (k t)"), HL[:, :, :, :].rearrange("b p k t -> (b p) (k t)"), reads=[HLin], writes=[HL])
        P.phase_end()
        P.phase_begin()
        modS = P.sb("modS", [128, 96], F32)
        P.dma(modS[:, :].rearrange("p (r j) -> p r j", r=8), MOD[:, :, l, :].rearrange("r p j -> p r j"), reads=[MOD], writes=[modS])
        cc = lc_alloc(P)
        cc["xhi"] = P.sb("xhi", [128, 8, KC * 2], F32)
        psR = RR([P.ps(f"cps{i}") for i in range(8)])
        xdst = xfin if l == 1 else xnext
        HLr = HL[0:64, :, :, :].rearrange("(i r) p k t -> p i r (k t)", i=8, r=8)
        lc_phase(P, cc, psR, xmid, HLr[:, :, bass.ds(rank, 1), :], HL, xdst, modS, Buf(nffn[l], "nf"), Buf(cwd[l], "cw"), Buf(cbd[l], "cb"), flagd,
                 Buf(Wout[u0 + U_OFF["up"]:u0 + U_OFF["up"] + 88], "wup"),
                 Buf(Wout[u0 + U_OFF["dn"]:u0 + U_OFF["dn"] + 48].rearrange("(f g) p k n -> f p g k n", g=3), "wdn"), l == 1)
        P.phase_end()
        xcur = xnext
    return P.finish()


def build_single():
    P = Prog()
    X = lambda n, shp, dt: P.dram(n, shp, dt, kind="ExternalInput")
    xT_in = X("xT", [8, 128, KC, T], F32)
    posA = X("pos", [8, 128, T], I32)
    c_in = X("c_in", [128, KC], F32)
    wada = X("wada", [2, 8, 128, KC, 1536], F32)
    bada = X("bada", [8, 128, 2, 12], F32)
    wsh = X("wsh", [NU, 128, KC, 128], F32)
    nmix = X("nmix", [2, 128, KC], F32)
    nffn = X("nffn", [2, 128, KC], F32)
    cst = X("cst", [2, 128, 8], F32)
    cwd = X("cw", [2, 128, 3, 88], F32)
    cbd = X("cb", [2, 128, 88], F32)
    dond = X("don", [2, 128, 128], F32)
    gnd = X("gn", [2, 128, 16], F32)
    lamd = X("lamv", [2, 64, 4], F32)
    dec = X("dec", [128, 8, 128], F32)
    bones = X("bones", [128, 128], F32)
    uincl = X("uincl", [128, 128], BF16)
    ident = X("ident", [128, 128], F32)
    masksA = X("masks", [8, 128, 2, 2, 512], BF16)
    dmask = X("dmask", [128, 4, 128], F32)
    flagA = X("flag", [8, 128, 8, 2], F32)
    xfin = P.dram("xoutT", [8, 128, KC, T], F32, kind="ExternalOutput")

    WoutL = [P.dram(f"Wout{l}", [NU_LAYER, 128, KC, 128], BF16) for l in range(2)]
    MOD = P.dram("MOD", [8, 128, 2, 12], F32)
    GF = P.dram("GF", [1024, S], BF16)
    GT = P.dram("GT", [1024 + S, 2560], BF16)
    HL = P.dram("HL", [65, 128, KC, 2], F32)
    OT = P.dram("OT", [T, 2560], BF16)
    obfA = [P.dram(f"obf{r}", [LA_BF_ROWS, T], BF16) for r in range(8)]
    of32A = [P.dram(f"of32{r}", [LA_F32_ROWS, T], F32) for r in range(8)]
    xmidA = [P.dram(f"xmid{r}", [128, KC, T], F32) for r in range(8)]
    xnextA = [P.dram(f"xnext{r}", [128, KC, T], F32) for r in range(8)]

    P.phase_begin()
    Zt = P.sb("Zt", [128, 8192], BF16)
    Zf = P.sb("Zf", [128, 32], F32)
    P.op("pool", lambda e: e.memset(Zt[:], 0.0), writes=[Zt])
    P.op("pool", lambda e: e.memset(Zf[:], 0.0), writes=[Zf])
    for b0 in range(0, 8, 2):
        P.dma(GT[b0 * 128:(b0 + 2) * 128, :].rearrange("(b p) d -> p b d", p=128), Zt[:, 0:5120].rearrange("p (b d) -> p b d", d=2560),
              reads=[Zt], writes=[GT])
    P.dma(HL[0, :, :, :].rearrange("p k t -> p (k t)"), Zf[:, :], reads=[Zf], writes=[HL])
    wst = RR([P.sb(f"Wst{i}", [128, KC, 128], F32) for i in range(3)])
    wcb = RR([P.sb(f"Wcb{i}", [128, KC, 128], BF16) for i in range(3)])
    cast_engs = ["pool", "dve", "act"]
    for j in range(NU):
        st, wb = wst.next(), wcb.next()
        P.dma(st[:, :, :], wsh[j], writes=[st])
        ce = cast_engs[j % 3]
        if ce == "act":
            P.op("act", lambda e: e.copy(out=wb[:, :, :], in_=st[:, :, :]), reads=[st], writes=[wb])
        else:
            P.op(ce, lambda e: e.tensor_copy(out=wb[:, :, :], in_=st[:, :, :]), reads=[st], writes=[wb])
        P.dma(WoutL[j // NU_LAYER][j % NU_LAYER], wb[:, :, :], reads=[wb], writes=[WoutL[j // NU_LAYER]])
    ct = P.sb("ct", [128, KC], F32)
    ca = P.sb("ca", [128, KC], F32)
    wt = P.sb("wadat", [128, KC, 1536], F32)
    bt = P.sb("badat", [128, 2, 12], F32)
    moR = RR([P.sb(f"modown{i}", [128, 2, 12], F32) for i in range(2)])
    pmR = RR([P.ps(f"pmod{i}") for i in range(2)])
    P.dma(ct[:], c_in[:, :], writes=[ct])
    P.op("act", lambda e: e.activation(out=ca[:], in_=ct[:], func=AF.Silu), reads=[ct], writes=[ca])
    for r in range(8):
        pmod, mo = pmR.next(), moR.next()
        P.dma(bt[:], bada[r], writes=[bt])
        for l in range(2):
            for q in range(4):
                P.dma(wt[:, 4 * q:4 * q + 4, :], wada[l, r, :, 4 * q:4 * q + 4, :], writes=[wt])
            for fch in range(12):
                for kc in range(KC):
                    P.op("pe", lambda e, fch=fch, kc=kc, l=l: e.matmul(pmod[:, l * 12 + fch:l * 12 + fch + 1], lhsT=wt[:, kc, fch * 128:(fch + 1) * 128],
                                                                    rhs=ca[:, kc:kc + 1], start=(kc == 0), stop=(kc == KC - 1)),
                         reads=[wt, ca], writes=[pmod], inc=(kc == KC - 1))
        P.op("dve", lambda e: e.tensor_tensor(out=mo[:, :, :], in0=pmod[:, 0:24].rearrange("p (l j) -> p l j", l=2), in1=bt[:, :, :], op=ALU.add),
             reads=[pmod, bt], writes=[mo])
        P.dma(MOD[r], mo[:, :, :], reads=[mo], writes=[MOD])
    P.phase_end()

    xcurA = [Buf(xT_in[r], f"xin{r}") for r in range(8)]
    for l in range(2):
        u0 = 0
        Wout = WoutL[l]
        for rank in range(8):
            xcur, obf, of32 = xcurA[rank], obfA[rank], of32A[rank]
            P.phase_begin()
            modTd = Buf(MOD[:, :, l, :].rearrange("r p j -> p r j"), "modTd")
            d = dict(xT=xcur, modT=modTd, nmix=Buf(nmix[l], "nm"), pos=Buf(posA[rank], "pos"), cst=Buf(cst[l], "cs"), dec=dec, bones=bones,
                     wsl=Buf(Wout[u0 + U_OFF["la"]:u0 + U_OFF["la"] + 112], "wsl"), obf=obf, of32=of32)
            la_phase(P, d, True)
            identS = P.sb("identS", [128, 128], F32)
            P.dma(identS[:], ident[:, :], writes=[identS])
            tin = RR([P.sb(f"tin{i}", [128, T], BF16) for i in range(2)])
            tf = RR([P.sb(f"tf{i}", [128, T], F32) for i in range(2)])
            tout = RR([P.sb(f"tout{i}", [128, 8, 128], BF16) for i in range(2)])
            tps = RR([P.ps(f"tps{i}") for i in range(2)])
            jobs = [(LA_BF["vsb"] + k * 128, GT_V_SB + k * 128) for k in range(4)] + [(LA_BF["dv"] + k * 128, GT_V_D + k * 128) for k in range(4)] + \
                   [(LA_BF["rkz"] + k * 128, GT_KZ + k * 128) for k in range(4)] + [(LA_BF["rv"] + k * 128, GT_RV + k * 128) for k in range(8)]
            GTv = GT[1024:, :].rearrange("(i r p) d -> p i r d", i=8, r=8)
            for row, col in jobs:
                ti, tff, to = tin.next(), tf.next(), tout.next()
                P.dma(ti[:, :], obf[row:row + 128, :], reads=[obf], writes=[ti])
                P.op("dve", lambda e: e.tensor_copy(out=tff[:, :], in_=ti[:, :]), reads=[ti], writes=[tff])
                for hb in range(2):
                    tp = tps.next()
                    for b4 in range(4):
                        b = hb * 4 + b4
                        P.op("pe", lambda e, b=b, b4=b4: e.transpose(tp[:, b4 * 128:(b4 + 1) * 128], tff[:, b * 128:(b + 1) * 128], identS[:, :]),
                             reads=[tff, identS], writes=[tp])
                    P.op("act", lambda e, hb=hb: e.copy(out=to[:, hb * 4:(hb + 1) * 4, :], in_=tp[:, :].rearrange("p (b f) -> p b f", f=128)),
                         reads=[tp], writes=[to])
                P.dma(GTv[:, :, rank, col:col + 128], to[:, :, :], reads=[to], writes=[GT], q="pool")
            GFv = GF[:, :].rearrange("f (i r t) -> f i r t", i=8, r=8)
            P.dma(GFv[0:512, :, rank, :], obf[LA_BF["ksb"]:LA_BF["ksb"] + 512, :].rearrange("f (i t) -> f i t", i=8), reads=[obf], writes=[GF])
            P.dma(GFv[512:1024, :, rank, :], obf[LA_BF["dk"]:LA_BF["dk"] + 512, :].rearrange("f (i t) -> f i t", i=8), reads=[obf], writes=[GF])
            P.phase_end()
        for rank in range(8):
            xcur, obf, of32, xmid = xcurA[rank], obfA[rank], of32A[rank], xmidA[rank]
            P.phase_begin()
            modS = P.sb("modS", [128, 96], F32)
            P.dma(modS[:, :].rearrange("p (r j) -> p r j", r=8), MOD[:, :, l, :].rearrange("r p j -> p r j"), reads=[MOD], writes=[modS])
            srcs = dict(uincl=uincl, ident=ident, masks=Buf(masksA[rank], "masks"), dmask=dmask, don=Buf(dond[l], "don"), gn=Buf(gnd[l], "gn"),
                        lamv=Buf(lamd[l], "lamv"))
            c = lb_consts(P, srcs)
            c["hs"] = P.sb("hs", [128, 8, KC, 2], F32)
            lb_load_consts(P, c, srcs)
            lb_load_layer(P, c, srcs, LAM_INIT[l])
            lb_phase(P, c, obf, of32, GF, GT, xcur, xmid, modS, Buf(Wout[u0 + U_OFF["br"]:u0 + U_OFF["br"] + 16], "wbr"),
                     Buf(Wout[u0 + U_OFF["out"]:u0 + U_OFF["out"] + 16], "wout"), rank, LAM_INIT[l])
            HLv = HL[1:65, :, :, :].rearrange("(i r) p k t -> p i r (k t)", i=8, r=8)
            P.dma(HLv[:, :, rank, :], c["hs"][:, :, :, :].rearrange("p i k t -> p i (k t)"), reads=[c["hs"]], writes=[HL])
            P.phase_end()
        for rank in range(8):
            xmid = xmidA[rank]
            P.phase_begin()
            modS = P.sb("modS", [128, 96], F32)
            P.dma(modS[:, :].rearrange("p (r j) -> p r j", r=8), MOD[:, :, l, :].rearrange("r p j -> p r j"), reads=[MOD], writes=[modS])
            cc = lc_alloc(P)
            cc["xhi"] = P.sb("xhi", [128, 8, KC * 2], F32)
            psR = RR([P.ps(f"cps{i}") for i in range(8)])
            xdst = Buf(xfin[rank], "xfin") if l == 1 else xnextA[rank]
            HLr = HL[0:64, :, :, :].rearrange("(i r) p k t -> p i r (k t)", i=8, r=8)
            lc_phase(P, cc, psR, xmid, HLr[:, :, rank, :], HL, xdst, modS, Buf(nffn[l], "nf"), Buf(cwd[l], "cw"), Buf(cbd[l], "cb"), Buf(flagA[rank], "flag"),
                     Buf(Wout[u0 + U_OFF["up"]:u0 + U_OFF["up"] + 88], "wup"),
                     Buf(Wout[u0 + U_OFF["dn"]:u0 + U_OFF["dn"] + 48].rearrange("(f g) p k n -> f p g k n", g=3), "wdn"), l == 1)
            P.phase_end()
        xcurA = xnextA
    print("n_ins", P.n_ins)
    return P.finish()


def _weight_units(inputs):
    us = []
    for l in range(2):
        us.append(la_weights(inputs["w_in"][l], inputs["w_gate"][l]))
        us.append(slabs(inputs["w_branch"][l], KC))
        us.append(slabs(inputs["w_out"][l], KC))
        us.append(slabs(inputs["w_up"][l], KC))
        wd = slabs(inputs["w_down"][l], 44)
        wdp = np.zeros((16, 128, 48, 128), np.float32)
        wdp[:, :, :44, :] = wd
        us.append(np.ascontiguousarray(wdp.reshape(16, 128, 3, 16, 128).transpose(0, 2, 1, 3, 4)).reshape(48, 128, 16, 128))
    return np.concatenate(us, axis=0)


def kernel(**inputs):
    inputs = {k: np.asarray(v) for k, v in inputs.items()}
    nc = get_nc("single", build_single)
    x = inputs["x"][0]
    units = _weight_units(inputs)
    c_in = np.ascontiguousarray(inputs["c"].reshape(KC, 128).T)
    nmix = np.stack([vec_pm(inputs["norm_mix"][l]) for l in range(2)])
    nffn = np.stack([vec_pm(inputs["norm_ffn"][l]) for l in range(2)])
    lc = [la_consts(inputs["diff_qn"][l], inputs["diff_kn"][l]) for l in range(2)]
    cst = np.stack([lc[l][0] for l in range(2)])
    dec, bones = lc[0][1], lc[0][2]
    cw = np.stack([np.ascontiguousarray(inputs["conv_w"][l].reshape(3, 88, 128).transpose(2, 0, 1)) for l in range(2)])
    cb = np.stack([vec_pm(inputs["conv_b"][l]) for l in range(2)])
    don = np.stack([np.ascontiguousarray(np.broadcast_to(inputs["diff_on"][l][None, :], (128, 128))) for l in range(2)]).astype(np.float32)
    gn = np.stack([np.concatenate([vec_pm(inputs["ret_gn_g"][l]), vec_pm(inputs["ret_gn_b"][l])], 1) for l in range(2)])
    lamv = np.stack([np.stack([inputs["lam_q1"][l], inputs["lam_k1"][l], inputs["lam_q2"][l], inputs["lam_k2"][l]], 1) for l in range(2)]).astype(np.float32)
    xT, pos, masks, flags, bada = [], [], [], [], []
    for r in range(NCORE):
        tok = own_tokens(r)
        uincl, ident, mk, dmask = lb_host_consts(r)
        flag = np.ones((128, 8, 2), np.float32)
        if r == 0:
            flag[:, 0, :] = 0.0
        xT.append(to_fm(x[tok]))
        pos.append(np.ascontiguousarray(np.broadcast_to(inputs["positions"][0, tok][None, :], (128, T))).astype(np.int32))
        masks.append(mk)
        flags.append(flag)
        bada.append(np.ascontiguousarray(inputs["b_ada"][:, r * 1536:(r + 1) * 1536].reshape(2, 12, 128).transpose(2, 0, 1)))
    wa = np.ascontiguousarray(inputs["w_ada"].reshape(2, KC, 128, 8, 1536).transpose(0, 3, 2, 1, 4))
    im = dict(xT=np.stack(xT), pos=np.stack(pos), c_in=c_in, wada=wa, bada=np.stack(bada), wsh=units,
              nmix=nmix, nffn=nffn, cst=cst, cw=cw, cb=cb, don=don, gn=gn, lamv=lamv, dec=dec, bones=bones,
              uincl=uincl, ident=ident, masks=np.stack(masks), dmask=dmask, flag=np.stack(flags))
    res = run_bass_kernel_spmd(nc, [im], core_ids=[0]).results
    xo = res[0]["xoutT"]
    out = np.zeros((S, D), np.float32)
    for r in range(NCORE):
        out[own_tokens(r)] = xo[r].transpose(2, 1, 0).reshape(T, D)
    return out[None]
```

```python
from contextlib import ExitStack
import numpy as np
import ml_dtypes
import concourse.bass as bass
import concourse.mybir as mybir
from concourse.bass_utils import run_bass_kernel_spmd

F32, BF16, I32 = mybir.dt.float32, mybir.dt.bfloat16, mybir.dt.int32
AF = mybir.ActivationFunctionType
ALU = mybir.AluOpType
NPBF = ml_dtypes.bfloat16


class Buf:
    def __init__(self, t, name):
        self.t = t
        self.name = name
        self.w = {}
        self.r = {}

    def __getitem__(self, k):
        return self.t[k]


class Prog:
    def __init__(self, n_dma_sems=24):
        self.nc = bass.Bass("TRN2", target_bir_lowering=False)
        nc = self.nc
        self.es = ExitStack()
        self.eng = dict(pe=nc.tensor, act=nc.scalar, dve=nc.vector, pool=nc.gpsimd, sp=nc.sync)
        self.semh = {k: nc.alloc_semaphore("s_" + k) for k in self.eng}
        self.cnt = {k: 0 for k in self.eng}
        self.waited = {k: {} for k in self.eng}
        self.nd = n_dma_sems
        for i in range(self.nd):
            self.semh[("d", i)] = nc.alloc_semaphore(f"dsem{i}")
        self.dcnt = [0] * self.nd
        self.drr = 0
        self.out_tokens = []
        self.n_ins = 0

    def _u(self, name):
        self.uid = getattr(self, "uid", 0) + 1
        return f"{name}_{self.uid}"

    def sb(self, name, shape, dt):
        name = self._u(name)
        t = self.es.enter_context(self.nc.sbuf_tensor(name, list(shape), dt))
        return Buf(t, name)

    def ps(self, name, shape=(128, 512), dt=F32):
        name = self._u(name)
        t = self.es.enter_context(self.nc.psum_tensor(name, list(shape), dt))
        return Buf(t, name)

    def dram(self, name, shape, dt, kind="Internal", shared=False):
        if kind == "Internal":
            name = self._u(name)
        if shared:
            t = self.nc.dram_tensor(name, list(shape), dt, kind=kind, addr_space="Shared")
        else:
            t = self.nc.dram_tensor(name, list(shape), dt, kind=kind)
        return Buf(t.ap(), name)

    def _deps(self, reads, writes):
        deps = {}

        def add(k, v):
            if deps.get(k, 0) < v:
                deps[k] = v

        for b in reads:
            for k, v in b.w.items():
                add(k, v)
        for b in writes:
            for k, v in b.w.items():
                add(k, v)
            for k, v in b.r.items():
                add(k, v)
        return deps

    def _wait(self, e, deps):
        for k, v in deps.items():
            if k == "pe" and e == "pe":
                continue
            if self.waited[e].get(k, 0) < v:
                self.eng[e].wait_ge(self.semh[k], v)
                self.waited[e][k] = v

    def _record(self, tok, reads, writes):
        k, v = tok
        for b in reads:
            if b.r.get(k, 0) < v:
                b.r[k] = v
        for b in writes:
            if b.w.get(k, 0) < v:
                b.w[k] = v
            b.r = {}

    def op(self, e, fn, reads=(), writes=(), inc=True):
        self._wait(e, self._deps(reads, writes))
        ins = fn(self.eng[e])
        self.n_ins += 1
        if inc:
            self.cnt[e] += 1
            ins.then_inc(self.semh[e], 1)
            tok = (e, self.cnt[e])
        else:
            tok = (e, self.cnt[e] + 1)
        self._record(tok, reads, writes)
        return ins

    def dma(self, out_ap, in_ap, reads=(), writes=(), q="sp", is_output=False, **kw):
        k = self.drr
        self.drr = (self.drr + 1) % self.nd
        deps = self._deps(reads, writes)
        key = ("d", k)
        if self.dcnt[k] > 0 and deps.get(key, 0) < self.dcnt[k] * 16:
            deps[key] = self.dcnt[k] * 16
        self._wait(q, deps)
        ins = self.eng[q].dma_start(out=out_ap, in_=in_ap, **kw)
        self.n_ins += 1
        self.dcnt[k] += 1
        ins.then_inc(self.semh[key], 16)
        tok = (key, self.dcnt[k] * 16)
        self._record(tok, reads, writes)
        if is_output:
            self.out_tokens.append(tok)
        return ins

    def barrier(self):
        full = {k: v for k, v in self.cnt.items() if v > 0}
        for i in range(self.nd):
            if self.dcnt[i] > 0:
                full[("d", i)] = self.dcnt[i] * 16
        for e in self.eng:
            self._wait(e, dict(full))

    def phase_begin(self):
        self.barrier()
        self.es_outer = self.es
        self.es = ExitStack()

    def phase_end(self):
        self.barrier()
        self.es.close()
        self.es = self.es_outer

    def all_reduce(self, in_ap, out_ap, reads=(), writes=()):
        k = self.drr
        self.drr = (self.drr + 1) % self.nd
        deps = self._deps(reads, writes)
        key = ("d", k)
        if self.dcnt[k] > 0 and deps.get(key, 0) < self.dcnt[k] * 16:
            deps[key] = self.dcnt[k] * 16
        self._wait("pool", deps)
        ins = self.eng["pool"].collective_compute("AllReduce", ALU.add, replica_groups=[list(range(8))], ins=[in_ap], outs=[out_ap])
        self.dcnt[k] += 1
        ins.then_inc(self.semh[key], 16)
        tok = (key, self.dcnt[k] * 16)
        self._record(tok, reads, writes)

    def finish(self):
        final = {}
        for k, v in self.out_tokens:
            if final.get(k, 0) < v:
                final[k] = v
        self._wait("sp", final)
        self.es.close()
        return self.nc


class RR:
    def __init__(self, bufs):
        self.bufs = bufs
        self.i = 0

    def next(self):
        b = self.bufs[self.i]
        self.i = (self.i + 1) % len(self.bufs)
        return b


D = 2048
S = 8192
NCORE = 8
T = 1024
NB = 8
KC = D // 128
D_FF = 5632
EPS = 1e-6
THETA = 10000.0
LOG_GAMMA = np.log(1.0 - 2.0 ** (-5.0 - np.arange(4, dtype=np.float64)))
PI = float(np.pi)
TWO_PI = float(2 * np.pi)


def _run(nc, in_maps):
    res = run_bass_kernel_spmd(nc, in_maps, core_ids=list(range(NCORE)))
    return res.results


def build_L0():
    P = Prog()
    nc = P.nc
    c_in = P.dram("c_in", [128, KC], F32, kind="ExternalInput")
    wada = P.dram("wada", [2, 128, KC, 1536], F32, kind="ExternalInput")
    bada = P.dram("bada", [2, 1536], F32, kind="ExternalInput")
    modo = P.dram("modo", [2, 1536], F32, kind="ExternalOutput")
    ct = P.sb("ct", [128, KC], F32)
    ca = P.sb("ca", [128, KC], F32)
    wt = P.sb("wt", [128, KC, 1536], F32)
    bt = P.sb("bt", [1, 2, 1536], F32)
    ot = P.sb("ot", [1, 2, 1536], F32)
    pss = [P.ps(f"ps{i}") for i in range(3)]
    P.dma(ct[:], c_in[:, :], writes=[ct])
    P.dma(bt[0:1, :, :], bada[:, :].unsqueeze(0), writes=[bt])
    P.op("act", lambda e: e.activation(out=ca[:], in_=ct[:], func=AF.Silu), reads=[ct], writes=[ca])
    for l in range(2):
        for q in range(4):
            P.dma(wt[:, 4 * q:4 * q + 4, :], wada[l, :, 4 * q:4 * q + 4, :], writes=[wt])
        for n in range(3):
            for kc in range(KC):
                P.op("pe", lambda e, n=n, kc=kc: e.matmul(pss[n][0:1, :], lhsT=ca[:, kc:kc + 1],
                                                       rhs=wt[:, kc, n * 512:(n + 1) * 512],
                                                       start=(kc == 0), stop=(kc == KC - 1)),
                     reads=[ca, wt], writes=[pss[n]], inc=(kc == KC - 1))
            P.op("dve", lambda e, n=n, l=l: e.tensor_tensor(out=ot[0:1, l, n * 512:(n + 1) * 512], in0=pss[n][0:1, :],
                                                         in1=bt[0:1, l, n * 512:(n + 1) * 512], op=ALU.add),
                 reads=[pss[n], bt], writes=[ot])
    P.dma(modo[:, :].unsqueeze(0), ot[0:1, :, :], reads=[ot], writes=[modo], is_output=True)
    return P.finish()


def run_L0(inputs):
    nc = build_L0()
    c = np.ascontiguousarray(inputs["c"].reshape(KC, 128).T)
    in_maps = []
    for r in range(NCORE):
        w = inputs["w_ada"][:, :, r * 1536:(r + 1) * 1536]
        w = np.ascontiguousarray(w.reshape(2, KC, 128, 1536).transpose(0, 2, 1, 3))
        b = np.ascontiguousarray(inputs["b_ada"][:, r * 1536:(r + 1) * 1536])
        in_maps.append({"c_in": c, "wada": w, "bada": b})
    res = _run(nc, in_maps)
    mod = np.concatenate([res[r]["modo"] for r in range(NCORE)], axis=1)
    return mod


def load_slab(P, wst, wbf, wsrc_ap, nkc, cast_eng="pool"):
    st = wst.next()
    P.dma(st[:, 0:nkc, :], wsrc_ap, writes=[st])
    wb = wbf.next()
    if cast_eng == "act":
        P.op("act", lambda e: e.copy(out=wb[:, 0:nkc, :], in_=st[:, 0:nkc, :]), reads=[st], writes=[wb])
    else:
        P.op(cast_eng, lambda e: e.tensor_copy(out=wb[:, 0:nkc, :], in_=st[:, 0:nkc, :]), reads=[st], writes=[wb])
    return wb


def adaln_norm(P, xT, hT, ntok, gs, shift_ap_fn, ones32, psA, psB, rstd, tmpRR):
    nt = (ntok + 511) // 512
    for h in range(nt):
        t0, t1 = h * 512, min(ntok, (h + 1) * 512)
        w = t1 - t0
        pb = psA if h % 2 == 0 else psB
        for kc in range(KC):
            sq = tmpRR.next()
            P.op("act", lambda e, kc=kc, sq=sq: e.activation(out=sq[:, 0:w], in_=xT[:, kc, t0:t1], func=AF.Square),
                 reads=[xT], writes=[sq])
            P.op("pe", lambda e, kc=kc, sq=sq: e.matmul(pb[:, 0:w], lhsT=ones32[:, :], rhs=sq[:, 0:w],
                                                       start=(kc == 0), stop=(kc == KC - 1)),
                 reads=[ones32, sq], writes=[pb], inc=True)
        P.op("dve", lambda e: e.tensor_scalar(out=rstd[:, t0:t1], in0=pb[:, 0:w], scalar1=1.0 / D, scalar2=EPS,
                                              op0=ALU.mult, op1=ALU.add), reads=[pb], writes=[rstd])
        P.op("act", lambda e: e.activation(out=rstd[:, t0:t1], in_=rstd[:, t0:t1], func=AF.Sqrt), reads=[rstd], writes=[rstd])
        P.op("dve", lambda e: e.reciprocal(out=rstd[:, t0:t1], in_=rstd[:, t0:t1]), reads=[rstd], writes=[rstd])
        for kc in range(KC):
            tm = tmpRR.next()
            P.op("dve", lambda e, kc=kc, tm=tm: e.tensor_tensor(out=tm[:, 0:w], in0=xT[:, kc, t0:t1], in1=rstd[:, t0:t1],
                                                             op=ALU.mult), reads=[xT, rstd], writes=[tm])
            P.op("act", lambda e, kc=kc, tm=tm: e.activation(out=hT[:, kc, t0:t1], in_=tm[:, 0:w], func=AF.Identity,
                                                          scale=gs[:, kc:kc + 1], bias=shift_ap_fn(kc)),
                 reads=[tm, gs], writes=[hT])


LA_BF = dict(qsb=0, ksb=512, vsb=1024, rq=1536, rqx=2048, rk=2560, rkz=3072, rv=3584, dq=4608, dk=5120, dv=5632)
LA_BF_ROWS = 6144
LA_F32 = dict(rg=0, gates=1024)
LA_F32_ROWS = 1024 + 6144
N_SLAB_A = 48 + 16 + 48


def la_phase(P, d, w_bf16):
    xTd, modT, nmix, posd, cst, dec, bones, wsl, obf, of32 = (d[k] for k in
        ("xT", "modT", "nmix", "pos", "cst", "dec", "bones", "wsl", "obf", "of32"))
    out_flag = d.get("is_output", False)

    xT = P.sb("xTs", [128, KC, T], F32)
    hT = P.sb("hT", [128, KC, T], BF16)
    modS = P.sb("modS", [128, 96], F32)
    nmS = P.sb("nmS", [128, KC], F32)
    gs = P.sb("gs", [128, KC], F32)
    cS = P.sb("cS", [128, 8], F32)
    decS = P.sb("decS", [128, 8, 128], F32)
    posI = P.sb("posI", [128, T], I32)
    posF = P.sb("posF", [128, T], F32)
    cosR = P.sb("cosR", [128, T], F32)
    sinR = P.sb("sinR", [128, T], F32)
    cosD = P.sb("cosD", [128, T], F32)
    sinD = P.sb("sinD", [128, T], F32)
    cqT = P.sb("cqT", [128, T], F32)
    sqT = P.sb("sqT", [128, T], F32)
    ones32 = P.sb("ones32", [128, 128], F32)
    bonesB = P.sb("bonesB", [128, 128], BF16)
    bonesF = P.sb("bonesF", [128, 128], F32)
    rstd = P.sb("rstd", [128, T], F32)
    ang = rstd
    ckT, skT = cosD, sinD
    pib = P.sb("pib", [128, 1], F32)
    tmpRR = RR([P.sb(f"tmp{i}", [128, 512], F32) for i in range(4)])
    wst = None if w_bf16 else RR([P.sb(f"wst{i}", [128, KC, 128], F32) for i in range(2)])
    wbf = RR([P.sb(f"wbf{i}", [128, KC, 128], BF16) for i in range(4)])
    obR = RR([P.sb(f"ob{i}", [128, T], BF16) for i in range(4)])
    ofR = RR([P.sb(f"of{i}", [128, T], F32) for i in range(3)])
    sqR = RR([P.sb(f"sqb{i}", [128, 512], BF16) for i in range(2)])
    psR = RR([P.ps(f"ps{i}") for i in range(6 if w_bf16 else 8)])

    for q in range(4):
        P.dma(xT[:, 4 * q:4 * q + 4, :], xTd[:, 4 * q:4 * q + 4, :], writes=[xT])
    if w_bf16:
        P.dma(modS[:, :].rearrange("p (r j) -> p r j", r=8), modT[:, :, :], writes=[modS])
    else:
        P.dma(modS[:], modT[:, :], writes=[modS])
    P.dma(nmS[:], nmix[:, :], writes=[nmS])
    P.dma(cS[:], cst[:, :], writes=[cS])
    P.dma(decS[:], dec[:, :, :], writes=[decS])
    P.dma(posI[:], posd[:, :], writes=[posI])
    P.dma(bonesF[:], bones[:, :], writes=[bonesF])
    P.op("pool", lambda e: e.memset(ones32[:], 1.0), writes=[ones32])
    P.op("pool", lambda e: e.memset(pib[:], PI), writes=[pib])
    P.op("pool", lambda e: e.tensor_copy(out=bonesB[:], in_=bonesF[:]), reads=[bonesF], writes=[bonesB])
    P.op("dve", lambda e: e.scalar_tensor_tensor(out=gs[:], in0=modS[:, 16:32], scalar=1.0, in1=nmS[:],
                                                 op0=ALU.add, op1=ALU.mult), reads=[modS, nmS], writes=[gs])
    P.op("dve", lambda e: e.tensor_copy(out=posF[:], in_=posI[:]), reads=[posI], writes=[posF])

    def sincos(inv_col, sign_col, cosT, sinT):
        def one(outT, shift):
            P.op("dve", lambda e: e.tensor_scalar(out=ang[:], in0=posF[:], scalar1=cS[:, inv_col:inv_col + 1], scalar2=shift,
                                                  op0=ALU.mult, op1=ALU.add), reads=[posF, cS], writes=[ang])
            P.op("dve", lambda e: e.tensor_copy(out=posI[:], in_=ang[:]), reads=[ang], writes=[posI])
            P.op("dve", lambda e: e.tensor_copy(out=outT[:], in_=posI[:]), reads=[posI], writes=[outT])
            P.op("dve", lambda e: e.tensor_tensor(out=ang[:], in0=ang[:], in1=outT[:], op=ALU.subtract), reads=[ang, outT], writes=[ang])
            P.op("dve", lambda e: e.tensor_single_scalar(out=outT[:], in_=ang[:], scalar=0.5, op=ALU.is_gt), reads=[ang], writes=[outT])
            P.op("dve", lambda e: e.tensor_tensor(out=ang[:], in0=ang[:], in1=outT[:], op=ALU.subtract), reads=[ang, outT], writes=[ang])
            P.op("dve", lambda e: e.tensor_single_scalar(out=outT[:], in_=ang[:], scalar=-0.5, op=ALU.is_lt), reads=[ang], writes=[outT])
            P.op("dve", lambda e: e.tensor_tensor(out=ang[:], in0=ang[:], in1=outT[:], op=ALU.add), reads=[ang, outT], writes=[ang])
            P.op("act", lambda e: e.activation(out=outT[:], in_=ang[:], func=AF.Sin, scale=TWO_PI), reads=[ang], writes=[outT])
        one(sinT, 0.0)
        P.op("dve", lambda e: e.tensor_scalar(out=sinT[:], in0=sinT[:], scalar1=cS[:, sign_col:sign_col + 1], scalar2=None,
                                              op0=ALU.mult), reads=[sinT, cS], writes=[sinT])
        one(cosT, 0.25)

    sincos(0, 1, cosR, sinR)
    sincos(2, 3, cosD, sinD)
    P.op("dve", lambda e: e.tensor_scalar(out=cqT[:], in0=cosD[:], scalar1=cS[:, 4:5], scalar2=0.125, op0=ALU.mult, op1=ALU.mult),
         reads=[cosD, cS], writes=[cqT])
    P.op("dve", lambda e: e.tensor_scalar(out=sqT[:], in0=sinD[:], scalar1=cS[:, 5:6], scalar2=0.125, op0=ALU.mult, op1=ALU.mult),
         reads=[sinD, cS], writes=[sqT])
    P.op("dve", lambda e: e.tensor_scalar(out=ckT[:], in0=cosD[:], scalar1=cS[:, 6:7], scalar2=None, op0=ALU.mult),
         reads=[cosD, cS], writes=[ckT])
    P.op("dve", lambda e: e.tensor_scalar(out=skT[:], in0=sinD[:], scalar1=cS[:, 7:8], scalar2=None, op0=ALU.mult),
         reads=[sinD, cS], writes=[skT])

    psA, psB = psR.next(), psR.next()
    adaln_norm(P, xT, hT, T, gs, lambda kc: modS[:, kc:kc + 1], ones32, psA, psB, rstd, tmpRR)

    def proj(slab_idx):
        if w_bf16:
            wb = wbf.next()
            P.dma(wb[:, :, :], wsl[slab_idx], reads=[wsl], writes=[wb])
        else:
            wb = load_slab(P, wst, wbf, wsl[slab_idx], KC)
        outs = []
        for h in range(2):
            pb = psR.next()
            for kc in range(KC):
                P.op("pe", lambda e, kc=kc, pb=pb, h=h: e.matmul(pb[:, :], lhsT=wb[:, kc, :], rhs=hT[:, kc, h * 512:(h + 1) * 512],
                                                              start=(kc == 0), stop=(kc == KC - 1)),
                     reads=[wb, hT], writes=[pb], inc=(kc == KC - 1))
            outs.append(pb)
        return outs

    def store_bf(ob, row):
        P.dma(obf[row:row + 128, :], ob[:, :], reads=[ob], writes=[obf], is_output=out_flag, q="pool")

    def store_f32(of, row):
        P.dma(of32[row:row + 128, :], of[:, :], reads=[of], writes=[of32], is_output=out_flag, q="pool")

    def simple(slab_idx, row, scale=None):
        pbs = proj(slab_idx)
        ob = obR.next()
        for h in range(2):
            if scale is None:
                P.op("act", lambda e, h=h: e.copy(out=ob[:, h * 512:(h + 1) * 512], in_=pbs[h][:, :]), reads=[pbs[h]], writes=[ob])
            else:
                P.op("act", lambda e, h=h: e.mul(out=ob[:, h * 512:(h + 1) * 512], in_=pbs[h][:, :], mul=scale), reads=[pbs[h]], writes=[ob])
        store_bf(ob, row)

    def actf32(slab_idx, row, func):
        pbs = proj(slab_idx)
        of = ofR.next()
        for h in range(2):
            P.op("act", lambda e, h=h: e.activation(out=of[:, h * 512:(h + 1) * 512], in_=pbs[h][:, :], func=func),
                 reads=[pbs[h]], writes=[of])
        store_f32(of, row)

    def rope_pair(slab_idx, rot_idx, cT, sT, h):
        raise NotImplementedError

    def rope_ret(slab_idx, rot_idx, head, is_q):
        pa = proj(slab_idx)
        pbr = proj(rot_idx)
        o1, o2 = obR.next(), obR.next()
        for h in range(2):
            sl = slice(h * 512, (h + 1) * 512)
            t1, t2 = tmpRR.next(), tmpRR.next()
            P.op("dve", lambda e: e.tensor_tensor(out=t1[:, :], in0=pa[h][:, :], in1=cosR[:, sl], op=ALU.mult),
                 reads=[pa[h], cosR], writes=[t1])
            P.op("dve", lambda e: e.tensor_tensor(out=t2[:, :], in0=pbr[h][:, :], in1=sinR[:, sl], op=ALU.mult),
                 reads=[pbr[h], sinR], writes=[t2])
            P.op("pool", lambda e: e.tensor_tensor(out=t1[:, :], in0=t1[:, :], in1=t2[:, :], op=ALU.add),
                 reads=[t1, t2], writes=[t1])
            tab = decS[:, (head if is_q else 4 + head), :].unsqueeze(1).broadcast_to([128, 4, 128])
            t1v = t1[:, :].rearrange("p (b t) -> p b t", t=128)
            if is_q:
                P.op("act", lambda e: e.copy(out=o1[:, sl], in_=t1[:, :]), reads=[t1], writes=[o1])
                P.op("pool", lambda e: e.tensor_tensor(out=o2[:, sl].rearrange("p (b t) -> p b t", t=128), in0=t1v, in1=tab, op=ALU.mult),
                     reads=[t1, decS], writes=[o2])
            else:
                sc = 128.0 ** -0.5
                P.op("act", lambda e: e.mul(out=o1[:, sl], in_=t1[:, :], mul=sc), reads=[t1], writes=[o1])
                P.op("dve", lambda e: e.scalar_tensor_tensor(out=o2[:, sl].rearrange("p (b t) -> p b t", t=128), in0=t1v, scalar=sc,
                                                              in1=tab, op0=ALU.mult, op1=ALU.mult),
                     reads=[t1, decS], writes=[o2])
        if is_q:
            store_bf(o1, LA_BF["rq"] + head * 128)
            store_bf(o2, LA_BF["rqx"] + head * 128)
        else:
            store_bf(o1, LA_BF["rk"] + head * 128)
            store_bf(o2, LA_BF["rkz"] + head * 128)

    def rope_diff(slab_idx, rot_idx, head, is_q):
        pa = proj(slab_idx)
        pbr = proj(rot_idx)
        cT, sT = (cqT, sqT) if is_q else (ckT, skT)
        o1 = obR.next()
        for h in range(2):
            sl = slice(h * 512, (h + 1) * 512)
            sq = sqR.next()
            P.op("act", lambda e: e.activation(out=sq[:, :], in_=pa[h][:, :], func=AF.Square), reads=[pa[h]], writes=[sq])
            pc = psR.next()
            P.op("pe", lambda e: e.matmul(pc[:, :], lhsT=bonesB[:, :], rhs=sq[:, :], start=True, stop=True),
                 reads=[bonesB, sq], writes=[pc])
            t1, t2, t3 = tmpRR.next(), tmpRR.next(), tmpRR.next()
            P.op("dve", lambda e: e.tensor_scalar(out=t3[:, :], in0=pc[:, :], scalar1=1.0 / 64, scalar2=EPS, op0=ALU.mult, op1=ALU.add),
                 reads=[pc], writes=[t3])
            P.op("act", lambda e: e.activation(out=t3[:, :], in_=t3[:, :], func=AF.Sqrt), reads=[t3], writes=[t3])
            P.op("dve", lambda e: e.reciprocal(out=t3[:, :], in_=t3[:, :]), reads=[t3], writes=[t3])
            P.op("dve", lambda e: e.tensor_tensor(out=t1[:, :], in0=pa[h][:, :], in1=cT[:, sl], op=ALU.mult),
                 reads=[pa[h], cT], writes=[t1])
            P.op("dve", lambda e: e.tensor_tensor(out=t2[:, :], in0=pbr[h][:, :], in1=sT[:, sl], op=ALU.mult),
                 reads=[pbr[h], sT], writes=[t2])
            P.op("pool", lambda e: e.tensor_tensor(out=t1[:, :], in0=t1[:, :], in1=t2[:, :], op=ALU.add), reads=[t1, t2], writes=[t1])
            P.op("pool", lambda e: e.tensor_tensor(out=o1[:, sl], in0=t1[:, :], in1=t3[:, :], op=ALU.mult), reads=[t1, t3], writes=[o1])
        store_bf(o1, (LA_BF["dq"] if is_q else LA_BF["dk"]) + head * 128)

    for j in range(4):
        simple(j, LA_BF["qsb"] + j * 128, scale=128.0 ** -0.5)
    for j in range(4):
        simple(4 + j, LA_BF["ksb"] + j * 128)
    for j in range(4):
        simple(8 + j, LA_BF["vsb"] + j * 128)
    for j in range(4):
        rope_ret(12 + j, 48 + j, j, True)
    for j in range(4):
        rope_ret(16 + j, 52 + j, j, False)
    for j in range(8):
        simple(20 + j, LA_BF["rv"] + j * 128)
    for j in range(8):
        actf32(28 + j, LA_F32["rg"] + j * 128, AF.Silu)
    for j in range(4):
        rope_diff(36 + j, 56 + j, j, True)
    for j in range(4):
        rope_diff(40 + j, 60 + j, j, False)
    for j in range(4):
        simple(44 + j, LA_BF["dv"] + j * 128)
    for j in range(48):
        actf32(64 + j, LA_F32["gates"] + j * 128, AF.Sigmoid)


def build_LA():
    P = Prog()
    d = dict(xT=P.dram("xT", [128, KC, T], F32, kind="ExternalInput"), modT=P.dram("modT", [128, 96], F32, kind="ExternalInput"),
             nmix=P.dram("nmix", [128, KC], F32, kind="ExternalInput"), pos=P.dram("pos", [128, T], I32, kind="ExternalInput"),
             cst=P.dram("cst", [128, 8], F32, kind="ExternalInput"), dec=P.dram("dec", [128, 8, 128], F32, kind="ExternalInput"),
             bones=P.dram("bones", [128, 128], F32, kind="ExternalInput"), wsl=P.dram("wsl", [N_SLAB_A, 128, KC, 128], F32, kind="ExternalInput"),
             obf=P.dram("obf", [LA_BF_ROWS, T], BF16, kind="ExternalOutput"), of32=P.dram("of32", [LA_F32_ROWS, T], F32, kind="ExternalOutput"),
             is_output=True)
    la_phase(P, d, False)
    return P.finish()


def own_tokens(core):
    return np.concatenate([np.arange((8 * i + core) * 128, (8 * i + core + 1) * 128) for i in range(NB)])


def to_fm(x_tok):
    Tn, Fn = x_tok.shape
    return np.ascontiguousarray(x_tok.T.reshape(Fn // 128, 128, Tn).transpose(1, 0, 2))


def vec_pm(v):
    return np.ascontiguousarray(v.reshape(-1, 128).T)


def slabs(W, nkc):
    K_, N_ = W.shape
    return np.ascontiguousarray(W.reshape(nkc, 128, N_ // 128, 128).transpose(2, 1, 0, 3))


def la_consts(diff_qn, diff_kn):
    p = np.arange(128)
    inv_ret = THETA ** (-(2.0 * (p % 64)) / 128.0)
    sign_ret = np.where(p < 64, -1.0, 1.0)
    q = p % 64
    inv_diff = THETA ** (-(2.0 * (q % 32)) / 64.0)
    sign_diff = np.where(q < 32, -1.0, 1.0)
    partner = np.where(q < 32, q + 32, q - 32)
    cst = np.stack([inv_ret / (2 * np.pi), sign_ret, inv_diff / (2 * np.pi), sign_diff, diff_qn[q], diff_qn[partner], diff_kn[q], diff_kn[partner]], axis=1)
    tl = np.arange(128)
    dec = np.zeros((8, 128), np.float64)
    for h in range(4):
        dec[h] = np.exp((tl + 1.0) * LOG_GAMMA[h])
        dec[4 + h] = np.exp((127.0 - tl) * LOG_GAMMA[h])
    dec = np.broadcast_to(dec[None], (128, 8, 128))
    bones = np.zeros((128, 128), np.float32)
    bones[:64, :64] = 1.0
    bones[64:, 64:] = 1.0
    return cst.astype(np.float32), np.ascontiguousarray(dec).astype(np.float32), bones


def la_weights(w_in, w_gate):
    cols = np.arange(6144)
    perm = cols.copy()
    for base in (1536, 2048):
        for hd in range(4):
            o = base + hd * 128
            perm[o:o + 64] = np.arange(o + 64, o + 128)
            perm[o + 64:o + 128] = np.arange(o, o + 64)
    for base in (4608, 5120):
        for m in range(8):
            o = base + m * 64
            perm[o:o + 32] = np.arange(o + 32, o + 64)
            perm[o + 32:o + 64] = np.arange(o, o + 32)
    rot_cols = np.concatenate([np.arange(1536, 2048), np.arange(2048, 2560), np.arange(4608, 5120), np.arange(5120, 5632)])
    Wcat = np.concatenate([w_in, w_in[:, perm[rot_cols]], w_gate], axis=1)
    return slabs(Wcat, KC)


_CACHE = {}


def get_nc(name, fn):
    if name not in _CACHE:
        _CACHE[name] = fn()
    return _CACHE[name]


def run_LA(inputs, l, x_full, mod):
    nc = get_nc("LA", build_LA)
    cst, dec, bones = la_consts(inputs["diff_qn"][l], inputs["diff_kn"][l])
    wsl = la_weights(inputs["w_in"][l], inputs["w_gate"][l])
    modT = vec_pm(mod[l])
    nmix = vec_pm(inputs["norm_mix"][l])
    in_maps = []
    for r in range(NCORE):
        tok = own_tokens(r)
        pos = np.ascontiguousarray(np.broadcast_to(inputs["positions"][0, tok][None, :], (128, T))).astype(np.int32)
        in_maps.append(dict(xT=to_fm(x_full[tok]), modT=modT, nmix=nmix, pos=pos, cst=cst, dec=dec, bones=bones, wsl=wsl))
    res = _run(nc, in_maps)
    return [(res[r]["obf"], res[r]["of32"]) for r in range(NCORE)]


GT_V_SB, GT_V_D, GT_KZ, GT_RV = 0, 512, 1024, 1536


def lb_phase(P, c, obf, of32, GF, GT, xTd, xoutd, modS, wbr, wout, rank, lam_init, dbg=None):
    psR = c["psR"]
    ident = c["identF"]
    ysbT = c["ybr"]
    qsb = c["qown"]
    P.dma(qsb[:, :, :], obf[LA_BF["qsb"]:LA_BF["qsb"] + 512, :].rearrange("(h p) t -> p h t", p=128), writes=[qsb])
    vS = c["vS"]
    R = c["R"]
    nq = c["nq"]
    P.op("pool", lambda e: e.tensor_scalar(out=nq[:, :, :], in0=qsb[:, :, :], scalar1=-1.0, scalar2=None, op0=ALU.mult),
         reads=[qsb], writes=[nq])
    psA = c["psA"]
    for hd in range(4):
        kT = c["kTR"].next()
        P.dma(kT[:, :], GF[hd * 128:(hd + 1) * 128, :], reads=[GF], writes=[kT])
        P.dma(vS[:, :, 0:128], GT[1024:, GT_V_SB + hd * 128:GT_V_SB + (hd + 1) * 128].rearrange("(b p) d -> p b d", p=128),
              reads=[GT], writes=[vS])
        items = [(i, g) for i in range(NB) for g in range(2 * i + 1, -1, -1)]
        st = {}

        def stA(k, hd=hd, kT=kT):
            i, g = items[k]
            diag = g >= 2 * i
            Z = psR.next()
            for j in range(4):
                blk = 4 * g + j
                P.op("pe", lambda e, j=j, blk=blk: e.matmul(Z[:, j * 128:(j + 1) * 128], lhsT=kT[:, blk * 128:(blk + 1) * 128],
                                                         rhs=qsb[:, hd, i * 128:(i + 1) * 128], start=True, stop=True),
                     reads=[kT, qsb], writes=[Z], inc=(j == 3))
            ef = c["efR"].next()
            P.op("act", lambda e: e.activation(out=ef[:, :], in_=Z[:, :], func=AF.Exp), reads=[Z], writes=[ef])
            sp = c["spR"].next()
            P.op("act", lambda e: e.activation(out=sp[:, :], in_=ef[:, :], func=AF.Ln, bias=c["one"][:, 0:1]),
                 reads=[ef, c["one"]], writes=[sp])
            if diag:
                P.op("dve", lambda e: e.tensor_tensor(out=sp[:, :], in0=sp[:, :], in1=c["msb"][:, g - 2 * i, :], op=ALU.mult),
                     reads=[sp, c["msb"]], writes=[sp])
            st[k] = dict(sp=sp)

        def stB(k, hd=hd, kT=kT):
            i, g = items[k]
            diag = g >= 2 * i
            sp = st[k]["sp"]
            if g == 2 * i + 1:
                P.op("pool", lambda e: e.memset(R[:, :], 0.0), writes=[R])
            Pb = psR.next()
            Tb = psR.next()
            for j in range(4):
                blk = 4 * g + j
                P.op("pe", lambda e, j=j: e.matmul(Pb[:, j * 128:(j + 1) * 128], lhsT=c["uincl"][:, :], rhs=sp[:, j * 128:(j + 1) * 128],
                                                 start=True, stop=False), reads=[c["uincl"], sp], writes=[Pb], inc=False)
                for j2 in range(j + 1, 4):
                    P.op("pe", lambda e, j=j, j2=j2: e.matmul(Pb[:, j * 128:(j + 1) * 128], lhsT=c["onesB"][:, :], rhs=sp[:, j2 * 128:(j2 + 1) * 128],
                                                           start=False, stop=False), reads=[c["onesB"], sp], writes=[Pb], inc=False)
                P.op("pe", lambda e, j=j, blk=blk: e.matmul(Pb[:, j * 128:(j + 1) * 128], lhsT=kT[:, blk * 128:(blk + 1) * 128],
                                                         rhs=nq[:, hd, i * 128:(i + 1) * 128], start=False, stop=True),
                     reads=[kT, nq], writes=[Pb], inc=False)
            for j in range(4):
                P.op("pe", lambda e, j=j: e.matmul(Tb[:, 0:128], lhsT=c["onesB"][:, :], rhs=sp[:, j * 128:(j + 1) * 128],
                                                 start=(j == 0), stop=(j == 3)), reads=[c["onesB"], sp], writes=[Tb, Pb], inc=(j == 3))
            tf = c["efR"].next()
            P.op("dve", lambda e: e.tensor_tensor(out=tf[:, :].rearrange("p (b t) -> p b t", t=128),
                                                  in0=Pb[:, :].rearrange("p (b t) -> p b t", t=128),
                                                  in1=R[:, :].unsqueeze(1).broadcast_to([128, 4, 128]), op=ALU.add),
                 reads=[Pb, R], writes=[tf])
            wt = c["wR"].next()
            P.op("act", lambda e: e.activation(out=wt[:, :], in_=tf[:, :], func=AF.Exp, scale=-1.0), reads=[tf], writes=[wt])
            if diag:
                P.op("dve", lambda e: e.tensor_tensor(out=wt[:, :], in0=wt[:, :], in1=c["msb"][:, g - 2 * i, :], op=ALU.mult),
                     reads=[wt, c["msb"]], writes=[wt])
            P.op("dve", lambda e: e.tensor_tensor(out=R[:, :], in0=R[:, :], in1=Tb[:, 0:128], op=ALU.add), reads=[R, Tb], writes=[R])
            st[k]["wt"] = wt

        def stC(k, hd=hd):
            i, g = items[k]
            wt = st[k]["wt"]
            Y = psA[i % 2]
            for j in range(4):
                blk = 4 * g + j
                P.op("pe", lambda e, j=j, blk=blk: e.matmul(Y[:, 0:128], lhsT=vS[:, blk, 0:128], rhs=wt[:, j * 128:(j + 1) * 128],
                                                         start=(g == 2 * i + 1 and j == 0), stop=(g == 0 and j == 3)),
                     reads=[vS, wt], writes=[Y], inc=(j == 3))
            if g == 0:
                P.op("act", lambda e: e.copy(out=ysbT[:, hd, i * 128:(i + 1) * 128], in_=Y[:, 0:128]), reads=[Y], writes=[ysbT])
            del st[k]

        n = len(items)
        for k in range(n + 2):
            if k < n:
                stA(k)
            if 0 <= k - 1 < n:
                stB(k - 1)
            if 0 <= k - 2 < n:
                stC(k - 2)

    qd = c["qown"]
    P.dma(qd[:, :, :], obf[LA_BF["dq"]:LA_BF["dq"] + 512, :].rearrange("(h p) t -> p h t", p=128), writes=[qd])
    for hd in range(4):
        kT = c["kTR"].next()
        P.dma(kT[:, :], GF[512 + hd * 128:512 + (hd + 1) * 128, :], reads=[GF], writes=[kT])
        P.dma(vS[:, :, 0:128], GT[1024:, GT_V_D + hd * 128:GT_V_D + (hd + 1) * 128].rearrange("(b p) d -> p b d", p=128),
              reads=[GT], writes=[vS])
        P.op("pool", lambda e: e.memset(vS[:, :, 128:129], 1.0), writes=[vS])
        items = [(i, g, m) for i in range(NB) for g in range(2 * i + 2) for m in range(2)]
        st = {}

        def dA(k, hd=hd, kT=kT):
            i, g, m = items[k]
            diag = g >= 2 * i
            Sb = psR.next()
            for j in range(4):
                blk = 4 * g + j
                P.op("pe", lambda e, j=j, blk=blk: e.matmul(
                    Sb[:, j * 128:(j + 1) * 128], lhsT=kT[m * 64:(m + 1) * 64, blk * 128:(blk + 1) * 128],
                    rhs=qd[m * 64:(m + 1) * 64, hd, i * 128:(i + 1) * 128], start=True, stop=True),
                    reads=[kT, qd], writes=[Sb], inc=(j == 3))
            eb = c["wR"].next()
            P.op("act", lambda e: e.activation(out=eb[:, :], in_=Sb[:, :], func=AF.Exp), reads=[Sb], writes=[eb])
            if diag:
                P.op("dve", lambda e: e.tensor_tensor(out=eb[:, :], in0=eb[:, :], in1=c["mdf"][:, g - 2 * i, :], op=ALU.mult),
                     reads=[eb, c["mdf"]], writes=[eb])
            st[k] = eb

        def dC(k, hd=hd):
            i, g, m = items[k]
            ng = 2 * i + 2
            eb = st.pop(k)
            Ym = [psA[2 * (i % 2)], psA[2 * (i % 2) + 1]]
            for j in range(4):
                blk = 4 * g + j
                P.op("pe", lambda e, j=j, blk=blk: e.matmul(
                    Ym[m][:, 0:129], lhsT=eb[:, j * 128:(j + 1) * 128], rhs=vS[:, blk, 0:129],
                    start=(g == 0 and j == 0), stop=(g == ng - 1 and j == 3)),
                    reads=[eb, vS], writes=[Ym[m]], inc=(j == 3))
            if not (g == ng - 1 and m == 1):
                return
            rc = c["small"].next()
            P.op("dve", lambda e: e.reciprocal(out=rc[:, 0:1], in_=Ym[0][:, 128:129]), reads=[Ym[0]], writes=[rc])
            P.op("dve", lambda e: e.reciprocal(out=rc[:, 1:2], in_=Ym[1][:, 128:129]), reads=[Ym[1]], writes=[rc])
            P.op("dve", lambda e: e.tensor_tensor(out=rc[:, 1:2], in0=rc[:, 1:2], in1=c["nlam"][:, 0:1], op=ALU.mult),
                 reads=[rc, c["nlam"]], writes=[rc])
            ya = c["tokR"].next()
            yb = c["tokR"].next()
            P.op("dve", lambda e: e.tensor_scalar(out=ya[:, 0:128], in0=Ym[0][:, 0:128], scalar1=rc[:, 0:1], scalar2=None, op0=ALU.mult),
                 reads=[Ym[0], rc], writes=[ya])
            P.op("dve", lambda e: e.scalar_tensor_tensor(out=yb[:, 0:128], in0=Ym[1][:, 0:128], scalar=rc[:, 1:2], in1=ya[:, 0:128],
                                                         op0=ALU.mult, op1=ALU.add), reads=[Ym[1], rc, ya], writes=[yb])
            P.op("act", lambda e: e.activation(out=ya[:, 0:128], in_=yb[:, 0:128], func=AF.Square, accum_out=rc[:, 2:3]),
                 reads=[yb], writes=[ya, rc])
            P.op("dve", lambda e: e.tensor_scalar(out=rc[:, 2:3], in0=rc[:, 2:3], scalar1=1.0 / 128, scalar2=EPS, op0=ALU.mult, op1=ALU.add),
                 reads=[rc], writes=[rc])
            P.op("act", lambda e: e.activation(out=rc[:, 2:3], in_=rc[:, 2:3], func=AF.Sqrt), reads=[rc], writes=[rc])
            P.op("dve", lambda e: e.reciprocal(out=rc[:, 2:3], in_=rc[:, 2:3]), reads=[rc], writes=[rc])
            P.op("dve", lambda e: e.scalar_tensor_tensor(out=ya[:, 0:128], in0=yb[:, 0:128], scalar=rc[:, 2:3], in1=c["don"][:, :],
                                                         op0=ALU.mult, op1=ALU.mult), reads=[yb, rc, c["don"]], writes=[ya])
            P.op("dve", lambda e: e.tensor_scalar(out=ya[:, 0:128], in0=ya[:, 0:128], scalar1=float(1.0 - lam_init), scalar2=None, op0=ALU.mult),
                 reads=[ya], writes=[ya])
            Tp = psR.next()
            P.op("pe", lambda e: e.transpose(Tp[:, 0:128], ya[:, 0:128], ident[:, :]), reads=[ya, ident], writes=[Tp])
            P.op("act", lambda e: e.copy(out=ysbT[:, 12 + hd, i * 128:(i + 1) * 128], in_=Tp[:, 0:128]), reads=[Tp], writes=[ysbT])

        n = len(items)
        for k in range(n + 2):
            if k < n:
                dA(k)
            if 0 <= k - 2 < n:
                dC(k - 2)

    rq, rqx, rk, rvO, st32, stB = c["rq"], c["rqx"], c["rk"], c["rvO"], c["st32"], c["stB"]
    GTb = GT[:, :].rearrange("(b p) d -> b p d", p=128)
    if isinstance(rank, int):
        P.dma(c["RS"][:, :].rearrange("(b p) d -> b p d", p=128), GTb[rank:rank + 65, :, GT_KZ:GT_KZ + 1536], reads=[GT], writes=[c["RS"]])
    else:
        P.dma(c["RS"][:, :].rearrange("(b p) d -> b p d", p=128), GTb[bass.ds(rank, 65), :, GT_KZ:GT_KZ + 1536],
              reads=[GT], writes=[c["RS"]], q="pool")
    rgT = c["gch"]
    for hd in range(4):
        P.dma(rq[:, :], obf[LA_BF["rq"] + hd * 128:LA_BF["rq"] + (hd + 1) * 128, :], writes=[rq])
        P.dma(rqx[:, :], obf[LA_BF["rqx"] + hd * 128:LA_BF["rqx"] + (hd + 1) * 128, :], writes=[rqx])
        P.dma(rk[:, :], obf[LA_BF["rk"] + hd * 128:LA_BF["rk"] + (hd + 1) * 128, :], writes=[rk])
        P.op("pool", lambda e: e.memset(st32[:, :], 0.0), writes=[st32])
        g128 = float(np.exp(128.0 * LOG_GAMMA[hd]))
        for i in range(NB):
            kzs, vss = c["kzs"].next(), c["vss"].next()
            RSb = c["RS"][:, :].rearrange("(b p) d -> b p d", p=128)
            P.dma(kzs[:, :, :], RSb[8 * i:8 * i + 8, :, hd * 128:(hd + 1) * 128].rearrange("b p d -> p b d"),
                  reads=[c["RS"]], writes=[kzs])
            P.dma(vss[:, :, :], RSb[8 * i:8 * i + 9, :, 512 + hd * 256:512 + (hd + 1) * 256].rearrange("b p d -> p b d"),
                  reads=[c["RS"]], writes=[vss])
            for n in range(8):
                Sp = psR.next()
                P.op("pe", lambda e, n=n: e.matmul(Sp[:, 0:256], lhsT=kzs[:, n, :], rhs=vss[:, n, :], start=True, stop=True),
                     reads=[kzs, vss], writes=[Sp])
                P.op("dve", lambda e: e.scalar_tensor_tensor(out=st32[:, :], in0=st32[:, :], scalar=g128, in1=Sp[:, 0:256],
                                                             op0=ALU.mult, op1=ALU.add), reads=[st32, Sp], writes=[st32])
            P.op("act", lambda e: e.copy(out=stB[:, :], in_=st32[:, :]), reads=[st32], writes=[stB])
            Sc = psR.next()
            P.op("pe", lambda e: e.matmul(Sc[:, 0:128], lhsT=rk[:, i * 128:(i + 1) * 128], rhs=rq[:, i * 128:(i + 1) * 128], start=True, stop=True),
                 reads=[rk, rq], writes=[Sc])
            scb = c["scb"].next()
            P.op("dve", lambda e: e.tensor_tensor(out=scb[:, :], in0=Sc[:, 0:128], in1=c["dmask"][:, hd, :], op=ALU.mult),
                 reads=[Sc, c["dmask"]], writes=[scb])
            for ec in range(2):
                Yr = psR.next()
                P.op("pe", lambda e, ec=ec: e.matmul(Yr[:, 0:128], lhsT=vss[:, 8, ec * 128:(ec + 1) * 128], rhs=scb[:, :], start=True, stop=False),
                     reads=[vss, scb], writes=[Yr], inc=False)
                P.op("pe", lambda e, ec=ec: e.matmul(Yr[:, 0:128], lhsT=stB[:, ec * 128:(ec + 1) * 128], rhs=rqx[:, i * 128:(i + 1) * 128],
                                                     start=False, stop=True), reads=[stB, rqx], writes=[Yr])
                yr = c["yr"][ec]
                P.op("act", lambda e: e.copy(out=yr[:, :], in_=Yr[:, 0:128]), reads=[Yr], writes=[yr])
            St = psR.next()
            for ec in range(2):
                P.op("pe", lambda e, ec=ec: e.matmul(St[:, 0:128], lhsT=c["ones32"][:, :], rhs=c["yr"][ec][:, :], start=(ec == 0), stop=(ec == 1)),
                     reads=[c["ones32"], c["yr"][ec]], writes=[St], inc=(ec == 1))
            for ec in range(2):
                P.op("act", lambda e, ec=ec: e.activation(out=c["ysq"][ec][:, :], in_=c["yr"][ec][:, :], func=AF.Square),
                     reads=[c["yr"][ec]], writes=[c["ysq"][ec]])
            for ec in range(2):
                P.op("pe", lambda e, ec=ec: e.matmul(St[:, 128:256], lhsT=c["ones32"][:, :], rhs=c["ysq"][ec][:, :], start=(ec == 0), stop=(ec == 1)),
                     reads=[c["ones32"], c["ysq"][ec]], writes=[St], inc=(ec == 1))
            mu = c["tokR"].next()
            P.op("dve", lambda e: e.tensor_scalar(out=mu[:, 0:256], in0=St[:, 0:256], scalar1=1.0 / 256, scalar2=None, op0=ALU.mult),
                 reads=[St], writes=[mu])
            var = c["tokR"].next()
            P.op("dve", lambda e: e.tensor_tensor(out=var[:, 0:128], in0=mu[:, 0:128], in1=mu[:, 0:128], op=ALU.mult), reads=[mu], writes=[var])
            P.op("dve", lambda e: e.tensor_tensor(out=var[:, 0:128], in0=mu[:, 128:256], in1=var[:, 0:128], op=ALU.subtract),
                 reads=[mu, var], writes=[var])
            P.op("dve", lambda e: e.tensor_scalar(out=var[:, 0:128], in0=var[:, 0:128], scalar1=EPS, scalar2=None, op0=ALU.add),
                 reads=[var], writes=[var])
            P.op("act", lambda e: e.activation(out=var[:, 0:128], in_=var[:, 0:128], func=AF.Sqrt), reads=[var], writes=[var])
            P.op("dve", lambda e: e.reciprocal(out=var[:, 0:128], in_=var[:, 0:128]), reads=[var], writes=[var])
            for ec in range(2):
                yr = c["yr"][ec]
                fch = hd * 2 + ec
                P.dma(rgT[:, 0:128], of32[LA_F32["rg"] + fch * 128:LA_F32["rg"] + (fch + 1) * 128, i * 128:(i + 1) * 128], writes=[rgT])
                P.op("dve", lambda e: e.tensor_tensor(out=yr[:, :], in0=yr[:, :], in1=mu[:, 0:128], op=ALU.subtract), reads=[yr, mu], writes=[yr])
                P.op("dve", lambda e: e.tensor_tensor(out=yr[:, :], in0=yr[:, :], in1=var[:, 0:128], op=ALU.mult), reads=[yr, var], writes=[yr])
                P.op("dve", lambda e, fch=fch: e.tensor_scalar(out=yr[:, :], in0=yr[:, :], scalar1=c["gng"][:, fch:fch + 1], scalar2=c["gnb"][:, fch:fch + 1],
                                                            op0=ALU.mult, op1=ALU.add), reads=[yr, c["gng"], c["gnb"]], writes=[yr])
                P.op("dve", lambda e, fch=fch: e.tensor_tensor(out=ysbT[:, 4 + fch, i * 128:(i + 1) * 128], in0=yr[:, :], in1=rgT[:, 0:128], op=ALU.mult),
                     reads=[yr, rgT], writes=[ysbT])

    if dbg is not None:
        P.dma(dbg[:, :, :], ysbT[:, :, :], reads=[ysbT], writes=[dbg], is_output=True)
    mT = c["mT"]
    wbfR = c["wbf"]

    def slab(srcbuf, idx):
        wb = wbfR.next()
        P.dma(wb[:, :, :], srcbuf[idx], reads=[srcbuf], writes=[wb])
        return wb

    br_k = [(0, 4), (4, 12), (12, 16)]
    for fc in range(16):
        wb = slab(wbr, fc)
        acc = c["acc"]
        for b, (k0, k1) in enumerate(br_k):
            gch = c["gch"]
            P.dma(gch[:, :], of32[LA_F32["gates"] + b * 2048 + fc * 128:LA_F32["gates"] + b * 2048 + (fc + 1) * 128, :], writes=[gch])
            for h in range(2):
                Bp = psR.next()
                for kc in range(k0, k1):
                    P.op("pe", lambda e, kc=kc, h=h: e.matmul(Bp[:, :], lhsT=wb[:, kc, :], rhs=ysbT[:, kc, h * 512:(h + 1) * 512],
                                                           start=(kc == k0), stop=(kc == k1 - 1)), reads=[wb, ysbT], writes=[Bp], inc=(kc == k1 - 1))
                sl = slice(h * 512, (h + 1) * 512)
                if b == 0:
                    P.op("dve", lambda e: e.tensor_tensor(out=acc[:, sl], in0=Bp[:, :], in1=gch[:, sl], op=ALU.mult), reads=[Bp, gch], writes=[acc])
                else:
                    tmp = c["efR"].next()
                    P.op("dve", lambda e: e.tensor_tensor(out=tmp[:, :], in0=Bp[:, :], in1=gch[:, sl], op=ALU.mult), reads=[Bp, gch], writes=[tmp])
                    if b == 1:
                        P.op("pool", lambda e: e.tensor_tensor(out=acc[:, sl], in0=acc[:, sl], in1=tmp[:, :], op=ALU.add), reads=[acc, tmp], writes=[acc])
                    else:
                        P.op("pool", lambda e: e.tensor_tensor(out=mT[:, fc, sl], in0=acc[:, sl], in1=tmp[:, :], op=ALU.add), reads=[acc, tmp], writes=[mT])
    for fc in range(16):
        wb = slab(wout, fc)
        xch = c["xch"].next()
        P.dma(xch[:, :], xTd[:, fc, :], reads=[xTd], writes=[xch])
        for h in range(2):
            Op = psR.next()
            for kc in range(KC):
                P.op("pe", lambda e, kc=kc, h=h: e.matmul(Op[:, :], lhsT=wb[:, kc, :], rhs=mT[:, kc, h * 512:(h + 1) * 512],
                                                       start=(kc == 0), stop=(kc == KC - 1)), reads=[wb, mT], writes=[Op], inc=(kc == KC - 1))
            sl = slice(h * 512, (h + 1) * 512)
            P.op("dve", lambda e: e.scalar_tensor_tensor(out=xch[:, sl], in0=Op[:, :], scalar=modS[:, 32 + fc:33 + fc], in1=xch[:, sl],
                                                         op0=ALU.mult, op1=ALU.add), reads=[Op, modS, xch], writes=[xch])
        if c.get("hs") is not None:
            P.op("pool", lambda e, fc=fc: e.tensor_copy(out=c["hs"][:, :, fc, :], in_=xch[:, :].rearrange("p (i t) -> p i t", t=128)[:, :, 126:128]),
                 reads=[xch], writes=[c["hs"]])
        P.dma(xoutd[:, fc, :], xch[:, :], reads=[xch], writes=[xoutd], is_output=(dbg is not None), q="pool")


def lb_consts(P, srcs):
    c = {}
    c["RS"] = P.dram("RSscr", [65 * 128, 1536], BF16)
    c["psR"] = RR([P.ps(f"lps{i}") for i in range(4)])
    c["psA"] = [P.ps(f"lpsA{i}") for i in range(4)]
    c["ybr"] = P.sb("ybr", [128, 16, T], BF16)
    c["mT"] = P.sb("mT", [128, 16, T], BF16)
    c["qown"] = P.sb("qown", [128, 4, T], BF16)
    c["kTR"] = RR([P.sb(f"kTs{i}", [128, S], BF16) for i in range(2)])
    c["nq"] = P.sb("nq", [128, 4, T], BF16)
    c["vS"] = P.sb("vS", [128, 64, 130], BF16)
    c["R"] = P.sb("Rsum", [128, 128], F32)
    c["efR"] = RR([P.sb(f"ef{i}", [128, 512], F32) for i in range(3)])
    c["spR"] = RR([P.sb(f"sp{i}", [128, 512], BF16) for i in range(3)])
    c["wR"] = RR([P.sb(f"wt{i}", [128, 512], BF16) for i in range(4)])
    c["small"] = RR([P.sb(f"sm{i}", [128, 4], F32) for i in range(2)])
    c["tokR"] = RR([P.sb(f"tok{i}", [128, 256], F32) for i in range(3)])
    c["rq"] = P.sb("rq", [128, T], BF16)
    c["rqx"] = P.sb("rqx", [128, T], BF16)
    c["rk"] = P.sb("rk", [128, T], BF16)
    c["rvO"] = P.sb("rvO", [128, 256], BF16)
    c["st32"] = P.sb("st32", [128, 256], F32)
    c["stB"] = P.sb("stB", [128, 256], BF16)
    c["kzs"] = RR([P.sb(f"kzs{i}", [128, 8, 128], BF16) for i in range(2)])
    c["vss"] = RR([P.sb(f"vss{i}", [128, 9, 256], BF16) for i in range(2)])
    c["scb"] = RR([P.sb(f"scb{i}", [128, 128], BF16) for i in range(2)])
    c["yr"] = [P.sb(f"yr{i}", [128, 128], F32) for i in range(2)]
    c["ysq"] = [P.sb(f"ysq{i}", [128, 128], F32) for i in range(2)]
    c["gch"] = P.sb("gch", [128, T], F32)
    c["acc"] = P.sb("acc", [128, T], F32)
    c["xch"] = RR([P.sb(f"xch{i}", [128, T], F32) for i in range(2)])
    c["wbf"] = RR([P.sb(f"lwbf{i}", [128, KC, 128], BF16) for i in range(3)])
    c["one"] = P.sb("one", [128, 1], F32)
    c["ones32"] = P.sb("lones32", [128, 128], F32)
    c["onesB"] = P.sb("onesB", [128, 128], BF16)
    c["uincl"] = P.sb("uincl", [128, 128], BF16)
    c["identF"] = P.sb("identF", [128, 128], F32)
    c["msb"] = P.sb("msb", [128, 2, 512], BF16)
    c["mdf"] = P.sb("mdf", [128, 2, 512], BF16)
    c["dmask"] = P.sb("dmask", [128, 4, 128], F32)
    c["don"] = P.sb("don", [128, 128], F32)
    c["gng"] = P.sb("gng", [128, 8], F32)
    c["gnb"] = P.sb("gnb", [128, 8], F32)
    c["nlam"] = P.sb("nlam", [128, 1], F32)
    c["lamv"] = P.sb("lamv", [64, 4], F32)
    P.op("pool", lambda e: e.memset(c["one"][:], 1.0), writes=[c["one"]])
    P.op("pool", lambda e: e.memset(c["ones32"][:], 1.0), writes=[c["ones32"]])
    P.op("pool", lambda e: e.memset(c["onesB"][:], 1.0), writes=[c["onesB"]])
    return c


def lb_load_consts(P, c, s):
    P.dma(c["uincl"][:], s["uincl"][:, :], writes=[c["uincl"]])
    P.dma(c["identF"][:], s["ident"][:, :], writes=[c["identF"]])
    P.dma(c["msb"][:], s["masks"][:, 0, :, :], writes=[c["msb"]])
    P.dma(c["mdf"][:], s["masks"][:, 1, :, :], writes=[c["mdf"]])
    P.dma(c["dmask"][:], s["dmask"][:, :, :], writes=[c["dmask"]])


def lb_load_layer(P, c, s, lam_init):
    P.dma(c["don"][:], s["don"][:, :], writes=[c["don"]])
    P.dma(c["gng"][:], s["gn"][:, 0:8], writes=[c["gng"]])
    P.dma(c["gnb"][:], s["gn"][:, 8:16], writes=[c["gnb"]])
    P.dma(c["lamv"][:], s["lamv"][:, :], writes=[c["lamv"]])
    pr = c["small"].next()
    P.op("dve", lambda e: e.tensor_tensor(out=pr[0:64, 0:1], in0=c["lamv"][:, 0:1], in1=c["lamv"][:, 1:2], op=ALU.mult), reads=[c["lamv"]], writes=[pr])
    P.op("dve", lambda e: e.tensor_tensor(out=pr[0:64, 1:2], in0=c["lamv"][:, 2:3], in1=c["lamv"][:, 3:4], op=ALU.mult), reads=[c["lamv"]], writes=[pr])
    Lp = c["psR"].next()
    P.op("pe", lambda e: e.matmul(Lp[:, 0:2], lhsT=c["ones32"][0:64, :], rhs=pr[0:64, 0:2], start=True, stop=True),
         reads=[c["ones32"], pr], writes=[Lp])
    ex = c["small"].next()
    P.op("act", lambda e: e.activation(out=ex[:, 0:2], in_=Lp[:, 0:2], func=AF.Exp), reads=[Lp], writes=[ex])
    P.op("dve", lambda e: e.tensor_tensor(out=c["nlam"][:, 0:1], in0=ex[:, 1:2], in1=ex[:, 0:1], op=ALU.subtract), reads=[ex], writes=[c["nlam"]])
    P.op("dve", lambda e: e.tensor_scalar(out=c["nlam"][:, 0:1], in0=c["nlam"][:, 0:1], scalar1=float(-lam_init), scalar2=None, op0=ALU.add),
         reads=[c["nlam"]], writes=[c["nlam"]])


def lb_host_consts(core):
    p = np.arange(128)
    uincl = (p[:, None] >= p[None, :]).astype(np.float32)
    ident = np.eye(128, dtype=np.float32)
    masks = np.zeros((128, 2, 2, 512), np.float32)
    for g2 in range(2):
        for j in range(4):
            kb = 4 * g2 + j
            sl = slice(j * 128, (j + 1) * 128)
            if kb < core:
                masks[:, 0, g2, sl] = 1.0
                masks[:, 1, g2, sl] = 1.0
            elif kb == core:
                masks[:, 0, g2, sl] = (p[:, None] < p[None, :])
                masks[:, 1, g2, sl] = ((p[:, None] // 64) <= (p[None, :] // 64))
    dm = np.zeros((128, 4, 128), np.float64)
    n = p[None, :]
    m = p[:, None]
    for h in range(4):
        same = (m // 64) == (n // 64)
        earlier = (m // 64) < (n // 64)
        dm[:, h, :] = np.where(same, np.exp(np.abs(n - m) * LOG_GAMMA[h]), np.where(earlier, np.exp((n - m) * LOG_GAMMA[h]), 0.0))
    return uincl.astype(NPBF), ident, masks.astype(NPBF), dm.astype(np.float32)


def build_LB_test(lam_init):
    P = Prog()
    obf = P.dram("obf", [LA_BF_ROWS, T], BF16, kind="ExternalInput")
    of32 = P.dram("of32", [LA_F32_ROWS, T], F32, kind="ExternalInput")
    GF = P.dram("GF", [1024, S], BF16, kind="ExternalInput")
    GT = P.dram("GT", [1024 + S, 2560], BF16, kind="ExternalInput")
    xTd = P.dram("xT", [128, KC, T], F32, kind="ExternalInput")
    modT = P.dram("modT", [128, 96], F32, kind="ExternalInput")
    wbr = P.dram("wbr", [16, 128, KC, 128], BF16, kind="ExternalInput")
    wout = P.dram("wout", [16, 128, KC, 128], BF16, kind="ExternalInput")
    s = {k: P.dram("i_" + k, shp, dt, kind="ExternalInput") for k, shp, dt in [
        ("uincl", [128, 128], BF16), ("ident", [128, 128], F32), ("masks", [128, 2, 2, 512], BF16), ("dmask", [128, 4, 128], F32),
        ("don", [128, 128], F32), ("gn", [128, 16], F32), ("lamv", [64, 4], F32)]}
    xoutd = P.dram("xout", [128, KC, T], F32, kind="ExternalOutput")
    dbg = P.dram("dbg", [128, 16, T], BF16, kind="ExternalOutput")
    modS = P.sb("modS", [128, 96], F32)
    P.dma(modS[:], modT[:, :], writes=[modS])
    c = lb_consts(P, s)
    lb_load_consts(P, c, s)
    lb_load_layer(P, c, s, lam_init)
    rank = P.nc.gpsimd.partition_id()
    lb_phase(P, c, obf, of32, GF, GT, xTd, xoutd, modS, wbr, wout, rank, lam_init, dbg=dbg)
    return P.finish()


def lc_alloc(P):
    c = {}
    c["xT"] = P.sb("c_xT", [128, KC, T], F32)
    c["hT"] = P.sb("c_hT", [128, KC, T], BF16)
    c["xh"] = P.sb("c_xh", [128, KC, 16], F32)
    c["hh"] = P.sb("c_hh", [128, KC, 16], BF16)
    c["actT"] = P.sb("c_actT", [128, 44, 512], BF16)
    c["U"] = RR([P.sb(f"c_U{i}", [128, 4, 130], F32) for i in range(3)])
    c["Y"] = RR([P.sb(f"c_Y{i}", [128, 4, 128], F32) for i in range(4)])
    c["wup"] = RR([P.sb(f"c_wup{i}", [128, KC, 128], BF16) for i in range(3)])
    c["wdn"] = RR([P.sb(f"c_wdn{i}", [128, KC, 128], BF16) for i in range(3)])
    c["xo"] = RR([P.sb(f"c_xo{i}", [128, 512], F32) for i in range(2)])
    c["gs"] = P.sb("c_gs", [128, KC], F32)
    c["nf"] = P.sb("c_nf", [128, KC], F32)
    c["cw"] = P.sb("c_cw", [128, 3, 88], F32)
    c["cb"] = P.sb("c_cb", [128, 88], F32)
    c["flag"] = P.sb("c_flag", [128, 8, 2], F32)
    c["rstd"] = P.sb("c_rstd", [128, T], F32)
    c["tmp"] = RR([P.sb(f"c_tmp{i}", [128, 512], F32) for i in range(3)])
    c["ones32"] = P.sb("c_ones32", [128, 128], F32)
    P.op("pool", lambda e: e.memset(c["ones32"][:], 1.0), writes=[c["ones32"]])
    return c


def lc_phase(P, c, psR, xmid, halo_ap, halo_buf, xout, modS, nffn_d, cw_d, cb_d, flag_d, wup, wdn, out_is_final):
    xT, hT, xh, hh, actT = c["xT"], c["hT"], c["xh"], c["hh"], c["actT"]
    for q in range(4):
        P.dma(xT[:, 4 * q:4 * q + 4, :], xmid[:, 4 * q:4 * q + 4, :], reads=[xmid], writes=[xT])
    if len(halo_ap.shape) == 3:
        P.dma(c["xhi"][:, :, :], halo_ap, reads=[halo_buf], writes=[c["xhi"]])
    else:
        P.dma(c["xhi"][:, :, :].unsqueeze(2), halo_ap, reads=[halo_buf], writes=[c["xhi"]], q="pool")
    P.op("pool", lambda e: e.tensor_copy(out=xh[:, :, :].rearrange("p k (i t) -> p k i t", t=2),
                                         in_=c["xhi"][:, :, :].rearrange("p i (k t) -> p k i t", t=2)), reads=[c["xhi"]], writes=[xh])
    P.dma(c["nf"][:], nffn_d[:, :], writes=[c["nf"]])
    P.dma(c["cw"][:], cw_d[:, :, :], writes=[c["cw"]])
    P.dma(c["cb"][:], cb_d[:, :], writes=[c["cb"]])
    P.dma(c["flag"][:], flag_d[:, :, :], writes=[c["flag"]])
    gs = c["gs"]
    P.op("dve", lambda e: e.scalar_tensor_tensor(out=gs[:], in0=modS[:, 64:80], scalar=1.0, in1=c["nf"][:], op0=ALU.add, op1=ALU.mult),
         reads=[modS, c["nf"]], writes=[gs])
    psA, psB = psR.next(), psR.next()
    adaln_norm(P, xT, hT, T, gs, lambda kc: modS[:, 48 + kc:49 + kc], c["ones32"], psA, psB, c["rstd"], c["tmp"])
    adaln_norm(P, xh, hh, 16, gs, lambda kc: modS[:, 48 + kc:49 + kc], c["ones32"], psA, psB, c["rstd"], c["tmp"])
    for hf in range(2):
        tsl = slice(hf * 512, (hf + 1) * 512)
        for j in range(44):
            Ys = []
            for which in range(2):
                sidx = j + 44 * which
                wb = c["wup"].next()
                P.dma(wb[:, :, :], wup[sidx], reads=[wup], writes=[wb])
                pm, ph = psR.next(), psR.next()
                for kc in range(KC):
                    P.op("pe", lambda e, kc=kc: e.matmul(pm[:, :], lhsT=wb[:, kc, :], rhs=hT[:, kc, tsl], start=(kc == 0), stop=(kc == KC - 1)),
                         reads=[wb, hT], writes=[pm], inc=(kc == KC - 1))
                for kc in range(KC):
                    P.op("pe", lambda e, kc=kc: e.matmul(ph[:, 0:8], lhsT=wb[:, kc, :], rhs=hh[:, kc, hf * 8:(hf + 1) * 8], start=(kc == 0), stop=(kc == KC - 1)),
                         reads=[wb, hh], writes=[ph], inc=(kc == KC - 1))
                U = c["U"].next()
                P.op("act", lambda e: e.copy(out=U[:, :, 2:130], in_=pm[:, :].rearrange("p (b t) -> p b t", t=128)), reads=[pm], writes=[U])
                P.op("dve", lambda e: e.tensor_tensor(out=U[:, :, 0:2], in0=ph[:, 0:8].rearrange("p (b t) -> p b t", t=2),
                                                      in1=c["flag"][:, hf * 4:(hf + 1) * 4, :], op=ALU.mult), reads=[ph, c["flag"]], writes=[U])
                Y = c["Y"].next()
                P.op("act", lambda e, sidx=sidx: e.activation(out=Y[:, :, :], in_=U[:, :, 2:130], func=AF.Identity,
                                                          scale=c["cw"][:, 2, sidx:sidx + 1], bias=c["cb"][:, sidx:sidx + 1]),
                     reads=[U, c["cw"], c["cb"]], writes=[Y])
                P.op("dve", lambda e, sidx=sidx: e.scalar_tensor_tensor(out=Y[:, :, :], in0=U[:, :, 1:129], scalar=c["cw"][:, 1, sidx:sidx + 1], in1=Y[:, :, :],
                                                                     op0=ALU.mult, op1=ALU.add), reads=[U, c["cw"], Y], writes=[Y])
                P.op("dve", lambda e, sidx=sidx: e.scalar_tensor_tensor(out=Y[:, :, :], in0=U[:, :, 0:128], scalar=c["cw"][:, 0, sidx:sidx + 1], in1=Y[:, :, :],
                                                                     op0=ALU.mult, op1=ALU.add), reads=[U, c["cw"], Y], writes=[Y])
                Ys.append(Y)
            sg = c["Y"].next()
            P.op("act", lambda e: e.activation(out=sg[:, :, :], in_=Ys[0][:, :, :], func=AF.Silu), reads=[Ys[0]], writes=[sg])
            P.op("pool", lambda e, j=j: e.tensor_tensor(out=actT[:, j, :].rearrange("p (b t) -> p b t", t=128), in0=sg[:, :, :], in1=Ys[1][:, :, :], op=ALU.mult),
                 reads=[sg, Ys[1]], writes=[actT])
        for fc in range(KC):
            wbs = []
            for g3 in range(3):
                wb = c["wdn"].next()
                P.dma(wb[:, :, :], wdn[fc][:, g3, :, :], reads=[wdn], writes=[wb])
                wbs.append(wb)
            po = psR.next()
            for kc in range(44):
                wb = wbs[kc // 16]
                P.op("pe", lambda e, kc=kc, wb=wb: e.matmul(po[:, :], lhsT=wb[:, kc % 16, :], rhs=actT[:, kc, :], start=(kc == 0), stop=(kc == 43)),
                     reads=[wb, actT], writes=[po], inc=(kc == 43 or kc % 16 == 15))
            xo = c["xo"].next()
            P.op("dve", lambda e, fc=fc: e.scalar_tensor_tensor(out=xo[:, :], in0=po[:, :], scalar=modS[:, 80 + fc:81 + fc], in1=xT[:, fc, tsl],
                                                             op0=ALU.mult, op1=ALU.add), reads=[po, modS, xT], writes=[xo])
            P.dma(xout[:, fc, tsl], xo[:, :], reads=[xo], writes=[xout], is_output=out_is_final, q="pool")


NU_LAYER = 280
U_OFF = dict(la=0, br=112, out=128, up=144, dn=232)
NU = 2 * NU_LAYER
NU_CORE = NU // NCORE
AR_CHUNK = 56
LAM_INIT = [0.8 - 0.6 * float(np.exp(-0.3 * l)) for l in range(2)]


def build_full():
    P = Prog()
    X = lambda n, shp, dt: P.dram(n, shp, dt, kind="ExternalInput")
    xT_in = X("xT", [128, KC, T], F32)
    pos = X("pos", [128, T], I32)
    c_in = X("c_in", [128, KC], F32)
    wada = X("wada", [2, 128, KC, 1536], F32)
    bada = X("bada", [128, 2, 12], F32)
    wsh = X("wsh", [NU_CORE, 128, KC, 128], F32)
    nmix = X("nmix", [2, 128, KC], F32)
    nffn = X("nffn", [2, 128, KC], F32)
    cst = X("cst", [2, 128, 8], F32)
    cwd = X("cw", [2, 128, 3, 88], F32)
    cbd = X("cb", [2, 128, 88], F32)
    dond = X("don", [2, 128, 128], F32)
    gnd = X("gn", [2, 128, 16], F32)
    lamd = X("lamv", [2, 64, 4], F32)
    dec = X("dec", [128, 8, 128], F32)
    bones = X("bones", [128, 128], F32)
    uincl = X("uincl", [128, 128], BF16)
    ident = X("ident", [128, 128], F32)
    masks = X("masks", [128, 2, 2, 512], BF16)
    dmask = X("dmask", [128, 4, 128], F32)
    flagd = X("flag", [128, 8, 2], F32)
    xfin = P.dram("xoutT", [128, KC, T], F32, kind="ExternalOutput")

    Wown = P.dram("Wown", [NU_CORE, 128, KC, 128], BF16)
    WinL = [P.dram(f"Win{l}", [NU_LAYER, 128, KC, 128], BF16) for l in range(2)]
    WoutL = [P.dram(f"Wout{l}", [NU_LAYER, 128, KC, 128], BF16, shared=True) for l in range(2)]
    MODin = P.dram("MODin", [8, 128, 2, 12], F32)
    MOD = P.dram("MOD", [8, 128, 2, 12], F32, shared=True)
    GFin = P.dram("GFin", [1024, S], BF16)
    GF = P.dram("GF", [1024, S], BF16, shared=True)
    GTin = P.dram("GTin", [1024 + S, 2560], BF16)
    GT = P.dram("GT", [1024 + S, 2560], BF16, shared=True)
    HLin = P.dram("HLin", [65, 128, KC, 2], F32)
    HL = P.dram("HL", [65, 128, KC, 2], F32, shared=True)
    OT = P.dram("OT", [T, 2560], BF16)
    obf = P.dram("obf", [LA_BF_ROWS, T], BF16)
    of32 = P.dram("of32", [LA_F32_ROWS, T], F32)
    xmid = P.dram("xmid", [128, KC, T], F32)
    xnext = P.dram("xnext", [128, KC, T], F32)
    rank = P.nc.gpsimd.partition_id()

    P.phase_begin()
    Zt = P.sb("Zt", [128, 8192], BF16)
    Zf = P.sb("Zf", [128, 65 * 32], F32)
    P.op("pool", lambda e: e.memset(Zt[:], 0.0), writes=[Zt])
    P.op("pool", lambda e: e.memset(Zf[:], 0.0), writes=[Zf])
    for Win in WinL:
        for u in range(0, NU_LAYER, 4):
            P.dma(Win[u:u + 4].rearrange("u p k n -> p u (k n)"), Zt[:, :].rearrange("p (u x) -> p u x", u=4), reads=[Zt], writes=[Win])
    for r0 in range(0, 1024, 128):
        P.dma(GFin[r0:r0 + 128, :], Zt[:, :], reads=[Zt], writes=[GFin])
    for b0 in range(0, 72, 3):
        P.dma(GTin[b0 * 128:(b0 + 3) * 128, :].rearrange("(b p) d -> p b d", p=128), Zt[:, 0:7680].rearrange("p (b d) -> p b d", d=2560),
              reads=[Zt], writes=[GTin])
    P.dma(HLin[:, :, :, :].rearrange("b p k t -> p b (k t)"), Zf[:, :].rearrange("p (b x) -> p b x", x=32), reads=[Zf], writes=[HLin])
    P.dma(MODin[:, :, :, :].rearrange("r p l j -> p r (l j)"), Zf[:, 0:192].rearrange("p (r x) -> p r x", x=24), reads=[Zf], writes=[MODin])
    wst = RR([P.sb(f"Wst{i}", [128, KC, 128], F32) for i in range(3)])
    wcb = RR([P.sb(f"Wcb{i}", [128, KC, 128], BF16) for i in range(3)])
    cast_engs = ["pool", "dve", "act"]
    for j in range(NU_CORE):
        st, wb = wst.next(), wcb.next()
        P.dma(st[:, :, :], wsh[j], writes=[st])
        ce = cast_engs[j % 3]
        if ce == "act":
            P.op("act", lambda e: e.copy(out=wb[:, :, :], in_=st[:, :, :]), reads=[st], writes=[wb])
        else:
            P.op(ce, lambda e: e.tensor_copy(out=wb[:, :, :], in_=st[:, :, :]), reads=[st], writes=[wb])
        P.dma(Wown[j], wb[:, :, :], reads=[wb], writes=[Wown])
    for l in range(2):
        Win, Wout = WinL[l], WoutL[l]
        P.dma(Win[:, :, :, :].rearrange("(r j) p k n -> r (j p) (k n)", r=8)[bass.ds(rank, 1), :, :],
              Wown[l * 35:(l + 1) * 35, :, :, :].rearrange("j p k n -> (j p) (k n)").unsqueeze(0), reads=[Wown], writes=[Win], q="pool")
        Win2 = Win[:, :, :, :].rearrange("u p k n -> (u p) (k n)")
        Wout2 = Wout[:, :, :, :].rearrange("u p k n -> (u p) (k n)")
        for u in range(0, NU_LAYER, AR_CHUNK):
            P.all_reduce(Win2[u * 128:(u + AR_CHUNK) * 128, :], Wout2[u * 128:(u + AR_CHUNK) * 128, :], reads=[Win], writes=[Wout])
    ct = P.sb("ct", [128, KC], F32)
    ca = P.sb("ca", [128, KC], F32)
    wt = P.sb("wadat", [128, KC, 1536], F32)
    bt = P.sb("badat", [128, 2, 12], F32)
    mo = P.sb("modown", [128, 2, 12], F32)
    pmod = P.ps("pmod")
    P.dma(ct[:], c_in[:, :], writes=[ct])
    P.dma(bt[:], bada[:, :, :], writes=[bt])
    P.op("act", lambda e: e.activation(out=ca[:], in_=ct[:], func=AF.Silu), reads=[ct], writes=[ca])
    for l in range(2):
        for q in range(4):
            P.dma(wt[:, 4 * q:4 * q + 4, :], wada[l, :, 4 * q:4 * q + 4, :], writes=[wt])
        for fch in range(12):
            for kc in range(KC):
                P.op("pe", lambda e, fch=fch, kc=kc, l=l: e.matmul(pmod[:, l * 12 + fch:l * 12 + fch + 1], lhsT=wt[:, kc, fch * 128:(fch + 1) * 128],
                                                                rhs=ca[:, kc:kc + 1], start=(kc == 0), stop=(kc == KC - 1)),
                     reads=[wt, ca], writes=[pmod], inc=(kc == KC - 1))
    P.op("dve", lambda e: e.tensor_tensor(out=mo[:, :, :], in0=pmod[:, 0:24].rearrange("p (l j) -> p l j", l=2), in1=bt[:, :, :], op=ALU.add),
         reads=[pmod, bt], writes=[mo])
    P.dma(MODin[bass.ds(rank, 1), :, :, :].rearrange("o p l j -> p o l j"), mo[:, :, :].unsqueeze(1), reads=[mo], writes=[MODin], q="pool")
    P.all_reduce(MODin[:, :, :, :].rearrange("r p l j -> (r p) (l j)"), MOD[:, :, :, :].rearrange("r p l j -> (r p) (l j)"),
                 reads=[MODin], writes=[MOD])
    P.phase_end()

    xcur = xT_in
    for l in range(2):
        u0 = 0
        Wout = WoutL[l]
        P.phase_begin()
        modTd = Buf(MOD[:, :, l, :].rearrange("r p j -> p r j"), "modTd")
        d = dict(xT=xcur, modT=modTd, nmix=Buf(nmix[l], "nm"), pos=pos, cst=Buf(cst[l], "cs"), dec=dec, bones=bones,
                 wsl=Buf(Wout[u0 + U_OFF["la"]:u0 + U_OFF["la"] + 112], "wsl"), obf=obf, of32=of32)
        la_phase(P, d, True)
        identS = P.sb("identS", [128, 128], F32)
        P.dma(identS[:], ident[:, :], writes=[identS])
        tin = RR([P.sb(f"tin{i}", [128, T], BF16) for i in range(2)])
        tf = RR([P.sb(f"tf{i}", [128, T], F32) for i in range(2)])
        tout = RR([P.sb(f"tout{i}", [128, 8, 128], BF16) for i in range(2)])
        tps = RR([P.ps(f"tps{i}") for i in range(2)])
        jobs = [(LA_BF["vsb"] + k * 128, GT_V_SB + k * 128) for k in range(4)] + [(LA_BF["dv"] + k * 128, GT_V_D + k * 128) for k in range(4)] + \
               [(LA_BF["rkz"] + k * 128, GT_KZ + k * 128) for k in range(4)] + [(LA_BF["rv"] + k * 128, GT_RV + k * 128) for k in range(8)]
        for row, col in jobs:
            ti, tff, to = tin.next(), tf.next(), tout.next()
            P.dma(ti[:, :], obf[row:row + 128, :], reads=[obf], writes=[ti])
            P.op("dve", lambda e: e.tensor_copy(out=tff[:, :], in_=ti[:, :]), reads=[ti], writes=[tff])
            for hb in range(2):
                tp = tps.next()
                for b4 in range(4):
                    b = hb * 4 + b4
                    P.op("pe", lambda e, b=b, b4=b4: e.transpose(tp[:, b4 * 128:(b4 + 1) * 128], tff[:, b * 128:(b + 1) * 128], identS[:, :]),
                         reads=[tff, identS], writes=[tp])
                P.op("act", lambda e, hb=hb: e.copy(out=to[:, hb * 4:(hb + 1) * 4, :], in_=tp[:, :].rearrange("p (b f) -> p b f", f=128)),
                     reads=[tp], writes=[to])
            P.dma(OT[:, col:col + 128].rearrange("(b p) d -> p b d", p=128), to[:, :, :], reads=[to], writes=[OT])
        GFv = GFin[:, :].rearrange("f (i r t) -> f i r t", i=8, r=8)
        P.dma(GFv[0:512, :, bass.ds(rank, 1), :], obf[LA_BF["ksb"]:LA_BF["ksb"] + 512, :].rearrange("f (i t) -> f i t", i=8).unsqueeze(2),
              reads=[obf], writes=[GFin], q="pool")
        P.dma(GFv[512:1024, :, bass.ds(rank, 1), :], obf[LA_BF["dk"]:LA_BF["dk"] + 512, :].rearrange("f (i t) -> f i t", i=8).unsqueeze(2),
              reads=[obf], writes=[GFin], q="pool")
        GTv = GTin[1024:, :].rearrange("(i r p) d -> i r p d", i=8, r=8)
        P.dma(GTv[:, bass.ds(rank, 1), :, :], OT[:, :].rearrange("(i p) d -> i p d", i=8).unsqueeze(1), reads=[OT], writes=[GTin], q="pool")
        P.all_reduce(GFin[:, :], GF[:, :], reads=[GFin], writes=[GF])
        hr = (1024 + S) // 2
        P.all_reduce(GTin[0:hr, :], GT[0:hr, :], reads=[GTin], writes=[GT])
        P.all_reduce(GTin[hr:, :], GT[hr:, :], reads=[GTin], writes=[GT])
        P.phase_end()
        P.phase_begin()
        modS = P.sb("modS", [128, 96], F32)
        P.dma(modS[:, :].rearrange("p (r j) -> p r j", r=8), MOD[:, :, l, :].rearrange("r p j -> p r j"), reads=[MOD], writes=[modS])
        srcs = dict(uincl=uincl, ident=ident, masks=masks, dmask=dmask, don=Buf(dond[l], "don"), gn=Buf(gnd[l], "gn"), lamv=Buf(lamd[l], "lamv"))
        c = lb_consts(P, srcs)
        c["hs"] = P.sb("hs", [128, 8, KC, 2], F32)
        lb_load_consts(P, c, srcs)
        lb_load_layer(P, c, srcs, LAM_INIT[l])
        lb_phase(P, c, obf, of32, GF, GT, xcur, xmid, modS, Buf(Wout[u0 + U_OFF["br"]:u0 + U_OFF["br"] + 16], "wbr"),
                 Buf(Wout[u0 + U_OFF["out"]:u0 + U_OFF["out"] + 16], "wout"), rank, LAM_INIT[l])
        HLv = HLin[1:65, :, :, :].rearrange("(i r) p k t -> p i r (k t)", i=8, r=8)
        P.dma(HLv[:, :, bass.ds(rank, 1), :], c["hs"][:, :, :, :].rearrange("p i k t -> p i (k t)").unsqueeze(2), reads=[c["hs"]], writes=[HLin], q="pool")
        P.all_reduce(HLin[:, :, :, :].rearrange("b p k t -> (b p) (k t)"), HL[:, :, :, :].rearrange("b p k t -> (b p) (k t)"), reads=[HLin], writes=[HL])
        P.phase_end()
        P.phase_begin()
        modS = P.sb("modS", [128, 96], F32)
        P.dma(modS[:, :].rearrange("p (r j) -> p r j", r=8), MOD[:, :, l, :].rearrange("r p j -> p r j"), reads=[MOD], writes=[modS])
        cc = lc_alloc(P)
        cc["xhi"] = P.sb("xhi", [128, 8, KC * 2], F32)
        psR = RR([P.ps(f"cps{i}") for i in range(8)])
        xdst = xfin if l == 1 else xnext
        HLr = HL[0:64, :, :, :].rearrange("(i r) p k t -> p i r (k t)", i=8, r=8)
        lc_phase(P, cc, psR, xmid, HLr[:, :, bass.ds(rank, 1), :], HL, xdst, modS, Buf(nffn[l], "nf"), Buf(cwd[l], "cw"), Buf(cbd[l], "cb"), flagd,
                 Buf(Wout[u0 + U_OFF["up"]:u0 + U_OFF["up"] + 88], "wup"),
                 Buf(Wout[u0 + U_OFF["dn"]:u0 + U_OFF["dn"] + 48].rearrange("(f g) p k n -> f p g k n", g=3), "wdn"), l == 1)
        P.phase_end()
        xcur = xnext
    return P.finish()


def build_single():
    P = Prog()
    X = lambda n, shp, dt: P.dram(n, shp, dt, kind="ExternalInput")
    xT_in = X("xT", [8, 128, KC, T], F32)
    posA = X("pos", [8, 128, T], I32)
    c_in = X("c_in", [128, KC], F32)
    wada = X("wada", [2, 8, 128, KC, 1536], F32)
    bada = X("bada", [8, 128, 2, 12], F32)
    wsh = X("wsh", [NU, 128, KC, 128], F32)
    nmix = X("nmix", [2, 128, KC], F32)
    nffn = X("nffn", [2, 128, KC], F32)
    cst = X("cst", [2, 128, 8], F32)
    cwd = X("cw", [2, 128, 3, 88], F32)
    cbd = X("cb", [2, 128, 88], F32)
    dond = X("don", [2, 128, 128], F32)
    gnd = X("gn", [2, 128, 16], F32)
    lamd = X("lamv", [2, 64, 4], F32)
    dec = X("dec", [128, 8, 128], F32)
    bones = X("bones", [128, 128], F32)
    uincl = X("uincl", [128, 128], BF16)
    ident = X("ident", [128, 128], F32)
    masksA = X("masks", [8, 128, 2, 2, 512], BF16)
    dmask = X("dmask", [128, 4, 128], F32)
    flagA = X("flag", [8, 128, 8, 2], F32)
    xfin = P.dram("xoutT", [8, 128, KC, T], F32, kind="ExternalOutput")

    WoutL = [P.dram(f"Wout{l}", [NU_LAYER, 128, KC, 128], BF16) for l in range(2)]
    MOD = P.dram("MOD", [8, 128, 2, 12], F32)
    GF = P.dram("GF", [1024, S], BF16)
    GT = P.dram("GT", [1024 + S, 2560], BF16)
    HL = P.dram("HL", [65, 128, KC, 2], F32)
    OT = P.dram("OT", [T, 2560], BF16)
    obfA = [P.dram(f"obf{r}", [LA_BF_ROWS, T], BF16) for r in range(8)]
    of32A = [P.dram(f"of32{r}", [LA_F32_ROWS, T], F32) for r in range(8)]
    xmidA = [P.dram(f"xmid{r}", [128, KC, T], F32) for r in range(8)]
    xnextA = [P.dram(f"xnext{r}", [128, KC, T], F32) for r in range(8)]

    P.phase_begin()
    Zt = P.sb("Zt", [128, 8192], BF16)
    Zf = P.sb("Zf", [128, 32], F32)
    P.op("pool", lambda e: e.memset(Zt[:], 0.0), writes=[Zt])
    P.op("pool", lambda e: e.memset(Zf[:], 0.0), writes=[Zf])
    for b0 in range(0, 8, 2):
        P.dma(GT[b0 * 128:(b0 + 2) * 128, :].rearrange("(b p) d -> p b d", p=128), Zt[:, 0:5120].rearrange("p (b d) -> p b d", d=2560),
              reads=[Zt], writes=[GT])
    P.dma(HL[0, :, :, :].rearrange("p k t -> p (k t)"), Zf[:, :], reads=[Zf], writes=[HL])
    wst = RR([P.sb(f"Wst{i}", [128, KC, 128], F32) for i in range(3)])
    wcb = RR([P.sb(f"Wcb{i}", [128, KC, 128], BF16) for i in range(3)])
    cast_engs = ["pool", "dve", "act"]
    for j in range(NU):
        st, wb = wst.next(), wcb.next()
        P.dma(st[:, :, :], wsh[j], writes=[st])
        ce = cast_engs[j % 3]
        if ce == "act":
            P.op("act", lambda e: e.copy(out=wb[:, :, :], in_=st[:, :, :]), reads=[st], writes=[wb])
        else:
            P.op(ce, lambda e: e.tensor_copy(out=wb[:, :, :], in_=st[:, :, :]), reads=[st], writes=[wb])
        P.dma(WoutL[j // NU_LAYER][j % NU_LAYER], wb[:, :, :], reads=[wb], writes=[WoutL[j // NU_LAYER]])
    ct = P.sb("ct", [128, KC], F32)
    ca = P.sb("ca", [128, KC], F32)
    wt = P.sb("wadat", [128, KC, 1536], F32)
    bt = P.sb("badat", [128, 2, 12], F32)
    moR = RR([P.sb(f"modown{i}", [128, 2, 12], F32) for i in range(2)])
    pmR = RR([P.ps(f"pmod{i}") for i in range(2)])
    P.dma(ct[:], c_in[:, :], writes=[ct])
    P.op("act", lambda e: e.activation(out=ca[:], in_=ct[:], func=AF.Silu), reads=[ct], writes=[ca])
    for r in range(8):
        pmod, mo = pmR.next(), moR.next()
        P.dma(bt[:], bada[r], writes=[bt])
        for l in range(2):
            for q in range(4):
                P.dma(wt[:, 4 * q:4 * q + 4, :], wada[l, r, :, 4 * q:4 * q + 4, :], writes=[wt])
            for fch in range(12):
                for kc in range(KC):
                    P.op("pe", lambda e, fch=fch, kc=kc, l=l: e.matmul(pmod[:, l * 12 + fch:l * 12 + fch + 1], lhsT=wt[:, kc, fch * 128:(fch + 1) * 128],
                                                                    rhs=ca[:, kc:kc + 1], start=(kc == 0), stop=(kc == KC - 1)),
                         reads=[wt, ca], writes=[pmod], inc=(kc == KC - 1))
        P.op("dve", lambda e: e.tensor_tensor(out=mo[:, :, :], in0=pmod[:, 0:24].rearrange("p (l j) -> p l j", l=2), in1=bt[:, :, :], op=ALU.add),
             reads=[pmod, bt], writes=[mo])
        P.dma(MOD[r], mo[:, :, :], reads=[mo], writes=[MOD])
    P.phase_end()

    xcurA = [Buf(xT_in[r], f"xin{r}") for r in range(8)]
    for l in range(2):
        u0 = 0
        Wout = WoutL[l]
        for rank in range(8):
            xcur, obf, of32 = xcurA[rank], obfA[rank], of32A[rank]
            P.phase_begin()
            modTd = Buf(MOD[:, :, l, :].rearrange("r p j -> p r j"), "modTd")
            d = dict(xT=xcur, modT=modTd, nmix=Buf(nmix[l], "nm"), pos=Buf(posA[rank], "pos"), cst=Buf(cst[l], "cs"), dec=dec, bones=bones,
                     wsl=Buf(Wout[u0 + U_OFF["la"]:u0 + U_OFF["la"] + 112], "wsl"), obf=obf, of32=of32)
            la_phase(P, d, True)
            identS = P.sb("identS", [128, 128], F32)
            P.dma(identS[:], ident[:, :], writes=[identS])
            tin = RR([P.sb(f"tin{i}", [128, T], BF16) for i in range(2)])
            tf = RR([P.sb(f"tf{i}", [128, T], F32) for i in range(2)])
            tout = RR([P.sb(f"tout{i}", [128, 8, 128], BF16) for i in range(2)])
            tps = RR([P.ps(f"tps{i}") for i in range(2)])
            jobs = [(LA_BF["vsb"] + k * 128, GT_V_SB + k * 128) for k in range(4)] + [(LA_BF["dv"] + k * 128, GT_V_D + k * 128) for k in range(4)] + \
                   [(LA_BF["rkz"] + k * 128, GT_KZ + k * 128) for k in range(4)] + [(LA_BF["rv"] + k * 128, GT_RV + k * 128) for k in range(8)]
            GTv = GT[1024:, :].rearrange("(i r p) d -> p i r d", i=8, r=8)
            for row, col in jobs:
                ti, tff, to = tin.next(), tf.next(), tout.next()
                P.dma(ti[:, :], obf[row:row + 128, :], reads=[obf], writes=[ti])
                P.op("dve", lambda e: e.tensor_copy(out=tff[:, :], in_=ti[:, :]), reads=[ti], writes=[tff])
                for hb in range(2):
                    tp = tps.next()
                    for b4 in range(4):
                        b = hb * 4 + b4
                        P.op("pe", lambda e, b=b, b4=b4: e.transpose(tp[:, b4 * 128:(b4 + 1) * 128], tff[:, b * 128:(b + 1) * 128], identS[:, :]),
                             reads=[tff, identS], writes=[tp])
                    P.op("act", lambda e, hb=hb: e.copy(out=to[:, hb * 4:(hb + 1) * 4, :], in_=tp[:, :].rearrange("p (b f) -> p b f", f=128)),
                         reads=[tp], writes=[to])
                P.dma(GTv[:, :, rank, col:col + 128], to[:, :, :], reads=[to], writes=[GT], q="pool")
            GFv = GF[:, :].rearrange("f (i r t) -> f i r t", i=8, r=8)
            P.dma(GFv[0:512, :, rank, :], obf[LA_BF["ksb"]:LA_BF["ksb"] + 512, :].rearrange("f (i t) -> f i t", i=8), reads=[obf], writes=[GF])
            P.dma(GFv[512:1024, :, rank, :], obf[LA_BF["dk"]:LA_BF["dk"] + 512, :].rearrange("f (i t) -> f i t", i=8), reads=[obf], writes=[GF])
            P.phase_end()
        for rank in range(8):
            xcur, obf, of32, xmid = xcurA[rank], obfA[rank], of32A[rank], xmidA[rank]
            P.phase_begin()
            modS = P.sb("modS", [128, 96], F32)
            P.dma(modS[:, :].rearrange("p (r j) -> p r j", r=8), MOD[:, :, l, :].rearrange("r p j -> p r j"), reads=[MOD], writes=[modS])
            srcs = dict(uincl=uincl, ident=ident, masks=Buf(masksA[rank], "masks"), dmask=dmask, don=Buf(dond[l], "don"), gn=Buf(gnd[l], "gn"),
                        lamv=Buf(lamd[l], "lamv"))
            c = lb_consts(P, srcs)
            c["hs"] = P.sb("hs", [128, 8, KC, 2], F32)
            lb_load_consts(P, c, srcs)
            lb_load_layer(P, c, srcs, LAM_INIT[l])
            lb_phase(P, c, obf, of32, GF, GT, xcur, xmid, modS, Buf(Wout[u0 + U_OFF["br"]:u0 + U_OFF["br"] + 16], "wbr"),
                     Buf(Wout[u0 + U_OFF["out"]:u0 + U_OFF["out"] + 16], "wout"), rank, LAM_INIT[l])
            HLv = HL[1:65, :, :, :].rearrange("(i r) p k t -> p i r (k t)", i=8, r=8)
            P.dma(HLv[:, :, rank, :], c["hs"][:, :, :, :].rearrange("p i k t -> p i (k t)"), reads=[c["hs"]], writes=[HL])
            P.phase_end()
        for rank in range(8):
            xmid = xmidA[rank]
            P.phase_begin()
            modS = P.sb("modS", [128, 96], F32)
            P.dma(modS[:, :].rearrange("p (r j) -> p r j", r=8), MOD[:, :, l, :].rearrange("r p j -> p r j"), reads=[MOD], writes=[modS])
            cc = lc_alloc(P)
            cc["xhi"] = P.sb("xhi", [128, 8, KC * 2], F32)
            psR = RR([P.ps(f"cps{i}") for i in range(8)])
            xdst = Buf(xfin[rank], "xfin") if l == 1 else xnextA[rank]
            HLr = HL[0:64, :, :, :].rearrange("(i r) p k t -> p i r (k t)", i=8, r=8)
            lc_phase(P, cc, psR, xmid, HLr[:, :, rank, :], HL, xdst, modS, Buf(nffn[l], "nf"), Buf(cwd[l], "cw"), Buf(cbd[l], "cb"), Buf(flagA[rank], "flag"),
                     Buf(Wout[u0 + U_OFF["up"]:u0 + U_OFF["up"] + 88], "wup"),
                     Buf(Wout[u0 + U_OFF["dn"]:u0 + U_OFF["dn"] + 48].rearrange("(f g) p k n -> f p g k n", g=3), "wdn"), l == 1)
            P.phase_end()
        xcurA = xnextA
    print("n_ins", P.n_ins)
    return P.finish()


def _weight_units(inputs):
    us = []
    for l in range(2):
        us.append(la_weights(inputs["w_in"][l], inputs["w_gate"][l]))
        us.append(slabs(inputs["w_branch"][l], KC))
        us.append(slabs(inputs["w_out"][l], KC))
        us.append(slabs(inputs["w_up"][l], KC))
        wd = slabs(inputs["w_down"][l], 44)
        wdp = np.zeros((16, 128, 48, 128), np.float32)
        wdp[:, :, :44, :] = wd
        us.append(np.ascontiguousarray(wdp.reshape(16, 128, 3, 16, 128).transpose(0, 2, 1, 3, 4)).reshape(48, 128, 16, 128))
    return np.concatenate(us, axis=0)


def kernel(**inputs):
    inputs = {k: np.asarray(v) for k, v in inputs.items()}
    nc = get_nc("single", build_single)
    x = inputs["x"][0]
    units = _weight_units(inputs)
    c_in = np.ascontiguousarray(inputs["c"].reshape(KC, 128).T)
    nmix = np.stack([vec_pm(inputs["norm_mix"][l]) for l in range(2)])
    nffn = np.stack([vec_pm(inputs["norm_ffn"][l]) for l in range(2)])
    lc = [la_consts(inputs["diff_qn"][l], inputs["diff_kn"][l]) for l in range(2)]
    cst = np.stack([lc[l][0] for l in range(2)])
    dec, bones = lc[0][1], lc[0][2]
    cw = np.stack([np.ascontiguousarray(inputs["conv_w"][l].reshape(3, 88, 128).transpose(2, 0, 1)) for l in range(2)])
    cb = np.stack([vec_pm(inputs["conv_b"][l]) for l in range(2)])
    don = np.stack([np.ascontiguousarray(np.broadcast_to(inputs["diff_on"][l][None, :], (128, 128))) for l in range(2)]).astype(np.float32)
    gn = np.stack([np.concatenate([vec_pm(inputs["ret_gn_g"][l]), vec_pm(inputs["ret_gn_b"][l])], 1) for l in range(2)])
    lamv = np.stack([np.stack([inputs["lam_q1"][l], inputs["lam_k1"][l], inputs["lam_q2"][l], inputs["lam_k2"][l]], 1) for l in range(2)]).astype(np.float32)
    xT, pos, masks, flags, bada = [], [], [], [], []
    for r in range(NCORE):
        tok = own_tokens(r)
        uincl, ident, mk, dmask = lb_host_consts(r)
        flag = np.ones((128, 8, 2), np.float32)
        if r == 0:
            flag[:, 0, :] = 0.0
        xT.append(to_fm(x[tok]))
        pos.append(np.ascontiguousarray(np.broadcast_to(inputs["positions"][0, tok][None, :], (128, T))).astype(np.int32))
        masks.append(mk)
        flags.append(flag)
        bada.append(np.ascontiguousarray(inputs["b_ada"][:, r * 1536:(r + 1) * 1536].reshape(2, 12, 128).transpose(2, 0, 1)))
    wa = np.ascontiguousarray(inputs["w_ada"].reshape(2, KC, 128, 8, 1536).transpose(0, 3, 2, 1, 4))
    im = dict(xT=np.stack(xT), pos=np.stack(pos), c_in=c_in, wada=wa, bada=np.stack(bada), wsh=units,
              nmix=nmix, nffn=nffn, cst=cst, cw=cw, cb=cb, don=don, gn=gn, lamv=lamv, dec=dec, bones=bones,
              uincl=uincl, ident=ident, masks=np.stack(masks), dmask=dmask, flag=np.stack(flags))
    res = run_bass_kernel_spmd(nc, [im], core_ids=[0]).results
    xo = res[0]["xoutT"]
    out = np.zeros((S, D), np.float32)
    for r in range(NCORE):
        out[own_tokens(r)] = xo[r].transpose(2, 1, 0).reshape(T, D)
    return out[None]
```

```python
from contextlib import ExitStack
import numpy as np
import ml_dtypes
import concourse.bass as bass
import concourse.mybir as mybir
from concourse.bass_utils import run_bass_kernel_spmd

F32, BF16, I32 = mybir.dt.float32, mybir.dt.bfloat16, mybir.dt.int32
AF = mybir.ActivationFunctionType
ALU = mybir.AluOpType
NPBF = ml_dtypes.bfloat16


class Buf:
    def __init__(self, t, name):
        self.t = t
        self.name = name
        self.w = {}
        self.r = {}

    def __getitem__(self, k):
        return self.t[k]


class Prog:
    def __init__(self, n_dma_sems=24):
        self.nc = bass.Bass("TRN2", target_bir_lowering=False)
        nc = self.nc
        self.es = ExitStack()
        self.eng = dict(pe=nc.tensor, act=nc.scalar, dve=nc.vector, pool=nc.gpsimd, sp=nc.sync)
        self.semh = {k: nc.alloc_semaphore("s_" + k) for k in self.eng}
        self.cnt = {k: 0 for k in self.eng}
        self.waited = {k: {} for k in self.eng}
        self.nd = n_dma_sems
        for i in range(self.nd):
            self.semh[("d", i)] = nc.alloc_semaphore(f"dsem{i}")
        self.dcnt = [0] * self.nd
        self.drr = 0
        self.out_tokens = []
        self.n_ins = 0

    def _u(self, name):
        self.uid = getattr(self, "uid", 0) + 1
        return f"{name}_{self.uid}"

    def sb(self, name, shape, dt):
        name = self._u(name)
        t = self.es.enter_context(self.nc.sbuf_tensor(name, list(shape), dt))
        return Buf(t, name)

    def ps(self, name, shape=(128, 512), dt=F32):
        name = self._u(name)
        t = self.es.enter_context(self.nc.psum_tensor(name, list(shape), dt))
        return Buf(t, name)

    def dram(self, name, shape, dt, kind="Internal", shared=False):
        if kind == "Internal":
            name = self._u(name)
        if shared:
            t = self.nc.dram_tensor(name, list(shape), dt, kind=kind, addr_space="Shared")
        else:
            t = self.nc.dram_tensor(name, list(shape), dt, kind=kind)
        return Buf(t.ap(), name)

    def _deps(self, reads, writes):
        deps = {}

        def add(k, v):
            if deps.get(k, 0) < v:
                deps[k] = v

        for b in reads:
            for k, v in b.w.items():
                add(k, v)
        for b in writes:
            for k, v in b.w.items():
                add(k, v)
            for k, v in b.r.items():
                add(k, v)
        return deps

    def _wait(self, e, deps):
        for k, v in deps.items():
            if k == "pe" and e == "pe":
                continue
            if self.waited[e].get(k, 0) < v:
                self.eng[e].wait_ge(self.semh[k], v)
                self.waited[e][k] = v

    def _record(self, tok, reads, writes):
        k, v = tok
        for b in reads:
            if b.r.get(k, 0) < v:
                b.r[k] = v
        for b in writes:
            if b.w.get(k, 0) < v:
                b.w[k] = v
            b.r = {}

    def op(self, e, fn, reads=(), writes=(), inc=True):
        self._wait(e, self._deps(reads, writes))
        ins = fn(self.eng[e])
        self.n_ins += 1
        if inc:
            self.cnt[e] += 1
            ins.then_inc(self.semh[e], 1)
            tok = (e, self.cnt[e])
        else:
            tok = (e, self.cnt[e] + 1)
        self._record(tok, reads, writes)
        return ins

    def dma(self, out_ap, in_ap, reads=(), writes=(), q="sp", is_output=False, **kw):
        k = self.drr
        self.drr = (self.drr + 1) % self.nd
        deps = self._deps(reads, writes)
        key = ("d", k)
        if self.dcnt[k] > 0 and deps.get(key, 0) < self.dcnt[k] * 16:
            deps[key] = self.dcnt[k] * 16
        self._wait(q, deps)
        ins = self.eng[q].dma_start(out=out_ap, in_=in_ap, **kw)
        self.n_ins += 1
        self.dcnt[k] += 1
        ins.then_inc(self.semh[key], 16)
        tok = (key, self.dcnt[k] * 16)
        self._record(tok, reads, writes)
        if is_output:
            self.out_tokens.append(tok)
        return ins

    def barrier(self):
        full = {k: v for k, v in self.cnt.items() if v > 0}
        for i in range(self.nd):
            if self.dcnt[i] > 0:
                full[("d", i)] = self.dcnt[i] * 16
        for e in self.eng:
            self._wait(e, dict(full))

    def phase_begin(self):
        self.barrier()
        self.es_outer = self.es
        self.es = ExitStack()

    def phase_end(self):
        self.barrier()
        self.es.close()
        self.es = self.es_outer

    def all_reduce(self, in_ap, out_ap, reads=(), writes=()):
        k = self.drr
        self.drr = (self.drr + 1) % self.nd
        deps = self._deps(reads, writes)
        key = ("d", k)
        if self.dcnt[k] > 0 and deps.get(key, 0) < self.dcnt[k] * 16:
            deps[key] = self.dcnt[k] * 16
        self._wait("pool", deps)
        ins = self.eng["pool"].collective_compute("AllReduce", ALU.add, replica_groups=[list(range(8))], ins=[in_ap], outs=[out_ap])
        self.dcnt[k] += 1
        ins.then_inc(self.semh[key], 16)
        tok = (key, self.dcnt[k] * 16)
        self._record(tok, reads, writes)

    def finish(self):
        final = {}
        for k, v in self.out_tokens:
            if final.get(k, 0) < v:
                final[k] = v
        self._wait("sp", final)
        self.es.close()
        return self.nc


class RR:
    def __init__(self, bufs):
        self.bufs = bufs
        self.i = 0

    def next(self):
        b = self.bufs[self.i]
        self.i = (self.i + 1) % len(self.bufs)
        return b


D = 2048
S = 8192
NCORE = 8
T = 1024
NB = 8
KC = D // 128
D_FF = 5632
EPS = 1e-6
THETA = 10000.0
LOG_GAMMA = np.log(1.0 - 2.0 ** (-5.0 - np.arange(4, dtype=np.float64)))
PI = float(np.pi)
TWO_PI = float(2 * np.pi)


def _run(nc, in_maps):
    res = run_bass_kernel_spmd(nc, in_maps, core_ids=list(range(NCORE)))
    return res.results


def build_L0():
    P = Prog()
    nc = P.nc
    c_in = P.dram("c_in", [128, KC], F32, kind="ExternalInput")
    wada = P.dram("wada", [2, 128, KC, 1536], F32, kind="ExternalInput")
    bada = P.dram("bada", [2, 1536], F32, kind="ExternalInput")
    modo = P.dram("modo", [2, 1536], F32, kind="ExternalOutput")
    ct = P.sb("ct", [128, KC], F32)
    ca = P.sb("ca", [128, KC], F32)
    wt = P.sb("wt", [128, KC, 1536], F32)
    bt = P.sb("bt", [1, 2, 1536], F32)
    ot = P.sb("ot", [1, 2, 1536], F32)
    pss = [P.ps(f"ps{i}") for i in range(3)]
    P.dma(ct[:], c_in[:, :], writes=[ct])
    P.dma(bt[0:1, :, :], bada[:, :].unsqueeze(0), writes=[bt])
    P.op("act", lambda e: e.activation(out=ca[:], in_=ct[:], func=AF.Silu), reads=[ct], writes=[ca])
    for l in range(2):
        for q in range(4):
            P.dma(wt[:, 4 * q:4 * q + 4, :], wada[l, :, 4 * q:4 * q + 4, :], writes=[wt])
        for n in range(3):
            for kc in range(KC):
                P.op("pe", lambda e, n=n, kc=kc: e.matmul(pss[n][0:1, :], lhsT=ca[:, kc:kc + 1],
                                                       rhs=wt[:, kc, n * 512:(n + 1) * 512],
                                                       start=(kc == 0), stop=(kc == KC - 1)),
                     reads=[ca, wt], writes=[pss[n]], inc=(kc == KC - 1))
            P.op("dve", lambda e, n=n, l=l: e.tensor_tensor(out=ot[0:1, l, n * 512:(n + 1) * 512], in0=pss[n][0:1, :],
                                                         in1=bt[0:1, l, n * 512:(n + 1) * 512], op=ALU.add),
                 reads=[pss[n], bt], writes=[ot])
    P.dma(modo[:, :].unsqueeze(0), ot[0:1, :, :], reads=[ot], writes=[modo], is_output=True)
    return P.finish()


def run_L0(inputs):
    nc = build_L0()
    c = np.ascontiguousarray(inputs["c"].reshape(KC, 128).T)
    in_maps = []
    for r in range(NCORE):
        w = inputs["w_ada"][:, :, r * 1536:(r + 1) * 1536]
        w = np.ascontiguousarray(w.reshape(2, KC, 128, 1536).transpose(0, 2, 1, 3))
        b = np.ascontiguousarray(inputs["b_ada"][:, r * 1536:(r + 1) * 1536])
        in_maps.append({"c_in": c, "wada": w, "bada": b})
    res = _run(nc, in_maps)
    mod = np.concatenate([res[r]["modo"] for r in range(NCORE)], axis=1)
    return mod


def load_slab(P, wst, wbf, wsrc_ap, nkc, cast_eng="pool"):
    st = wst.next()
    P.dma(st[:, 0:nkc, :], wsrc_ap, writes=[st])
    wb = wbf.next()
    if cast_eng == "act":
        P.op("act", lambda e: e.copy(out=wb[:, 0:nkc, :], in_=st[:, 0:nkc, :]), reads=[st], writes=[wb])
    else:
        P.op(cast_eng, lambda e: e.tensor_copy(out=wb[:, 0:nkc, :], in_=st[:, 0:nkc, :]), reads=[st], writes=[wb])
    return wb


def adaln_norm(P, xT, hT, ntok, gs, shift_ap_fn, ones32, psA, psB, rstd, tmpRR):
    nt = (ntok + 511) // 512
    for h in range(nt):
        t0, t1 = h * 512, min(ntok, (h + 1) * 512)
        w = t1 - t0
        pb = psA if h % 2 == 0 else psB
        for kc in range(KC):
            sq = tmpRR.next()
            P.op("act", lambda e, kc=kc, sq=sq: e.activation(out=sq[:, 0:w], in_=xT[:, kc, t0:t1], func=AF.Square),
                 reads=[xT], writes=[sq])
            P.op("pe", lambda e, kc=kc, sq=sq: e.matmul(pb[:, 0:w], lhsT=ones32[:, :], rhs=sq[:, 0:w],
                                                       start=(kc == 0), stop=(kc == KC - 1)),
                 reads=[ones32, sq], writes=[pb], inc=True)
        P.op("dve", lambda e: e.tensor_scalar(out=rstd[:, t0:t1], in0=pb[:, 0:w], scalar1=1.0 / D, scalar2=EPS,
                                              op0=ALU.mult, op1=ALU.add), reads=[pb], writes=[rstd])
        P.op("act", lambda e: e.activation(out=rstd[:, t0:t1], in_=rstd[:, t0:t1], func=AF.Sqrt), reads=[rstd], writes=[rstd])
        P.op("dve", lambda e: e.reciprocal(out=rstd[:, t0:t1], in_=rstd[:, t0:t1]), reads=[rstd], writes=[rstd])
        for kc in range(KC):
            tm = tmpRR.next()
            P.op("dve", lambda e, kc=kc, tm=tm: e.tensor_tensor(out=tm[:, 0:w], in0=xT[:, kc, t0:t1], in1=rstd[:, t0:t1],
                                                             op=ALU.mult), reads=[xT, rstd], writes=[tm])
            P.op("act", lambda e, kc=kc, tm=tm: e.activation(out=hT[:, kc, t0:t1], in_=tm[:, 0:w], func=AF.Identity,
                                                          scale=gs[:, kc:kc + 1], bias=shift_ap_fn(kc)),
                 reads=[tm, gs], writes=[hT])


LA_BF = dict(qsb=0, ksb=512, vsb=1024, rq=1536, rqx=2048, rk=2560, rkz=3072, rv=3584, dq=4608, dk=5120, dv=5632)
LA_BF_ROWS = 6144
LA_F32 = dict(rg=0, gates=1024)
LA_F32_ROWS = 1024 + 6144
N_SLAB_A = 48 + 16 + 48


def la_phase(P, d, w_bf16):
    xTd, modT, nmix, posd, cst, dec, bones, wsl, obf, of32 = (d[k] for k in
        ("xT", "modT", "nmix", "pos", "cst", "dec", "bones", "wsl", "obf", "of32"))
    out_flag = d.get("is_output", False)

    xT = P.sb("xTs", [128, KC, T], F32)
    hT = P.sb("hT", [128, KC, T], BF16)
    modS = P.sb("modS", [128, 96], F32)
    nmS = P.sb("nmS", [128, KC], F32)
    gs = P.sb("gs", [128, KC], F32)
    cS = P.sb("cS", [128, 8], F32)
    decS = P.sb("decS", [128, 8, 128], F32)
    posI = P.sb("posI", [128, T], I32)
    posF = P.sb("posF", [128, T], F32)
    cosR = P.sb("cosR", [128, T], F32)
    sinR = P.sb("sinR", [128, T], F32)
    cosD = P.sb("cosD", [128, T], F32)
    sinD = P.sb("sinD", [128, T], F32)
    cqT = P.sb("cqT", [128, T], F32)
    sqT = P.sb("sqT", [128, T], F32)
    ones32 = P.sb("ones32", [128, 128], F32)
    bonesB = P.sb("bonesB", [128, 128], BF16)
    bonesF = P.sb("bonesF", [128, 128], F32)
    rstd = P.sb("rstd", [128, T], F32)
    ang = rstd
    ckT, skT = cosD, sinD
    pib = P.sb("pib", [128, 1], F32)
    tmpRR = RR([P.sb(f"tmp{i}", [128, 512], F32) for i in range(4)])
    wst = None if w_bf16 else RR([P.sb(f"wst{i}", [128, KC, 128], F32) for i in range(2)])
    wbf = RR([P.sb(f"wbf{i}", [128, KC, 128], BF16) for i in range(4)])
    obR = RR([P.sb(f"ob{i}", [128, T], BF16) for i in range(4)])
    ofR = RR([P.sb(f"of{i}", [128, T], F32) for i in range(3)])
    sqR = RR([P.sb(f"sqb{i}", [128, 512], BF16) for i in range(2)])
    psR = RR([P.ps(f"ps{i}") for i in range(6 if w_bf16 else 8)])

    for q in range(4):
        P.dma(xT[:, 4 * q:4 * q + 4, :], xTd[:, 4 * q:4 * q + 4, :], writes=[xT])
    if w_bf16:
        P.dma(modS[:, :].rearrange("p (r j) -> p r j", r=8), modT[:, :, :], writes=[modS])
    else:
        P.dma(modS[:], modT[:, :], writes=[modS])
    P.dma(nmS[:], nmix[:, :], writes=[nmS])
    P.dma(cS[:], cst[:, :], writes=[cS])
    P.dma(decS[:], dec[:, :, :], writes=[decS])
    P.dma(posI[:], posd[:, :], writes=[posI])
    P.dma(bonesF[:], bones[:, :], writes=[bonesF])
    P.op("pool", lambda e: e.memset(ones32[:], 1.0), writes=[ones32])
    P.op("pool", lambda e: e.memset(pib[:], PI), writes=[pib])
    P.op("pool", lambda e: e.tensor_copy(out=bonesB[:], in_=bonesF[:]), reads=[bonesF], writes=[bonesB])
    P.op("dve", lambda e: e.scalar_tensor_tensor(out=gs[:], in0=modS[:, 16:32], scalar=1.0, in1=nmS[:],
                                                 op0=ALU.add, op1=ALU.mult), reads=[modS, nmS], writes=[gs])
    P.op("dve", lambda e: e.tensor_copy(out=posF[:], in_=posI[:]), reads=[posI], writes=[posF])

    def sincos(inv_col, sign_col, cosT, sinT):
        def one(outT, shift):
            P.op("dve", lambda e: e.tensor_scalar(out=ang[:], in0=posF[:], scalar1=cS[:, inv_col:inv_col + 1], scalar2=shift,
                                                  op0=ALU.mult, op1=ALU.add), reads=[posF, cS], writes=[ang])
            P.op("dve", lambda e: e.tensor_copy(out=posI[:], in_=ang[:]), reads=[ang], writes=[posI])
            P.op("dve", lambda e: e.tensor_copy(out=outT[:], in_=posI[:]), reads=[posI], writes=[outT])
            P.op("dve", lambda e: e.tensor_tensor(out=ang[:], in0=ang[:], in1=outT[:], op=ALU.subtract), reads=[ang, outT], writes=[ang])
            P.op("dve", lambda e: e.tensor_single_scalar(out=outT[:], in_=ang[:], scalar=0.5, op=ALU.is_gt), reads=[ang], writes=[outT])
            P.op("dve", lambda e: e.tensor_tensor(out=ang[:], in0=ang[:], in1=outT[:], op=ALU.subtract), reads=[ang, outT], writes=[ang])
            P.op("dve", lambda e: e.tensor_single_scalar(out=outT[:], in_=ang[:], scalar=-0.5, op=ALU.is_lt), reads=[ang], writes=[outT])
            P.op("dve", lambda e: e.tensor_tensor(out=ang[:], in0=ang[:], in1=outT[:], op=ALU.add), reads=[ang, outT], writes=[ang])
            P.op("act", lambda e: e.activation(out=outT[:], in_=ang[:], func=AF.Sin, scale=TWO_PI), reads=[ang], writes=[outT])
        one(sinT, 0.0)
        P.op("dve", lambda e: e.tensor_scalar(out=sinT[:], in0=sinT[:], scalar1=cS[:, sign_col:sign_col + 1], scalar2=None,
                                              op0=ALU.mult), reads=[sinT, cS], writes=[sinT])
        one(cosT, 0.25)

    sincos(0, 1, cosR, sinR)
    sincos(2, 3, cosD, sinD)
    P.op("dve", lambda e: e.tensor_scalar(out=cqT[:], in0=cosD[:], scalar1=cS[:, 4:5], scalar2=0.125, op0=ALU.mult, op1=ALU.mult),
         reads=[cosD, cS], writes=[cqT])
    P.op("dve", lambda e: e.tensor_scalar(out=sqT[:], in0=sinD[:], scalar1=cS[:, 5:6], scalar2=0.125, op0=ALU.mult, op1=ALU.mult),
         reads=[sinD, cS], writes=[sqT])
    P.op("dve", lambda e: e.tensor_scalar(out=ckT[:], in0=cosD[:], scalar1=cS[:, 6:7], scalar2=None, op0=ALU.mult),
         reads=[cosD, cS], writes=[ckT])
    P.op("dve", lambda e: e.tensor_scalar(out=skT[:], in0=sinD[:], scalar1=cS[:, 7:8], scalar2=None, op0=ALU.mult),
         reads=[sinD, cS], writes=[skT])

    psA, psB = psR.next(), psR.next()
    adaln_norm(P, xT, hT, T, gs, lambda kc: modS[:, kc:kc + 1], ones32, psA, psB, rstd, tmpRR)

    def proj(slab_idx):
        if w_bf16:
            wb = wbf.next()
            P.dma(wb[:, :, :], wsl[slab_idx], reads=[wsl], writes=[wb])
        else:
            wb = load_slab(P, wst, wbf, wsl[slab_idx], KC)
        outs = []
        for h in range(2):
            pb = psR.next()
            for kc in range(KC):
                P.op("pe", lambda e, kc=kc, pb=pb, h=h: e.matmul(pb[:, :], lhsT=wb[:, kc, :], rhs=hT[:, kc, h * 512:(h + 1) * 512],
                                                              start=(kc == 0), stop=(kc == KC - 1)),
                     reads=[wb, hT], writes=[pb], inc=(kc == KC - 1))
            outs.append(pb)
        return outs

    def store_bf(ob, row):
        P.dma(obf[row:row + 128, :], ob[:, :], reads=[ob], writes=[obf], is_output=out_flag, q="pool")

    def store_f32(of, row):
        P.dma(of32[row:row + 128, :], of[:, :], reads=[of], writes=[of32], is_output=out_flag, q="pool")

    def simple(slab_idx, row, scale=None):
        pbs = proj(slab_idx)
        ob = obR.next()
        for h in range(2):
            if scale is None:
                P.op("act", lambda e, h=h: e.copy(out=ob[:, h * 512:(h + 1) * 512], in_=pbs[h][:, :]), reads=[pbs[h]], writes=[ob])
            else:
                P.op("act", lambda e, h=h: e.mul(out=ob[:, h * 512:(h + 1) * 512], in_=pbs[h][:, :], mul=scale), reads=[pbs[h]], writes=[ob])
        store_bf(ob, row)

    def actf32(slab_idx, row, func):
        pbs = proj(slab_idx)
        of = ofR.next()
        for h in range(2):
            P.op("act", lambda e, h=h: e.activation(out=of[:, h * 512:(h + 1) * 512], in_=pbs[h][:, :], func=func),
                 reads=[pbs[h]], writes=[of])
        store_f32(of, row)

    def rope_pair(slab_idx, rot_idx, cT, sT, h):
        raise NotImplementedError

    def rope_ret(slab_idx, rot_idx, head, is_q):
        pa = proj(slab_idx)
        pbr = proj(rot_idx)
        o1, o2 = obR.next(), obR.next()
        for h in range(2):
            sl = slice(h * 512, (h + 1) * 512)
            t1, t2 = tmpRR.next(), tmpRR.next()
            P.op("dve", lambda e: e.tensor_tensor(out=t1[:, :], in0=pa[h][:, :], in1=cosR[:, sl], op=ALU.mult),
                 reads=[pa[h], cosR], writes=[t1])
            P.op("dve", lambda e: e.tensor_tensor(out=t2[:, :], in0=pbr[h][:, :], in1=sinR[:, sl], op=ALU.mult),
                 reads=[pbr[h], sinR], writes=[t2])
            P.op("pool", lambda e: e.tensor_tensor(out=t1[:, :], in0=t1[:, :], in1=t2[:, :], op=ALU.add),
                 reads=[t1, t2], writes=[t1])
            tab = decS[:, (head if is_q else 4 + head), :].unsqueeze(1).broadcast_to([128, 4, 128])
            t1v = t1[:, :].rearrange("p (b t) -> p b t", t=128)
            if is_q:
                P.op("act", lambda e: e.copy(out=o1[:, sl], in_=t1[:, :]), reads=[t1], writes=[o1])
                P.op("pool", lambda e: e.tensor_tensor(out=o2[:, sl].rearrange("p (b t) -> p b t", t=128), in0=t1v, in1=tab, op=ALU.mult),
                     reads=[t1, decS], writes=[o2])
            else:
                sc = 128.0 ** -0.5
                P.op("act", lambda e: e.mul(out=o1[:, sl], in_=t1[:, :], mul=sc), reads=[t1], writes=[o1])
                P.op("dve", lambda e: e.scalar_tensor_tensor(out=o2[:, sl].rearrange("p (b t) -> p b t", t=128), in0=t1v, scalar=sc,
                                                              in1=tab, op0=ALU.mult, op1=ALU.mult),
                     reads=[t1, decS], writes=[o2])
        if is_q:
            store_bf(o1, LA_BF["rq"] + head * 128)
            store_bf(o2, LA_BF["rqx"] + head * 128)
        else:
            store_bf(o1, LA_BF["rk"] + head * 128)
            store_bf(o2, LA_BF["rkz"] + head * 128)

    def rope_diff(slab_idx, rot_idx, head, is_q):
        pa = proj(slab_idx)
        pbr = proj(rot_idx)
        cT, sT = (cqT, sqT) if is_q else (ckT, skT)
        o1 = obR.next()
        for h in range(2):
            sl = slice(h * 512, (h + 1) * 512)
            sq = sqR.next()
            P.op("act", lambda e: e.activation(out=sq[:, :], in_=pa[h][:, :], func=AF.Square), reads=[pa[h]], writes=[sq])
            pc = psR.next()
            P.op("pe", lambda e: e.matmul(pc[:, :], lhsT=bonesB[:, :], rhs=sq[:, :], start=True, stop=True),
                 reads=[bonesB, sq], writes=[pc])
            t1, t2, t3 = tmpRR.next(), tmpRR.next(), tmpRR.next()
            P.op("dve", lambda e: e.tensor_scalar(out=t3[:, :], in0=pc[:, :], scalar1=1.0 / 64, scalar2=EPS, op0=ALU.mult, op1=ALU.add),
                 reads=[pc], writes=[t3])
            P.op("act", lambda e: e.activation(out=t3[:, :], in_=t3[:, :], func=AF.Sqrt), reads=[t3], writes=[t3])
            P.op("dve", lambda e: e.reciprocal(out=t3[:, :], in_=t3[:, :]), reads=[t3], writes=[t3])
            P.op("dve", lambda e: e.tensor_tensor(out=t1[:, :], in0=pa[h][:, :], in1=cT[:, sl], op=ALU.mult),
                 reads=[pa[h], cT], writes=[t1])
            P.op("dve", lambda e: e.tensor_tensor(out=t2[:, :], in0=pbr[h][:, :], in1=sT[:, sl], op=ALU.mult),
                 reads=[pbr[h], sT], writes=[t2])
            P.op("pool", lambda e: e.tensor_tensor(out=t1[:, :], in0=t1[:, :], in1=t2[:, :], op=ALU.add), reads=[t1, t2], writes=[t1])
            P.op("pool", lambda e: e.tensor_tensor(out=o1[:, sl], in0=t1[:, :], in1=t3[:, :], op=ALU.mult), reads=[t1, t3], writes=[o1])
        store_bf(o1, (LA_BF["dq"] if is_q else LA_BF["dk"]) + head * 128)

    for j in range(4):
        simple(j, LA_BF["qsb"] + j * 128, scale=128.0 ** -0.5)
    for j in range(4):
        simple(4 + j, LA_BF["ksb"] + j * 128)
    for j in range(4):
        simple(8 + j, LA_BF["vsb"] + j * 128)
    for j in range(4):
        rope_ret(12 + j, 48 + j, j, True)
    for j in range(4):
        rope_ret(16 + j, 52 + j, j, False)
    for j in range(8):
        simple(20 + j, LA_BF["rv"] + j * 128)
    for j in range(8):
        actf32(28 + j, LA_F32["rg"] + j * 128, AF.Silu)
    for j in range(4):
        rope_diff(36 + j, 56 + j, j, True)
    for j in range(4):
        rope_diff(40 + j, 60 + j, j, False)
    for j in range(4):
        simple(44 + j, LA_BF["dv"] + j * 128)
    for j in range(48):
        actf32(64 + j, LA_F32["gates"] + j * 128, AF.Sigmoid)


def build_LA():
    P = Prog()
    d = dict(xT=P.dram("xT", [128, KC, T], F32, kind="ExternalInput"), modT=P.dram("modT", [128, 96], F32, kind="ExternalInput"),
             nmix=P.dram("nmix", [128, KC], F32, kind="ExternalInput"), pos=P.dram("pos", [128, T], I32, kind="ExternalInput"),
             cst=P.dram("cst", [128, 8], F32, kind="ExternalInput"), dec=P.dram("dec", [128, 8, 128], F32, kind="ExternalInput"),
             bones=P.dram("bones", [128, 128], F32, kind="ExternalInput"), wsl=P.dram("wsl", [N_SLAB_A, 128, KC, 128], F32, kind="ExternalInput"),
             obf=P.dram("obf", [LA_BF_ROWS, T], BF16, kind="ExternalOutput"), of32=P.dram("of32", [LA_F32_ROWS, T], F32, kind="ExternalOutput"),
             is_output=True)
    la_phase(P, d, False)
    return P.finish()


def own_tokens(core):
    return np.concatenate([np.arange((8 * i + core) * 128, (8 * i + core + 1) * 128) for i in range(NB)])


def to_fm(x_tok):
    Tn, Fn = x_tok.shape
    return np.ascontiguousarray(x_tok.T.reshape(Fn // 128, 128, Tn).transpose(1, 0, 2))


def vec_pm(v):
    return np.ascontiguousarray(v.reshape(-1, 128).T)


def slabs(W, nkc):
    K_, N_ = W.shape
    return np.ascontiguousarray(W.reshape(nkc, 128, N_ // 128, 128).transpose(2, 1, 0, 3))


def la_consts(diff_qn, diff_kn):
    p = np.arange(128)
    inv_ret = THETA ** (-(2.0 * (p % 64)) / 128.0)
    sign_ret = np.where(p < 64, -1.0, 1.0)
    q = p % 64
    inv_diff = THETA ** (-(2.0 * (q % 32)) / 64.0)
    sign_diff = np.where(q < 32, -1.0, 1.0)
    partner = np.where(q < 32, q + 32, q - 32)
    cst = np.stack([inv_ret / (2 * np.pi), sign_ret, inv_diff / (2 * np.pi), sign_diff, diff_qn[q], diff_qn[partner], diff_kn[q], diff_kn[partner]], axis=1)
    tl = np.arange(128)
    dec = np.zeros((8, 128), np.float64)
    for h in range(4):
        dec[h] = np.exp((tl + 1.0) * LOG_GAMMA[h])
        dec[4 + h] = np.exp((127.0 - tl) * LOG_GAMMA[h])
    dec = np.broadcast_to(dec[None], (128, 8, 128))
    bones = np.zeros((128, 128), np.float32)
    bones[:64, :64] = 1.0
    bones[64:, 64:] = 1.0
    return cst.astype(np.float32), np.ascontiguousarray(dec).astype(np.float32), bones


def la_weights(w_in, w_gate):
    cols = np.arange(6144)
    perm = cols.copy()
    for base in (1536, 2048):
        for hd in range(4):
            o = base + hd * 128
            perm[o:o + 64] = np.arange(o + 64, o + 128)
            perm[o + 64:o + 128] = np.arange(o, o + 64)
    for base in (4608, 5120):
        for m in range(8):
            o = base + m * 64
            perm[o:o + 32] = np.arange(o + 32, o + 64)
            perm[o + 32:o + 64] = np.arange(o, o + 32)
    rot_cols = np.concatenate([np.arange(1536, 2048), np.arange(2048, 2560), np.arange(4608, 5120), np.arange(5120, 5632)])
    Wcat = np.concatenate([w_in, w_in[:, perm[rot_cols]], w_gate], axis=1)
    return slabs(Wcat, KC)


_CACHE = {}


def get_nc(name, fn):
    if name not in _CACHE:
        _CACHE[name] = fn()
    return _CACHE[name]


def run_LA(inputs, l, x_full, mod):
    nc = get_nc("LA", build_LA)
    cst, dec, bones = la_consts(inputs["diff_qn"][l], inputs["diff_kn"][l])
    wsl = la_weights(inputs["w_in"][l], inputs["w_gate"][l])
    modT = vec_pm(mod[l])
    nmix = vec_pm(inputs["norm_mix"][l])
    in_maps = []
    for r in range(NCORE):
        tok = own_tokens(r)
        pos = np.ascontiguousarray(np.broadcast_to(inputs["positions"][0, tok][None, :], (128, T))).astype(np.int32)
        in_maps.append(dict(xT=to_fm(x_full[tok]), modT=modT, nmix=nmix, pos=pos, cst=cst, dec=dec, bones=bones, wsl=wsl))
    res = _run(nc, in_maps)
    return [(res[r]["obf"], res[r]["of32"]) for r in range(NCORE)]


GT_V_SB, GT_V_D, GT_KZ, GT_RV = 0, 512, 1024, 1536


def lb_phase(P, c, obf, of32, GF, GT, xTd, xoutd, modS, wbr, wout, rank, lam_init, dbg=None):
    psR = c["psR"]
    ident = c["identF"]
    ysbT = c["ybr"]
    qsb = c["qown"]
    P.dma(qsb[:, :, :], obf[LA_BF["qsb"]:LA_BF["qsb"] + 512, :].rearrange("(h p) t -> p h t", p=128), writes=[qsb])
    R = c["R"]
    nq = c["nq"]
    P.op("pool", lambda e: e.tensor_scalar(out=nq[:, :, :], in0=qsb[:, :, :], scalar1=-1.0, scalar2=None, op0=ALU.mult),
         reads=[qsb], writes=[nq])
    psA = c["psA"]
    for hd in range(4):
        kT = c["kTR"].next()
        vS = c["vSR"].next()
        P.dma(kT[:, :], GF[hd * 128:(hd + 1) * 128, :], reads=[GF], writes=[kT])
        P.dma(vS[:, :, 0:128], GT[1024:, GT_V_SB + hd * 128:GT_V_SB + (hd + 1) * 128].rearrange("(b p) d -> p b d", p=128),
              reads=[GT], writes=[vS])
        items = [(i, g) for i in range(NB) for g in range(2 * i + 1, -1, -1)]
        st = {}

        def stA(k, hd=hd, kT=kT):
            i, g = items[k]
            diag = g >= 2 * i
            Z = psR.next()
            for j in range(4):
                blk = 4 * g + j
                P.op("pe", lambda e, j=j, blk=blk: e.matmul(Z[:, j * 128:(j + 1) * 128], lhsT=kT[:, blk * 128:(blk + 1) * 128],
                                                         rhs=qsb[:, hd, i * 128:(i + 1) * 128], start=True, stop=True),
                     reads=[kT, qsb], writes=[Z], inc=(j == 3))
            ef = c["efR"].next()
            P.op("act", lambda e: e.activation(out=ef[:, :], in_=Z[:, :], func=AF.Exp), reads=[Z], writes=[ef])
            sp = c["spR"].next()
            P.op("act", lambda e: e.activation(out=sp[:, :], in_=ef[:, :], func=AF.Ln, bias=c["one"][:, 0:1]),
                 reads=[ef, c["one"]], writes=[sp])
            if diag:
                P.op("dve", lambda e: e.tensor_tensor(out=sp[:, :], in0=sp[:, :], in1=c["msb"][:, g - 2 * i, :], op=ALU.mult),
                     reads=[sp, c["msb"]], writes=[sp])
            st[k] = dict(sp=sp)

        def stB(k, hd=hd, kT=kT):
            i, g = items[k]
            diag = g >= 2 * i
            sp = st[k]["sp"]
            if g == 2 * i + 1:
                P.op("pool", lambda e: e.memset(R[:, :], 0.0), writes=[R])
            Pb = psR.next()
            Tb = psR.next()
            for j in range(4):
                blk = 4 * g + j
                P.op("pe", lambda e, j=j: e.matmul(Pb[:, j * 128:(j + 1) * 128], lhsT=c["uincl"][:, :], rhs=sp[:, j * 128:(j + 1) * 128],
                                                 start=True, stop=False), reads=[c["uincl"], sp], writes=[Pb], inc=False)
                for j2 in range(j + 1, 4):
                    P.op("pe", lambda e, j=j, j2=j2: e.matmul(Pb[:, j * 128:(j + 1) * 128], lhsT=c["onesB"][:, :], rhs=sp[:, j2 * 128:(j2 + 1) * 128],
                                                           start=False, stop=False), reads=[c["onesB"], sp], writes=[Pb], inc=False)
                P.op("pe", lambda e, j=j, blk=blk: e.matmul(Pb[:, j * 128:(j + 1) * 128], lhsT=kT[:, blk * 128:(blk + 1) * 128],
                                                         rhs=nq[:, hd, i * 128:(i + 1) * 128], start=False, stop=True),
                     reads=[kT, nq], writes=[Pb], inc=False)
            for j in range(4):
                P.op("pe", lambda e, j=j: e.matmul(Tb[:, 0:128], lhsT=c["onesB"][:, :], rhs=sp[:, j * 128:(j + 1) * 128],
                                                 start=(j == 0), stop=(j == 3)), reads=[c["onesB"], sp], writes=[Tb, Pb], inc=(j == 3))
            tf = c["efR"].next()
            P.op("dve", lambda e: e.tensor_tensor(out=tf[:, :].rearrange("p (b t) -> p b t", t=128),
                                                  in0=Pb[:, :].rearrange("p (b t) -> p b t", t=128),
                                                  in1=R[:, :].unsqueeze(1).broadcast_to([128, 4, 128]), op=ALU.add),
                 reads=[Pb, R], writes=[tf])
            wt = c["wR"].next()
            P.op("act", lambda e: e.activation(out=wt[:, :], in_=tf[:, :], func=AF.Exp, scale=-1.0), reads=[tf], writes=[wt])
            if diag:
                P.op("dve", lambda e: e.tensor_tensor(out=wt[:, :], in0=wt[:, :], in1=c["msb"][:, g - 2 * i, :], op=ALU.mult),
                     reads=[wt, c["msb"]], writes=[wt])
            P.op("dve", lambda e: e.tensor_tensor(out=R[:, :], in0=R[:, :], in1=Tb[:, 0:128], op=ALU.add), reads=[R, Tb], writes=[R])
            st[k]["wt"] = wt

        def stC(k, hd=hd, vS=vS):
            i, g = items[k]
            wt = st[k]["wt"]
            Y = psA[i % 2]
            for j in range(4):
                blk = 4 * g + j
                P.op("pe", lambda e, j=j, blk=blk: e.matmul(Y[:, 0:128], lhsT=vS[:, blk, 0:128], rhs=wt[:, j * 128:(j + 1) * 128],
                                                         start=(g == 2 * i + 1 and j == 0), stop=(g == 0 and j == 3)),
                     reads=[vS, wt], writes=[Y], inc=(j == 3))
            if g == 0:
                P.op("act", lambda e: e.copy(out=ysbT[:, hd, i * 128:(i + 1) * 128], in_=Y[:, 0:128]), reads=[Y], writes=[ysbT])
            del st[k]

        n = len(items)
        for k in range(n + 2):
            if k < n:
                stA(k)
            if 0 <= k - 1 < n:
                stB(k - 1)
            if 0 <= k - 2 < n:
                stC(k - 2)

    qd = c["qown"]
    P.dma(qd[:, :, :], obf[LA_BF["dq"]:LA_BF["dq"] + 512, :].rearrange("(h p) t -> p h t", p=128), writes=[qd])
    for hd in range(4):
        kT = c["kTR"].next()
        vS = c["vSR"].next()
        P.dma(kT[:, :], GF[512 + hd * 128:512 + (hd + 1) * 128, :], reads=[GF], writes=[kT])
        P.dma(vS[:, :, 0:128], GT[1024:, GT_V_D + hd * 128:GT_V_D + (hd + 1) * 128].rearrange("(b p) d -> p b d", p=128),
              reads=[GT], writes=[vS])
        P.op("pool", lambda e: e.memset(vS[:, :, 128:129], 1.0), writes=[vS])
        items = [(i, g, m) for i in range(NB) for g in range(2 * i + 2) for m in range(2)]
        st = {}

        def dA(k, hd=hd, kT=kT):
            i, g, m = items[k]
            diag = g >= 2 * i
            Sb = psR.next()
            for j in range(4):
                blk = 4 * g + j
                P.op("pe", lambda e, j=j, blk=blk: e.matmul(
                    Sb[:, j * 128:(j + 1) * 128], lhsT=kT[m * 64:(m + 1) * 64, blk * 128:(blk + 1) * 128],
                    rhs=qd[m * 64:(m + 1) * 64, hd, i * 128:(i + 1) * 128], start=True, stop=True),
                    reads=[kT, qd], writes=[Sb], inc=(j == 3))
            eb = c["wR"].next()
            P.op("act", lambda e: e.activation(out=eb[:, :], in_=Sb[:, :], func=AF.Exp), reads=[Sb], writes=[eb])
            if diag:
                P.op("dve", lambda e: e.tensor_tensor(out=eb[:, :], in0=eb[:, :], in1=c["mdf"][:, g - 2 * i, :], op=ALU.mult),
                     reads=[eb, c["mdf"]], writes=[eb])
            st[k] = eb

        def dC(k, hd=hd, vS=vS):
            i, g, m = items[k]
            ng = 2 * i + 2
            eb = st.pop(k)
            Ym = [psA[2 * (i % 2)], psA[2 * (i % 2) + 1]]
            for j in range(4):
                blk = 4 * g + j
                P.op("pe", lambda e, j=j, blk=blk: e.matmul(
                    Ym[m][:, 0:129], lhsT=eb[:, j * 128:(j + 1) * 128], rhs=vS[:, blk, 0:129],
                    start=(g == 0 and j == 0), stop=(g == ng - 1 and j == 3)),
                    reads=[eb, vS], writes=[Ym[m]], inc=(j == 3))
            if not (g == ng - 1 and m == 1):
                return
            rc = c["small"].next()
            P.op("dve", lambda e: e.reciprocal(out=rc[:, 0:1], in_=Ym[0][:, 128:129]), reads=[Ym[0]], writes=[rc])
            P.op("dve", lambda e: e.reciprocal(out=rc[:, 1:2], in_=Ym[1][:, 128:129]), reads=[Ym[1]], writes=[rc])
            P.op("dve", lambda e: e.tensor_tensor(out=rc[:, 1:2], in0=rc[:, 1:2], in1=c["nlam"][:, 0:1], op=ALU.mult),
                 reads=[rc, c["nlam"]], writes=[rc])
            ya = c["tokR"].next()
            yb = c["tokR"].next()
            P.op("dve", lambda e: e.tensor_scalar(out=ya[:, 0:128], in0=Ym[0][:, 0:128], scalar1=rc[:, 0:1], scalar2=None, op0=ALU.mult),
                 reads=[Ym[0], rc], writes=[ya])
            P.op("dve", lambda e: e.scalar_tensor_tensor(out=yb[:, 0:128], in0=Ym[1][:, 0:128], scalar=rc[:, 1:2], in1=ya[:, 0:128],
                                                         op0=ALU.mult, op1=ALU.add), reads=[Ym[1], rc, ya], writes=[yb])
            P.op("act", lambda e: e.activation(out=ya[:, 0:128], in_=yb[:, 0:128], func=AF.Square, accum_out=rc[:, 2:3]),
                 reads=[yb], writes=[ya, rc])
            P.op("dve", lambda e: e.tensor_scalar(out=rc[:, 2:3], in0=rc[:, 2:3], scalar1=1.0 / 128, scalar2=EPS, op0=ALU.mult, op1=ALU.add),
                 reads=[rc], writes=[rc])
            P.op("act", lambda e: e.activation(out=rc[:, 2:3], in_=rc[:, 2:3], func=AF.Sqrt), reads=[rc], writes=[rc])
            P.op("dve", lambda e: e.reciprocal(out=rc[:, 2:3], in_=rc[:, 2:3]), reads=[rc], writes=[rc])
            P.op("dve", lambda e: e.scalar_tensor_tensor(out=ya[:, 0:128], in0=yb[:, 0:128], scalar=rc[:, 2:3], in1=c["don"][:, :],
                                                         op0=ALU.mult, op1=ALU.mult), reads=[yb, rc, c["don"]], writes=[ya])
            P.op("dve", lambda e: e.tensor_scalar(out=ya[:, 0:128], in0=ya[:, 0:128], scalar1=float(1.0 - lam_init), scalar2=None, op0=ALU.mult),
                 reads=[ya], writes=[ya])
            Tp = psR.next()
            P.op("pe", lambda e: e.transpose(Tp[:, 0:128], ya[:, 0:128], ident[:, :]), reads=[ya, ident], writes=[Tp])
            P.op("act", lambda e: e.copy(out=ysbT[:, 12 + hd, i * 128:(i + 1) * 128], in_=Tp[:, 0:128]), reads=[Tp], writes=[ysbT])

        n = len(items)
        for k in range(n + 2):
            if k < n:
                dA(k)
            if 0 <= k - 2 < n:
                dC(k - 2)

    rq, rqx, rk, rvO, st32, stB = c["rq"], c["rqx"], c["rk"], c["rvO"], c["st32"], c["stB"]
    GTb = GT[:, :].rearrange("(b p) d -> b p d", p=128)
    if isinstance(rank, int):
        P.dma(c["RS"][:, :].rearrange("(b p) d -> b p d", p=128), GTb[rank:rank + 65, :, GT_KZ:GT_KZ + 1536], reads=[GT], writes=[c["RS"]])
    else:
        P.dma(c["RS"][:, :].rearrange("(b p) d -> b p d", p=128), GTb[bass.ds(rank, 65), :, GT_KZ:GT_KZ + 1536],
              reads=[GT], writes=[c["RS"]], q="pool")
    rgT = c["gch"]
    for hd in range(4):
        P.dma(rq[:, :], obf[LA_BF["rq"] + hd * 128:LA_BF["rq"] + (hd + 1) * 128, :], writes=[rq])
        P.dma(rqx[:, :], obf[LA_BF["rqx"] + hd * 128:LA_BF["rqx"] + (hd + 1) * 128, :], writes=[rqx])
        P.dma(rk[:, :], obf[LA_BF["rk"] + hd * 128:LA_BF["rk"] + (hd + 1) * 128, :], writes=[rk])
        P.op("pool", lambda e: e.memset(st32[:, :], 0.0), writes=[st32])
        g128 = float(np.exp(128.0 * LOG_GAMMA[hd]))
        for i in range(NB):
            kzs, vss = c["kzs"].next(), c["vss"].next()
            RSb = c["RS"][:, :].rearrange("(b p) d -> b p d", p=128)
            P.dma(kzs[:, :, :], RSb[8 * i:8 * i + 8, :, hd * 128:(hd + 1) * 128].rearrange("b p d -> p b d"),
                  reads=[c["RS"]], writes=[kzs])
            P.dma(vss[:, :, :], RSb[8 * i:8 * i + 9, :, 512 + hd * 256:512 + (hd + 1) * 256].rearrange("b p d -> p b d"),
                  reads=[c["RS"]], writes=[vss])
            for n in range(8):
                Sp = psR.next()
                P.op("pe", lambda e, n=n: e.matmul(Sp[:, 0:256], lhsT=kzs[:, n, :], rhs=vss[:, n, :], start=True, stop=True),
                     reads=[kzs, vss], writes=[Sp])
                P.op("dve", lambda e: e.scalar_tensor_tensor(out=st32[:, :], in0=st32[:, :], scalar=g128, in1=Sp[:, 0:256],
                                                             op0=ALU.mult, op1=ALU.add), reads=[st32, Sp], writes=[st32])
            P.op("act", lambda e: e.copy(out=stB[:, :], in_=st32[:, :]), reads=[st32], writes=[stB])
            Sc = psR.next()
            P.op("pe", lambda e: e.matmul(Sc[:, 0:128], lhsT=rk[:, i * 128:(i + 1) * 128], rhs=rq[:, i * 128:(i + 1) * 128], start=True, stop=True),
                 reads=[rk, rq], writes=[Sc])
            scb = c["scb"].next()
            P.op("dve", lambda e: e.tensor_tensor(out=scb[:, :], in0=Sc[:, 0:128], in1=c["dmask"][:, hd, :], op=ALU.mult),
                 reads=[Sc, c["dmask"]], writes=[scb])
            for ec in range(2):
                Yr = psR.next()
                P.op("pe", lambda e, ec=ec: e.matmul(Yr[:, 0:128], lhsT=vss[:, 8, ec * 128:(ec + 1) * 128], rhs=scb[:, :], start=True, stop=False),
                     reads=[vss, scb], writes=[Yr], inc=False)
                P.op("pe", lambda e, ec=ec: e.matmul(Yr[:, 0:128], lhsT=stB[:, ec * 128:(ec + 1) * 128], rhs=rqx[:, i * 128:(i + 1) * 128],
                                                     start=False, stop=True), reads=[stB, rqx], writes=[Yr])
                yr = c["yr"][ec]
                P.op("act", lambda e: e.copy(out=yr[:, :], in_=Yr[:, 0:128]), reads=[Yr], writes=[yr])
            St = psR.next()
            for ec in range(2):
                P.op("pe", lambda e, ec=ec: e.matmul(St[:, 0:128], lhsT=c["ones32"][:, :], rhs=c["yr"][ec][:, :], start=(ec == 0), stop=(ec == 1)),
                     reads=[c["ones32"], c["yr"][ec]], writes=[St], inc=(ec == 1))
            for ec in range(2):
                P.op("act", lambda e, ec=ec: e.activation(out=c["ysq"][ec][:, :], in_=c["yr"][ec][:, :], func=AF.Square),
                     reads=[c["yr"][ec]], writes=[c["ysq"][ec]])
            for ec in range(2):
                P.op("pe", lambda e, ec=ec: e.matmul(St[:, 128:256], lhsT=c["ones32"][:, :], rhs=c["ysq"][ec][:, :], start=(ec == 0), stop=(ec == 1)),
                     reads=[c["ones32"], c["ysq"][ec]], writes=[St], inc=(ec == 1))
            mu = c["tokR"].next()
            P.op("dve", lambda e: e.tensor_scalar(out=mu[:, 0:256], in0=St[:, 0:256], scalar1=1.0 / 256, scalar2=None, op0=ALU.mult),
                 reads=[St], writes=[mu])
            var = c["tokR"].next()
            P.op("dve", lambda e: e.tensor_tensor(out=var[:, 0:128], in0=mu[:, 0:128], in1=mu[:, 0:128], op=ALU.mult), reads=[mu], writes=[var])
            P.op("dve", lambda e: e.tensor_tensor(out=var[:, 0:128], in0=mu[:, 128:256], in1=var[:, 0:128], op=ALU.subtract),
                 reads=[mu, var], writes=[var])
            P.op("dve", lambda e: e.tensor_scalar(out=var[:, 0:128], in0=var[:, 0:128], scalar1=EPS, scalar2=None, op0=ALU.add),
                 reads=[var], writes=[var])
            P.op("act", lambda e: e.activation(out=var[:, 0:128], in_=var[:, 0:128], func=AF.Sqrt), reads=[var], writes=[var])
            P.op("dve", lambda e: e.reciprocal(out=var[:, 0:128], in_=var[:, 0:128]), reads=[var], writes=[var])
            for ec in range(2):
                yr = c["yr"][ec]
                fch = hd * 2 + ec
                P.dma(rgT[:, 0:128], of32[LA_F32["rg"] + fch * 128:LA_F32["rg"] + (fch + 1) * 128, i * 128:(i + 1) * 128], writes=[rgT])
                P.op("dve", lambda e: e.tensor_tensor(out=yr[:, :], in0=yr[:, :], in1=mu[:, 0:128], op=ALU.subtract), reads=[yr, mu], writes=[yr])
                P.op("dve", lambda e: e.tensor_tensor(out=yr[:, :], in0=yr[:, :], in1=var[:, 0:128], op=ALU.mult), reads=[yr, var], writes=[yr])
                P.op("dve", lambda e, fch=fch: e.tensor_scalar(out=yr[:, :], in0=yr[:, :], scalar1=c["gng"][:, fch:fch + 1], scalar2=c["gnb"][:, fch:fch + 1],
                                                            op0=ALU.mult, op1=ALU.add), reads=[yr, c["gng"], c["gnb"]], writes=[yr])
                P.op("dve", lambda e, fch=fch: e.tensor_tensor(out=ysbT[:, 4 + fch, i * 128:(i + 1) * 128], in0=yr[:, :], in1=rgT[:, 0:128], op=ALU.mult),
                     reads=[yr, rgT], writes=[ysbT])

    if dbg is not None:
        P.dma(dbg[:, :, :], ysbT[:, :, :], reads=[ysbT], writes=[dbg], is_output=True)
    P.barrier()
    mT = c["mT"]
    wbfR = c["wbf"]

    def slab(srcbuf, idx):
        wb = wbfR.next()
        P.dma(wb[:, :, :], srcbuf[idx], reads=[srcbuf], writes=[wb])
        return wb

    br_k = [(0, 4), (4, 12), (12, 16)]
    for fc in range(16):
        wb = slab(wbr, fc)
        acc = c["acc"]
        for b, (k0, k1) in enumerate(br_k):
            gch = c["gch"]
            P.dma(gch[:, :], of32[LA_F32["gates"] + b * 2048 + fc * 128:LA_F32["gates"] + b * 2048 + (fc + 1) * 128, :], writes=[gch])
            for h in range(2):
                Bp = psR.next()
                for kc in range(k0, k1):
                    P.op("pe", lambda e, kc=kc, h=h: e.matmul(Bp[:, :], lhsT=wb[:, kc, :], rhs=ysbT[:, kc, h * 512:(h + 1) * 512],
                                                           start=(kc == k0), stop=(kc == k1 - 1)), reads=[wb, ysbT], writes=[Bp], inc=(kc == k1 - 1))
                sl = slice(h * 512, (h + 1) * 512)
                if b == 0:
                    P.op("dve", lambda e: e.tensor_tensor(out=acc[:, sl], in0=Bp[:, :], in1=gch[:, sl], op=ALU.mult), reads=[Bp, gch], writes=[acc])
                else:
                    tmp = c["efR"].next()
                    P.op("dve", lambda e: e.tensor_tensor(out=tmp[:, :], in0=Bp[:, :], in1=gch[:, sl], op=ALU.mult), reads=[Bp, gch], writes=[tmp])
                    if b == 1:
                        P.op("pool", lambda e: e.tensor_tensor(out=acc[:, sl], in0=acc[:, sl], in1=tmp[:, :], op=ALU.add), reads=[acc, tmp], writes=[acc])
                    else:
                        P.op("pool", lambda e: e.tensor_tensor(out=mT[:, fc, sl], in0=acc[:, sl], in1=tmp[:, :], op=ALU.add), reads=[acc, tmp], writes=[mT])
    for fc in range(16):
        wb = slab(wout, fc)
        xch = c["xch"].next()
        P.dma(xch[:, :], xTd[:, fc, :], reads=[xTd], writes=[xch])
        for h in range(2):
            Op = psR.next()
            for kc in range(KC):
                P.op("pe", lambda e, kc=kc, h=h: e.matmul(Op[:, :], lhsT=wb[:, kc, :], rhs=mT[:, kc, h * 512:(h + 1) * 512],
                                                       start=(kc == 0), stop=(kc == KC - 1)), reads=[wb, mT], writes=[Op], inc=(kc == KC - 1))
            sl = slice(h * 512, (h + 1) * 512)
            P.op("dve", lambda e: e.scalar_tensor_tensor(out=xch[:, sl], in0=Op[:, :], scalar=modS[:, 32 + fc:33 + fc], in1=xch[:, sl],
                                                         op0=ALU.mult, op1=ALU.add), reads=[Op, modS, xch], writes=[xch])
        if c.get("hs") is not None:
            P.op("pool", lambda e, fc=fc: e.tensor_copy(out=c["hs"][:, :, fc, :], in_=xch[:, :].rearrange("p (i t) -> p i t", t=128)[:, :, 126:128]),
                 reads=[xch], writes=[c["hs"]])
        P.dma(xoutd[:, fc, :], xch[:, :], reads=[xch], writes=[xoutd], is_output=(dbg is not None), q="pool")


def lb_consts(P, srcs):
    c = {}
    c["RS"] = P.dram("RSscr", [65 * 128, 1536], BF16)
    c["psR"] = RR([P.ps(f"lps{i}") for i in range(4)])
    c["psA"] = [P.ps(f"lpsA{i}") for i in range(4)]
    c["ybr"] = P.sb("ybr", [128, 16, T], BF16)
    c["qown"] = P.sb("qown", [128, 4, T], BF16)
    kbig = P.sb("kbig", [128, 2, S], BF16)
    c["kTR"] = RR([Buf(kbig.t[:, 0, :], "kT0"), Buf(kbig.t[:, 1, :], "kT1")])
    c["mT"] = Buf(kbig.t[:, :, :].rearrange("p a (b t) -> p (a b) t", t=T), "mT")
    c["nq"] = P.sb("nq", [128, 4, T], BF16)
    c["vSR"] = RR([P.sb(f"vS{i}", [128, 64, 130], BF16) for i in range(2)])
    c["R"] = P.sb("Rsum", [128, 128], F32)
    c["efR"] = RR([P.sb(f"ef{i}", [128, 512], F32) for i in range(3)])
    c["spR"] = RR([P.sb(f"sp{i}", [128, 512], BF16) for i in range(3)])
    c["wR"] = RR([P.sb(f"wt{i}", [128, 512], BF16) for i in range(4)])
    c["small"] = RR([P.sb(f"sm{i}", [128, 4], F32) for i in range(2)])
    c["tokR"] = RR([P.sb(f"tok{i}", [128, 256], F32) for i in range(3)])
    c["rq"] = P.sb("rq", [128, T], BF16)
    c["rqx"] = P.sb("rqx", [128, T], BF16)
    c["rk"] = P.sb("rk", [128, T], BF16)
    c["rvO"] = P.sb("rvO", [128, 256], BF16)
    c["st32"] = P.sb("st32", [128, 256], F32)
    c["stB"] = P.sb("stB", [128, 256], BF16)
    c["kzs"] = RR([P.sb(f"kzs{i}", [128, 8, 128], BF16) for i in range(2)])
    c["vss"] = RR([P.sb(f"vss{i}", [128, 9, 256], BF16) for i in range(2)])
    c["scb"] = RR([P.sb(f"scb{i}", [128, 128], BF16) for i in range(2)])
    c["yr"] = [P.sb(f"yr{i}", [128, 128], F32) for i in range(2)]
    c["ysq"] = [P.sb(f"ysq{i}", [128, 128], F32) for i in range(2)]
    c["gch"] = P.sb("gch", [128, T], F32)
    c["acc"] = P.sb("acc", [128, T], F32)
    c["xch"] = RR([P.sb(f"xch{i}", [128, T], F32) for i in range(2)])
    c["wbf"] = RR([P.sb(f"lwbf{i}", [128, KC, 128], BF16) for i in range(3)])
    c["one"] = P.sb("one", [128, 1], F32)
    c["ones32"] = P.sb("lones32", [128, 128], F32)
    c["onesB"] = P.sb("onesB", [128, 128], BF16)
    c["uincl"] = P.sb("uincl", [128, 128], BF16)
    c["identF"] = P.sb("identF", [128, 128], F32)
    c["msb"] = P.sb("msb", [128, 2, 512], BF16)
    c["mdf"] = P.sb("mdf", [128, 2, 512], BF16)
    c["dmask"] = P.sb("dmask", [128, 4, 128], F32)
    c["don"] = P.sb("don", [128, 128], F32)
    c["gng"] = P.sb("gng", [128, 8], F32)
    c["gnb"] = P.sb("gnb", [128, 8], F32)
    c["nlam"] = P.sb("nlam", [128, 1], F32)
    c["lamv"] = P.sb("lamv", [64, 4], F32)
    P.op("pool", lambda e: e.memset(c["one"][:], 1.0), writes=[c["one"]])
    P.op("pool", lambda e: e.memset(c["ones32"][:], 1.0), writes=[c["ones32"]])
    P.op("pool", lambda e: e.memset(c["onesB"][:], 1.0), writes=[c["onesB"]])
    return c


def lb_load_consts(P, c, s):
    P.dma(c["uincl"][:], s["uincl"][:, :], writes=[c["uincl"]])
    P.dma(c["identF"][:], s["ident"][:, :], writes=[c["identF"]])
    P.dma(c["msb"][:], s["masks"][:, 0, :, :], writes=[c["msb"]])
    P.dma(c["mdf"][:], s["masks"][:, 1, :, :], writes=[c["mdf"]])
    P.dma(c["dmask"][:], s["dmask"][:, :, :], writes=[c["dmask"]])


def lb_load_layer(P, c, s, lam_init):
    P.dma(c["don"][:], s["don"][:, :], writes=[c["don"]])
    P.dma(c["gng"][:], s["gn"][:, 0:8], writes=[c["gng"]])
    P.dma(c["gnb"][:], s["gn"][:, 8:16], writes=[c["gnb"]])
    P.dma(c["lamv"][:], s["lamv"][:, :], writes=[c["lamv"]])
    pr = c["small"].next()
    P.op("dve", lambda e: e.tensor_tensor(out=pr[0:64, 0:1], in0=c["lamv"][:, 0:1], in1=c["lamv"][:, 1:2], op=ALU.mult), reads=[c["lamv"]], writes=[pr])
    P.op("dve", lambda e: e.tensor_tensor(out=pr[0:64, 1:2], in0=c["lamv"][:, 2:3], in1=c["lamv"][:, 3:4], op=ALU.mult), reads=[c["lamv"]], writes=[pr])
    Lp = c["psR"].next()
    P.op("pe", lambda e: e.matmul(Lp[:, 0:2], lhsT=c["ones32"][0:64, :], rhs=pr[0:64, 0:2], start=True, stop=True),
         reads=[c["ones32"], pr], writes=[Lp])
    ex = c["small"].next()
    P.op("act", lambda e: e.activation(out=ex[:, 0:2], in_=Lp[:, 0:2], func=AF.Exp), reads=[Lp], writes=[ex])
    P.op("dve", lambda e: e.tensor_tensor(out=c["nlam"][:, 0:1], in0=ex[:, 1:2], in1=ex[:, 0:1], op=ALU.subtract), reads=[ex], writes=[c["nlam"]])
    P.op("dve", lambda e: e.tensor_scalar(out=c["nlam"][:, 0:1], in0=c["nlam"][:, 0:1], scalar1=float(-lam_init), scalar2=None, op0=ALU.add),
         reads=[c["nlam"]], writes=[c["nlam"]])


def lb_host_consts(core):
    p = np.arange(128)
    uincl = (p[:, None] >= p[None, :]).astype(np.float32)
    ident = np.eye(128, dtype=np.float32)
    masks = np.zeros((128, 2, 2, 512), np.float32)
    for g2 in range(2):
        for j in range(4):
            kb = 4 * g2 + j
            sl = slice(j * 128, (j + 1) * 128)
            if kb < core:
                masks[:, 0, g2, sl] = 1.0
                masks[:, 1, g2, sl] = 1.0
            elif kb == core:
                masks[:, 0, g2, sl] = (p[:, None] < p[None, :])
                masks[:, 1, g2, sl] = ((p[:, None] // 64) <= (p[None, :] // 64))
    dm = np.zeros((128, 4, 128), np.float64)
    n = p[None, :]
    m = p[:, None]
    for h in range(4):
        same = (m // 64) == (n // 64)
        earlier = (m // 64) < (n // 64)
        dm[:, h, :] = np.where(same, np.exp(np.abs(n - m) * LOG_GAMMA[h]), np.where(earlier, np.exp((n - m) * LOG_GAMMA[h]), 0.0))
    return uincl.astype(NPBF), ident, masks.astype(NPBF), dm.astype(np.float32)


def build_LB_test(lam_init):
    P = Prog()
    obf = P.dram("obf", [LA_BF_ROWS, T], BF16, kind="ExternalInput")
    of32 = P.dram("of32", [LA_F32_ROWS, T], F32, kind="ExternalInput")
    GF = P.dram("GF", [1024, S], BF16, kind="ExternalInput")
    GT = P.dram("GT", [1024 + S, 2560], BF16, kind="ExternalInput")
    xTd = P.dram("xT", [128, KC, T], F32, kind="ExternalInput")
    modT = P.dram("modT", [128, 96], F32, kind="ExternalInput")
    wbr = P.dram("wbr", [16, 128, KC, 128], BF16, kind="ExternalInput")
    wout = P.dram("wout", [16, 128, KC, 128], BF16, kind="ExternalInput")
    s = {k: P.dram("i_" + k, shp, dt, kind="ExternalInput") for k, shp, dt in [
        ("uincl", [128, 128], BF16), ("ident", [128, 128], F32), ("masks", [128, 2, 2, 512], BF16), ("dmask", [128, 4, 128], F32),
        ("don", [128, 128], F32), ("gn", [128, 16], F32), ("lamv", [64, 4], F32)]}
    xoutd = P.dram("xout", [128, KC, T], F32, kind="ExternalOutput")
    dbg = P.dram("dbg", [128, 16, T], BF16, kind="ExternalOutput")
    modS = P.sb("modS", [128, 96], F32)
    P.dma(modS[:], modT[:, :], writes=[modS])
    c = lb_consts(P, s)
    lb_load_consts(P, c, s)
    lb_load_layer(P, c, s, lam_init)
    rank = P.nc.gpsimd.partition_id()
    lb_phase(P, c, obf, of32, GF, GT, xTd, xoutd, modS, wbr, wout, rank, lam_init, dbg=dbg)
    return P.finish()


def lc_alloc(P):
    c = {}
    c["xT"] = P.sb("c_xT", [128, KC, T], F32)
    c["hT"] = P.sb("c_hT", [128, KC, T], BF16)
    c["xh"] = P.sb("c_xh", [128, KC, 16], F32)
    c["hh"] = P.sb("c_hh", [128, KC, 16], BF16)
    c["actT"] = P.sb("c_actT", [128, 44, 512], BF16)
    c["U"] = RR([P.sb(f"c_U{i}", [128, 4, 130], F32) for i in range(3)])
    c["Y"] = RR([P.sb(f"c_Y{i}", [128, 4, 128], F32) for i in range(4)])
    c["wup"] = RR([P.sb(f"c_wup{i}", [128, KC, 128], BF16) for i in range(3)])
    c["wdn"] = RR([P.sb(f"c_wdn{i}", [128, KC, 128], BF16) for i in range(3)])
    c["xo"] = RR([P.sb(f"c_xo{i}", [128, 512], F32) for i in range(2)])
    c["gs"] = P.sb("c_gs", [128, KC], F32)
    c["nf"] = P.sb("c_nf", [128, KC], F32)
    c["cw"] = P.sb("c_cw", [128, 3, 88], F32)
    c["cb"] = P.sb("c_cb", [128, 88], F32)
    c["flag"] = P.sb("c_flag", [128, 8, 2], F32)
    c["rstd"] = P.sb("c_rstd", [128, T], F32)
    c["tmp"] = RR([P.sb(f"c_tmp{i}", [128, 512], F32) for i in range(3)])
    c["ones32"] = P.sb("c_ones32", [128, 128], F32)
    P.op("pool", lambda e: e.memset(c["ones32"][:], 1.0), writes=[c["ones32"]])
    return c


def lc_phase(P, c, psR, xmid, halo_ap, halo_buf, xout, modS, nffn_d, cw_d, cb_d, flag_d, wup, wdn, out_is_final):
    xT, hT, xh, hh, actT = c["xT"], c["hT"], c["xh"], c["hh"], c["actT"]
    for q in range(4):
        P.dma(xT[:, 4 * q:4 * q + 4, :], xmid[:, 4 * q:4 * q + 4, :], reads=[xmid], writes=[xT])
    if len(halo_ap.shape) == 3:
        P.dma(c["xhi"][:, :, :], halo_ap, reads=[halo_buf], writes=[c["xhi"]])
    else:
        P.dma(c["xhi"][:, :, :].unsqueeze(2), halo_ap, reads=[halo_buf], writes=[c["xhi"]], q="pool")
    P.op("pool", lambda e: e.tensor_copy(out=xh[:, :, :].rearrange("p k (i t) -> p k i t", t=2),
                                         in_=c["xhi"][:, :, :].rearrange("p i (k t) -> p k i t", t=2)), reads=[c["xhi"]], writes=[xh])
    P.dma(c["nf"][:], nffn_d[:, :], writes=[c["nf"]])
    P.dma(c["cw"][:], cw_d[:, :, :], writes=[c["cw"]])
    P.dma(c["cb"][:], cb_d[:, :], writes=[c["cb"]])
    P.dma(c["flag"][:], flag_d[:, :, :], writes=[c["flag"]])
    gs = c["gs"]
    P.op("dve", lambda e: e.scalar_tensor_tensor(out=gs[:], in0=modS[:, 64:80], scalar=1.0, in1=c["nf"][:], op0=ALU.add, op1=ALU.mult),
         reads=[modS, c["nf"]], writes=[gs])
    psA, psB = psR.next(), psR.next()
    adaln_norm(P, xT, hT, T, gs, lambda kc: modS[:, 48 + kc:49 + kc], c["ones32"], psA, psB, c["rstd"], c["tmp"])
    adaln_norm(P, xh, hh, 16, gs, lambda kc: modS[:, 48 + kc:49 + kc], c["ones32"], psA, psB, c["rstd"], c["tmp"])
    for hf in range(2):
        tsl = slice(hf * 512, (hf + 1) * 512)
        for j in range(44):
            Ys = []
            for which in range(2):
                sidx = j + 44 * which
                wb = c["wup"].next()
                P.dma(wb[:, :, :], wup[sidx], reads=[wup], writes=[wb])
                pm, ph = psR.next(), psR.next()
                for kc in range(KC):
                    P.op("pe", lambda e, kc=kc: e.matmul(pm[:, :], lhsT=wb[:, kc, :], rhs=hT[:, kc, tsl], start=(kc == 0), stop=(kc == KC - 1)),
                         reads=[wb, hT], writes=[pm], inc=(kc == KC - 1))
                for kc in range(KC):
                    P.op("pe", lambda e, kc=kc: e.matmul(ph[:, 0:8], lhsT=wb[:, kc, :], rhs=hh[:, kc, hf * 8:(hf + 1) * 8], start=(kc == 0), stop=(kc == KC - 1)),
                         reads=[wb, hh], writes=[ph], inc=(kc == KC - 1))
                U = c["U"].next()
                P.op("act", lambda e: e.copy(out=U[:, :, 2:130], in_=pm[:, :].rearrange("p (b t) -> p b t", t=128)), reads=[pm], writes=[U])
                P.op("dve", lambda e: e.tensor_tensor(out=U[:, :, 0:2], in0=ph[:, 0:8].rearrange("p (b t) -> p b t", t=2),
                                                      in1=c["flag"][:, hf * 4:(hf + 1) * 4, :], op=ALU.mult), reads=[ph, c["flag"]], writes=[U])
                Y = c["Y"].next()
                P.op("act", lambda e, sidx=sidx: e.activation(out=Y[:, :, :], in_=U[:, :, 2:130], func=AF.Identity,
                                                          scale=c["cw"][:, 2, sidx:sidx + 1], bias=c["cb"][:, sidx:sidx + 1]),
                     reads=[U, c["cw"], c["cb"]], writes=[Y])
                P.op("dve", lambda e, sidx=sidx: e.scalar_tensor_tensor(out=Y[:, :, :], in0=U[:, :, 1:129], scalar=c["cw"][:, 1, sidx:sidx + 1], in1=Y[:, :, :],
                                                                     op0=ALU.mult, op1=ALU.add), reads=[U, c["cw"], Y], writes=[Y])
                P.op("dve", lambda e, sidx=sidx: e.scalar_tensor_tensor(out=Y[:, :, :], in0=U[:, :, 0:128], scalar=c["cw"][:, 0, sidx:sidx + 1], in1=Y[:, :, :],
                                                                     op0=ALU.mult, op1=ALU.add), reads=[U, c["cw"], Y], writes=[Y])
                Ys.append(Y)
            sg = c["Y"].next()
            P.op("act", lambda e: e.activation(out=sg[:, :, :], in_=Ys[0][:, :, :], func=AF.Silu), reads=[Ys[0]], writes=[sg])
            P.op("pool", lambda e, j=j: e.tensor_tensor(out=actT[:, j, :].rearrange("p (b t) -> p b t", t=128), in0=sg[:, :, :], in1=Ys[1][:, :, :], op=ALU.mult),
                 reads=[sg, Ys[1]], writes=[actT])
        for fc in range(KC):
            wbs = []
            for g3 in range(3):
                wb = c["wdn"].next()
                P.dma(wb[:, :, :], wdn[fc][:, g3, :, :], reads=[wdn], writes=[wb])
                wbs.append(wb)
            po = psR.next()
            for kc in range(44):
                wb = wbs[kc // 16]
                P.op("pe", lambda e, kc=kc, wb=wb: e.matmul(po[:, :], lhsT=wb[:, kc % 16, :], rhs=actT[:, kc, :], start=(kc == 0), stop=(kc == 43)),
                     reads=[wb, actT], writes=[po], inc=(kc == 43 or kc % 16 == 15))
            xo = c["xo"].next()
            P.op("dve", lambda e, fc=fc: e.scalar_tensor_tensor(out=xo[:, :], in0=po[:, :], scalar=modS[:, 80 + fc:81 + fc], in1=xT[:, fc, tsl],
                                                             op0=ALU.mult, op1=ALU.add), reads=[po, modS, xT], writes=[xo])
            P.dma(xout[:, fc, tsl], xo[:, :], reads=[xo], writes=[xout], is_output=out_is_final, q="pool")


NU_LAYER = 280
U_OFF = dict(la=0, br=112, out=128, up=144, dn=232)
NU = 2 * NU_LAYER
NU_CORE = NU // NCORE
AR_CHUNK = 56
LAM_INIT = [0.8 - 0.6 * float(np.exp(-0.3 * l)) for l in range(2)]


def build_full():
    P = Prog()
    X = lambda n, shp, dt: P.dram(n, shp, dt, kind="ExternalInput")
    xT_in = X("xT", [128, KC, T], F32)
    pos = X("pos", [128, T], I32)
    c_in = X("c_in", [128, KC], F32)
    wada = X("wada", [2, 128, KC, 1536], F32)
    bada = X("bada", [128, 2, 12], F32)
    wsh = X("wsh", [NU_CORE, 128, KC, 128], F32)
    nmix = X("nmix", [2, 128, KC], F32)
    nffn = X("nffn", [2, 128, KC], F32)
    cst = X("cst", [2, 128, 8], F32)
    cwd = X("cw", [2, 128, 3, 88], F32)
    cbd = X("cb", [2, 128, 88], F32)
    dond = X("don", [2, 128, 128], F32)
    gnd = X("gn", [2, 128, 16], F32)
    lamd = X("lamv", [2, 64, 4], F32)
    dec = X("dec", [128, 8, 128], F32)
    bones = X("bones", [128, 128], F32)
    uincl = X("uincl", [128, 128], BF16)
    ident = X("ident", [128, 128], F32)
    masks = X("masks", [128, 2, 2, 512], BF16)
    dmask = X("dmask", [128, 4, 128], F32)
    flagd = X("flag", [128, 8, 2], F32)
    xfin = P.dram("xoutT", [128, KC, T], F32, kind="ExternalOutput")

    Wown = P.dram("Wown", [NU_CORE, 128, KC, 128], BF16)
    WinL = [P.dram(f"Win{l}", [NU_LAYER, 128, KC, 128], BF16) for l in range(2)]
    WoutL = [P.dram(f"Wout{l}", [NU_LAYER, 128, KC, 128], BF16, shared=True) for l in range(2)]
    MODin = P.dram("MODin", [8, 128, 2, 12], F32)
    MOD = P.dram("MOD", [8, 128, 2, 12], F32, shared=True)
    GFin = P.dram("GFin", [1024, S], BF16)
    GF = P.dram("GF", [1024, S], BF16, shared=True)
    GTin = P.dram("GTin", [1024 + S, 2560], BF16)
    GT = P.dram("GT", [1024 + S, 2560], BF16, shared=True)
    HLin = P.dram("HLin", [65, 128, KC, 2], F32)
    HL = P.dram("HL", [65, 128, KC, 2], F32, shared=True)
    OT = P.dram("OT", [T, 2560], BF16)
    obf = P.dram("obf", [LA_BF_ROWS, T], BF16)
    of32 = P.dram("of32", [LA_F32_ROWS, T], F32)
    xmid = P.dram("xmid", [128, KC, T], F32)
    xnext = P.dram("xnext", [128, KC, T], F32)
    rank = P.nc.gpsimd.partition_id()

    P.phase_begin()
    Zt = P.sb("Zt", [128, 8192], BF16)
    Zf = P.sb("Zf", [128, 65 * 32], F32)
    P.op("pool", lambda e: e.memset(Zt[:], 0.0), writes=[Zt])
    P.op("pool", lambda e: e.memset(Zf[:], 0.0), writes=[Zf])
    for Win in WinL:
        for u in range(0, NU_LAYER, 4):
            P.dma(Win[u:u + 4].rearrange("u p k n -> p u (k n)"), Zt[:, :].rearrange("p (u x) -> p u x", u=4), reads=[Zt], writes=[Win])
    for r0 in range(0, 1024, 128):
        P.dma(GFin[r0:r0 + 128, :], Zt[:, :], reads=[Zt], writes=[GFin])
    for b0 in range(0, 72, 3):
        P.dma(GTin[b0 * 128:(b0 + 3) * 128, :].rearrange("(b p) d -> p b d", p=128), Zt[:, 0:7680].rearrange("p (b d) -> p b d", d=2560),
              reads=[Zt], writes=[GTin])
    P.dma(HLin[:, :, :, :].rearrange("b p k t -> p b (k t)"), Zf[:, :].rearrange("p (b x) -> p b x", x=32), reads=[Zf], writes=[HLin])
    P.dma(MODin[:, :, :, :].rearrange("r p l j -> p r (l j)"), Zf[:, 0:192].rearrange("p (r x) -> p r x", x=24), reads=[Zf], writes=[MODin])
    wst = RR([P.sb(f"Wst{i}", [128, KC, 128], F32) for i in range(3)])
    wcb = RR([P.sb(f"Wcb{i}", [128, KC, 128], BF16) for i in range(3)])
    cast_engs = ["pool", "dve", "act"]
    for j in range(NU_CORE):
        st, wb = wst.next(), wcb.next()
        P.dma(st[:, :, :], wsh[j], writes=[st])
        ce = cast_engs[j % 3]
        if ce == "act":
            P.op("act", lambda e: e.copy(out=wb[:, :, :], in_=st[:, :, :]), reads=[st], writes=[wb])
        else:
            P.op(ce, lambda e: e.tensor_copy(out=wb[:, :, :], in_=st[:, :, :]), reads=[st], writes=[wb])
        P.dma(Wown[j], wb[:, :, :], reads=[wb], writes=[Wown])
    for l in range(2):
        Win, Wout = WinL[l], WoutL[l]
        P.dma(Win[:, :, :, :].rearrange("(r j) p k n -> r (j p) (k n)", r=8)[bass.ds(rank, 1), :, :],
              Wown[l * 35:(l + 1) * 35, :, :, :].rearrange("j p k n -> (j p) (k n)").unsqueeze(0), reads=[Wown], writes=[Win], q="pool")
        Win2 = Win[:, :, :, :].rearrange("u p k n -> (u p) (k n)")
        Wout2 = Wout[:, :, :, :].rearrange("u p k n -> (u p) (k n)")
        for u in range(0, NU_LAYER, AR_CHUNK):
            P.all_reduce(Win2[u * 128:(u + AR_CHUNK) * 128, :], Wout2[u * 128:(u + AR_CHUNK) * 128, :], reads=[Win], writes=[Wout])
    ct = P.sb("ct", [128, KC], F32)
    ca = P.sb("ca", [128, KC], F32)
    wt = P.sb("wadat", [128, KC, 1536], F32)
    bt = P.sb("badat", [128, 2, 12], F32)
    mo = P.sb("modown", [128, 2, 12], F32)
    pmod = P.ps("pmod")
    P.dma(ct[:], c_in[:, :], writes=[ct])
    P.dma(bt[:], bada[:, :, :], writes=[bt])
    P.op("act", lambda e: e.activation(out=ca[:], in_=ct[:], func=AF.Silu), reads=[ct], writes=[ca])
    for l in range(2):
        for q in range(4):
            P.dma(wt[:, 4 * q:4 * q + 4, :], wada[l, :, 4 * q:4 * q + 4, :], writes=[wt])
        for fch in range(12):
            for kc in range(KC):
                P.op("pe", lambda e, fch=fch, kc=kc, l=l: e.matmul(pmod[:, l * 12 + fch:l * 12 + fch + 1], lhsT=wt[:, kc, fch * 128:(fch + 1) * 128],
                                                                rhs=ca[:, kc:kc + 1], start=(kc == 0), stop=(kc == KC - 1)),
                     reads=[wt, ca], writes=[pmod], inc=(kc == KC - 1))
    P.op("dve", lambda e: e.tensor_tensor(out=mo[:, :, :], in0=pmod[:, 0:24].rearrange("p (l j) -> p l j", l=2), in1=bt[:, :, :], op=ALU.add),
         reads=[pmod, bt], writes=[mo])
    P.dma(MODin[bass.ds(rank, 1), :, :, :].rearrange("o p l j -> p o l j"), mo[:, :, :].unsqueeze(1), reads=[mo], writes=[MODin], q="pool")
    P.all_reduce(MODin[:, :, :, :].rearrange("r p l j -> (r p) (l j)"), MOD[:, :, :, :].rearrange("r p l j -> (r p) (l j)"),
                 reads=[MODin], writes=[MOD])
    P.phase_end()

    xcur = xT_in
    for l in range(2):
        u0 = 0
        Wout = WoutL[l]
        P.phase_begin()
        modTd = Buf(MOD[:, :, l, :].rearrange("r p j -> p r j"), "modTd")
        d = dict(xT=xcur, modT=modTd, nmix=Buf(nmix[l], "nm"), pos=pos, cst=Buf(cst[l], "cs"), dec=dec, bones=bones,
                 wsl=Buf(Wout[u0 + U_OFF["la"]:u0 + U_OFF["la"] + 112], "wsl"), obf=obf, of32=of32)
        la_phase(P, d, True)
        identS = P.sb("identS", [128, 128], F32)
        P.dma(identS[:], ident[:, :], writes=[identS])
        tin = RR([P.sb(f"tin{i}", [128, T], BF16) for i in range(2)])
        tf = RR([P.sb(f"tf{i}", [128, T], F32) for i in range(2)])
        tout = RR([P.sb(f"tout{i}", [128, 8, 128], BF16) for i in range(2)])
        tps = RR([P.ps(f"tps{i}") for i in range(2)])
        jobs = [(LA_BF["vsb"] + k * 128, GT_V_SB + k * 128) for k in range(4)] + [(LA_BF["dv"] + k * 128, GT_V_D + k * 128) for k in range(4)] + \
               [(LA_BF["rkz"] + k * 128, GT_KZ + k * 128) for k in range(4)] + [(LA_BF["rv"] + k * 128, GT_RV + k * 128) for k in range(8)]
        for row, col in jobs:
            ti, tff, to = tin.next(), tf.next(), tout.next()
            P.dma(ti[:, :], obf[row:row + 128, :], reads=[obf], writes=[ti])
            P.op("dve", lambda e: e.tensor_copy(out=tff[:, :], in_=ti[:, :]), reads=[ti], writes=[tff])
            for hb in range(2):
                tp = tps.next()
                for b4 in range(4):
                    b = hb * 4 + b4
                    P.op("pe", lambda e, b=b, b4=b4: e.transpose(tp[:, b4 * 128:(b4 + 1) * 128], tff[:, b * 128:(b + 1) * 128], identS[:, :]),
                         reads=[tff, identS], writes=[tp])
                P.op("act", lambda e, hb=hb: e.copy(out=to[:, hb * 4:(hb + 1) * 4, :], in_=tp[:, :].rearrange("p (b f) -> p b f", f=128)),
                     reads=[tp], writes=[to])
            P.dma(OT[:, col:col + 128].rearrange("(b p) d -> p b d", p=128), to[:, :, :], reads=[to], writes=[OT])
        GFv = GFin[:, :].rearrange("f (i r t) -> f i r t", i=8, r=8)
        P.dma(GFv[0:512, :, bass.ds(rank, 1), :], obf[LA_BF["ksb"]:LA_BF["ksb"] + 512, :].rearrange("f (i t) -> f i t", i=8).unsqueeze(2),
              reads=[obf], writes=[GFin], q="pool")
        P.dma(GFv[512:1024, :, bass.ds(rank, 1), :], obf[LA_BF["dk"]:LA_BF["dk"] + 512, :].rearrange("f (i t) -> f i t", i=8).unsqueeze(2),
              reads=[obf], writes=[GFin], q="pool")
        GTv = GTin[1024:, :].rearrange("(i r p) d -> i r p d", i=8, r=8)
        P.dma(GTv[:, bass.ds(rank, 1), :, :], OT[:, :].rearrange("(i p) d -> i p d", i=8).unsqueeze(1), reads=[OT], writes=[GTin], q="pool")
        P.all_reduce(GFin[:, :], GF[:, :], reads=[GFin], writes=[GF])
        hr = (1024 + S) // 2
        P.all_reduce(GTin[0:hr, :], GT[0:hr, :], reads=[GTin], writes=[GT])
        P.all_reduce(GTin[hr:, :], GT[hr:, :], reads=[GTin], writes=[GT])
        P.phase_end()
        P.phase_begin()
        modS = P.sb("modS", [128, 96], F32)
        P.dma(modS[:, :].rearrange("p (r j) -> p r j", r=8), MOD[:, :, l, :].rearrange("r p j -> p r j"), reads=[MOD], writes=[modS])
        srcs = dict(uincl=uincl, ident=ident, masks=masks, dmask=dmask, don=Buf(dond[l], "don"), gn=Buf(gnd[l], "gn"), lamv=Buf(lamd[l], "lamv"))
        c = lb_consts(P, srcs)
        c["hs"] = P.sb("hs", [128, 8, KC, 2], F32)
        lb_load_consts(P, c, srcs)
        lb_load_layer(P, c, srcs, LAM_INIT[l])
        lb_phase(P, c, obf, of32, GF, GT, xcur, xmid, modS, Buf(Wout[u0 + U_OFF["br"]:u0 + U_OFF["br"] + 16], "wbr"),
                 Buf(Wout[u0 + U_OFF["out"]:u0 + U_OFF["out"] + 16], "wout"), rank, LAM_INIT[l])
        HLv = HLin[1:65, :, :, :].rearrange("(i r) p k t -> p i r (k t)", i=8, r=8)
        P.dma(HLv[:, :, bass.ds(rank, 1), :], c["hs"][:, :, :, :].rearrange("p i k t -> p i (k t)").unsqueeze(2), reads=[c["hs"]], writes=[HLin], q="pool")
        P.all_reduce(HLin[:, :, :, :].rearrange("b p k t -> (b p) (k t)"), HL[:, :, :, :].rearrange("b p k t -> (b p) (k t)"), reads=[HLin], writes=[HL])
        P.phase_end()
        P.phase_begin()
        modS = P.sb("modS", [128, 96], F32)
        P.dma(modS[:, :].rearrange("p (r j) -> p r j", r=8), MOD[:, :, l, :].rearrange("r p j -> p r j"), reads=[MOD], writes=[modS])
        cc = lc_alloc(P)
        cc["xhi"] = P.sb("xhi", [128, 8, KC * 2], F32)
        psR = RR([P.ps(f"cps{i}") for i in range(8)])
        xdst = xfin if l == 1 else xnext
        HLr = HL[0:64, :, :, :].rearrange("(i r) p k t -> p i r (k t)", i=8, r=8)
        lc_phase(P, cc, psR, xmid, HLr[:, :, bass.ds(rank, 1), :], HL, xdst, modS, Buf(nffn[l], "nf"), Buf(cwd[l], "cw"), Buf(cbd[l], "cb"), flagd,
                 Buf(Wout[u0 + U_OFF["up"]:u0 + U_OFF["up"] + 88], "wup"),
                 Buf(Wout[u0 + U_OFF["dn"]:u0 + U_OFF["dn"] + 48].rearrange("(f g) p k n -> f p g k n", g=3), "wdn"), l == 1)
        P.phase_end()
        xcur = xnext
    return P.finish()


def build_single():
    P = Prog()
    X = lambda n, shp, dt: P.dram(n, shp, dt, kind="ExternalInput")
    xT_in = X("xT", [8, 128, KC, T], F32)
    posA = X("pos", [8, 128, T], I32)
    c_in = X("c_in", [128, KC], F32)
    wada = X("wada", [2, 8, 128, KC, 1536], F32)
    bada = X("bada", [8, 128, 2, 12], F32)
    wsh = X("wsh", [NU, 128, KC, 128], F32)
    nmix = X("nmix", [2, 128, KC], F32)
    nffn = X("nffn", [2, 128, KC], F32)
    cst = X("cst", [2, 128, 8], F32)
    cwd = X("cw", [2, 128, 3, 88], F32)
    cbd = X("cb", [2, 128, 88], F32)
    dond = X("don", [2, 128, 128], F32)
    gnd = X("gn", [2, 128, 16], F32)
    lamd = X("lamv", [2, 64, 4], F32)
    dec = X("dec", [128, 8, 128], F32)
    bones = X("bones", [128, 128], F32)
    uincl = X("uincl", [128, 128], BF16)
    ident = X("ident", [128, 128], F32)
    masksA = X("masks", [8, 128, 2, 2, 512], BF16)
    dmask = X("dmask", [128, 4, 128], F32)
    flagA = X("flag", [8, 128, 8, 2], F32)
    xfin = P.dram("xoutT", [8, 128, KC, T], F32, kind="ExternalOutput")

    WoutL = [P.dram(f"Wout{l}", [NU_LAYER, 128, KC, 128], BF16) for l in range(2)]
    MOD = P.dram("MOD", [8, 128, 2, 12], F32)
    GF = P.dram("GF", [1024, S], BF16)
    GT = P.dram("GT", [1024 + S, 2560], BF16)
    HL = P.dram("HL", [65, 128, KC, 2], F32)
    OT = P.dram("OT", [T, 2560], BF16)
    obfA = [P.dram(f"obf{r}", [LA_BF_ROWS, T], BF16) for r in range(8)]
    of32A = [P.dram(f"of32{r}", [LA_F32_ROWS, T], F32) for r in range(8)]
    xmidA = [P.dram(f"xmid{r}", [128, KC, T], F32) for r in range(8)]
    xnextA = [P.dram(f"xnext{r}", [128, KC, T], F32) for r in range(8)]

    P.phase_begin()
    Zt = P.sb("Zt", [128, 8192], BF16)
    Zf = P.sb("Zf", [128, 32], F32)
    P.op("pool", lambda e: e.memset(Zt[:], 0.0), writes=[Zt])
    P.op("pool", lambda e: e.memset(Zf[:], 0.0), writes=[Zf])
    for b0 in range(0, 8, 2):
        P.dma(GT[b0 * 128:(b0 + 2) * 128, :].rearrange("(b p) d -> p b d", p=128), Zt[:, 0:5120].rearrange("p (b d) -> p b d", d=2560),
              reads=[Zt], writes=[GT])
    P.dma(HL[0, :, :, :].rearrange("p k t -> p (k t)"), Zf[:, :], reads=[Zf], writes=[HL])
    wst = RR([P.sb(f"Wst{i}", [128, KC, 128], F32) for i in range(3)])
    wcb = RR([P.sb(f"Wcb{i}", [128, KC, 128], BF16) for i in range(3)])
    cast_engs = ["pool", "dve", "act"]
    for j in range(NU):
        st, wb = wst.next(), wcb.next()
        P.dma(st[:, :, :], wsh[j], writes=[st])
        ce = cast_engs[j % 3]
        if ce == "act":
            P.op("act", lambda e: e.copy(out=wb[:, :, :], in_=st[:, :, :]), reads=[st], writes=[wb])
        else:
            P.op(ce, lambda e: e.tensor_copy(out=wb[:, :, :], in_=st[:, :, :]), reads=[st], writes=[wb])
        P.dma(WoutL[j // NU_LAYER][j % NU_LAYER], wb[:, :, :], reads=[wb], writes=[WoutL[j // NU_LAYER]], q="pool")
    ct = P.sb("ct", [128, KC], F32)
    ca = P.sb("ca", [128, KC], F32)
    wt = P.sb("wadat", [128, KC, 1536], F32)
    bt = P.sb("badat", [128, 2, 12], F32)
    moR = RR([P.sb(f"modown{i}", [128, 2, 12], F32) for i in range(2)])
    pmR = RR([P.ps(f"pmod{i}") for i in range(2)])
    P.dma(ct[:], c_in[:, :], writes=[ct])
    P.op("act", lambda e: e.activation(out=ca[:], in_=ct[:], func=AF.Silu), reads=[ct], writes=[ca])
    for r in range(8):
        pmod, mo = pmR.next(), moR.next()
        P.dma(bt[:], bada[r], writes=[bt])
        for l in range(2):
            for q in range(4):
                P.dma(wt[:, 4 * q:4 * q + 4, :], wada[l, r, :, 4 * q:4 * q + 4, :], writes=[wt])
            for fch in range(12):
                for kc in range(KC):
                    P.op("pe", lambda e, fch=fch, kc=kc, l=l: e.matmul(pmod[:, l * 12 + fch:l * 12 + fch + 1], lhsT=wt[:, kc, fch * 128:(fch + 1) * 128],
                                                                    rhs=ca[:, kc:kc + 1], start=(kc == 0), stop=(kc == KC - 1)),
                         reads=[wt, ca], writes=[pmod], inc=(kc == KC - 1))
        P.op("dve", lambda e: e.tensor_tensor(out=mo[:, :, :], in0=pmod[:, 0:24].rearrange("p (l j) -> p l j", l=2), in1=bt[:, :, :], op=ALU.add),
             reads=[pmod, bt], writes=[mo])
        P.dma(MOD[r], mo[:, :, :], reads=[mo], writes=[MOD])
    P.phase_end()

    xcurA = [Buf(xT_in[r], f"xin{r}") for r in range(8)]
    for l in range(2):
        u0 = 0
        Wout = WoutL[l]
        for rank in range(8):
            xcur, obf, of32 = xcurA[rank], obfA[rank], of32A[rank]
            P.phase_begin()
            modTd = Buf(MOD[:, :, l, :].rearrange("r p j -> p r j"), "modTd")
            d = dict(xT=xcur, modT=modTd, nmix=Buf(nmix[l], "nm"), pos=Buf(posA[rank], "pos"), cst=Buf(cst[l], "cs"), dec=dec, bones=bones,
                     wsl=Buf(Wout[u0 + U_OFF["la"]:u0 + U_OFF["la"] + 112], "wsl"), obf=obf, of32=of32)
            la_phase(P, d, True)
            identS = P.sb("identS", [128, 128], F32)
            P.dma(identS[:], ident[:, :], writes=[identS])
            tin = RR([P.sb(f"tin{i}", [128, T], BF16) for i in range(2)])
            tf = RR([P.sb(f"tf{i}", [128, T], F32) for i in range(2)])
            tout = RR([P.sb(f"tout{i}", [128, 8, 128], BF16) for i in range(2)])
            tps = RR([P.ps(f"tps{i}") for i in range(2)])
            jobs = [(LA_BF["vsb"] + k * 128, GT_V_SB + k * 128) for k in range(4)] + [(LA_BF["dv"] + k * 128, GT_V_D + k * 128) for k in range(4)] + \
                   [(LA_BF["rkz"] + k * 128, GT_KZ + k * 128) for k in range(4)] + [(LA_BF["rv"] + k * 128, GT_RV + k * 128) for k in range(8)]
            GTv = GT[1024:, :].rearrange("(i r p) d -> p i r d", i=8, r=8)
            for row, col in jobs:
                ti, tff, to = tin.next(), tf.next(), tout.next()
                P.dma(ti[:, :], obf[row:row + 128, :], reads=[obf], writes=[ti])
                P.op("dve", lambda e: e.tensor_copy(out=tff[:, :], in_=ti[:, :]), reads=[ti], writes=[tff])
                for hb in range(2):
                    tp = tps.next()
                    for b4 in range(4):
                        b = hb * 4 + b4
                        P.op("pe", lambda e, b=b, b4=b4: e.transpose(tp[:, b4 * 128:(b4 + 1) * 128], tff[:, b * 128:(b + 1) * 128], identS[:, :]),
                             reads=[tff, identS], writes=[tp])
                    P.op("act", lambda e, hb=hb: e.copy(out=to[:, hb * 4:(hb + 1) * 4, :], in_=tp[:, :].rearrange("p (b f) -> p b f", f=128)),
                         reads=[tp], writes=[to])
                P.dma(GTv[:, :, rank, col:col + 128], to[:, :, :], reads=[to], writes=[GT], q="pool")
            GFv = GF[:, :].rearrange("f (i r t) -> f i r t", i=8, r=8)
            P.dma(GFv[0:512, :, rank, :], obf[LA_BF["ksb"]:LA_BF["ksb"] + 512, :].rearrange("f (i t) -> f i t", i=8), reads=[obf], writes=[GF])
            P.dma(GFv[512:1024, :, rank, :], obf[LA_BF["dk"]:LA_BF["dk"] + 512, :].rearrange("f (i t) -> f i t", i=8), reads=[obf], writes=[GF])
            P.phase_end()
        for rank in range(8):
            xcur, obf, of32, xmid = xcurA[rank], obfA[rank], of32A[rank], xmidA[rank]
            P.phase_begin()
            modS = P.sb("modS", [128, 96], F32)
            P.dma(modS[:, :].rearrange("p (r j) -> p r j", r=8), MOD[:, :, l, :].rearrange("r p j -> p r j"), reads=[MOD], writes=[modS])
            srcs = dict(uincl=uincl, ident=ident, masks=Buf(masksA[rank], "masks"), dmask=dmask, don=Buf(dond[l], "don"), gn=Buf(gnd[l], "gn"),
                        lamv=Buf(lamd[l], "lamv"))
            c = lb_consts(P, srcs)
            c["hs"] = P.sb("hs", [128, 8, KC, 2], F32)
            lb_load_consts(P, c, srcs)
            lb_load_layer(P, c, srcs, LAM_INIT[l])
            lb_phase(P, c, obf, of32, GF, GT, xcur, xmid, modS, Buf(Wout[u0 + U_OFF["br"]:u0 + U_OFF["br"] + 16], "wbr"),
                     Buf(Wout[u0 + U_OFF["out"]:u0 + U_OFF["out"] + 16], "wout"), rank, LAM_INIT[l])
            HLv = HL[1:65, :, :, :].rearrange("(i r) p k t -> p i r (k t)", i=8, r=8)
            P.dma(HLv[:, :, rank, :], c["hs"][:, :, :, :].rearrange("p i k t -> p i (k t)"), reads=[c["hs"]], writes=[HL])
            P.phase_end()
        for rank in range(8):
            xmid = xmidA[rank]
            P.phase_begin()
            modS = P.sb("modS", [128, 96], F32)
            P.dma(modS[:, :].rearrange("p (r j) -> p r j", r=8), MOD[:, :, l, :].rearrange("r p j -> p r j"), reads=[MOD], writes=[modS])
            cc = lc_alloc(P)
            cc["xhi"] = P.sb("xhi", [128, 8, KC * 2], F32)
            psR = RR([P.ps(f"cps{i}") for i in range(8)])
            xdst = Buf(xfin[rank], "xfin") if l == 1 else xnextA[rank]
            HLr = HL[0:64, :, :, :].rearrange("(i r) p k t -> p i r (k t)", i=8, r=8)
            lc_phase(P, cc, psR, xmid, HLr[:, :, rank, :], HL, xdst, modS, Buf(nffn[l], "nf"), Buf(cwd[l], "cw"), Buf(cbd[l], "cb"), Buf(flagA[rank], "flag"),
                     Buf(Wout[u0 + U_OFF["up"]:u0 + U_OFF["up"] + 88], "wup"),
                     Buf(Wout[u0 + U_OFF["dn"]:u0 + U_OFF["dn"] + 48].rearrange("(f g) p k n -> f p g k n", g=3), "wdn"), l == 1)
            P.phase_end()
        xcurA = xnextA
    print("n_ins", P.n_ins)
    return P.finish()


def _weight_units(inputs):
    us = []
    for l in range(2):
        us.append(la_weights(inputs["w_in"][l], inputs["w_gate"][l]))
        us.append(slabs(inputs["w_branch"][l], KC))
        us.append(slabs(inputs["w_out"][l], KC))
        us.append(slabs(inputs["w_up"][l], KC))
        wd = slabs(inputs["w_down"][l], 44)
        wdp = np.zeros((16, 128, 48, 128), np.float32)
        wdp[:, :, :44, :] = wd
        us.append(np.ascontiguousarray(wdp.reshape(16, 128, 3, 16, 128).transpose(0, 2, 1, 3, 4)).reshape(48, 128, 16, 128))
    return np.concatenate(us, axis=0)


def kernel(**inputs):
    inputs = {k: np.asarray(v) for k, v in inputs.items()}
    nc = get_nc("single", build_single)
    x = inputs["x"][0]
    units = _weight_units(inputs)
    c_in = np.ascontiguousarray(inputs["c"].reshape(KC, 128).T)
    nmix = np.stack([vec_pm(inputs["norm_mix"][l]) for l in range(2)])
    nffn = np.stack([vec_pm(inputs["norm_ffn"][l]) for l in range(2)])
    lc = [la_consts(inputs["diff_qn"][l], inputs["diff_kn"][l]) for l in range(2)]
    cst = np.stack([lc[l][0] for l in range(2)])
    dec, bones = lc[0][1], lc[0][2]
    cw = np.stack([np.ascontiguousarray(inputs["conv_w"][l].reshape(3, 88, 128).transpose(2, 0, 1)) for l in range(2)])
    cb = np.stack([vec_pm(inputs["conv_b"][l]) for l in range(2)])
    don = np.stack([np.ascontiguousarray(np.broadcast_to(inputs["diff_on"][l][None, :], (128, 128))) for l in range(2)]).astype(np.float32)
    gn = np.stack([np.concatenate([vec_pm(inputs["ret_gn_g"][l]), vec_pm(inputs["ret_gn_b"][l])], 1) for l in range(2)])
    lamv = np.stack([np.stack([inputs["lam_q1"][l], inputs["lam_k1"][l], inputs["lam_q2"][l], inputs["lam_k2"][l]], 1) for l in range(2)]).astype(np.float32)
    xT, pos, masks, flags, bada = [], [], [], [], []
    for r in range(NCORE):
        tok = own_tokens(r)
        uincl, ident, mk, dmask = lb_host_consts(r)
        flag = np.ones((128, 8, 2), np.float32)
        if r == 0:
            flag[:, 0, :] = 0.0
        xT.append(to_fm(x[tok]))
        pos.append(np.ascontiguousarray(np.broadcast_to(inputs["positions"][0, tok][None, :], (128, T))).astype(np.int32))
        masks.append(mk)
        flags.append(flag)
        bada.append(np.ascontiguousarray(inputs["b_ada"][:, r * 1536:(r + 1) * 1536].reshape(2, 12, 128).transpose(2, 0, 1)))
    wa = np.ascontiguousarray(inputs["w_ada"].reshape(2, KC, 128, 8, 1536).transpose(0, 3, 2, 1, 4))
    im = dict(xT=np.stack(xT), pos=np.stack(pos), c_in=c_in, wada=wa, bada=np.stack(bada), wsh=units,
              nmix=nmix, nffn=nffn, cst=cst, cw=cw, cb=cb, don=don, gn=gn, lamv=lamv, dec=dec, bones=bones,
              uincl=uincl, ident=ident, masks=np.stack(masks), dmask=dmask, flag=np.stack(flags))
    res = run_bass_kernel_spmd(nc, [im], core_ids=[0]).results
    xo = res[0]["xoutT"]
    out = np.zeros((S, D), np.float32)
    for r in range(NCORE):
        out[own_tokens(r)] = xo[r].transpose(2, 1, 0).reshape(T, D)
    return out[None]
```
